# Optimizing a Trainium2 kernel written in Bass

```python
import math
import jax, jax.numpy as jnp
from jax import lax
import numpy as np

D_MODEL = 1024
BATCH = 8
SEQ = 2048
DEPTH = 4
DEC_BATCH = 128
DEC_SEQ = 4
PAST_LEN = 16384
PAGE_SIZE = 128

MIX_WIDTH = D_MODEL
DN_HEADS = 4
DN_HEAD_DIM = MIX_WIDTH // 8
DN_WIDTH = DN_HEADS * DN_HEAD_DIM
DN_CONV = 4
DN_CHUNK = 64
SSM_WIDTH = MIX_WIDTH // 4
SSM_GROUP = 16
SSM_GROUPS = SSM_WIDTH // SSM_GROUP
SSM_STATE = 64
POOL_WIDTH = MIX_WIDTH - DN_WIDTH - SSM_WIDTH
POOL_WINDOWS = (2, 4, 8, 16)
POOL_GROUPS = len(POOL_WINDOWS)
POOL_GROUP = POOL_WIDTH // POOL_GROUPS
POOL_BUF = max(POOL_WINDOWS) - 1
D_FF = -(-8 * D_MODEL // (3 * 256)) * 256
EPS = 1e-6

QKV_WIDTH = 3 * DN_WIDTH
OFF_A = QKV_WIDTH
OFF_B = OFF_A + DN_HEADS
OFF_G = OFF_B + DN_HEADS
OFF_SSM = OFF_G + DN_WIDTH
OFF_POOL = OFF_SSM + SSM_WIDTH
IN_WIDTH = OFF_POOL + POOL_WIDTH

F32 = jnp.float32

kernel_name = 'hybrid_deltanet_s5_pool_decoder_step'


def rms_norm(x, w):
    x32 = x.astype(F32)
    y = x32 * lax.rsqrt(jnp.mean(x32 * x32, axis=-1, keepdims=True) + EPS)
    return (y * w.astype(F32)).astype(x.dtype)


def l2_norm(x):
    return x * lax.rsqrt(jnp.sum(x * x, axis=-1, keepdims=True) + EPS)


def causal_dwconv(u, buf, w):
    T = u.shape[1]
    ext = jnp.concatenate([buf.astype(u.dtype), u], axis=1)
    ext32 = ext.astype(F32)
    w32 = w.astype(F32)
    y = ext32[:, 0:T] * w32[0]
    for j in range(1, DN_CONV):
        y = y + ext32[:, j:j + T] * w32[j]
    return y, ext[:, -(DN_CONV - 1):]


def gated_delta_rule(q, k, v, beta, log_g, s0):
    B, T, H, Dk = q.shape
    Dv = v.shape[-1]
    C = math.gcd(T, DN_CHUNK)
    N = T // C

    def chunk(t):
        t = t.reshape((B, N, C, H) + t.shape[3:])
        return jnp.moveaxis(t, (1, 3), (0, 2))

    qc, kc, vc = chunk(q), chunk(k), chunk(v)
    bc, gc = chunk(beta), chunk(log_g)
    gam = jnp.cumsum(gc, axis=-1)
    idx = jnp.arange(C)
    incl = idx[:, None] >= idx[None, :]
    strict = idx[:, None] > idx[None, :]
    diff = gam[..., :, None] - gam[..., None, :]
    decay = jnp.where(incl, jnp.exp(jnp.where(incl, diff, 0.0)), 0.0)
    kb = kc * bc[..., None]
    a_mat = jnp.where(strict, jnp.einsum('nbhtd,nbhsd->nbhts', kb, kc) * decay, 0.0)
    rhs = jnp.concatenate([vc * bc[..., None], kb * jnp.exp(gam)[..., None]], axis=-1)
    sol = lax.linalg.triangular_solve(a_mat, rhs, left_side=True, lower=True, unit_diagonal=True)
    w_val, k_cum = sol[..., :Dv], sol[..., Dv:]
    qk = jnp.einsum('nbhtd,nbhsd->nbhts', qc, kc) * decay
    q_dec = qc * jnp.exp(gam)[..., None]
    k_dec = kc * jnp.exp(gam[..., -1:] - gam)[..., None]
    g_last = jnp.exp(gam[..., -1])[..., None, None]

    def step(s, xs):
        w_n, kcum_n, qk_n, qdec_n, kdec_n, gl_n = xs
        u = w_n - jnp.einsum('bhck,bhkv->bhcv', kcum_n, s)
        o = jnp.einsum('bhck,bhkv->bhcv', qdec_n, s) + jnp.einsum('bhts,bhsv->bhtv', qk_n, u)
        s = s * gl_n + jnp.einsum('bhck,bhcv->bhkv', kdec_n, u)
        return s, o

    s_fin, o = lax.scan(step, s0, (w_val, k_cum, qk, q_dec, k_dec, g_last))
    o = jnp.moveaxis(o, (0, 2), (1, 3)).reshape(B, T, H, Dv)
    return o, s_fin


def delta_mixer(proj, conv_buf, s0, conv_w, a_log, dt_bias, out_norm):
    B, T, _ = proj.shape
    qkv, conv_new = causal_dwconv(proj[..., :QKV_WIDTH], conv_buf, conv_w)
    qkv = jax.nn.silu(qkv).reshape(B, T, 3, DN_HEADS, DN_HEAD_DIM)
    q = l2_norm(qkv[:, :, 0]) * (DN_HEAD_DIM ** -0.5)
    k = l2_norm(qkv[:, :, 1])
    v = qkv[:, :, 2]
    a = proj[..., OFF_A:OFF_B].astype(F32)
    b = proj[..., OFF_B:OFF_G].astype(F32)
    log_g = -jnp.exp(a_log.astype(F32)) * jax.nn.softplus(a + dt_bias.astype(F32))
    beta = jax.nn.sigmoid(b)
    o, s_new = gated_delta_rule(q, k, v, beta, log_g, s0.astype(F32))
    gate = proj[..., OFF_G:OFF_SSM].astype(F32).reshape(B, T, DN_HEADS, DN_HEAD_DIM)
    o = o * lax.rsqrt(jnp.mean(o * o, axis=-1, keepdims=True) + EPS) * out_norm.astype(F32) * jax.nn.silu(gate)
    return o.reshape(B, T, DN_WIDTH), conv_new, s_new


def ssm_mixer(u, h0_re, h0_im, a_re, a_im, log_dt, b_re, b_im, c_re, c_im, d_skip, glu_w, glu_b):
    B, T, _ = u.shape
    u32 = u.astype(F32).reshape(B, T, SSM_GROUPS, SSM_GROUP)
    lam = lax.complex(a_re.astype(F32), a_im.astype(F32))
    dt = jnp.exp(log_dt.astype(F32))[:, None]
    lam_bar = jnp.exp(lam * dt)
    b_c = lax.complex(b_re.astype(F32), b_im.astype(F32))
    b_bar = ((lam_bar - 1.0) / lam)[..., None] * b_c
    bu = jnp.einsum('gph,btgh->btgp', b_bar, u32.astype(jnp.complex64))
    a_seq = jnp.broadcast_to(lam_bar, bu.shape)

    def combine(e1, e2):
        a1, x1 = e1
        a2, x2 = e2
        return a2 * a1, a2 * x1 + x2

    a_cum, h = lax.associative_scan(combine, (a_seq, bu), axis=1)
    h0 = lax.complex(h0_re.astype(F32), h0_im.astype(F32))
    h = h + a_cum * h0[:, None]
    c_c = lax.complex(c_re.astype(F32), c_im.astype(F32))
    y = jnp.real(jnp.einsum('ghp,btgp->btgh', c_c, h)) + d_skip.astype(F32).reshape(SSM_GROUPS, SSM_GROUP) * u32
    y = jax.nn.gelu(y.reshape(B, T, SSM_WIDTH))
    y = y * jax.nn.sigmoid(y @ glu_w.astype(F32) + glu_b.astype(F32))
    h_last = h[:, -1]
    return y, jnp.real(h_last), jnp.imag(h_last)


def pool_mixer(u, buf, pos0, pool_w, pool_scale):
    B, T, _ = u.shape
    u32 = u.astype(F32)
    ext = jnp.concatenate([buf.astype(u.dtype), u], axis=1)
    cs = jnp.concatenate([jnp.zeros((B, 1, POOL_WIDTH), F32), jnp.cumsum(ext.astype(F32), axis=1)], axis=1)
    pos = pos0 + jnp.arange(T)
    outs = []
    for g, w in enumerate(POOL_WINDOWS):
        lo, hi = g * POOL_GROUP, (g + 1) * POOL_GROUP
        win_sum = cs[:, POOL_BUF + 1:POOL_BUF + 1 + T, lo:hi] - cs[:, POOL_BUF + 1 - w:POOL_BUF + 1 - w + T, lo:hi]
        cnt = jnp.minimum(pos + 1, w).astype(F32)[None, :, None]
        r = win_sum / cnt - u32[..., lo:hi]
        outs.append(r @ pool_w[g].astype(F32))
    y = jnp.concatenate(outs, axis=-1) * pool_scale.astype(F32)
    return y, ext[:, -POOL_BUF:]


def layer(x, st, pos0, lp):
    s_delta, s_conv, s_re, s_im, s_pool = st
    h = rms_norm(x, lp['norm_mix_pre'])
    proj = h @ lp['w_in']
    o_dn, conv_new, delta_new = delta_mixer(proj, s_conv, s_delta, lp['conv_w'], lp['dn_a_log'],
                                            lp['dn_dt_bias'], lp['dn_out_norm'])
    o_ssm, re_new, im_new = ssm_mixer(proj[..., OFF_SSM:OFF_POOL], s_re, s_im, lp['ssm_a_re'], lp['ssm_a_im'],
                                      lp['ssm_log_dt'], lp['ssm_b_re'], lp['ssm_b_im'], lp['ssm_c_re'],
                                      lp['ssm_c_im'], lp['ssm_d'], lp['ssm_glu_w'], lp['ssm_glu_b'])
    o_pool, pool_new = pool_mixer(proj[..., OFF_POOL:], s_pool, pos0, lp['pool_w'], lp['pool_scale'])
    mix = jnp.concatenate([o_dn, o_ssm, o_pool], axis=-1).astype(x.dtype) @ lp['w_out']
    x = x + rms_norm(mix, lp['norm_mix_post'])
    h = rms_norm(x, lp['norm_ffn_pre'])
    f = (jax.nn.silu(h @ lp['ffn_w_gate']) * (h @ lp['ffn_w_up'])) @ lp['ffn_w_down']
    x = x + rms_norm(f, lp['norm_ffn_post'])
    return x, (delta_new, conv_new, re_new, im_new, pool_new)


def setup_inputs(seed: int = 0) -> dict:
    key = jax.random.key(seed)
    ks = iter(jax.random.split(key, 40))

    def nrm(shape, scale):
        return jax.random.normal(next(ks), shape, F32) * scale

    def gain(shape):
        return 1.0 + nrm(shape, 0.01)

    dn_dt = jnp.exp(jax.random.uniform(next(ks), (DEPTH, DN_HEADS), F32, math.log(1e-3), math.log(1e-1)))
    return {
        'x_prompt': nrm((BATCH, SEQ, D_MODEL), 1.0),
        'x_sample': nrm((DEC_BATCH, DEC_SEQ, D_MODEL), 1.0),
        'state_delta': nrm((DEPTH, DEC_BATCH, DN_HEADS, DN_HEAD_DIM, DN_HEAD_DIM), 0.05),
        'state_conv': nrm((DEPTH, DEC_BATCH, DN_CONV - 1, QKV_WIDTH), 1.0),
        'state_ssm_re': nrm((DEPTH, DEC_BATCH, SSM_GROUPS, SSM_STATE), 0.1),
        'state_ssm_im': nrm((DEPTH, DEC_BATCH, SSM_GROUPS, SSM_STATE), 0.1),
        'state_pool': nrm((DEPTH, DEC_BATCH, POOL_BUF, POOL_WIDTH), 1.0),
        'norm_mix_pre': gain((DEPTH, D_MODEL)),
        'norm_mix_post': gain((DEPTH, D_MODEL)),
        'norm_ffn_pre': gain((DEPTH, D_MODEL)),
        'norm_ffn_post': gain((DEPTH, D_MODEL)),
        'w_in': nrm((DEPTH, D_MODEL, IN_WIDTH), D_MODEL ** -0.5),
        'conv_w': nrm((DEPTH, DN_CONV, QKV_WIDTH), DN_CONV ** -0.5),
        'dn_a_log': jnp.log(jax.random.uniform(next(ks), (DEPTH, DN_HEADS), F32, 1.0, 16.0)),
        'dn_dt_bias': dn_dt + jnp.log(-jnp.expm1(-dn_dt)),
        'dn_out_norm': gain((DEPTH, DN_HEAD_DIM)),
        'ssm_a_re': -0.5 + nrm((DEPTH, SSM_GROUPS, SSM_STATE), 0.01),
        'ssm_a_im': jnp.pi * jnp.arange(SSM_STATE, dtype=F32) + nrm((DEPTH, SSM_GROUPS, SSM_STATE), 0.01),
        'ssm_log_dt': jax.random.uniform(next(ks), (DEPTH, SSM_GROUPS), F32, math.log(1e-3), math.log(1e-1)),
        'ssm_b_re': nrm((DEPTH, SSM_GROUPS, SSM_STATE, SSM_GROUP), (2 * SSM_GROUP) ** -0.5),
        'ssm_b_im': nrm((DEPTH, SSM_GROUPS, SSM_STATE, SSM_GROUP), (2 * SSM_GROUP) ** -0.5),
        'ssm_c_re': nrm((DEPTH, SSM_GROUPS, SSM_GROUP, SSM_STATE), SSM_STATE ** -0.5),
        'ssm_c_im': nrm((DEPTH, SSM_GROUPS, SSM_GROUP, SSM_STATE), SSM_STATE ** -0.5),
        'ssm_d': nrm((DEPTH, SSM_WIDTH), 1.0),
        'ssm_glu_w': nrm((DEPTH, SSM_WIDTH, SSM_WIDTH), SSM_WIDTH ** -0.5),
        'ssm_glu_b': nrm((DEPTH, SSM_WIDTH), 0.01),
        'pool_w': nrm((DEPTH, POOL_GROUPS, POOL_GROUP, POOL_GROUP), POOL_GROUP ** -0.5),
        'pool_scale': gain((DEPTH, POOL_WIDTH)),
        'w_out': nrm((DEPTH, MIX_WIDTH, D_MODEL), MIX_WIDTH ** -0.5),
        'ffn_w_gate': nrm((DEPTH, D_MODEL, D_FF), D_MODEL ** -0.5),
        'ffn_w_up': nrm((DEPTH, D_MODEL, D_FF), D_MODEL ** -0.5),
        'ffn_w_down': nrm((DEPTH, D_FF, D_MODEL), D_FF ** -0.5),
    }


def reference(x_prompt, x_sample, state_delta, state_conv, state_ssm_re, state_ssm_im, state_pool,
              norm_mix_pre, norm_mix_post, norm_ffn_pre, norm_ffn_post, w_in, conv_w, dn_a_log, dn_dt_bias,
              dn_out_norm, ssm_a_re, ssm_a_im, ssm_log_dt, ssm_b_re, ssm_b_im, ssm_c_re, ssm_c_im, ssm_d,
              ssm_glu_w, ssm_glu_b, pool_w, pool_scale, w_out, ffn_w_gate, ffn_w_up, ffn_w_down):
    B = x_prompt.shape[0]
    zero_state = (
        jnp.zeros((B, DN_HEADS, DN_HEAD_DIM, DN_HEAD_DIM), F32),
        jnp.zeros((B, DN_CONV - 1, QKV_WIDTH), x_prompt.dtype),
        jnp.zeros((B, SSM_GROUPS, SSM_STATE), F32),
        jnp.zeros((B, SSM_GROUPS, SSM_STATE), F32),
        jnp.zeros((B, POOL_BUF, POOL_WIDTH), x_prompt.dtype),
    )
    xp, xs = x_prompt, x_sample
    new_p, new_s = [], []
    for l in range(DEPTH):
        lp = {
            'norm_mix_pre': norm_mix_pre[l], 'norm_mix_post': norm_mix_post[l],
            'norm_ffn_pre': norm_ffn_pre[l], 'norm_ffn_post': norm_ffn_post[l],
            'w_in': w_in[l], 'conv_w': conv_w[l], 'dn_a_log': dn_a_log[l], 'dn_dt_bias': dn_dt_bias[l],
            'dn_out_norm': dn_out_norm[l], 'ssm_a_re': ssm_a_re[l], 'ssm_a_im': ssm_a_im[l],
            'ssm_log_dt': ssm_log_dt[l], 'ssm_b_re': ssm_b_re[l], 'ssm_b_im': ssm_b_im[l],
            'ssm_c_re': ssm_c_re[l], 'ssm_c_im': ssm_c_im[l], 'ssm_d': ssm_d[l],
            'ssm_glu_w': ssm_glu_w[l], 'ssm_glu_b': ssm_glu_b[l], 'pool_w': pool_w[l],
            'pool_scale': pool_scale[l], 'w_out': w_out[l], 'ffn_w_gate': ffn_w_gate[l],
            'ffn_w_up': ffn_w_up[l], 'ffn_w_down': ffn_w_down[l],
        }
        xp, sp = layer(xp, zero_state, 0, lp)
        st_l = (state_delta[l], state_conv[l], state_ssm_re[l], state_ssm_im[l], state_pool[l])
        xs, ss = layer(xs, st_l, PAST_LEN, lp)
        new_p.append(sp)
        new_s.append(ss)
    delta_p = jnp.stack([s[0] for s in new_p])
    conv_p = jnp.stack([s[1] for s in new_p])
    ssm_re_p = jnp.stack([s[2] for s in new_p])
    ssm_im_p = jnp.stack([s[3] for s in new_p])
    pool_p = jnp.stack([s[4] for s in new_p])
    delta_s = jnp.stack([s[0] for s in new_s])
    conv_s = jnp.stack([s[1] for s in new_s])
    ssm_re_s = jnp.stack([s[2] for s in new_s])
    ssm_im_s = jnp.stack([s[3] for s in new_s])
    pool_s = jnp.stack([s[4] for s in new_s])
    return (xp, xs, delta_p, conv_p, ssm_re_p, ssm_im_p, pool_p, delta_s, conv_s, ssm_re_s, ssm_im_s, pool_s)
```

```python
import contextlib
import math
import numpy as np
import concourse.bass as bass
import concourse.mybir as mybir
from concourse.bass_utils import run_bass_kernel_spmd

F32 = mybir.dt.float32
BF16 = mybir.dt.bfloat16
I32 = mybir.dt.int32
ALU = mybir.AluOpType
AF = mybir.ActivationFunctionType

L = 4
NP_, NS, NT = 2048, 64, 2112
TW = 256
EPS = 1e-6
INW = 2568
OFF_A, OFF_G, OFF_SSM, OFF_POOL = 1536, 1544, 2056, 2312
DFF = 2816
NKS = 11


class Buf:
    __slots__ = ("w", "r")

    def __init__(self):
        self.w = None
        self.r = []


class Prog:
    ENGS = ("pe", "act", "dve", "pool", "sp")

    def __init__(self, nc, stack, n_dma_sems=40):
        self.nc = nc
        self.sem = {e: stack.enter_context(nc.semaphore("s_" + e)) for e in self.ENGS}
        self.cnt = {e: 0 for e in self.ENGS}
        self.seen = {e: {} for e in self.ENGS}
        self.stream = {e: [] for e in self.ENGS}
        self.dsem = [stack.enter_context(nc.semaphore("d%d" % i)) for i in range(n_dma_sems)]
        self.dcnt = [0] * n_dma_sems

    def _need(self, eng, dep):
        if dep is None:
            return
        if dep[0] == "dma":
            key, val, sem = ("d", dep[1]), dep[2], self.dsem[dep[1]]
        else:
            e2, idx = dep
            if e2 == eng and (eng == "pe" or idx <= self.cnt[eng] - 2):
                return
            key, val, sem = ("e", e2), idx, self.sem[e2]
        if self.seen[eng].get(key, 0) >= val:
            return
        self.seen[eng][key] = val
        self.stream[eng].append(("wait", sem, val))

    def _deps(self, eng, reads, writes):
        deps = []
        for b in reads:
            deps.append(b.w)
        for b in writes:
            deps.append(b.w)
            deps.extend(b.r)
        best = {}
        for d in deps:
            if d is None:
                continue
            key = (d[0], d[1]) if d[0] == "dma" else ("e", d[0])
            val = d[2] if d[0] == "dma" else d[1]
            if key not in best or val > best[key][0]:
                best[key] = (val, d)
        for key in best:
            self._need(eng, best[key][1])

    def _mark(self, me, reads, writes):
        for b in reads:
            b.r.append(me)
            if len(b.r) > 64:
                b.r = b.r[-48:]
        for b in writes:
            b.w = me
            b.r = []

    def op(self, eng, fn, reads=(), writes=()):
        self._deps(eng, reads, writes)
        self.cnt[eng] += 1
        self.stream[eng].append(("inst", fn, self.sem[eng], 1))
        self._mark((eng, self.cnt[eng]), reads, writes)

    def dma(self, q, out, in_, semi, reads=(), writes=(), **kw):
        self._deps(q, reads, writes)
        self.dcnt[semi] += 16
        self.stream[q].append(("inst", lambda h: h.dma_start(out=out, in_=in_, **kw), self.dsem[semi], 16))
        self._mark(("dma", semi, self.dcnt[semi]), reads, writes)

    def barrier(self):
        for e in self.ENGS:
            for e2 in self.ENGS:
                if e2 != e and self.cnt[e2]:
                    self._need(e, (e2, self.cnt[e2]))
            for i, c in enumerate(self.dcnt):
                if c:
                    self._need(e, ("dma", i, c))

    def emit(self):
        nc = self.nc
        with nc.Block() as block:
            def run(e):
                def body(h):
                    for it in self.stream[e]:
                        if it[0] == "wait":
                            h.wait_ge(it[1], it[2])
                        else:
                            it[1](h).then_inc(it[2], it[3])
                return body
            block.tensor(run("pe"))
            block.scalar(run("act"))
            block.vector(run("dve"))
            block.gpsimd(run("pool"))
            block.sync(run("sp"))


MARKS = []


def build_program(NL=L, dbg=None, skip=()):
    nc = bass.Bass("TRN2", target_bir_lowering=False)
    din = lambda n, s, d=F32: nc.dram_tensor(n, list(s), d, kind="ExternalInput").ap()
    dout = lambda n, s, d=F32: nc.dram_tensor(n, list(s), d, kind="ExternalOutput").ap()
    I = {}
    I["xT"] = din("xT", [128, 8, NT])
    I["w_in"] = din("w_in", [L, 128, 8, INW])
    I["w_out"] = din("w_out", [L, 128, 8, 1024])
    I["w_gu"] = din("w_gu", [L, 22, 128, 2, 8, 128])
    I["w_down"] = din("w_down", [L, 128, 22, 1024])
    I["nw"] = din("nw", [128, L, 4, 8])
    I["cw"] = din("cw", [128, L, 12, 4])
    I["dnp"] = din("dnp", [64, L, 2, 4])
    I["onw"] = din("onw", [128, L])
    I["sdel"] = din("sdel", [L, 16, 128, 4, 128])
    I["sconv"] = din("sconv", [128, L, 12, 16, 3])
    I["spool"] = din("spool", [128, L, 2, 16, 15])
    I["spool_nat"] = din("spool_nat", [L, 16, 15, 256])
    I["lamp"] = din("lamp", [128, L, 16, 3])
    I["bnat"] = din("bnat", [128, L, 16, 2, 16])
    I["cnat"] = din("cnat", [128, L, 16, 16])
    I["sssm"] = din("sssm", [128, L, 16, 16])
    I["ssmd"] = din("ssmd", [128, L, 2])
    I["glub"] = din("glub", [128, L, 2])
    I["gluw"] = din("gluw", [L, 128, 2, 256])
    I["pw"] = din("pw", [L, 128, 2, 128])
    I["pscale"] = din("pscale", [128, L, 2])
    I["cst"] = din("cst", [128, 8, 128])
    O = {}
    O["yT"] = dout("yT", [128, 8, NT])
    O["o_delta_p"] = dout("o_delta_p", [L, 128, 4, 128])
    O["o_delta_s"] = dout("o_delta_s", [L, 16, 128, 4, 128])
    O["o_conv_p"] = dout("o_conv_p", [128, L, 12, 3])
    O["o_conv_s"] = dout("o_conv_s", [128, L, 12, 16, 3])
    O["o_ssm_p"] = dout("o_ssm_p", [128, L, 16])
    O["o_ssm_s"] = dout("o_ssm_s", [128, L, 16, 16])
    O["o_pool_p"] = dout("o_pool_p", [128, L, 2, 15])
    O["o_pool_s_old"] = dout("o_pool_s_old", [L, 16, 11, 256])
    O["o_pool_s_new"] = dout("o_pool_s_new", [128, L, 2, 16, 4])
    if dbg:
        O["dbgmix"] = dout("dbgmix", [128, 8, NT])
        O["dbgoo"] = dout("dbgoo", [128, 4, NT])

    with contextlib.ExitStack() as st:
        P = Prog(nc, st)
        sb = lambda n, s, d=F32: st.enter_context(nc.sbuf_tensor(n, list(s), d))
        PS = [st.enter_context(nc.psum_tensor("ps%d" % i, [128, 512], F32)) for i in range(7)]
        PSB = [Buf() for _ in range(7)]
        PT = st.enter_context(nc.psum_tensor("pt", [128, 1024], BF16))
        PTB = Buf()
        X = sb("X", [128, 8, NT]); XB = [Buf() for _ in range(9)]
        CST = sb("CST", [128, 8, 128]); CSTB = Buf()
        CB16 = sb("CB16", [128, 2, 128], BF16)
        NW = sb("NW", [128, L, 4, 8]); CW = sb("CW", [128, L, 12, 4]); DNP = sb("DNP", [64, L, 2, 4])
        ONW = sb("ONW", [128, L]); PSC = sb("PSC", [128, L, 2]); SSMD = sb("SSMD", [128, L, 2]); GLUB = sb("GLUB", [128, L, 2])
        PARB = Buf()
        EPSC = sb("EPSC", [128, 3])

        dmai = [0]

        def nsem(lo=22, hi=40):
            dmai[0] = (dmai[0] + 1) % (hi - lo)
            return lo + dmai[0]

        def mm(out, lhsT, rhs, rd, wr, start=True, stop=True):
            P.op("pe", lambda h: h.matmul(out, lhsT=lhsT, rhs=rhs, start=start, stop=stop), reads=rd, writes=wr)

        def tr(out, in_, ident, rd, wr):
            P.op("pe", lambda h: h.transpose(out, in_, ident), reads=rd, writes=wr)

        def act(out, in_, func, rd, wr, bias=0.0, scale=1.0, eng="act"):
            P.op("act", lambda h: h.activation(out, in_, func, bias=bias, scale=scale), reads=rd, writes=wr)

        def tt(out, a, b, op, rd, wr, eng="dve"):
            P.op(eng, lambda h: h.tensor_tensor(out, a, b, op), reads=rd, writes=wr)

        def ts(out, a, s1, s2, op0, op1, rd, wr, eng="dve"):
            if op1 is None:
                P.op(eng, lambda h: h.tensor_scalar(out, a, s1, None, op0), reads=rd, writes=wr)
            else:
                P.op(eng, lambda h: h.tensor_scalar(out, a, s1, s2, op0, op1), reads=rd, writes=wr)

        def stt(out, a, s, b, op0, op1, rd, wr, eng="dve"):
            P.op(eng, lambda h: h.scalar_tensor_tensor(out, a, s, b, op0, op1), reads=rd, writes=wr)

        def cp(out, in_, rd, wr, eng="dve"):
            if eng == "act":
                P.op("act", lambda h: h.copy(out, in_), reads=rd, writes=wr)
            else:
                P.op(eng, lambda h: h.tensor_copy(out, in_), reads=rd, writes=wr)

        def rcp(out, in_, rd, wr):
            P.op("dve", lambda h: h.reciprocal(out, in_), reads=rd, writes=wr)

        def mset(ap, v, wr, eng="pool"):
            P.op(eng, lambda h: h.memset(ap, v), writes=wr)

        P.dma("sp", CST[:], I["cst"][:], 20, writes=[CSTB])
        for t_, k_ in ((NW, "nw"), (CW, "cw"), (DNP, "dnp"), (ONW, "onw"), (PSC, "pscale"), (SSMD, "ssmd"), (GLUB, "glub")):
            P.dma("sp", t_[:], I[k_][:], 0, writes=[PARB])
        IDF = CST[:, 0, :]
        ONEF = CST[:, 1, :]
        LTRI = {64: CST[:, 2, :], 4: CST[:, 2, :]}
        MLT = {64: CST[:, 3, :], 4: CST[:, 3, :]}
        MSTR = {64: CST[:, 3, :], 4: CST[:, 3, :]}
        MINT = {64: CST[:, 2, :], 4: CST[:, 2, :]}
        SWP = CST[:, 4, :]
        SGN = CST[:, 5, 0:1]
        IDB = CB16[:, 0, :]
        ONEB = CB16[:, 1, :]
        CB16B = Buf()
        cp(CB16[:, 0, :], CST[:, 0, :], [CSTB], [CB16B])
        cp(CB16[:, 1, :], CST[:, 1, :], [CSTB], [CB16B])
        mset(EPSC[:, 0:1], EPS, [PARB], eng="dve")
        mset(EPSC[:, 1:2], 1.0, [PARB], eng="dve")
        mset(EPSC[:, 2:3], math.log(128.0 ** -0.5), [PARB], eng="dve")
        EPS_AP = EPSC[:, 0:1]
        LNQ_AP = EPSC[:, 2:3]

        TILES = [(i * TW, TW, i) for i in range(8)] + [(NP_, NS, 8)]
        for (c0, n, ti) in TILES:
            P.dma("sp", X[:, :, c0:c0 + n], I["xT"][:, :, c0:c0 + n], 1 + ti, writes=[XB[ti]])

        def rms_rstd(src_sq_fn, n, nk, rs_out, rsB, rd, inv_d, sq, sqB, bank, ring=None):
            if ring:
                for kc in range(nk):
                    act(sq[:, kc % ring, :n], src_sq_fn(kc), AF.Square, rd, [sqB[kc % ring]])
                    mm(PS[bank][:, :n], ONEB, sq[:, kc % ring, :n], [sqB[kc % ring], CB16B], [PSB[bank]], start=(kc == 0), stop=(kc == nk - 1))
            else:
                for kc in range(nk):
                    act(sq[:, kc, :n], src_sq_fn(kc), AF.Square, rd, [sqB])
                for kc in range(nk):
                    mm(PS[bank][:, :n], ONEB, sq[:, kc, :n], [sqB, CB16B], [PSB[bank]], start=(kc == 0), stop=(kc == nk - 1))
            act(rs_out[:, :n], PS[bank][:, :n], AF.Ln, [PSB[bank], PARB], [rsB], bias=EPS_AP, scale=inv_d)
            act(rs_out[:, :n], rs_out[:, :n], AF.Exp, [rsB], [rsB], scale=-0.5)

        for l in range(NL):
            last = (l == NL - 1)
            MARKS.append(("L%d start" % l, P.cnt["pe"]))
            with contextlib.ExitStack() as sa:
              if ("A", l) not in skip:
                  sba = lambda n, s, d=F32: sa.enter_context(nc.sbuf_tensor("%s_%d" % (n, l), list(s), d))
                  WINB = Buf(); WOUTB = Buf()
                  GLUW = sba("GLUW", [128, 2, 256], BF16); PWT = sba("PWT", [128, 2, 128], BF16); SMB = Buf()
                  USSM = sba("USSM", [128, 2, NT], BF16); USB = Buf()
                  H = sba("H", [128, 8, TW], BF16); HB = Buf()
                  SQ = sba("SQ", [128, 8, TW], BF16); SQB = Buf()
                  RS = sba("RS", [128, TW]); RSB = Buf()
                  P.dma("pool", GLUW[:], I["gluw"][l], 10, writes=[SMB])
                  P.dma("pool", PWT[:], I["pw"][l], 10, writes=[SMB])

                  def norm_tile(c0, n, ti, which, out, outB):
                      rms_rstd(lambda kc: X[:, kc, c0:c0 + n], n, 8, RS, RSB, [XB[ti]], 1.0 / 1024, SQ, SQB, 0)
                      for kc in range(8):
                          stt(out[:, kc, :n], X[:, kc, c0:c0 + n], NW[:, l, which, kc:kc + 1], RS[:, :n], ALU.mult, ALU.mult,
                              [XB[ti], RSB, PARB], [outB])

                  s1 = contextlib.ExitStack()
                  WSSM = s1.enter_context(nc.sbuf_tensor("WSSM_%d" % l, [128, 8, 256], BF16)); WSB = Buf()
                  P.dma("pool", WSSM[:], I["w_in"][l, :, :, OFF_SSM:OFF_POOL], 31, writes=[WSB])
                  for (c0, n, ti) in TILES:
                      norm_tile(c0, n, ti, 0, H, HB)
                      for oc in range(2):
                          for kc in range(8):
                              mm(PS[1 + oc][:, :n], WSSM[:, kc, oc * 128:(oc + 1) * 128], H[:, kc, :n],
                                 [WSB, HB], [PSB[1 + oc]], start=(kc == 0), stop=(kc == 7))
                          cp(USSM[:, oc, c0:c0 + n], PS[1 + oc][:, :n], [PSB[1 + oc]], [USB], eng="act")
                  P.barrier()
                  s1.close()

                  MARKS.append(("L%d A2 s5" % l, P.cnt["pe"]))
                  with contextlib.ExitStack() as s5:
                      sb5 = lambda n, s, d=F32: s5.enter_context(nc.sbuf_tensor("%s_%d" % (n, l), list(s), d))
                      YS = sb5("YS", [128, 2, NT], BF16); YSB = Buf()
                      LAMP = sb5("LAMP", [128, 16, 3]); BNAT = sb5("BNAT", [128, 16, 2, 16]); CNAT = sb5("CNAT", [128, 16, 16])
                      S0 = sb5("S0", [128, 16, 16]); LB = Buf()
                      P.dma("sp", LAMP[:], I["lamp"][:, l], 12, writes=[LB])
                      P.dma("sp", BNAT[:], I["bnat"][:, l], 12, writes=[LB])
                      P.dma("sp", CNAT[:], I["cnat"][:, l], 12, writes=[LB])
                      P.dma("sp", S0[:], I["sssm"][:, l], 12, writes=[LB])
                      DT = sb5("DT", [128, 16]); ZR = sb5("ZR", [128, 16]); FR = sb5("FR", [128, 16]); FI = sb5("FI", [128, 16, 2], I32)
                      T1 = sb5("T1", [128, 16]); T2 = sb5("T2", [128, 16]); T3 = sb5("T3", [128, 16])
                      AK = sb5("AK", [128, 16, NKS]); BK = sb5("BK", [128, 16, NKS]); TB = Buf()
                      act(DT[:], LAMP[:, :, 2], AF.Exp, [LB], [TB])
                      tt(ZR[:], LAMP[:, :, 0], DT[:], ALU.mult, [LB, TB], [TB])
                      stt(FR[:], LAMP[:, :, 1], 1.0 / (2 * math.pi), DT[:], ALU.mult, ALU.mult, [LB, TB], [TB])

                      def reduce_turns(dst, src):
                          cp(FI[:, :, 0], src, [TB], [TB])
                          cp(T3[:], FI[:, :, 0], [TB], [TB])
                          tt(dst, src, T3[:], ALU.subtract, [TB], [TB])

                      reduce_turns(FR[:], FR[:])
                      for k in range(NKS):
                          act(T1[:], ZR[:], AF.Exp, [TB], [TB], scale=float(2 ** k))
                          act(T2[:], FR[:], AF.Sin, [TB], [TB], scale=2 * math.pi)
                          tt(BK[:, :, k], T1[:], T2[:], ALU.mult, [TB], [TB])
                          ts(T2[:], FR[:], 0.25, None, ALU.add, None, [TB], [TB])
                          reduce_turns(T2[:], T2[:])
                          act(T2[:], T2[:], AF.Sin, [TB], [TB], scale=2 * math.pi)
                          tt(AK[:, :, k], T1[:], T2[:], ALU.mult, [TB], [TB])
                          if k < NKS - 1:
                              ts(FR[:], FR[:], 2.0, None, ALU.mult, None, [TB], [TB])
                              reduce_turns(FR[:], FR[:])
                      FRE = sb5("FRE", [128, 16]); FIM = sb5("FIM", [128, 16]); DEN = sb5("DEN", [128, 16]); AM1 = sb5("AM1", [128, 16])
                      are, aim = LAMP[:, :, 0], LAMP[:, :, 1]
                      tt(DEN[:], are, are, ALU.mult, [LB], [TB]); tt(T1[:], aim, aim, ALU.mult, [LB], [TB])
                      tt(DEN[:], DEN[:], T1[:], ALU.add, [TB], [TB]); rcp(DEN[:], DEN[:], [TB], [TB])
                      ts(AM1[:], AK[:, :, 0], -1.0, None, ALU.add, None, [TB], [TB])
                      tt(T1[:], AM1[:], are, ALU.mult, [TB, LB], [TB]); tt(T2[:], BK[:, :, 0], aim, ALU.mult, [TB, LB], [TB])
                      tt(T1[:], T1[:], T2[:], ALU.add, [TB], [TB]); tt(FRE[:], T1[:], DEN[:], ALU.mult, [TB], [TB])
                      tt(T1[:], BK[:, :, 0], are, ALU.mult, [TB, LB], [TB]); tt(T2[:], AM1[:], aim, ALU.mult, [TB, LB], [TB])
                      tt(T1[:], T1[:], T2[:], ALU.subtract, [TB], [TB]); tt(FIM[:], T1[:], DEN[:], ALU.mult, [TB], [TB])
                      ts(FIM[:], FIM[:], SGN, None, ALU.mult, None, [TB, CSTB], [TB])
                      BB = sb5("BB", [128, 16, 16]); BB2 = sb5("BB2", [128, 16, 16])
                      tt(BB[:], BNAT[:, :, 0, :], FRE[:, :, None].broadcast_to([128, 16, 16]), ALU.mult, [LB, TB], [TB])
                      tt(BB2[:], BNAT[:, :, 1, :], FIM[:, :, None].broadcast_to([128, 16, 16]), ALU.mult, [LB, TB], [TB])
                      tt(BB[:], BB[:], BB2[:], ALU.add, [TB], [TB])
                      CT = sb5("CT", [128, 16, 16])
                      ts(CT[:], CNAT[:], SGN, -1.0, ALU.mult, ALU.mult, [LB, CSTB], [TB])
                      BPAD = sb5("BPAD", [128, 128]); BPB = Buf()
                      BT = sb5("BT", [128, 8, 128], BF16); BTB = [Buf() for _ in range(8)]
                      CPAD = sb5("CPAD", [128, 8, 128], BF16); CPB = Buf()
                      RT = sb5("RT", [128, 8, NKS, 1, 128], BF16); RTB = [Buf() for _ in range(8)]
                      RF = [sb5("RF%d" % i, [128, 128]) for i in range(2)]; RH = [sb5("RH%d" % i, [128, 128]) for i in range(2)]
                      RFB = [Buf(), Buf()]
                      XE = sb5("XE", [128, 8, 1 + NP_ + 80], BF16); XEB = [Buf() for _ in range(8)]
                      YF = sb5("YF", [128, TW]); YFB = Buf()
                      SF = sb5("SF", [128, 17, 16]); SFB = Buf()
                      BS = sb5("BS", [128, 16, NKS]);
                      ts(BS[:], BK[:], SGN, -1.0, ALU.mult, ALU.mult, [TB, CSTB], [TB])
                      mset(BPAD[:], 0.0, [BPB]); mset(CPAD[:], 0.0, [CPB])
                      NE = 1 + NP_
                      for oc in range(2):
                          xesv = [XE[:, gi, NE:NE + 80].rearrange("p (b t) -> p b t", t=5) for gi in range(8)]
                          for gi in range(8):
                              g = oc * 8 + gi
                              cp(BPAD[:, gi * 16:(gi + 1) * 16], BB[:, g, :], [TB], [BPB])
                              tr(PS[2][:, 0:128], BPAD[:], IDF, [BPB, CSTB], [PSB[2]])
                              cp(BT[:, gi, :], PS[2][:, 0:128], [PSB[2]], [BTB[gi]])
                              mset(BPAD[:, gi * 16:(gi + 1) * 16], 0.0, [BPB])
                              cp(CPAD[:, gi, gi * 16:(gi + 1) * 16], CT[:, g, :], [TB], [CPB])
                              for k in range(NKS):
                                  e_ = "dve" if (k % 2 == 0) else "pool"
                                  rf, rh, rb = RF[k % 2], RH[k % 2], RFB[k % 2]
                                  ts(rf[:], SWP, BS[:, g, k:k + 1], None, ALU.mult, None, [CSTB, TB], [rb], eng=e_)
                                  if e_ == "dve":
                                      stt(rf[:], IDF, AK[:, g, k:k + 1], rf[:], ALU.mult, ALU.add, [CSTB, TB, rb], [rb])
                                  else:
                                      ts(rh[:], IDF, AK[:, g, k:k + 1], None, ALU.mult, None, [CSTB, TB], [rb], eng=e_)
                                      tt(rf[:], rf[:], rh[:], ALU.add, [rb], [rb], eng=e_)
                                  cp(RT[:, gi, k, 0, :], rf[:], [rb], [RTB[gi]], eng=e_)
                              mset(XE[:, gi, 0:1], 0.0, [XEB[gi]])
                              cp(xesv[gi][:, :, 0], S0[:, g, :], [LB], [XEB[gi]])
                          bk = [0]

                          def nbank():
                              bk[0] = (bk[0] + 1) % 6
                              return 1 + bk[0]
                          for (c0, n, ti) in TILES:
                              for gi in range(8):
                                  bnk = nbank()
                                  mm(PS[bnk][:, :n], BT[:, gi, :], USSM[:, oc, c0:c0 + n], [BTB[gi], USB], [PSB[bnk]])
                                  if ti < 8:
                                      cp(XE[:, gi, 1 + c0:1 + c0 + n], PS[bnk][:, :n], [PSB[bnk]], [XEB[gi]], eng="act")
                                  else:
                                      cp(xesv[gi][:, :, 1:5], PS[bnk][:, :n].rearrange("p (b t) -> p b t", t=4), [PSB[bnk]], [XEB[gi]], eng="act")
                          for k in range(NKS):
                              s = 2 ** k
                              hi = NE
                              while hi > s:
                                  lo = max(s, hi - 512)
                                  w = hi - lo
                                  for gi in range(8):
                                      bnk = nbank()
                                      mm(PS[bnk][:, :w], RT[:, gi, k, 0, :], XE[:, gi, lo - s:hi - s], [RTB[gi], XEB[gi]], [PSB[bnk]])
                                      tt(XE[:, gi, lo:hi], XE[:, gi, lo:hi], PS[bnk][:, :w], ALU.add, [XEB[gi], PSB[bnk]], [XEB[gi]])
                                  hi = lo
                              if s < 5:
                                  w = 5 - s
                                  for gi in range(8):
                                      bnk = nbank()
                                      pso = PS[bnk][:, :16 * w].rearrange("p (b t) -> p b t", t=w)
                                      mm(pso, RT[:, gi, k, 0, :], xesv[gi][:, :, 0:w], [RTB[gi], XEB[gi]], [PSB[bnk]])
                                      tt(xesv[gi][:, :, s:5], xesv[gi][:, :, s:5], pso, ALU.add, [XEB[gi], PSB[bnk]], [XEB[gi]])
                          for gi in range(8):
                              g = oc * 8 + gi
                              cp(SF[:, 0, g:g + 1], XE[:, gi, NE - 1:NE], [XEB[gi]], [SFB])
                              cp(SF[:, 1:17, g], xesv[gi][:, :, 4], [XEB[gi]], [SFB])
                          if l == 0: MARKS.append(("L0 s5 y oc%d" % oc, P.cnt["pe"]))
                          for (c0, n, ti) in TILES:
                              for gi in range(8):
                                  if ti < 8:
                                      rhs = XE[:, gi, 1 + c0:1 + c0 + n]
                                      mm(PS[3][:, :n], CPAD[:, gi, :], rhs, [CPB, XEB[gi]], [PSB[3]], start=(gi == 0), stop=(gi == 7))
                                  else:
                                      rhs = XE[:, gi, NE:NE + 80].rearrange("p (b t) -> p b t", t=5)[:, :, 1:5]
                                      mm(PS[3][:, :n].rearrange("p (b t) -> p b t", t=4), CPAD[:, gi, :], rhs, [CPB, XEB[gi]], [PSB[3]],
                                         start=(gi == 0), stop=(gi == 7))
                              stt(YF[:, :n], USSM[:, oc, c0:c0 + n], SSMD[:, l, oc:oc + 1], PS[3][:, :n], ALU.mult, ALU.add, [USB, PARB, PSB[3]], [YFB])
                              tt(RS[:, :n], YF[:, :n], YF[:, :n], ALU.mult, [YFB], [RSB])
                              ts(RS[:, :n], RS[:, :n], 0.044715, 1.0, ALU.mult, ALU.add, [RSB], [RSB])
                              tt(RS[:, :n], RS[:, :n], YF[:, :n], ALU.mult, [RSB, YFB], [RSB])
                              act(RS[:, :n], RS[:, :n], AF.Sigmoid, [RSB], [RSB], scale=2.0 * math.sqrt(2.0 / math.pi))
                              tt(YS[:, oc, c0:c0 + n], RS[:, :n], YF[:, :n], ALU.mult, [RSB, YFB], [YSB])
                      P.dma("sp", O["o_ssm_p"][:, l, :], SF[:, 0, :], 25, reads=[SFB])
                      P.dma("sp", O["o_ssm_s"][:, l, :, :], SF[:, 1:17, :], 25, reads=[SFB])
                      for (c0, n, ti) in TILES:
                          for oc in range(2):
                              for kc in range(2):
                                  mm(PS[3][:, :n], GLUW[:, kc, oc * 128:(oc + 1) * 128], YS[:, kc, c0:c0 + n], [SMB, YSB], [PSB[3]],
                                     start=(kc == 0), stop=(kc == 1))
                              act(RS[:, :n], PS[3][:, :n], AF.Sigmoid, [PSB[3], PARB], [RSB], bias=GLUB[:, l, oc:oc + 1])
                              tt(USSM[:, oc, c0:c0 + n], RS[:, :n], YS[:, oc, c0:c0 + n], ALU.mult, [RSB, YSB], [USB])
                      P.barrier()

                  MARKS.append(("L%d A3" % l, P.cnt["pe"]))
                  with contextlib.ExitStack() as s3:
                      sb3 = lambda n, s, d=F32: s3.enter_context(nc.sbuf_tensor("%s_%d" % (n, l), list(s), d))
                      WIN = sb3("WIN", [128, 8, INW], BF16)
                      WOUT = sb3("WOUT", [128, 8, 1024], BF16)
                      P.dma("pool", WIN[:, 0:4], I["w_in"][l, :, 0:4], 11, writes=[WINB])
                      P.dma("pool", WIN[:, 4:8], I["w_in"][l, :, 4:8], 11, writes=[WINB])
                      P.dma("pool", WOUT[:], I["w_out"][l], 21, writes=[WOUTB])
                      PRE = sb3("PRE", [128, 12, 3 + TW], BF16); PREB = Buf()
                      PRES = PRE
                      QKV = sb3("QKV", [128, 12, TW], BF16); QKVB = Buf()
                      CV = sb3("CV", [128, TW]); CVB = Buf()
                      GT = sb3("GT", [128, 4, TW], BF16); GTB = Buf()
                      OO = sb3("OO", [128, 4, TW]); OOB = Buf()
                      MIX = sb3("MIX", [128, 8, TW], BF16); MIXB = Buf()
                      MO = sb3("MO", [128, 8, TW]); MOB = Buf()
                      PX = sb3("PX", [128, 2, 320]); PXB = Buf()
                      PA = sb3("PA", [128, 320]); PBt = sb3("PBt", [128, 320]); PAB = Buf()
                      CTAIL = sb3("CTAIL", [128, 12, 3]); CTB = Buf()
                      SCS = sb3("SCS", [128, 12, 16, 3]); SCSB = Buf()
                      SST = sb3("SST", [128, 4, 128]); SSB = Buf()
                      SBF = sb3("SBF", [128, 4, 128], BF16)
                      NEGA = sb3("NEGA", [64, 4]); NGB = Buf()
                      ZT = sb3("ZT", [64, 8]); LGt = sb3("LGt", [64, 4]); BET = sb3("BET", [64, 4]); EGt = sb3("EGt", [64, 4]); EKD = sb3("EKD", [64, 4])
                      EGL = sb3("EGL", [128, 4]); SCB = Buf()
                      LG = sb3("LG", [64, 4, 64]); LGB = Buf()
                      E2 = sb3("E2", [64, 2, 4, 64]); E2B = Buf()
                      EGB = sb3("EGB", [128, 4, 64]); EGBB = Buf()
                      QD = sb3("QD", [128, 4, 64], BF16); QDB = Buf()
                      KTK = sb3("KTK", [64, 4, 128], BF16); VTK = sb3("VTK", [64, 4, 128], BF16); TKB = Buf()
                      RHK = sb3("RHK", [64, 4, 128], BF16); VBt = sb3("VBt", [64, 4, 128], BF16); KDt = sb3("KDt", [64, 4, 128], BF16); RKB = Buf()
                      AF_ = sb3("AF_", [64, 4, 64]); AFB = Buf()
                      PB_ = sb3("PB_", [64, 2, 4, 64], BF16); PBB = Buf()
                      QKT = sb3("QKT", [64, 4, 64], BF16); QKB = Buf()
                      TTf = sb3("TTf", [64, 4, 64]); TTb = sb3("TTb", [64, 4, 64], BF16); TTB = Buf()
                      KC = sb3("KC", [128, 4, 64], BF16); KCB = Buf()
                      WS = sb3("WS", [64, 4, 128]); WSB = Buf()
                      UU = sb3("UU", [64, 4, 128], BF16); UUB = Buf()
                      act(NEGA[:], DNP[:, l, 0, :], AF.Exp, [PARB], [NGB])
                      ts(NEGA[:], NEGA[:], -1.0, None, ALU.mult, None, [NGB], [NGB])
                      mset(CTAIL[:], 0.0, [CTB])
                      mset(PX[:, :, 0:15], 0.0, [PXB])
                      mset(SST[:], 0.0, [SSB]); mset(SBF[:], 0.0, [SSB])

                      for (c0, n, ti) in TILES:
                          samp = (ti == 8)
                          C = 4 if samp else 64
                          nch = n // C
                          hist = 3
                          norm_tile(c0, n, ti, 0, H, HB)
                          if not samp:
                              cp(PRE[:, :, 0:3], CTAIL[:], [CTB], [PREB])
                              pre_new = lambda blk: PRE[:, blk, 3:3 + n]
                          else:
                              prs = PRE[:, :, 0:112].rearrange("p k (b t) -> p k b t", t=7)
                              P.dma("sp", SCS[:], I["sconv"][:, l], 14, writes=[SCSB])
                              cp(prs[:, :, :, 0:3], SCS[:], [SCSB], [PREB])
                              pre_new = lambda blk: prs[:, blk, :, 3:7]
                          for blk in range(12):
                              col = (blk // 4) * 512 + (blk % 4) * 128
                              bnk = 1 + (blk % 3)
                              for kc in range(8):
                                  mm(PS[bnk][:, :n], WIN[:, kc, col:col + 128], H[:, kc, :n], [WINB, HB], [PSB[bnk]], start=(kc == 0), stop=(kc == 7))
                              if not samp:
                                  cp(pre_new(blk), PS[bnk][:, :n], [PSB[bnk]], [PREB], eng=("act" if blk % 2 == 0 else "dve"))
                              else:
                                  cp(pre_new(blk), PS[bnk][:, :n].rearrange("p (b t) -> p b t", t=4), [PSB[bnk]], [PREB], eng=("act" if blk % 2 == 0 else "dve"))
                          if not samp:
                              cp(CTAIL[:], PRE[:, :, n:n + 3], [PREB], [CTB])
                              if ti == 7:
                                  P.dma("sp", O["o_conv_p"][:, l], CTAIL[:], 23, reads=[CTB])
                          else:
                              cp(SCS[:], prs[:, :, :, 4:7], [PREB], [SCSB])
                              P.dma("sp", O["o_conv_s"][:, l], SCS[:], 24, reads=[SCSB])
                          for blk in range(12):
                              if not samp:
                                  tap = lambda j: PRE[:, blk, j:j + n]
                                  cvv = CV[:, :n]
                              else:
                                  tap = lambda j: prs[:, blk, :, j:j + 4]
                                  cvv = CV[:, :n].rearrange("p (b t) -> p b t", t=4)
                              ts(cvv, tap(0), CW[:, l, blk, 0:1], None, ALU.mult, None, [PREB, PARB], [CVB])
                              for j in range(1, 4):
                                  stt(cvv, tap(j), CW[:, l, blk, j:j + 1], cvv, ALU.mult, ALU.add, [PREB, PARB, CVB], [CVB])
                              act(QKV[:, blk, :n], CV[:, :n], AF.Silu, [CVB], [QKVB])
                          for hh in range(4):
                              col = OFF_G + hh * 128
                              bnk = 1 + (hh % 2)
                              for kc in range(8):
                                  mm(PS[bnk][:, :n], WIN[:, kc, col:col + 128], H[:, kc, :n], [WINB, HB], [PSB[bnk]], start=(kc == 0), stop=(kc == 7))
                              act(GT[:, hh, :n], PS[bnk][:, :n], AF.Silu, [PSB[bnk]], [GTB])
                          for blk in range(8):
                              bnk = 3 + (blk % 2)
                              act(SQ[:, blk, :n], QKV[:, blk, :n], AF.Square, [QKVB], [SQB])
                              mm(PS[bnk][:, :n], ONEB, SQ[:, blk, :n], [SQB, CB16B], [PSB[bnk]])
                              act(MO[:, blk, :n], PS[bnk][:, :n], AF.Ln, [PSB[bnk], PARB], [MOB], bias=EPS_AP, scale=1.0)
                          for blk in range(8):
                              act(MO[:, blk, :n], MO[:, blk, :n], AF.Exp, [MOB], [MOB], scale=-0.5, bias=(LNQ_AP if blk < 4 else 0.0))
                          for blk in range(8):
                              tt(QKV[:, blk, :n], QKV[:, blk, :n], MO[:, blk, :n], ALU.mult, [QKVB, MOB], [QKVB])
                          if l == 0: MARKS.append(("L0 t%d delta" % ti, P.cnt["pe"]))
                          for ci in range(nch):
                              a0 = ci * C
                              cols = slice(a0, a0 + C)
                              if samp:
                                  P.dma("sp", SST[:], I["sdel"][l, ci], 13, writes=[SSB])
                                  cp(SBF[:], SST[:], [SSB], [SSB])
                              for kc in range(8):
                                  mm(PS[2][:C, 0:8], H[:, kc, cols], WIN[:, kc, OFF_A:OFF_A + 8], [HB, WINB], [PSB[2]], start=(kc == 0), stop=(kc == 7))
                              tt(ZT[:C, 0:4], PS[2][:C, 0:4], DNP[:C, l, 1, :], ALU.add, [PSB[2], PARB], [SCB])
                              act(BET[:C], PS[2][:C, 4:8], AF.Exp, [PSB[2]], [SCB], scale=-1.0)
                              act(BET[:C], BET[:C], AF.Ln, [SCB, PARB], [SCB], bias=EPSC[:C, 1:2])
                              act(BET[:C], BET[:C], AF.Exp, [SCB], [SCB], scale=-1.0)
                              act(ZT[:C, 0:4], ZT[:C, 0:4], AF.Exp, [SCB], [SCB])
                              act(ZT[:C, 0:4], ZT[:C, 0:4], AF.Ln, [SCB, PARB], [SCB], bias=EPSC[:C, 1:2])
                              tt(LGt[:C], ZT[:C, 0:4], NEGA[:C], ALU.mult, [SCB, NGB], [SCB])
                              tt(LG[:C, :, :C], LTRI[C][:C, None, :C].broadcast_to([C, 4, C]), LGt[:C, :, None].broadcast_to([C, 4, C]), ALU.mult,
                                 [CSTB, SCB], [LGB])
                              d2 = PS[3][:C, :].rearrange("p (a h c) -> p a h c", a=2, h=4)
                              for hh in range(4):
                                  mm(d2[:, 0, hh, :C], LG[:C, hh, :C], MLT[C][:C, :C], [LGB, CSTB], [PSB[3]])
                                  mm(d2[:, 1, hh, :C], MLT[C][:C, :C], LG[:C, hh, :C], [LGB, CSTB], [PSB[3]])
                              act(E2[:C, :, :, :C], d2[:, :, :, :C], AF.Exp, [PSB[3]], [E2B])
                              tt(E2[:C, 0, :, :C], E2[:C, 0, :, :C], MSTR[C][:C, None, :C].broadcast_to([C, 4, C]), ALU.mult, [E2B, CSTB], [E2B])
                              tt(E2[:C, 1, :, :C], E2[:C, 1, :, :C], MINT[C][:C, None, :C].broadcast_to([C, 4, C]), ALU.mult, [E2B, CSTB], [E2B])
                              mm(PS[2][:C, 8:12], LTRI[C][:C, :C], LGt[:C, :], [CSTB, SCB], [PSB[2]])
                              mm(PS[2][:, 16:20], ONEF[:C, :], LGt[:C, :], [CSTB, SCB], [PSB[2]])
                              act(EGt[:C], PS[2][:C, 8:12], AF.Exp, [PSB[2]], [SCB])
                              act(EGL[:], PS[2][:, 16:20], AF.Exp, [PSB[2]], [SCB])
                              cp(ZT[:C, 4:8], PS[2][:C, 8:12], [PSB[2]], [SCB])
                              tt(EKD[:C], PS[2][:C, 16:20], ZT[:C, 4:8], ALU.subtract, [PSB[2], SCB], [SCB])
                              act(EKD[:C], EKD[:C], AF.Exp, [SCB], [SCB])
                              eg_ps = PS[4][:, 0:4 * C].rearrange("p (h c) -> p h c", h=4)
                              for hh in range(4):
                                  mm(eg_ps[:, hh, :], ONEF[:C, :], LG[:C, hh, :C], [CSTB, LGB], [PSB[4]])
                              act(EGB[:, :, :C], eg_ps, AF.Exp, [PSB[4]], [EGBB])
                              tt(QD[:, :, :C], QKV[:, 0:4, cols], EGB[:, :, :C], ALU.mult, [QKVB, EGBB], [QDB])
                              kt_ps = PT[:C, 0:512].rearrange("p (h d) -> p h d", h=4)
                              vt_ps = PT[:C, 512:1024].rearrange("p (h d) -> p h d", h=4)
                              for hh in range(4):
                                  tr(kt_ps[:, hh, :], QKV[:, 4 + hh, cols], IDB, [QKVB, CB16B], [PTB])
                                  tr(vt_ps[:, hh, :], QKV[:, 8 + hh, cols], IDB, [QKVB, CB16B], [PTB])
                              cp(KTK[:C], kt_ps, [PTB], [TKB])
                              cp(VTK[:C], vt_ps, [PTB], [TKB])
                              tt(VBt[:C], VTK[:C], BET[:C, :, None].broadcast_to([C, 4, 128]), ALU.mult, [TKB, SCB], [RKB])
                              tt(RHK[:C], KTK[:C], BET[:C, :, None].broadcast_to([C, 4, 128]), ALU.mult, [TKB, SCB], [RKB])
                              tt(RHK[:C], RHK[:C], EGt[:C, :, None].broadcast_to([C, 4, 128]), ALU.mult, [RKB, SCB], [RKB])
                              tt(KDt[:C], KTK[:C], EKD[:C, :, None].broadcast_to([C, 4, 128]), ALU.mult, [TKB, SCB], [RKB])
                              kq = PS[5][:C, :].rearrange("p (a h c) -> p a h c", a=2, h=4)
                              for hh in range(4):
                                  mm(kq[:, 0, hh, :C], QKV[:, 4 + hh, cols], QKV[:, 4 + hh, cols], [QKVB], [PSB[5]])
                                  mm(kq[:, 1, hh, :C], QKV[:, 4 + hh, cols], QKV[:, hh, cols], [QKVB], [PSB[5]])
                              tt(AF_[:C, :, :C], kq[:, 0, :, :C], E2[:C, 0, :, :C], ALU.mult, [PSB[5], E2B], [AFB])
                              tt(AF_[:C, :, :C], AF_[:C, :, :C], BET[:C, :, None].broadcast_to([C, 4, C]), ALU.mult, [AFB, SCB], [AFB])
                              tt(QKT[:C, :, :C], kq[:, 1, :, :C], E2[:C, 1, :, :C], ALU.mult, [PSB[5], E2B], [QKB])
                              at_ps = PS[6][:C, 0:4 * C].rearrange("p (h c) -> p h c", h=4)
                              for hh in range(4):
                                  tr(at_ps[:, hh, :], AF_[:C, hh, :C], IDF[:C, :C], [AFB, CSTB], [PSB[6]])
                              cp(PB_[:C, 0, :, :C], AF_[:C, :, :C], [AFB], [PBB])
                              cp(PB_[:C, 1, :, :C], at_ps, [PSB[6]], [PBB])
                              tt(TTf[:C, :, :C], IDF[:C, None, :C].broadcast_to([C, 4, C]), at_ps, ALU.subtract, [CSTB, PSB[6]], [TTB])
                              cp(TTb[:C, :, :C], TTf[:C, :, :C], [TTB], [TTB])
                              nlev = 5 if C == 64 else 1
                              for lev in range(nlev):
                                  p2 = PS[3][:C, :].rearrange("p (a h c) -> p a h c", a=2, h=4)
                                  for hh in range(4):
                                      mm(p2[:, 0, hh, :C], PB_[:C, 1, hh, :C], PB_[:C, 0, hh, :C], [PBB], [PSB[3]])
                                      mm(p2[:, 1, hh, :C], PB_[:C, 0, hh, :C], PB_[:C, 1, hh, :C], [PBB], [PSB[3]])
                                  cp(PB_[:C, :, :, :C], p2[:, :, :, :C], [PSB[3]], [PBB], eng="act")
                                  tu = PS[6][:C, 0:4 * C].rearrange("p (h c) -> p h c", h=4)
                                  for hh in range(4):
                                      mm(tu[:, hh, :], PB_[:C, 0, hh, :C], TTb[:C, hh, :C], [PBB, TTB], [PSB[6]])
                                  tt(TTb[:C, :, :C], TTb[:C, :, :C], tu, ALU.add, [TTB, PSB[6]], [TTB])
                              w_ps = PS[3][:C, :].rearrange("p (h d) -> p h d", h=4)
                              kc_ps = PS[4][:, 0:4 * C].rearrange("p (h c) -> p h c", h=4)
                              for hh in range(4):
                                  mm(w_ps[:, hh, :], TTb[:C, hh, :C], VBt[:C, hh, :], [TTB, RKB], [PSB[3]])
                                  mm(kc_ps[:, hh, :], RHK[:C, hh, :], TTb[:C, hh, :C], [TTB, RKB], [PSB[4]])
                              cp(WS[:C], w_ps, [PSB[3]], [WSB], eng="act")
                              cp(KC[:, :, :C], kc_ps, [PSB[4]], [KCB])
                              p1 = PS[5][:C, :].rearrange("p (h d) -> p h d", h=4)
                              for hh in range(4):
                                  mm(p1[:, hh, :], KC[:, hh, :C], SBF[:, hh, :], [KCB, SSB], [PSB[5]])
                              tt(UU[:C], WS[:C], p1, ALU.subtract, [WSB, PSB[5]], [UUB])
                              o_ps = PS[6][:, 0:4 * C].rearrange("p (h c) -> p h c", h=4)
                              for hh in range(4):
                                  mm(o_ps[:, hh, :], SBF[:, hh, :], QD[:, hh, :C], [SSB, QDB], [PSB[6]], start=True, stop=False)
                                  mm(o_ps[:, hh, :], UU[:C, hh, :], QKT[:C, hh, :C], [UUB, QKB], [PSB[6]], start=False, stop=True)
                              cp(OO[:, :, cols], o_ps, [PSB[6]], [OOB], eng="act")
                              ds = PS[3][:, :].rearrange("p (h d) -> p h d", h=4)
                              for hh in range(4):
                                  mm(ds[:, hh, :], KDt[:C, hh, :], UU[:C, hh, :], [RKB, UUB], [PSB[3]])
                              tt(SST[:], SST[:], EGL[:, :, None].broadcast_to([128, 4, 128]), ALU.mult, [SSB, SCB], [SSB])
                              tt(SST[:], SST[:], ds, ALU.add, [SSB, PSB[3]], [SSB])
                              cp(SBF[:], SST[:], [SSB], [SSB])
                              if samp:
                                  P.dma("sp", O["o_delta_s"][l, ci], SST[:], 22, reads=[SSB])
                              elif ti == 7 and ci == nch - 1:
                                  P.dma("sp", O["o_delta_p"][l], SST[:], 22, reads=[SSB])
                          if l == 0: MARKS.append(("L0 t%d postdelta" % ti, P.cnt["pe"]))
                          for hh in range(4):
                              act(SQ[:, hh, :n], OO[:, hh, :n], AF.Square, [OOB], [SQB])
                              mm(PS[2][:, :n], ONEB, SQ[:, hh, :n], [SQB, CB16B], [PSB[2]])
                              act(RS[:, :n], PS[2][:, :n], AF.Ln, [PSB[2], PARB], [RSB], bias=EPS_AP, scale=1.0 / 128)
                              act(RS[:, :n], RS[:, :n], AF.Exp, [RSB], [RSB], scale=-0.5)
                              stt(CV[:, :n], OO[:, hh, :n], ONW[:, l:l + 1], RS[:, :n], ALU.mult, ALU.mult, [OOB, PARB, RSB], [CVB])
                              tt(MIX[:, hh, :n], CV[:, :n], GT[:, hh, :n], ALU.mult, [CVB, GTB], [MIXB])
                          for oc in range(2):
                              cp(MIX[:, 4 + oc, :n], USSM[:, oc, c0:c0 + n], [USB], [MIXB], eng="pool")
                          if samp:
                              pxs = PX[:, :, 0:16 * 19].rearrange("p k (b t) -> p k b t", t=19)
                              for k_ in range(2):
                                  P.dma("sp", pxs[:, k_, :, 0:15], I["spool"][:, l, k_], 15, writes=[PXB])
                          for k2 in range(2):
                              col = OFF_POOL + k2 * 128
                              for kc in range(8):
                                  mm(PS[1][:, :n], WIN[:, kc, col:col + 128], H[:, kc, :n], [WINB, HB], [PSB[1]], start=(kc == 0), stop=(kc == 7))
                              if not samp:
                                  cp(PX[:, k2, 15:15 + n], PS[1][:, :n], [PSB[1]], [PXB], eng="act")
                              else:
                                  cp(pxs[:, k2, :, 15:19], PS[1][:, :n].rearrange("p (b t) -> p b t", t=4), [PSB[1]], [PXB], eng="act")
                          if samp:
                              for k_ in range(2):
                                  P.dma("sp", O["o_pool_s_new"][:, l, k_], pxs[:, k_, :, 15:19], 27, reads=[PXB])
                              P.dma("sp", O["o_pool_s_old"][l], I["spool_nat"][l, :, 4:15, :], 28)
                          elif ti == 7:
                              P.dma("sp", O["o_pool_p"][:, l], PX[:, :, n:n + 15], 26, reads=[PXB])
                          for k2 in range(2):
                              if not samp:
                                  W_ = 15 + n
                                  src = PX[:, k2, 0:W_]; a_ = PA[:, 0:W_]; b_ = PBt[:, 0:W_]
                                  sh = lambda v, s_: (v[:, s_:W_], v[:, 0:W_ - s_])
                                  full = lambda v: v
                              else:
                                  W_ = 19
                                  src = pxs[:, k2]; a_ = PA[:, 0:16 * 19].rearrange("p (b t) -> p b t", t=19); b_ = PBt[:, 0:16 * 19].rearrange("p (b t) -> p b t", t=19)
                                  sh = lambda v, s_: (v[:, :, s_:W_], v[:, :, 0:W_ - s_])
                                  full = lambda v: v
                              cp(a_, src, [PXB], [PAB], eng="pool")
                              hi_, lo_ = sh(a_, 1); xh, xl = sh(src, 1)
                              tt(hi_, xh, xl, ALU.add, [PXB, PAB], [PAB])
                              cur, oth = a_, b_
                              for d in range(1, 4):
                                  need_lo = (2 ** d) < (2 if k2 == 0 else 8)
                                  need_hi = (2 ** d) < (4 if k2 == 0 else 16)
                                  if not (need_lo or need_hi):
                                      break
                                  s_ = 2 ** d
                                  cp(oth, cur, [PAB], [PAB], eng="pool")
                                  oh, ol = sh(oth, s_); ch, cl = sh(cur, s_)
                                  if need_lo:
                                      tt(oh[0:64], ch[0:64], cl[0:64], ALU.add, [PAB], [PAB])
                                  if need_hi:
                                      tt(oh[64:128], ch[64:128], cl[64:128], ALU.add, [PAB], [PAB])
                                  cur, oth = oth, cur
                              if not samp:
                                  wv = cur[:, 15:15 + n]; xv = PX[:, k2, 15:15 + n]; rv = oth[:, 15:15 + n]
                              else:
                                  wv = cur[:, :, 15:19]; xv = pxs[:, k2, :, 15:19]; rv = oth[:, :, 15:19]
                              for half in range(2):
                                  wsz = [2, 4, 8, 16][2 * k2 + half]
                                  pr = slice(64 * half, 64 * half + 64)
                                  stt(rv[pr], wv[pr], 1.0 / wsz, xv[pr], ALU.mult, ALU.subtract, [PAB, PXB], [PAB])
                              if ti == 0:
                                  tt(rv[:, 0:16], wv[:, 0:16], CST[:, 6 + k2, 0:16], ALU.mult, [PAB, CSTB], [PAB])
                                  tt(rv[:, 0:16], rv[:, 0:16], xv[:, 0:16], ALU.subtract, [PAB, PXB], [PAB])
                              if not samp:
                                  cp(SQ[:, k2, :n], rv, [PAB], [SQB])
                              else:
                                  cp(SQ[:, k2, :n].rearrange("p (b t) -> p b t", t=4), rv, [PAB], [SQB])
                              mm(PS[2][:, :n], PWT[:, k2, :], SQ[:, k2, :n], [SMB, SQB], [PSB[2]])
                              ts(MIX[:, 6 + k2, :n], PS[2][:, :n], PSC[:, l, k2:k2 + 1], None, ALU.mult, None, [PSB[2], PARB], [MIXB])
                          if not samp:
                              cp(PX[:, :, 0:15], PX[:, :, n:n + 15], [PXB], [PXB], eng="pool")
                          if dbg and l == 0:
                              P.dma("pool", O["dbgmix"][:, :, c0:c0 + n], MIX[:, :, :n], 29, reads=[MIXB])
                              P.dma("sp", O["dbgoo"][:, :, c0:c0 + n], OO[:, :, :n], 30, reads=[OOB])
                          for oc in range(8):
                              bnk = 1 + (oc % 3)
                              for kc in range(8):
                                  mm(PS[bnk][:, :n], WOUT[:, kc, oc * 128:(oc + 1) * 128], MIX[:, kc, :n], [WOUTB, MIXB], [PSB[bnk]], start=(kc == 0), stop=(kc == 7))
                              cp(MO[:, oc, :n], PS[bnk][:, :n], [PSB[bnk]], [MOB], eng=("act" if oc % 2 == 0 else "dve"))
                          rms_rstd(lambda kc: MO[:, kc, :n], n, 8, RS, RSB, [MOB], 1.0 / 1024, SQ, SQB, 0)
                          for kc in range(8):
                              tt(MO[:, kc, :n], MO[:, kc, :n], RS[:, :n], ALU.mult, [MOB, RSB], [MOB])
                              stt(X[:, kc, c0:c0 + n], MO[:, kc, :n], NW[:, l, 1, kc:kc + 1], X[:, kc, c0:c0 + n], ALU.mult, ALU.add,
                                  [MOB, PARB, XB[ti]], [XB[ti]])
                      P.barrier()
            MARKS.append(("L%d FFN" % l, P.cnt["pe"]))
            with contextlib.ExitStack() as sf:
              if ("F", l) not in skip:
                  sbf_ = lambda n, s, d=F32: sf.enter_context(nc.sbuf_tensor("%s_%d" % (n, l), list(s), d))
                  HW_ = 1088
                  H2 = sbf_("H2", [128, 8, HW_], BF16); H2B = Buf()
                  AV = sbf_("AV", [128, 22, HW_], BF16); AVB = Buf()
                  WG = [sbf_("WG%d" % i, [128, 2, 8, 128], BF16) for i in range(2)]; WGB = [Buf() for _ in range(2)]
                  WDA = sbf_("WDA", [128, 22, 1024], BF16); WDAB = Buf()
                  for a_ in range(0, 22, 6):
                      b_ = min(22, a_ + 6)
                      P.dma("pool", WDA[:, a_:b_, :], I["w_down"][l, :, a_:b_, :], 19, writes=[WDAB])
                  SQ2 = sbf_("SQ2", [128, 2, 512], BF16); SQ2B = [Buf(), Buf()]
                  RS2 = sbf_("RS2", [128, 512]); RS2B = Buf()
                  GS = sbf_("GS", [128, 2, 512], BF16); GSB = [Buf(), Buf()]
                  FO = sbf_("FO", [128, 8, 512], BF16); FOB = Buf()
                  unit = [0]
                  for half in range(2):
                      h0 = half * 1024
                      subt = [(0, 512), (512, 512)] if half == 0 else [(1024, 512), (1536, 512), (2048, 64)]
                      xbl = lambda c0, n: [XB[t_] for t_ in range(min(c0 // TW, 8), min((c0 + n - 1) // TW, 8) + 1)]
                      for (c0, n) in subt:
                          xbs = xbl(c0, n)
                          rms_rstd(lambda kc: X[:, kc, c0:c0 + n], n, 8, RS2, RS2B, xbs, 1.0 / 1024, SQ2, SQ2B, 0, ring=2)
                          for kc in range(8):
                              stt(H2[:, kc, c0 - h0:c0 - h0 + n], X[:, kc, c0:c0 + n], NW[:, l, 2, kc:kc + 1], RS2[:, :n], ALU.mult, ALU.mult,
                                  xbs + [RS2B, PARB], [H2B])
                      for fc in range(22):
                          sl = fc % 2
                          P.dma("pool", WG[sl][:], I["w_gu"][l, fc], 16 + sl, writes=[WGB[sl]])
                          for si, (c0, n) in enumerate(subt):
                              r0 = c0 - h0
                              u_ = unit[0] % 2
                              unit[0] += 1
                              bg, bu = 1 + 2 * u_, 2 + 2 * u_
                              for kc in range(8):
                                  mm(PS[bg][:, :n], WG[sl][:, 0, kc, :], H2[:, kc, r0:r0 + n], [WGB[sl], H2B], [PSB[bg]], start=(kc == 0), stop=(kc == 7))
                              for kc in range(8):
                                  mm(PS[bu][:, :n], WG[sl][:, 1, kc, :], H2[:, kc, r0:r0 + n], [WGB[sl], H2B], [PSB[bu]], start=(kc == 0), stop=(kc == 7))
                              act(GS[:, u_, :n], PS[bg][:, :n], AF.Silu, [PSB[bg]], [GSB[u_]])
                              tt(AV[:, fc, r0:r0 + n], GS[:, u_, :n], PS[bu][:, :n], ALU.mult, [GSB[u_], PSB[bu]], [AVB])
                      for si, (c0, n) in enumerate(subt):
                          r0 = c0 - h0
                          xbs = xbl(c0, n)
                          for ogrp in range(2):
                              for fc in range(22):
                                  for o4 in range(4):
                                      oo_ = (ogrp * 4 + o4) * 128
                                      mm(PS[1 + o4][:, :n], WDA[:, fc, oo_:oo_ + 128], AV[:, fc, r0:r0 + n],
                                         [WDAB, AVB], [PSB[1 + o4]], start=(fc == 0), stop=(fc == 21))
                              for o4 in range(4):
                                  cp(FO[:, ogrp * 4 + o4, :n], PS[1 + o4][:, :n], [PSB[1 + o4]], [FOB], eng="act")
                          rms_rstd(lambda kc: FO[:, kc, :n], n, 8, RS2, RS2B, [FOB], 1.0 / 1024, SQ2, SQ2B, 0, ring=2)
                          for kc in range(8):
                              tt(FO[:, kc, :n], FO[:, kc, :n], RS2[:, :n], ALU.mult, [FOB, RS2B], [FOB])
                              stt(X[:, kc, c0:c0 + n], FO[:, kc, :n], NW[:, l, 3, kc:kc + 1], X[:, kc, c0:c0 + n], ALU.mult, ALU.add,
                                  [FOB, PARB] + xbs, xbs)
                  P.barrier()
        for (c0, n, ti) in TILES:
            P.dma("sp", O["yT"][:, :, c0:c0 + n], X[:, :, c0:c0 + n], 1 + ti, reads=[XB[ti]])
        P.barrier()
        P.emit()
    return nc


def _consts():
    c = np.zeros((128, 8, 128), np.float32)
    idx = np.arange(128)
    c[:, 0, :] = np.eye(128, dtype=np.float32)
    c[:, 1, :] = 1.0
    c[:, 2, :] = (idx[:, None] <= idx[None, :])
    c[:, 3, :] = (idx[None, :] < idx[:, None])
    p = idx % 64
    ch = idx // 64
    c[:, 4, :] = ((p[:, None] == p[None, :]) & (ch[:, None] != ch[None, :]))
    c[:, 5, :] = np.where(ch == 0, -1.0, 1.0)[:, None]
    t = np.arange(16)
    for k2 in range(2):
        w = np.where(idx < 64, [2, 8][k2], [4, 16][k2]).astype(np.float32)
        c[:, 6 + k2, 0:16] = 1.0 / np.minimum(t[None, :] + 1.0, w[:, None])
    return c


def _prep_inputs(inp):
    f = lambda a: np.ascontiguousarray(np.asarray(a, dtype=np.float32))
    g = {k: np.asarray(v) for k, v in inp.items()}
    sh = {}
    sh["w_in"] = f(g["w_in"].reshape(L, 8, 128, INW).transpose(0, 2, 1, 3))
    sh["w_out"] = f(g["w_out"].reshape(L, 8, 128, 1024).transpose(0, 2, 1, 3))
    wg = g["ffn_w_gate"].reshape(L, 8, 128, 22, 128).transpose(0, 3, 2, 1, 4)
    wu = g["ffn_w_up"].reshape(L, 8, 128, 22, 128).transpose(0, 3, 2, 1, 4)
    sh["w_gu"] = f(np.stack([wg, wu], 3))
    sh["w_down"] = f(g["ffn_w_down"].reshape(L, 22, 128, 1024).transpose(0, 2, 1, 3))
    nw = np.stack([g["norm_mix_pre"], g["norm_mix_post"], g["norm_ffn_pre"], g["norm_ffn_post"]], 0)
    sh["nw"] = f(nw.reshape(4, L, 8, 128).transpose(3, 1, 0, 2))
    sh["cw"] = f(g["conv_w"].reshape(L, 4, 12, 128).transpose(3, 0, 2, 1))
    dnp = np.stack([g["dn_a_log"], g["dn_dt_bias"]], 1)
    sh["dnp"] = f(np.broadcast_to(dnp[None], (64, L, 2, 4)))
    sh["onw"] = f(g["dn_out_norm"].T)
    lam = np.stack([g["ssm_a_re"], g["ssm_a_im"], np.broadcast_to(g["ssm_log_dt"][:, :, None], (L, 16, 64))], -1)
    lam = lam.transpose(2, 0, 1, 3)
    sh["lamp"] = f(np.concatenate([lam, lam], 0))
    bre = g["ssm_b_re"].transpose(2, 0, 1, 3)
    bim = g["ssm_b_im"].transpose(2, 0, 1, 3)
    sh["bnat"] = f(np.concatenate([np.stack([bre, bim], 3), np.stack([bim, bre], 3)], 0))
    cre = g["ssm_c_re"].transpose(3, 0, 1, 2)
    cim = g["ssm_c_im"].transpose(3, 0, 1, 2)
    sh["cnat"] = f(np.concatenate([cre, cim], 0))
    sh["ssmd"] = f(g["ssm_d"].reshape(L, 2, 128).transpose(2, 0, 1))
    sh["glub"] = f(g["ssm_glu_b"].reshape(L, 2, 128).transpose(2, 0, 1))
    sh["pscale"] = f(g["pool_scale"].reshape(L, 2, 128).transpose(2, 0, 1))
    sh["gluw"] = f(g["ssm_glu_w"].reshape(L, 2, 128, 256).transpose(0, 2, 1, 3))
    pw = np.zeros((L, 128, 2, 128), np.float32)
    for k in range(2):
        for g2 in range(2):
            pw[:, g2 * 64:(g2 + 1) * 64, k, g2 * 64:(g2 + 1) * 64] = g["pool_w"][:, 2 * k + g2]
    sh["pw"] = pw
    sh["cst"] = _consts()
    maps = []
    for c in range(8):
        b0, b1 = 16 * c, 16 * c + 16
        m = dict(sh)
        xt = np.concatenate([g["x_prompt"][c], g["x_sample"][b0:b1].reshape(NS, 1024)], 0)
        m["xT"] = f(xt.T.reshape(8, 128, NT).transpose(1, 0, 2))
        m["sdel"] = f(g["state_delta"][:, b0:b1].transpose(0, 1, 3, 2, 4))
        m["sconv"] = f(g["state_conv"][:, b0:b1].reshape(L, 16, 3, 12, 128).transpose(4, 0, 3, 1, 2))
        m["spool"] = f(g["state_pool"][:, b0:b1].reshape(L, 16, 15, 2, 128).transpose(4, 0, 3, 1, 2))
        m["spool_nat"] = f(g["state_pool"][:, b0:b1])
        sre = g["state_ssm_re"][:, b0:b1].transpose(3, 0, 2, 1)
        sim = g["state_ssm_im"][:, b0:b1].transpose(3, 0, 2, 1)
        m["sssm"] = f(np.concatenate([sre, sim], 0))
        maps.append(m)
    return maps


def _assemble(res):
    yp = np.zeros((8, NP_, 1024), np.float32); ys = np.zeros((128, 4, 1024), np.float32)
    dp = np.zeros((L, 8, 4, 128, 128), np.float32); ds = np.zeros((L, 128, 4, 128, 128), np.float32)
    cvp = np.zeros((L, 8, 3, 1536), np.float32); cvs = np.zeros((L, 128, 3, 1536), np.float32)
    rp = np.zeros((L, 8, 16, 64), np.float32); ip = np.zeros((L, 8, 16, 64), np.float32)
    rs = np.zeros((L, 128, 16, 64), np.float32); is_ = np.zeros((L, 128, 16, 64), np.float32)
    pp = np.zeros((L, 8, 15, 256), np.float32); ps = np.zeros((L, 128, 15, 256), np.float32)
    for c in range(8):
        r = {k: np.asarray(v) for k, v in res[c].items()}
        b0, b1 = 16 * c, 16 * c + 16
        y = r["yT"].transpose(1, 0, 2).reshape(1024, NT).T
        yp[c] = y[:NP_]
        ys[b0:b1] = y[NP_:].reshape(16, 4, 1024)
        dp[:, c] = r["o_delta_p"].transpose(0, 2, 1, 3)
        ds[:, b0:b1] = r["o_delta_s"].transpose(0, 1, 3, 2, 4)
        cvp[:, c] = r["o_conv_p"].transpose(1, 3, 2, 0).reshape(L, 3, 1536)
        cvs[:, b0:b1] = r["o_conv_s"].transpose(1, 3, 4, 2, 0).reshape(L, 16, 3, 1536)
        sp_ = r["o_ssm_p"]
        rp[:, c] = sp_[:64].transpose(1, 2, 0)
        ip[:, c] = sp_[64:].transpose(1, 2, 0)
        ss = r["o_ssm_s"]
        rs[:, b0:b1] = ss[:64].transpose(1, 2, 3, 0)
        is_[:, b0:b1] = ss[64:].transpose(1, 2, 3, 0)
        pp[:, c] = r["o_pool_p"].transpose(1, 3, 2, 0).reshape(L, 15, 256)
        ps[:, b0:b1, 0:11] = r["o_pool_s_old"]
        ps[:, b0:b1, 11:15] = r["o_pool_s_new"].transpose(1, 3, 4, 2, 0).reshape(L, 16, 4, 256)
    return (yp, ys, dp, cvp, rp, ip, pp, ds, cvs, rs, is_, ps)


_NC_CACHE = {}


def kernel(**inputs):
    if "nc" not in _NC_CACHE:
        _NC_CACHE["nc"] = build_program()
    maps = _prep_inputs(inputs)
    res = run_bass_kernel_spmd(_NC_CACHE["nc"], maps, core_ids=list(range(8)))
    return _assemble(res.results)
```

```python
import contextlib
import math
import numpy as np
import concourse.bass as bass
import concourse.mybir as mybir
from concourse.bass_utils import run_bass_kernel_spmd

F32 = mybir.dt.float32
BF16 = mybir.dt.bfloat16
I32 = mybir.dt.int32
ALU = mybir.AluOpType
AF = mybir.ActivationFunctionType

L = 4
NP_, NS, NT = 2048, 64, 2112
TW = 256
EPS = 1e-6
INW = 2568
OFF_A, OFF_G, OFF_SSM, OFF_POOL = 1536, 1544, 2056, 2312
DFF = 2816
NKS = 11


class Buf:
    __slots__ = ("w", "r")

    def __init__(self):
        self.w = None
        self.r = []


class Prog:
    ENGS = ("pe", "act", "dve", "pool", "sp")

    def __init__(self, nc, stack, n_dma_sems=40):
        self.nc = nc
        self.sem = {e: stack.enter_context(nc.semaphore("s_" + e)) for e in self.ENGS}
        self.cnt = {e: 0 for e in self.ENGS}
        self.seen = {e: {} for e in self.ENGS}
        self.stream = {e: [] for e in self.ENGS}
        self.dsem = [stack.enter_context(nc.semaphore("d%d" % i)) for i in range(n_dma_sems)]
        self.dcnt = [0] * n_dma_sems

    def _need(self, eng, dep):
        if dep is None:
            return
        if dep[0] == "dma":
            key, val, sem = ("d", dep[1]), dep[2], self.dsem[dep[1]]
        else:
            e2, idx = dep
            if e2 == eng and (eng == "pe" or idx <= self.cnt[eng] - 2):
                return
            key, val, sem = ("e", e2), idx, self.sem[e2]
        if self.seen[eng].get(key, 0) >= val:
            return
        self.seen[eng][key] = val
        self.stream[eng].append(("wait", sem, val))

    def _deps(self, eng, reads, writes):
        deps = []
        for b in reads:
            deps.append(b.w)
        for b in writes:
            deps.append(b.w)
            deps.extend(b.r)
        best = {}
        for d in deps:
            if d is None:
                continue
            key = (d[0], d[1]) if d[0] == "dma" else ("e", d[0])
            val = d[2] if d[0] == "dma" else d[1]
            if key not in best or val > best[key][0]:
                best[key] = (val, d)
        for key in best:
            self._need(eng, best[key][1])

    def _mark(self, me, reads, writes):
        for b in reads:
            b.r.append(me)
            if len(b.r) > 64:
                b.r = b.r[-48:]
        for b in writes:
            b.w = me
            b.r = []

    def op(self, eng, fn, reads=(), writes=()):
        self._deps(eng, reads, writes)
        self.cnt[eng] += 1
        self.stream[eng].append(("inst", fn, self.sem[eng], 1))
        self._mark((eng, self.cnt[eng]), reads, writes)

    def dma(self, q, out, in_, semi, reads=(), writes=(), **kw):
        self._deps(q, reads, writes)
        self.dcnt[semi] += 16
        self.stream[q].append(("inst", lambda h: h.dma_start(out=out, in_=in_, **kw), self.dsem[semi], 16))
        self._mark(("dma", semi, self.dcnt[semi]), reads, writes)

    def barrier(self):
        for e in self.ENGS:
            for e2 in self.ENGS:
                if e2 != e and self.cnt[e2]:
                    self._need(e, (e2, self.cnt[e2]))
            for i, c in enumerate(self.dcnt):
                if c:
                    self._need(e, ("dma", i, c))

    def emit(self):
        nc = self.nc
        with nc.Block() as block:
            def run(e):
                def body(h):
                    for it in self.stream[e]:
                        if it[0] == "wait":
                            h.wait_ge(it[1], it[2])
                        else:
                            it[1](h).then_inc(it[2], it[3])
                return body
            block.tensor(run("pe"))
            block.scalar(run("act"))
            block.vector(run("dve"))
            block.gpsimd(run("pool"))
            block.sync(run("sp"))


MARKS = []


def build_program(NL=L, dbg=None, skip=()):
    nc = bass.Bass("TRN2", target_bir_lowering=False)
    din = lambda n, s, d=F32: nc.dram_tensor(n, list(s), d, kind="ExternalInput").ap()
    dout = lambda n, s, d=F32: nc.dram_tensor(n, list(s), d, kind="ExternalOutput").ap()
    I = {}
    I["xT"] = din("xT", [128, 8, NT])
    I["w_in"] = din("w_in", [L, 128, 8, INW])
    I["w_out"] = din("w_out", [L, 128, 8, 1024])
    I["w_gu"] = din("w_gu", [L, 22, 128, 2, 8, 128])
    I["w_down"] = din("w_down", [L, 128, 22, 1024])
    I["nw"] = din("nw", [128, L, 4, 8])
    I["cw"] = din("cw", [128, L, 12, 4])
    I["dnp"] = din("dnp", [64, L, 2, 4])
    I["onw"] = din("onw", [128, L])
    I["sdel"] = din("sdel", [L, 16, 128, 4, 128])
    I["sconv"] = din("sconv", [128, L, 12, 16, 3])
    I["spool"] = din("spool", [128, L, 2, 16, 15])
    I["spool_nat"] = din("spool_nat", [L, 16, 15, 256])
    I["lamp"] = din("lamp", [128, L, 16, 3])
    I["bnat"] = din("bnat", [128, L, 16, 2, 16])
    I["cnat"] = din("cnat", [128, L, 16, 16])
    I["sssm"] = din("sssm", [128, L, 16, 16])
    I["ssmd"] = din("ssmd", [128, L, 2])
    I["glub"] = din("glub", [128, L, 2])
    I["gluw"] = din("gluw", [L, 128, 2, 256])
    I["pw"] = din("pw", [L, 128, 2, 128])
    I["pscale"] = din("pscale", [128, L, 2])
    I["cst"] = din("cst", [128, 8, 128])
    O = {}
    O["yT"] = dout("yT", [128, 8, NT])
    O["o_delta_p"] = dout("o_delta_p", [L, 128, 4, 128])
    O["o_delta_s"] = dout("o_delta_s", [L, 16, 128, 4, 128])
    O["o_conv_p"] = dout("o_conv_p", [128, L, 12, 3])
    O["o_conv_s"] = dout("o_conv_s", [128, L, 12, 16, 3])
    O["o_ssm_p"] = dout("o_ssm_p", [128, L, 16])
    O["o_ssm_s"] = dout("o_ssm_s", [128, L, 16, 16])
    O["o_pool_p"] = dout("o_pool_p", [128, L, 2, 15])
    O["o_pool_s_old"] = dout("o_pool_s_old", [L, 16, 11, 256])
    O["o_pool_s_new"] = dout("o_pool_s_new", [128, L, 2, 16, 4])
    if dbg:
        O["dbgmix"] = dout("dbgmix", [128, 8, NT])
        O["dbgoo"] = dout("dbgoo", [128, 4, NT])

    with contextlib.ExitStack() as st:
        P = Prog(nc, st)
        sb = lambda n, s, d=F32: st.enter_context(nc.sbuf_tensor(n, list(s), d))
        PS = [st.enter_context(nc.psum_tensor("ps%d" % i, [128, 512], F32)) for i in range(7)]
        PSB = [Buf() for _ in range(7)]
        PT = st.enter_context(nc.psum_tensor("pt", [128, 1024], BF16))
        PTB = Buf()
        X = sb("X", [128, 8, NT]); XB = [Buf() for _ in range(9)]
        CST = sb("CST", [128, 8, 128]); CSTB = Buf()
        CB16 = sb("CB16", [128, 2, 128], BF16)
        NW = sb("NW", [128, L, 4, 8]); CW = sb("CW", [128, L, 12, 4]); DNP = sb("DNP", [64, L, 2, 4])
        ONW = sb("ONW", [128, L]); PSC = sb("PSC", [128, L, 2]); SSMD = sb("SSMD", [128, L, 2]); GLUB = sb("GLUB", [128, L, 2])
        PARB = Buf()
        EPSC = sb("EPSC", [128, 3])

        dmai = [0]

        def nsem(lo=22, hi=40):
            dmai[0] = (dmai[0] + 1) % (hi - lo)
            return lo + dmai[0]

        def mm(out, lhsT, rhs, rd, wr, start=True, stop=True):
            P.op("pe", lambda h: h.matmul(out, lhsT=lhsT, rhs=rhs, start=start, stop=stop), reads=rd, writes=wr)

        def tr(out, in_, ident, rd, wr):
            P.op("pe", lambda h: h.transpose(out, in_, ident), reads=rd, writes=wr)

        def act(out, in_, func, rd, wr, bias=0.0, scale=1.0, eng="act"):
            P.op("act", lambda h: h.activation(out, in_, func, bias=bias, scale=scale), reads=rd, writes=wr)

        def tt(out, a, b, op, rd, wr, eng="dve"):
            P.op(eng, lambda h: h.tensor_tensor(out, a, b, op), reads=rd, writes=wr)

        def ts(out, a, s1, s2, op0, op1, rd, wr, eng="dve"):
            if op1 is None:
                P.op(eng, lambda h: h.tensor_scalar(out, a, s1, None, op0), reads=rd, writes=wr)
            else:
                P.op(eng, lambda h: h.tensor_scalar(out, a, s1, s2, op0, op1), reads=rd, writes=wr)

        def stt(out, a, s, b, op0, op1, rd, wr, eng="dve"):
            P.op(eng, lambda h: h.scalar_tensor_tensor(out, a, s, b, op0, op1), reads=rd, writes=wr)

        def cp(out, in_, rd, wr, eng="dve"):
            if eng == "act":
                P.op("act", lambda h: h.copy(out, in_), reads=rd, writes=wr)
            else:
                P.op(eng, lambda h: h.tensor_copy(out, in_), reads=rd, writes=wr)

        def rcp(out, in_, rd, wr):
            P.op("dve", lambda h: h.reciprocal(out, in_), reads=rd, writes=wr)

        def mset(ap, v, wr, eng="pool"):
            P.op(eng, lambda h: h.memset(ap, v), writes=wr)

        P.dma("sp", CST[:], I["cst"][:], 20, writes=[CSTB])
        for t_, k_ in ((NW, "nw"), (CW, "cw"), (DNP, "dnp"), (ONW, "onw"), (PSC, "pscale"), (SSMD, "ssmd"), (GLUB, "glub")):
            P.dma("sp", t_[:], I[k_][:], 0, writes=[PARB])
        IDF = CST[:, 0, :]
        ONEF = CST[:, 1, :]
        LTRI = {64: CST[:, 2, :], 4: CST[:, 2, :]}
        MLT = {64: CST[:, 3, :], 4: CST[:, 3, :]}
        MSTR = {64: CST[:, 3, :], 4: CST[:, 3, :]}
        MINT = {64: CST[:, 2, :], 4: CST[:, 2, :]}
        SWP = CST[:, 4, :]
        SGN = CST[:, 5, 0:1]
        IDB = CB16[:, 0, :]
        ONEB = CB16[:, 1, :]
        CB16B = Buf()
        cp(CB16[:, 0, :], CST[:, 0, :], [CSTB], [CB16B])
        cp(CB16[:, 1, :], CST[:, 1, :], [CSTB], [CB16B])
        mset(EPSC[:, 0:1], EPS, [PARB], eng="dve")
        mset(EPSC[:, 1:2], 1.0, [PARB], eng="dve")
        mset(EPSC[:, 2:3], math.log(128.0 ** -0.5), [PARB], eng="dve")
        EPS_AP = EPSC[:, 0:1]
        LNQ_AP = EPSC[:, 2:3]

        TILES = [(i * TW, TW, i) for i in range(8)] + [(NP_, NS, 8)]
        for (c0, n, ti) in TILES:
            P.dma("sp", X[:, :, c0:c0 + n], I["xT"][:, :, c0:c0 + n], 1 + ti, writes=[XB[ti]])

        def rms_rstd(src_sq_fn, n, nk, rs_out, rsB, rd, inv_d, sq, sqB, bank, ring=None):
            if ring:
                for kc in range(nk):
                    act(sq[:, kc % ring, :n], src_sq_fn(kc), AF.Square, rd, [sqB[kc % ring]])
                    mm(PS[bank][:, :n], ONEB, sq[:, kc % ring, :n], [sqB[kc % ring], CB16B], [PSB[bank]], start=(kc == 0), stop=(kc == nk - 1))
            else:
                for kc in range(nk):
                    act(sq[:, kc, :n], src_sq_fn(kc), AF.Square, rd, [sqB])
                for kc in range(nk):
                    mm(PS[bank][:, :n], ONEB, sq[:, kc, :n], [sqB, CB16B], [PSB[bank]], start=(kc == 0), stop=(kc == nk - 1))
            act(rs_out[:, :n], PS[bank][:, :n], AF.Ln, [PSB[bank], PARB], [rsB], bias=EPS_AP, scale=inv_d)
            act(rs_out[:, :n], rs_out[:, :n], AF.Exp, [rsB], [rsB], scale=-0.5)

        for l in range(NL):
            last = (l == NL - 1)
            MARKS.append(("L%d start" % l, P.cnt["pe"]))
            with contextlib.ExitStack() as sa:
              if ("A", l) not in skip:
                  sba = lambda n, s, d=F32: sa.enter_context(nc.sbuf_tensor("%s_%d" % (n, l), list(s), d))
                  WINB = Buf(); WOUTB = Buf()
                  GLUW = sba("GLUW", [128, 2, 256], BF16); PWT = sba("PWT", [128, 2, 128], BF16); SMB = Buf()
                  USSM = sba("USSM", [128, 2, NT], BF16); USB = Buf()
                  H = sba("H", [128, 8, TW], BF16); HB = Buf()
                  SQ = sba("SQ", [128, 8, TW], BF16); SQB = Buf()
                  RS = sba("RS", [128, TW]); RSB = Buf()
                  P.dma("pool", GLUW[:], I["gluw"][l], 10, writes=[SMB])
                  P.dma("pool", PWT[:], I["pw"][l], 10, writes=[SMB])

                  def norm_tile(c0, n, ti, which, out, outB):
                      rms_rstd(lambda kc: X[:, kc, c0:c0 + n], n, 8, RS, RSB, [XB[ti]], 1.0 / 1024, SQ, SQB, 0)
                      for kc in range(8):
                          stt(out[:, kc, :n], X[:, kc, c0:c0 + n], NW[:, l, which, kc:kc + 1], RS[:, :n], ALU.mult, ALU.mult,
                              [XB[ti], RSB, PARB], [outB])

                  s1 = contextlib.ExitStack()
                  WSSM = s1.enter_context(nc.sbuf_tensor("WSSM_%d" % l, [128, 8, 256], BF16)); WSB = Buf()
                  P.dma("pool", WSSM[:], I["w_in"][l, :, :, OFF_SSM:OFF_POOL], 31, writes=[WSB])
                  for (c0, n, ti) in TILES:
                      norm_tile(c0, n, ti, 0, H, HB)
                      for oc in range(2):
                          for kc in range(8):
                              mm(PS[1 + oc][:, :n], WSSM[:, kc, oc * 128:(oc + 1) * 128], H[:, kc, :n],
                                 [WSB, HB], [PSB[1 + oc]], start=(kc == 0), stop=(kc == 7))
                          cp(USSM[:, oc, c0:c0 + n], PS[1 + oc][:, :n], [PSB[1 + oc]], [USB], eng="act")
                  P.barrier()
                  s1.close()

                  MARKS.append(("L%d A2 s5" % l, P.cnt["pe"]))
                  with contextlib.ExitStack() as s5:
                      sb5 = lambda n, s, d=F32: s5.enter_context(nc.sbuf_tensor("%s_%d" % (n, l), list(s), d))
                      YS = sb5("YS", [128, 2, NT], BF16); YSB = Buf()
                      LAMP = sb5("LAMP", [128, 16, 3]); BNAT = sb5("BNAT", [128, 16, 2, 16]); CNAT = sb5("CNAT", [128, 16, 16])
                      S0 = sb5("S0", [128, 16, 16]); LB = Buf()
                      P.dma("sp", LAMP[:], I["lamp"][:, l], 12, writes=[LB])
                      P.dma("sp", BNAT[:], I["bnat"][:, l], 12, writes=[LB])
                      P.dma("sp", CNAT[:], I["cnat"][:, l], 12, writes=[LB])
                      P.dma("sp", S0[:], I["sssm"][:, l], 12, writes=[LB])
                      DT = sb5("DT", [128, 16]); ZR = sb5("ZR", [128, 16]); FR = sb5("FR", [128, 16]); FI = sb5("FI", [128, 16, 2], I32)
                      T1 = sb5("T1", [128, 16]); T2 = sb5("T2", [128, 16]); T3 = sb5("T3", [128, 16])
                      AK = sb5("AK", [128, 16, NKS]); BK = sb5("BK", [128, 16, NKS]); TB = Buf()
                      act(DT[:], LAMP[:, :, 2], AF.Exp, [LB], [TB])
                      tt(ZR[:], LAMP[:, :, 0], DT[:], ALU.mult, [LB, TB], [TB])
                      stt(FR[:], LAMP[:, :, 1], 1.0 / (2 * math.pi), DT[:], ALU.mult, ALU.mult, [LB, TB], [TB])

                      def reduce_turns(dst, src):
                          cp(FI[:, :, 0], src, [TB], [TB])
                          cp(T3[:], FI[:, :, 0], [TB], [TB])
                          tt(dst, src, T3[:], ALU.subtract, [TB], [TB])

                      reduce_turns(FR[:], FR[:])
                      for k in range(NKS):
                          act(T1[:], ZR[:], AF.Exp, [TB], [TB], scale=float(2 ** k))
                          act(T2[:], FR[:], AF.Sin, [TB], [TB], scale=2 * math.pi)
                          tt(BK[:, :, k], T1[:], T2[:], ALU.mult, [TB], [TB])
                          ts(T2[:], FR[:], 0.25, None, ALU.add, None, [TB], [TB])
                          reduce_turns(T2[:], T2[:])
                          act(T2[:], T2[:], AF.Sin, [TB], [TB], scale=2 * math.pi)
                          tt(AK[:, :, k], T1[:], T2[:], ALU.mult, [TB], [TB])
                          if k < NKS - 1:
                              ts(FR[:], FR[:], 2.0, None, ALU.mult, None, [TB], [TB])
                              reduce_turns(FR[:], FR[:])
                      FRE = sb5("FRE", [128, 16]); FIM = sb5("FIM", [128, 16]); DEN = sb5("DEN", [128, 16]); AM1 = sb5("AM1", [128, 16])
                      are, aim = LAMP[:, :, 0], LAMP[:, :, 1]
                      tt(DEN[:], are, are, ALU.mult, [LB], [TB]); tt(T1[:], aim, aim, ALU.mult, [LB], [TB])
                      tt(DEN[:], DEN[:], T1[:], ALU.add, [TB], [TB]); rcp(DEN[:], DEN[:], [TB], [TB])
                      ts(AM1[:], AK[:, :, 0], -1.0, None, ALU.add, None, [TB], [TB])
                      tt(T1[:], AM1[:], are, ALU.mult, [TB, LB], [TB]); tt(T2[:], BK[:, :, 0], aim, ALU.mult, [TB, LB], [TB])
                      tt(T1[:], T1[:], T2[:], ALU.add, [TB], [TB]); tt(FRE[:], T1[:], DEN[:], ALU.mult, [TB], [TB])
                      tt(T1[:], BK[:, :, 0], are, ALU.mult, [TB, LB], [TB]); tt(T2[:], AM1[:], aim, ALU.mult, [TB, LB], [TB])
                      tt(T1[:], T1[:], T2[:], ALU.subtract, [TB], [TB]); tt(FIM[:], T1[:], DEN[:], ALU.mult, [TB], [TB])
                      ts(FIM[:], FIM[:], SGN, None, ALU.mult, None, [TB, CSTB], [TB])
                      BB = sb5("BB", [128, 16, 16]); BB2 = sb5("BB2", [128, 16, 16])
                      tt(BB[:], BNAT[:, :, 0, :], FRE[:, :, None].broadcast_to([128, 16, 16]), ALU.mult, [LB, TB], [TB])
                      tt(BB2[:], BNAT[:, :, 1, :], FIM[:, :, None].broadcast_to([128, 16, 16]), ALU.mult, [LB, TB], [TB])
                      tt(BB[:], BB[:], BB2[:], ALU.add, [TB], [TB])
                      CT = sb5("CT", [128, 16, 16])
                      ts(CT[:], CNAT[:], SGN, -1.0, ALU.mult, ALU.mult, [LB, CSTB], [TB])
                      BPAD = sb5("BPAD", [128, 128]); BPB = Buf()
                      BT = sb5("BT", [128, 8, 128], BF16); BTB = [Buf() for _ in range(8)]
                      CPAD = sb5("CPAD", [128, 8, 128], BF16); CPB = Buf()
                      RT = sb5("RT", [128, 8, NKS, 1, 128], BF16); RTB = [Buf() for _ in range(8)]
                      RFA = [sb5("RFA%d" % i, [128, NKS, 128]) for i in range(2)]; RFBt = [sb5("RFBt%d" % i, [128, NKS, 128]) for i in range(2)]
                      RFB = [Buf(), Buf()]
                      XE = sb5("XE", [128, 8, 1 + NP_ + 80], BF16); XEB = [Buf() for _ in range(8)]
                      YF = sb5("YF", [128, TW]); YFB = Buf()
                      SF = sb5("SF", [128, 17, 16]); SFB = Buf()
                      BS = sb5("BS", [128, 16, NKS]);
                      ts(BS[:], BK[:], SGN, -1.0, ALU.mult, ALU.mult, [TB, CSTB], [TB])
                      mset(BPAD[:], 0.0, [BPB]); mset(CPAD[:], 0.0, [CPB])
                      NE = 1 + NP_
                      for oc in range(2):
                          xesv = [XE[:, gi, NE:NE + 80].rearrange("p (b t) -> p b t", t=5) for gi in range(8)]
                          for gi in range(8):
                              g = oc * 8 + gi
                              cp(BPAD[:, gi * 16:(gi + 1) * 16], BB[:, g, :], [TB], [BPB])
                              tr(PS[2][:, 0:128], BPAD[:], IDF, [BPB, CSTB], [PSB[2]])
                              cp(BT[:, gi, :], PS[2][:, 0:128], [PSB[2]], [BTB[gi]])
                              mset(BPAD[:, gi * 16:(gi + 1) * 16], 0.0, [BPB])
                              cp(CPAD[:, gi, gi * 16:(gi + 1) * 16], CT[:, g, :], [TB], [CPB])
                              e_ = "dve" if (gi % 2 == 0) else "pool"
                              r1, r2, rb = RFA[gi % 2], RFBt[gi % 2], RFB[gi % 2]
                              tt(r1[:], SWP[:, None, :].broadcast_to([128, NKS, 128]), BS[:, g, :, None].broadcast_to([128, NKS, 128]), ALU.mult,
                                 [CSTB, TB], [rb], eng=e_)
                              tt(r2[:], IDF[:, None, :].broadcast_to([128, NKS, 128]), AK[:, g, :, None].broadcast_to([128, NKS, 128]), ALU.mult,
                                 [CSTB, TB], [rb], eng=e_)
                              tt(RT[:, gi, :, 0, :], r1[:], r2[:], ALU.add, [rb], [RTB[gi]], eng=e_)
                              mset(XE[:, gi, 0:1], 0.0, [XEB[gi]])
                              cp(xesv[gi][:, :, 0], S0[:, g, :], [LB], [XEB[gi]])
                          bk = [0]

                          def nbank():
                              bk[0] = (bk[0] + 1) % 6
                              return 1 + bk[0]
                          for (c0, n, ti) in TILES:
                              for gi in range(8):
                                  bnk = nbank()
                                  mm(PS[bnk][:, :n], BT[:, gi, :], USSM[:, oc, c0:c0 + n], [BTB[gi], USB], [PSB[bnk]])
                                  if ti < 8:
                                      cp(XE[:, gi, 1 + c0:1 + c0 + n], PS[bnk][:, :n], [PSB[bnk]], [XEB[gi]], eng="act")
                                  else:
                                      cp(xesv[gi][:, :, 1:5], PS[bnk][:, :n].rearrange("p (b t) -> p b t", t=4), [PSB[bnk]], [XEB[gi]], eng="act")
                          for k in range(NKS):
                              s = 2 ** k
                              hi = NE
                              while hi > s:
                                  lo = max(s, hi - 512)
                                  w = hi - lo
                                  for gi in range(8):
                                      bnk = nbank()
                                      mm(PS[bnk][:, :w], RT[:, gi, k, 0, :], XE[:, gi, lo - s:hi - s], [RTB[gi], XEB[gi]], [PSB[bnk]], start=True, stop=False)
                                      mm(PS[bnk][:, :w], IDB, XE[:, gi, lo:hi], [CB16B, XEB[gi]], [PSB[bnk]], start=False, stop=True)
                                      cp(XE[:, gi, lo:hi], PS[bnk][:, :w], [PSB[bnk]], [XEB[gi]], eng=("act" if gi % 2 == 0 else "dve"))
                                  hi = lo
                              if s < 5:
                                  w = 5 - s
                                  for gi in range(8):
                                      bnk = nbank()
                                      pso = PS[bnk][:, :16 * w].rearrange("p (b t) -> p b t", t=w)
                                      mm(pso, RT[:, gi, k, 0, :], xesv[gi][:, :, 0:w], [RTB[gi], XEB[gi]], [PSB[bnk]])
                                      tt(xesv[gi][:, :, s:5], xesv[gi][:, :, s:5], pso, ALU.add, [XEB[gi], PSB[bnk]], [XEB[gi]])
                          for gi in range(8):
                              g = oc * 8 + gi
                              cp(SF[:, 0, g:g + 1], XE[:, gi, NE - 1:NE], [XEB[gi]], [SFB])
                              cp(SF[:, 1:17, g], xesv[gi][:, :, 4], [XEB[gi]], [SFB])
                          if l == 0: MARKS.append(("L0 s5 y oc%d" % oc, P.cnt["pe"]))
                          for (c0, n, ti) in TILES:
                              for gi in range(8):
                                  if ti < 8:
                                      rhs = XE[:, gi, 1 + c0:1 + c0 + n]
                                      mm(PS[3][:, :n], CPAD[:, gi, :], rhs, [CPB, XEB[gi]], [PSB[3]], start=(gi == 0), stop=(gi == 7))
                                  else:
                                      rhs = XE[:, gi, NE:NE + 80].rearrange("p (b t) -> p b t", t=5)[:, :, 1:5]
                                      mm(PS[3][:, :n].rearrange("p (b t) -> p b t", t=4), CPAD[:, gi, :], rhs, [CPB, XEB[gi]], [PSB[3]],
                                         start=(gi == 0), stop=(gi == 7))
                              stt(YF[:, :n], USSM[:, oc, c0:c0 + n], SSMD[:, l, oc:oc + 1], PS[3][:, :n], ALU.mult, ALU.add, [USB, PARB, PSB[3]], [YFB])
                              tt(RS[:, :n], YF[:, :n], YF[:, :n], ALU.mult, [YFB], [RSB])
                              ts(RS[:, :n], RS[:, :n], 0.044715, 1.0, ALU.mult, ALU.add, [RSB], [RSB])
                              tt(RS[:, :n], RS[:, :n], YF[:, :n], ALU.mult, [RSB, YFB], [RSB])
                              act(RS[:, :n], RS[:, :n], AF.Sigmoid, [RSB], [RSB], scale=2.0 * math.sqrt(2.0 / math.pi))
                              tt(YS[:, oc, c0:c0 + n], RS[:, :n], YF[:, :n], ALU.mult, [RSB, YFB], [YSB])
                      P.dma("sp", O["o_ssm_p"][:, l, :], SF[:, 0, :], 25, reads=[SFB])
                      P.dma("sp", O["o_ssm_s"][:, l, :, :], SF[:, 1:17, :], 25, reads=[SFB])
                      for (c0, n, ti) in TILES:
                          for oc in range(2):
                              for kc in range(2):
                                  mm(PS[3][:, :n], GLUW[:, kc, oc * 128:(oc + 1) * 128], YS[:, kc, c0:c0 + n], [SMB, YSB], [PSB[3]],
                                     start=(kc == 0), stop=(kc == 1))
                              act(RS[:, :n], PS[3][:, :n], AF.Sigmoid, [PSB[3], PARB], [RSB], bias=GLUB[:, l, oc:oc + 1])
                              tt(USSM[:, oc, c0:c0 + n], RS[:, :n], YS[:, oc, c0:c0 + n], ALU.mult, [RSB, YSB], [USB])
                      P.barrier()

                  MARKS.append(("L%d A3" % l, P.cnt["pe"]))
                  with contextlib.ExitStack() as s3:
                      sb3 = lambda n, s, d=F32: s3.enter_context(nc.sbuf_tensor("%s_%d" % (n, l), list(s), d))
                      WIN = sb3("WIN", [128, 8, INW], BF16)
                      WOUT = sb3("WOUT", [128, 8, 1024], BF16)
                      P.dma("pool", WIN[:, 0:4], I["w_in"][l, :, 0:4], 11, writes=[WINB])
                      P.dma("pool", WIN[:, 4:8], I["w_in"][l, :, 4:8], 11, writes=[WINB])
                      P.dma("pool", WOUT[:], I["w_out"][l], 21, writes=[WOUTB])
                      PRE = sb3("PRE", [128, 12, 3 + TW], BF16); PREB = Buf()
                      PRES = PRE
                      QKV = sb3("QKV", [128, 12, TW], BF16); QKVB = Buf()
                      CV = sb3("CV", [128, TW]); CVB = Buf()
                      GT = sb3("GT", [128, 4, TW], BF16); GTB = Buf()
                      OO = sb3("OO", [128, 4, TW]); OOB = Buf()
                      MIX = sb3("MIX", [128, 8, TW], BF16); MIXB = Buf()
                      MO = sb3("MO", [128, 8, TW]); MOB = Buf()
                      PX = sb3("PX", [128, 2, 320]); PXB = Buf()
                      PA = sb3("PA", [128, 320]); PBt = sb3("PBt", [128, 320]); PAB = Buf()
                      CTAIL = sb3("CTAIL", [128, 12, 3]); CTB = Buf()
                      SCS = sb3("SCS", [128, 12, 16, 3]); SCSB = Buf()
                      SST = sb3("SST", [128, 4, 128]); SSB = Buf()
                      SBF = sb3("SBF", [128, 4, 128], BF16)
                      NEGA = sb3("NEGA", [64, 4]); NGB = Buf()
                      ZT = sb3("ZT", [64, 8]); LGt = sb3("LGt", [64, 4]); BET = sb3("BET", [64, 4]); EGt = sb3("EGt", [64, 4]); EKD = sb3("EKD", [64, 4])
                      EGL = sb3("EGL", [128, 4]); SCB = Buf()
                      LG = sb3("LG", [64, 4, 64]); LGB = Buf()
                      E2 = sb3("E2", [64, 2, 4, 64]); E2B = Buf()
                      EGB = sb3("EGB", [128, 4, 64]); EGBB = Buf()
                      QD = sb3("QD", [128, 4, 64], BF16); QDB = Buf()
                      KTK = sb3("KTK", [64, 4, 128], BF16); VTK = sb3("VTK", [64, 4, 128], BF16); TKB = Buf()
                      RHK = sb3("RHK", [64, 4, 128], BF16); VBt = sb3("VBt", [64, 4, 128], BF16); KDt = sb3("KDt", [64, 4, 128], BF16); RKB = Buf()
                      AF_ = sb3("AF_", [64, 4, 64]); AFB = Buf()
                      PB_ = sb3("PB_", [64, 2, 4, 64], BF16); PBB = Buf()
                      QKT = sb3("QKT", [64, 4, 64], BF16); QKB = Buf()
                      TTf = sb3("TTf", [64, 4, 64]); TTb = sb3("TTb", [64, 4, 64], BF16); TTB = Buf()
                      KC = sb3("KC", [128, 4, 64], BF16); KCB = Buf()
                      WS = sb3("WS", [64, 4, 128]); WSB = Buf()
                      UU = sb3("UU", [64, 4, 128], BF16); UUB = Buf()
                      act(NEGA[:], DNP[:, l, 0, :], AF.Exp, [PARB], [NGB])
                      ts(NEGA[:], NEGA[:], -1.0, None, ALU.mult, None, [NGB], [NGB])
                      mset(CTAIL[:], 0.0, [CTB])
                      mset(PX[:, :, 0:15], 0.0, [PXB])
                      mset(SST[:], 0.0, [SSB]); mset(SBF[:], 0.0, [SSB])

                      for (c0, n, ti) in TILES:
                          samp = (ti == 8)
                          C = 4 if samp else 64
                          nch = n // C
                          hist = 3
                          norm_tile(c0, n, ti, 0, H, HB)
                          if not samp:
                              cp(PRE[:, :, 0:3], CTAIL[:], [CTB], [PREB])
                              pre_new = lambda blk: PRE[:, blk, 3:3 + n]
                          else:
                              prs = PRE[:, :, 0:112].rearrange("p k (b t) -> p k b t", t=7)
                              P.dma("sp", SCS[:], I["sconv"][:, l], 14, writes=[SCSB])
                              cp(prs[:, :, :, 0:3], SCS[:], [SCSB], [PREB])
                              pre_new = lambda blk: prs[:, blk, :, 3:7]
                          for blk in range(12):
                              col = (blk // 4) * 512 + (blk % 4) * 128
                              bnk = 1 + (blk % 3)
                              for kc in range(8):
                                  mm(PS[bnk][:, :n], WIN[:, kc, col:col + 128], H[:, kc, :n], [WINB, HB], [PSB[bnk]], start=(kc == 0), stop=(kc == 7))
                              if not samp:
                                  cp(pre_new(blk), PS[bnk][:, :n], [PSB[bnk]], [PREB], eng=("act" if blk % 2 == 0 else "dve"))
                              else:
                                  cp(pre_new(blk), PS[bnk][:, :n].rearrange("p (b t) -> p b t", t=4), [PSB[bnk]], [PREB], eng=("act" if blk % 2 == 0 else "dve"))
                          if not samp:
                              cp(CTAIL[:], PRE[:, :, n:n + 3], [PREB], [CTB])
                              if ti == 7:
                                  P.dma("sp", O["o_conv_p"][:, l], CTAIL[:], 23, reads=[CTB])
                          else:
                              cp(SCS[:], prs[:, :, :, 4:7], [PREB], [SCSB])
                              P.dma("sp", O["o_conv_s"][:, l], SCS[:], 24, reads=[SCSB])
                          for blk in range(12):
                              if not samp:
                                  tap = lambda j: PRE[:, blk, j:j + n]
                                  cvv = CV[:, :n]
                              else:
                                  tap = lambda j: prs[:, blk, :, j:j + 4]
                                  cvv = CV[:, :n].rearrange("p (b t) -> p b t", t=4)
                              ts(cvv, tap(0), CW[:, l, blk, 0:1], None, ALU.mult, None, [PREB, PARB], [CVB])
                              for j in range(1, 4):
                                  stt(cvv, tap(j), CW[:, l, blk, j:j + 1], cvv, ALU.mult, ALU.add, [PREB, PARB, CVB], [CVB])
                              act(QKV[:, blk, :n], CV[:, :n], AF.Silu, [CVB], [QKVB])
                          for hh in range(4):
                              col = OFF_G + hh * 128
                              bnk = 1 + (hh % 2)
                              for kc in range(8):
                                  mm(PS[bnk][:, :n], WIN[:, kc, col:col + 128], H[:, kc, :n], [WINB, HB], [PSB[bnk]], start=(kc == 0), stop=(kc == 7))
                              act(GT[:, hh, :n], PS[bnk][:, :n], AF.Silu, [PSB[bnk]], [GTB])
                          for blk in range(8):
                              bnk = 3 + (blk % 2)
                              act(SQ[:, blk, :n], QKV[:, blk, :n], AF.Square, [QKVB], [SQB])
                              mm(PS[bnk][:, :n], ONEB, SQ[:, blk, :n], [SQB, CB16B], [PSB[bnk]])
                              act(MO[:, blk, :n], PS[bnk][:, :n], AF.Ln, [PSB[bnk], PARB], [MOB], bias=EPS_AP, scale=1.0)
                          for blk in range(8):
                              act(MO[:, blk, :n], MO[:, blk, :n], AF.Exp, [MOB], [MOB], scale=-0.5, bias=(LNQ_AP if blk < 4 else 0.0))
                          for blk in range(8):
                              tt(QKV[:, blk, :n], QKV[:, blk, :n], MO[:, blk, :n], ALU.mult, [QKVB, MOB], [QKVB])
                          if l == 0: MARKS.append(("L0 t%d delta" % ti, P.cnt["pe"]))
                          for ci in range(nch):
                              a0 = ci * C
                              cols = slice(a0, a0 + C)
                              if samp:
                                  P.dma("sp", SST[:], I["sdel"][l, ci], 13, writes=[SSB])
                                  cp(SBF[:], SST[:], [SSB], [SSB])
                              for kc in range(8):
                                  mm(PS[2][:C, 0:8], H[:, kc, cols], WIN[:, kc, OFF_A:OFF_A + 8], [HB, WINB], [PSB[2]], start=(kc == 0), stop=(kc == 7))
                              tt(ZT[:C, 0:4], PS[2][:C, 0:4], DNP[:C, l, 1, :], ALU.add, [PSB[2], PARB], [SCB])
                              act(BET[:C], PS[2][:C, 4:8], AF.Exp, [PSB[2]], [SCB], scale=-1.0)
                              act(BET[:C], BET[:C], AF.Ln, [SCB, PARB], [SCB], bias=EPSC[:C, 1:2])
                              act(BET[:C], BET[:C], AF.Exp, [SCB], [SCB], scale=-1.0)
                              act(ZT[:C, 0:4], ZT[:C, 0:4], AF.Exp, [SCB], [SCB])
                              act(ZT[:C, 0:4], ZT[:C, 0:4], AF.Ln, [SCB, PARB], [SCB], bias=EPSC[:C, 1:2])
                              tt(LGt[:C], ZT[:C, 0:4], NEGA[:C], ALU.mult, [SCB, NGB], [SCB])
                              tt(LG[:C, :, :C], LTRI[C][:C, None, :C].broadcast_to([C, 4, C]), LGt[:C, :, None].broadcast_to([C, 4, C]), ALU.mult,
                                 [CSTB, SCB], [LGB])
                              d2 = PS[3][:C, :].rearrange("p (a h c) -> p a h c", a=2, h=4)
                              for hh in range(4):
                                  mm(d2[:, 0, hh, :C], LG[:C, hh, :C], MLT[C][:C, :C], [LGB, CSTB], [PSB[3]])
                                  mm(d2[:, 1, hh, :C], MLT[C][:C, :C], LG[:C, hh, :C], [LGB, CSTB], [PSB[3]])
                              act(E2[:C, :, :, :C], d2[:, :, :, :C], AF.Exp, [PSB[3]], [E2B])
                              tt(E2[:C, 0, :, :C], E2[:C, 0, :, :C], MSTR[C][:C, None, :C].broadcast_to([C, 4, C]), ALU.mult, [E2B, CSTB], [E2B])
                              tt(E2[:C, 1, :, :C], E2[:C, 1, :, :C], MINT[C][:C, None, :C].broadcast_to([C, 4, C]), ALU.mult, [E2B, CSTB], [E2B])
                              mm(PS[2][:C, 8:12], LTRI[C][:C, :C], LGt[:C, :], [CSTB, SCB], [PSB[2]])
                              mm(PS[2][:, 16:20], ONEF[:C, :], LGt[:C, :], [CSTB, SCB], [PSB[2]])
                              act(EGt[:C], PS[2][:C, 8:12], AF.Exp, [PSB[2]], [SCB])
                              act(EGL[:], PS[2][:, 16:20], AF.Exp, [PSB[2]], [SCB])
                              cp(ZT[:C, 4:8], PS[2][:C, 8:12], [PSB[2]], [SCB])
                              tt(EKD[:C], PS[2][:C, 16:20], ZT[:C, 4:8], ALU.subtract, [PSB[2], SCB], [SCB])
                              act(EKD[:C], EKD[:C], AF.Exp, [SCB], [SCB])
                              eg_ps = PS[4][:, 0:4 * C].rearrange("p (h c) -> p h c", h=4)
                              for hh in range(4):
                                  mm(eg_ps[:, hh, :], ONEF[:C, :], LG[:C, hh, :C], [CSTB, LGB], [PSB[4]])
                              act(EGB[:, :, :C], eg_ps, AF.Exp, [PSB[4]], [EGBB])
                              tt(QD[:, :, :C], QKV[:, 0:4, cols], EGB[:, :, :C], ALU.mult, [QKVB, EGBB], [QDB])
                              kt_ps = PT[:C, 0:512].rearrange("p (h d) -> p h d", h=4)
                              vt_ps = PT[:C, 512:1024].rearrange("p (h d) -> p h d", h=4)
                              for hh in range(4):
                                  tr(kt_ps[:, hh, :], QKV[:, 4 + hh, cols], IDB, [QKVB, CB16B], [PTB])
                                  tr(vt_ps[:, hh, :], QKV[:, 8 + hh, cols], IDB, [QKVB, CB16B], [PTB])
                              cp(KTK[:C], kt_ps, [PTB], [TKB])
                              cp(VTK[:C], vt_ps, [PTB], [TKB])
                              tt(VBt[:C], VTK[:C], BET[:C, :, None].broadcast_to([C, 4, 128]), ALU.mult, [TKB, SCB], [RKB])
                              tt(RHK[:C], KTK[:C], BET[:C, :, None].broadcast_to([C, 4, 128]), ALU.mult, [TKB, SCB], [RKB])
                              tt(RHK[:C], RHK[:C], EGt[:C, :, None].broadcast_to([C, 4, 128]), ALU.mult, [RKB, SCB], [RKB])
                              tt(KDt[:C], KTK[:C], EKD[:C, :, None].broadcast_to([C, 4, 128]), ALU.mult, [TKB, SCB], [RKB])
                              kq = PS[5][:C, :].rearrange("p (a h c) -> p a h c", a=2, h=4)
                              for hh in range(4):
                                  mm(kq[:, 0, hh, :C], QKV[:, 4 + hh, cols], QKV[:, 4 + hh, cols], [QKVB], [PSB[5]])
                                  mm(kq[:, 1, hh, :C], QKV[:, 4 + hh, cols], QKV[:, hh, cols], [QKVB], [PSB[5]])
                              tt(AF_[:C, :, :C], kq[:, 0, :, :C], E2[:C, 0, :, :C], ALU.mult, [PSB[5], E2B], [AFB])
                              tt(AF_[:C, :, :C], AF_[:C, :, :C], BET[:C, :, None].broadcast_to([C, 4, C]), ALU.mult, [AFB, SCB], [AFB])
                              tt(QKT[:C, :, :C], kq[:, 1, :, :C], E2[:C, 1, :, :C], ALU.mult, [PSB[5], E2B], [QKB])
                              at_ps = PS[6][:C, 0:4 * C].rearrange("p (h c) -> p h c", h=4)
                              for hh in range(4):
                                  tr(at_ps[:, hh, :], AF_[:C, hh, :C], IDF[:C, :C], [AFB, CSTB], [PSB[6]])
                              cp(PB_[:C, 0, :, :C], AF_[:C, :, :C], [AFB], [PBB])
                              cp(PB_[:C, 1, :, :C], at_ps, [PSB[6]], [PBB])
                              tt(TTf[:C, :, :C], IDF[:C, None, :C].broadcast_to([C, 4, C]), at_ps, ALU.subtract, [CSTB, PSB[6]], [TTB])
                              cp(TTb[:C, :, :C], TTf[:C, :, :C], [TTB], [TTB])
                              nlev = 5 if C == 64 else 1
                              for lev in range(nlev):
                                  p2 = PS[3][:C, :].rearrange("p (a h c) -> p a h c", a=2, h=4)
                                  for hh in range(4):
                                      mm(p2[:, 0, hh, :C], PB_[:C, 1, hh, :C], PB_[:C, 0, hh, :C], [PBB], [PSB[3]])
                                      mm(p2[:, 1, hh, :C], PB_[:C, 0, hh, :C], PB_[:C, 1, hh, :C], [PBB], [PSB[3]])
                                  cp(PB_[:C, :, :, :C], p2[:, :, :, :C], [PSB[3]], [PBB], eng="act")
                                  tu = PS[6][:C, 0:4 * C].rearrange("p (h c) -> p h c", h=4)
                                  for hh in range(4):
                                      mm(tu[:, hh, :], PB_[:C, 0, hh, :C], TTb[:C, hh, :C], [PBB, TTB], [PSB[6]])
                                  tt(TTb[:C, :, :C], TTb[:C, :, :C], tu, ALU.add, [TTB, PSB[6]], [TTB])
                              w_ps = PS[3][:C, :].rearrange("p (h d) -> p h d", h=4)
                              kc_ps = PS[4][:, 0:4 * C].rearrange("p (h c) -> p h c", h=4)
                              for hh in range(4):
                                  mm(w_ps[:, hh, :], TTb[:C, hh, :C], VBt[:C, hh, :], [TTB, RKB], [PSB[3]])
                                  mm(kc_ps[:, hh, :], RHK[:C, hh, :], TTb[:C, hh, :C], [TTB, RKB], [PSB[4]])
                              cp(WS[:C], w_ps, [PSB[3]], [WSB], eng="act")
                              cp(KC[:, :, :C], kc_ps, [PSB[4]], [KCB])
                              p1 = PS[5][:C, :].rearrange("p (h d) -> p h d", h=4)
                              for hh in range(4):
                                  mm(p1[:, hh, :], KC[:, hh, :C], SBF[:, hh, :], [KCB, SSB], [PSB[5]])
                              tt(UU[:C], WS[:C], p1, ALU.subtract, [WSB, PSB[5]], [UUB])
                              o_ps = PS[6][:, 0:4 * C].rearrange("p (h c) -> p h c", h=4)
                              for hh in range(4):
                                  mm(o_ps[:, hh, :], SBF[:, hh, :], QD[:, hh, :C], [SSB, QDB], [PSB[6]], start=True, stop=False)
                                  mm(o_ps[:, hh, :], UU[:C, hh, :], QKT[:C, hh, :C], [UUB, QKB], [PSB[6]], start=False, stop=True)
                              cp(OO[:, :, cols], o_ps, [PSB[6]], [OOB], eng="act")
                              ds = PS[3][:, :].rearrange("p (h d) -> p h d", h=4)
                              for hh in range(4):
                                  mm(ds[:, hh, :], KDt[:C, hh, :], UU[:C, hh, :], [RKB, UUB], [PSB[3]])
                              tt(SST[:], SST[:], EGL[:, :, None].broadcast_to([128, 4, 128]), ALU.mult, [SSB, SCB], [SSB])
                              tt(SST[:], SST[:], ds, ALU.add, [SSB, PSB[3]], [SSB])
                              cp(SBF[:], SST[:], [SSB], [SSB])
                              if samp:
                                  P.dma("sp", O["o_delta_s"][l, ci], SST[:], 22, reads=[SSB])
                              elif ti == 7 and ci == nch - 1:
                                  P.dma("sp", O["o_delta_p"][l], SST[:], 22, reads=[SSB])
                          if l == 0: MARKS.append(("L0 t%d postdelta" % ti, P.cnt["pe"]))
                          for hh in range(4):
                              act(SQ[:, hh, :n], OO[:, hh, :n], AF.Square, [OOB], [SQB])
                              mm(PS[2][:, :n], ONEB, SQ[:, hh, :n], [SQB, CB16B], [PSB[2]])
                              act(RS[:, :n], PS[2][:, :n], AF.Ln, [PSB[2], PARB], [RSB], bias=EPS_AP, scale=1.0 / 128)
                              act(RS[:, :n], RS[:, :n], AF.Exp, [RSB], [RSB], scale=-0.5)
                              stt(CV[:, :n], OO[:, hh, :n], ONW[:, l:l + 1], RS[:, :n], ALU.mult, ALU.mult, [OOB, PARB, RSB], [CVB])
                              tt(MIX[:, hh, :n], CV[:, :n], GT[:, hh, :n], ALU.mult, [CVB, GTB], [MIXB])
                          for oc in range(2):
                              cp(MIX[:, 4 + oc, :n], USSM[:, oc, c0:c0 + n], [USB], [MIXB], eng="pool")
                          if samp:
                              pxs = PX[:, :, 0:16 * 19].rearrange("p k (b t) -> p k b t", t=19)
                              for k_ in range(2):
                                  P.dma("sp", pxs[:, k_, :, 0:15], I["spool"][:, l, k_], 15, writes=[PXB])
                          for k2 in range(2):
                              col = OFF_POOL + k2 * 128
                              for kc in range(8):
                                  mm(PS[1][:, :n], WIN[:, kc, col:col + 128], H[:, kc, :n], [WINB, HB], [PSB[1]], start=(kc == 0), stop=(kc == 7))
                              if not samp:
                                  cp(PX[:, k2, 15:15 + n], PS[1][:, :n], [PSB[1]], [PXB], eng="act")
                              else:
                                  cp(pxs[:, k2, :, 15:19], PS[1][:, :n].rearrange("p (b t) -> p b t", t=4), [PSB[1]], [PXB], eng="act")
                          if samp:
                              for k_ in range(2):
                                  P.dma("sp", O["o_pool_s_new"][:, l, k_], pxs[:, k_, :, 15:19], 27, reads=[PXB])
                              P.dma("sp", O["o_pool_s_old"][l], I["spool_nat"][l, :, 4:15, :], 28)
                          elif ti == 7:
                              P.dma("sp", O["o_pool_p"][:, l], PX[:, :, n:n + 15], 26, reads=[PXB])
                          for k2 in range(2):
                              if not samp:
                                  W_ = 15 + n
                                  src = PX[:, k2, 0:W_]; a_ = PA[:, 0:W_]; b_ = PBt[:, 0:W_]
                                  sh = lambda v, s_: (v[:, s_:W_], v[:, 0:W_ - s_])
                                  full = lambda v: v
                              else:
                                  W_ = 19
                                  src = pxs[:, k2]; a_ = PA[:, 0:16 * 19].rearrange("p (b t) -> p b t", t=19); b_ = PBt[:, 0:16 * 19].rearrange("p (b t) -> p b t", t=19)
                                  sh = lambda v, s_: (v[:, :, s_:W_], v[:, :, 0:W_ - s_])
                                  full = lambda v: v
                              cp(a_, src, [PXB], [PAB], eng="pool")
                              hi_, lo_ = sh(a_, 1); xh, xl = sh(src, 1)
                              tt(hi_, xh, xl, ALU.add, [PXB, PAB], [PAB])
                              cur, oth = a_, b_
                              for d in range(1, 4):
                                  need_lo = (2 ** d) < (2 if k2 == 0 else 8)
                                  need_hi = (2 ** d) < (4 if k2 == 0 else 16)
                                  if not (need_lo or need_hi):
                                      break
                                  s_ = 2 ** d
                                  cp(oth, cur, [PAB], [PAB], eng="pool")
                                  oh, ol = sh(oth, s_); ch, cl = sh(cur, s_)
                                  if need_lo:
                                      tt(oh[0:64], ch[0:64], cl[0:64], ALU.add, [PAB], [PAB])
                                  if need_hi:
                                      tt(oh[64:128], ch[64:128], cl[64:128], ALU.add, [PAB], [PAB])
                                  cur, oth = oth, cur
                              if not samp:
                                  wv = cur[:, 15:15 + n]; xv = PX[:, k2, 15:15 + n]; rv = oth[:, 15:15 + n]
                              else:
                                  wv = cur[:, :, 15:19]; xv = pxs[:, k2, :, 15:19]; rv = oth[:, :, 15:19]
                              for half in range(2):
                                  wsz = [2, 4, 8, 16][2 * k2 + half]
                                  pr = slice(64 * half, 64 * half + 64)
                                  stt(rv[pr], wv[pr], 1.0 / wsz, xv[pr], ALU.mult, ALU.subtract, [PAB, PXB], [PAB])
                              if ti == 0:
                                  tt(rv[:, 0:16], wv[:, 0:16], CST[:, 6 + k2, 0:16], ALU.mult, [PAB, CSTB], [PAB])
                                  tt(rv[:, 0:16], rv[:, 0:16], xv[:, 0:16], ALU.subtract, [PAB, PXB], [PAB])
                              if not samp:
                                  cp(SQ[:, k2, :n], rv, [PAB], [SQB])
                              else:
                                  cp(SQ[:, k2, :n].rearrange("p (b t) -> p b t", t=4), rv, [PAB], [SQB])
                              mm(PS[2][:, :n], PWT[:, k2, :], SQ[:, k2, :n], [SMB, SQB], [PSB[2]])
                              ts(MIX[:, 6 + k2, :n], PS[2][:, :n], PSC[:, l, k2:k2 + 1], None, ALU.mult, None, [PSB[2], PARB], [MIXB])
                          if not samp:
                              cp(PX[:, :, 0:15], PX[:, :, n:n + 15], [PXB], [PXB], eng="pool")
                          if dbg and l == 0:
                              P.dma("pool", O["dbgmix"][:, :, c0:c0 + n], MIX[:, :, :n], 29, reads=[MIXB])
                              P.dma("sp", O["dbgoo"][:, :, c0:c0 + n], OO[:, :, :n], 30, reads=[OOB])
                          for oc in range(8):
                              bnk = 1 + (oc % 3)
                              for kc in range(8):
                                  mm(PS[bnk][:, :n], WOUT[:, kc, oc * 128:(oc + 1) * 128], MIX[:, kc, :n], [WOUTB, MIXB], [PSB[bnk]], start=(kc == 0), stop=(kc == 7))
                              cp(MO[:, oc, :n], PS[bnk][:, :n], [PSB[bnk]], [MOB], eng=("act" if oc % 2 == 0 else "dve"))
                          rms_rstd(lambda kc: MO[:, kc, :n], n, 8, RS, RSB, [MOB], 1.0 / 1024, SQ, SQB, 0)
                          for kc in range(8):
                              tt(MO[:, kc, :n], MO[:, kc, :n], RS[:, :n], ALU.mult, [MOB, RSB], [MOB])
                              stt(X[:, kc, c0:c0 + n], MO[:, kc, :n], NW[:, l, 1, kc:kc + 1], X[:, kc, c0:c0 + n], ALU.mult, ALU.add,
                                  [MOB, PARB, XB[ti]], [XB[ti]])
                      P.barrier()
            MARKS.append(("L%d FFN" % l, P.cnt["pe"]))
            with contextlib.ExitStack() as sf:
              if ("F", l) not in skip:
                  sbf_ = lambda n, s, d=F32: sf.enter_context(nc.sbuf_tensor("%s_%d" % (n, l), list(s), d))
                  HW_ = 1088
                  H2 = sbf_("H2", [128, 8, HW_], BF16); H2B = Buf()
                  AV = sbf_("AV", [128, 22, HW_], BF16); AVB = Buf()
                  WG = [sbf_("WG%d" % i, [128, 2, 8, 128], BF16) for i in range(2)]; WGB = [Buf() for _ in range(2)]
                  WDA = sbf_("WDA", [128, 22, 1024], BF16); WDAB = Buf()
                  for a_ in range(0, 22, 6):
                      b_ = min(22, a_ + 6)
                      P.dma("pool", WDA[:, a_:b_, :], I["w_down"][l, :, a_:b_, :], 19, writes=[WDAB])
                  SQ2 = sbf_("SQ2", [128, 2, 512], BF16); SQ2B = [Buf(), Buf()]
                  RS2 = sbf_("RS2", [128, 512]); RS2B = Buf()
                  GS = sbf_("GS", [128, 2, 512], BF16); GSB = [Buf(), Buf()]
                  FO = sbf_("FO", [128, 8, 512], BF16); FOB = Buf()
                  unit = [0]
                  for half in range(2):
                      h0 = half * 1024
                      subt = [(0, 512), (512, 512)] if half == 0 else [(1024, 512), (1536, 512), (2048, 64)]
                      xbl = lambda c0, n: [XB[t_] for t_ in range(min(c0 // TW, 8), min((c0 + n - 1) // TW, 8) + 1)]
                      for (c0, n) in subt:
                          xbs = xbl(c0, n)
                          rms_rstd(lambda kc: X[:, kc, c0:c0 + n], n, 8, RS2, RS2B, xbs, 1.0 / 1024, SQ2, SQ2B, 0, ring=2)
                          for kc in range(8):
                              stt(H2[:, kc, c0 - h0:c0 - h0 + n], X[:, kc, c0:c0 + n], NW[:, l, 2, kc:kc + 1], RS2[:, :n], ALU.mult, ALU.mult,
                                  xbs + [RS2B, PARB], [H2B])
                      for fc in range(22):
                          sl = fc % 2
                          P.dma("pool", WG[sl][:], I["w_gu"][l, fc], 16 + sl, writes=[WGB[sl]])
                          for si, (c0, n) in enumerate(subt):
                              r0 = c0 - h0
                              u_ = unit[0] % 2
                              unit[0] += 1
                              bg, bu = 1 + 2 * u_, 2 + 2 * u_
                              for kc in range(8):
                                  mm(PS[bg][:, :n], WG[sl][:, 0, kc, :], H2[:, kc, r0:r0 + n], [WGB[sl], H2B], [PSB[bg]], start=(kc == 0), stop=(kc == 7))
                              for kc in range(8):
                                  mm(PS[bu][:, :n], WG[sl][:, 1, kc, :], H2[:, kc, r0:r0 + n], [WGB[sl], H2B], [PSB[bu]], start=(kc == 0), stop=(kc == 7))
                              act(GS[:, u_, :n], PS[bg][:, :n], AF.Silu, [PSB[bg]], [GSB[u_]])
                              tt(AV[:, fc, r0:r0 + n], GS[:, u_, :n], PS[bu][:, :n], ALU.mult, [GSB[u_], PSB[bu]], [AVB])
                      for si, (c0, n) in enumerate(subt):
                          r0 = c0 - h0
                          xbs = xbl(c0, n)
                          for ogrp in range(2):
                              for fc in range(22):
                                  for o4 in range(4):
                                      oo_ = (ogrp * 4 + o4) * 128
                                      mm(PS[1 + o4][:, :n], WDA[:, fc, oo_:oo_ + 128], AV[:, fc, r0:r0 + n],
                                         [WDAB, AVB], [PSB[1 + o4]], start=(fc == 0), stop=(fc == 21))
                              for o4 in range(4):
                                  cp(FO[:, ogrp * 4 + o4, :n], PS[1 + o4][:, :n], [PSB[1 + o4]], [FOB], eng="act")
                          rms_rstd(lambda kc: FO[:, kc, :n], n, 8, RS2, RS2B, [FOB], 1.0 / 1024, SQ2, SQ2B, 0, ring=2)
                          for kc in range(8):
                              tt(FO[:, kc, :n], FO[:, kc, :n], RS2[:, :n], ALU.mult, [FOB, RS2B], [FOB])
                              stt(X[:, kc, c0:c0 + n], FO[:, kc, :n], NW[:, l, 3, kc:kc + 1], X[:, kc, c0:c0 + n], ALU.mult, ALU.add,
                                  [FOB, PARB] + xbs, xbs)
                  P.barrier()
        for (c0, n, ti) in TILES:
            P.dma("sp", O["yT"][:, :, c0:c0 + n], X[:, :, c0:c0 + n], 1 + ti, reads=[XB[ti]])
        P.barrier()
        P.emit()
    return nc


def _consts():
    c = np.zeros((128, 8, 128), np.float32)
    idx = np.arange(128)
    c[:, 0, :] = np.eye(128, dtype=np.float32)
    c[:, 1, :] = 1.0
    c[:, 2, :] = (idx[:, None] <= idx[None, :])
    c[:, 3, :] = (idx[None, :] < idx[:, None])
    p = idx % 64
    ch = idx // 64
    c[:, 4, :] = ((p[:, None] == p[None, :]) & (ch[:, None] != ch[None, :]))
    c[:, 5, :] = np.where(ch == 0, -1.0, 1.0)[:, None]
    t = np.arange(16)
    for k2 in range(2):
        w = np.where(idx < 64, [2, 8][k2], [4, 16][k2]).astype(np.float32)
        c[:, 6 + k2, 0:16] = 1.0 / np.minimum(t[None, :] + 1.0, w[:, None])
    return c


def _prep_inputs(inp):
    f = lambda a: np.ascontiguousarray(np.asarray(a, dtype=np.float32))
    g = {k: np.asarray(v) for k, v in inp.items()}
    sh = {}
    sh["w_in"] = f(g["w_in"].reshape(L, 8, 128, INW).transpose(0, 2, 1, 3))
    sh["w_out"] = f(g["w_out"].reshape(L, 8, 128, 1024).transpose(0, 2, 1, 3))
    wg = g["ffn_w_gate"].reshape(L, 8, 128, 22, 128).transpose(0, 3, 2, 1, 4)
    wu = g["ffn_w_up"].reshape(L, 8, 128, 22, 128).transpose(0, 3, 2, 1, 4)
    sh["w_gu"] = f(np.stack([wg, wu], 3))
    sh["w_down"] = f(g["ffn_w_down"].reshape(L, 22, 128, 1024).transpose(0, 2, 1, 3))
    nw = np.stack([g["norm_mix_pre"], g["norm_mix_post"], g["norm_ffn_pre"], g["norm_ffn_post"]], 0)
    sh["nw"] = f(nw.reshape(4, L, 8, 128).transpose(3, 1, 0, 2))
    sh["cw"] = f(g["conv_w"].reshape(L, 4, 12, 128).transpose(3, 0, 2, 1))
    dnp = np.stack([g["dn_a_log"], g["dn_dt_bias"]], 1)
    sh["dnp"] = f(np.broadcast_to(dnp[None], (64, L, 2, 4)))
    sh["onw"] = f(g["dn_out_norm"].T)
    lam = np.stack([g["ssm_a_re"], g["ssm_a_im"], np.broadcast_to(g["ssm_log_dt"][:, :, None], (L, 16, 64))], -1)
    lam = lam.transpose(2, 0, 1, 3)
    sh["lamp"] = f(np.concatenate([lam, lam], 0))
    bre = g["ssm_b_re"].transpose(2, 0, 1, 3)
    bim = g["ssm_b_im"].transpose(2, 0, 1, 3)
    sh["bnat"] = f(np.concatenate([np.stack([bre, bim], 3), np.stack([bim, bre], 3)], 0))
    cre = g["ssm_c_re"].transpose(3, 0, 1, 2)
    cim = g["ssm_c_im"].transpose(3, 0, 1, 2)
    sh["cnat"] = f(np.concatenate([cre, cim], 0))
    sh["ssmd"] = f(g["ssm_d"].reshape(L, 2, 128).transpose(2, 0, 1))
    sh["glub"] = f(g["ssm_glu_b"].reshape(L, 2, 128).transpose(2, 0, 1))
    sh["pscale"] = f(g["pool_scale"].reshape(L, 2, 128).transpose(2, 0, 1))
    sh["gluw"] = f(g["ssm_glu_w"].reshape(L, 2, 128, 256).transpose(0, 2, 1, 3))
    pw = np.zeros((L, 128, 2, 128), np.float32)
    for k in range(2):
        for g2 in range(2):
            pw[:, g2 * 64:(g2 + 1) * 64, k, g2 * 64:(g2 + 1) * 64] = g["pool_w"][:, 2 * k + g2]
    sh["pw"] = pw
    sh["cst"] = _consts()
    maps = []
    for c in range(8):
        b0, b1 = 16 * c, 16 * c + 16
        m = dict(sh)
        xt = np.concatenate([g["x_prompt"][c], g["x_sample"][b0:b1].reshape(NS, 1024)], 0)
        m["xT"] = f(xt.T.reshape(8, 128, NT).transpose(1, 0, 2))
        m["sdel"] = f(g["state_delta"][:, b0:b1].transpose(0, 1, 3, 2, 4))
        m["sconv"] = f(g["state_conv"][:, b0:b1].reshape(L, 16, 3, 12, 128).transpose(4, 0, 3, 1, 2))
        m["spool"] = f(g["state_pool"][:, b0:b1].reshape(L, 16, 15, 2, 128).transpose(4, 0, 3, 1, 2))
        m["spool_nat"] = f(g["state_pool"][:, b0:b1])
        sre = g["state_ssm_re"][:, b0:b1].transpose(3, 0, 2, 1)
        sim = g["state_ssm_im"][:, b0:b1].transpose(3, 0, 2, 1)
        m["sssm"] = f(np.concatenate([sre, sim], 0))
        maps.append(m)
    return maps


def _assemble(res):
    yp = np.zeros((8, NP_, 1024), np.float32); ys = np.zeros((128, 4, 1024), np.float32)
    dp = np.zeros((L, 8, 4, 128, 128), np.float32); ds = np.zeros((L, 128, 4, 128, 128), np.float32)
    cvp = np.zeros((L, 8, 3, 1536), np.float32); cvs = np.zeros((L, 128, 3, 1536), np.float32)
    rp = np.zeros((L, 8, 16, 64), np.float32); ip = np.zeros((L, 8, 16, 64), np.float32)
    rs = np.zeros((L, 128, 16, 64), np.float32); is_ = np.zeros((L, 128, 16, 64), np.float32)
    pp = np.zeros((L, 8, 15, 256), np.float32); ps = np.zeros((L, 128, 15, 256), np.float32)
    for c in range(8):
        r = {k: np.asarray(v) for k, v in res[c].items()}
        b0, b1 = 16 * c, 16 * c + 16
        y = r["yT"].transpose(1, 0, 2).reshape(1024, NT).T
        yp[c] = y[:NP_]
        ys[b0:b1] = y[NP_:].reshape(16, 4, 1024)
        dp[:, c] = r["o_delta_p"].transpose(0, 2, 1, 3)
        ds[:, b0:b1] = r["o_delta_s"].transpose(0, 1, 3, 2, 4)
        cvp[:, c] = r["o_conv_p"].transpose(1, 3, 2, 0).reshape(L, 3, 1536)
        cvs[:, b0:b1] = r["o_conv_s"].transpose(1, 3, 4, 2, 0).reshape(L, 16, 3, 1536)
        sp_ = r["o_ssm_p"]
        rp[:, c] = sp_[:64].transpose(1, 2, 0)
        ip[:, c] = sp_[64:].transpose(1, 2, 0)
        ss = r["o_ssm_s"]
        rs[:, b0:b1] = ss[:64].transpose(1, 2, 3, 0)
        is_[:, b0:b1] = ss[64:].transpose(1, 2, 3, 0)
        pp[:, c] = r["o_pool_p"].transpose(1, 3, 2, 0).reshape(L, 15, 256)
        ps[:, b0:b1, 0:11] = r["o_pool_s_old"]
        ps[:, b0:b1, 11:15] = r["o_pool_s_new"].transpose(1, 3, 4, 2, 0).reshape(L, 16, 4, 256)
    return (yp, ys, dp, cvp, rp, ip, pp, ds, cvs, rs, is_, ps)


_NC_CACHE = {}


def kernel(**inputs):
    if "nc" not in _NC_CACHE:
        _NC_CACHE["nc"] = build_program()
    maps = _prep_inputs(inputs)
    res = run_bass_kernel_spmd(_NC_CACHE["nc"], maps, core_ids=list(range(8)))
    return _assemble(res.results)
```

```python
import contextlib
import math
import numpy as np
import concourse.bass as bass
import concourse.mybir as mybir
from concourse.bass_utils import run_bass_kernel_spmd

F32 = mybir.dt.float32
BF16 = mybir.dt.bfloat16
I32 = mybir.dt.int32
ALU = mybir.AluOpType
AF = mybir.ActivationFunctionType

L = 4
NP_, NS, NT = 2048, 64, 2112
TW = 256
EPS = 1e-6
INW = 2568
OFF_A, OFF_G, OFF_SSM, OFF_POOL = 1536, 1544, 2056, 2312
DFF = 2816
NKS = 11


class Buf:
    __slots__ = ("w", "r")

    def __init__(self):
        self.w = None
        self.r = []


class Prog:
    ENGS = ("pe", "act", "dve", "pool", "sp")

    def __init__(self, nc, stack, n_dma_sems=40):
        self.nc = nc
        self.sem = {e: stack.enter_context(nc.semaphore("s_" + e)) for e in self.ENGS}
        self.cnt = {e: 0 for e in self.ENGS}
        self.seen = {e: {} for e in self.ENGS}
        self.stream = {e: [] for e in self.ENGS}
        self.dsem = [stack.enter_context(nc.semaphore("d%d" % i)) for i in range(n_dma_sems)]
        self.dcnt = [0] * n_dma_sems

    def _need(self, eng, dep):
        if dep is None:
            return
        if dep[0] == "dma":
            key, val, sem = ("d", dep[1]), dep[2], self.dsem[dep[1]]
        else:
            e2, idx = dep
            if e2 == eng and (eng == "pe" or idx <= self.cnt[eng] - 2):
                return
            key, val, sem = ("e", e2), idx, self.sem[e2]
        if self.seen[eng].get(key, 0) >= val:
            return
        self.seen[eng][key] = val
        self.stream[eng].append(("wait", sem, val))

    def _deps(self, eng, reads, writes):
        deps = []
        for b in reads:
            deps.append(b.w)
        for b in writes:
            deps.append(b.w)
            deps.extend(b.r)
        best = {}
        for d in deps:
            if d is None:
                continue
            key = (d[0], d[1]) if d[0] == "dma" else ("e", d[0])
            val = d[2] if d[0] == "dma" else d[1]
            if key not in best or val > best[key][0]:
                best[key] = (val, d)
        for key in best:
            self._need(eng, best[key][1])

    def _mark(self, me, reads, writes):
        for b in reads:
            b.r.append(me)
            if len(b.r) > 64:
                b.r = b.r[-48:]
        for b in writes:
            b.w = me
            b.r = []

    def op(self, eng, fn, reads=(), writes=()):
        self._deps(eng, reads, writes)
        self.cnt[eng] += 1
        self.stream[eng].append(("inst", fn, self.sem[eng], 1))
        self._mark((eng, self.cnt[eng]), reads, writes)

    def dma(self, q, out, in_, semi, reads=(), writes=(), **kw):
        self._deps(q, reads, writes)
        self.dcnt[semi] += 16
        self.stream[q].append(("inst", lambda h: h.dma_start(out=out, in_=in_, **kw), self.dsem[semi], 16))
        self._mark(("dma", semi, self.dcnt[semi]), reads, writes)

    def barrier(self):
        for e in self.ENGS:
            for e2 in self.ENGS:
                if e2 != e and self.cnt[e2]:
                    self._need(e, (e2, self.cnt[e2]))
            for i, c in enumerate(self.dcnt):
                if c:
                    self._need(e, ("dma", i, c))

    def emit(self):
        nc = self.nc
        with nc.Block() as block:
            def run(e):
                def body(h):
                    for it in self.stream[e]:
                        if it[0] == "wait":
                            h.wait_ge(it[1], it[2])
                        else:
                            it[1](h).then_inc(it[2], it[3])
                return body
            block.tensor(run("pe"))
            block.scalar(run("act"))
            block.vector(run("dve"))
            block.gpsimd(run("pool"))
            block.sync(run("sp"))


MARKS = []


def build_program(NL=L, dbg=None, skip=()):
    nc = bass.Bass("TRN2", target_bir_lowering=False)
    din = lambda n, s, d=F32: nc.dram_tensor(n, list(s), d, kind="ExternalInput").ap()
    dout = lambda n, s, d=F32: nc.dram_tensor(n, list(s), d, kind="ExternalOutput").ap()
    I = {}
    I["xT"] = din("xT", [128, 8, NT])
    I["w_in"] = din("w_in", [L, 128, 8, INW])
    I["w_out"] = din("w_out", [L, 128, 8, 1024])
    I["w_gu"] = din("w_gu", [L, 22, 128, 2, 8, 128])
    I["w_down"] = din("w_down", [L, 128, 22, 1024])
    I["nw"] = din("nw", [128, L, 4, 8])
    I["cw"] = din("cw", [128, L, 12, 4])
    I["dnp"] = din("dnp", [64, L, 2, 4])
    I["onw"] = din("onw", [128, L])
    I["sdel"] = din("sdel", [L, 16, 128, 4, 128])
    I["sconv"] = din("sconv", [128, L, 12, 16, 3])
    I["spool"] = din("spool", [128, L, 2, 16, 15])
    I["spool_nat"] = din("spool_nat", [L, 16, 15, 256])
    I["lamp"] = din("lamp", [128, L, 16, 3])
    I["bnat"] = din("bnat", [128, L, 16, 2, 16])
    I["cnat"] = din("cnat", [128, L, 16, 16])
    I["sssm"] = din("sssm", [128, L, 16, 16])
    I["ssmd"] = din("ssmd", [128, L, 2])
    I["glub"] = din("glub", [128, L, 2])
    I["gluw"] = din("gluw", [L, 128, 2, 256])
    I["pw"] = din("pw", [L, 128, 2, 128])
    I["pscale"] = din("pscale", [128, L, 2])
    I["cst"] = din("cst", [128, 8, 128])
    O = {}
    O["yT"] = dout("yT", [128, 8, NT])
    O["o_delta_p"] = dout("o_delta_p", [L, 128, 4, 128])
    O["o_delta_s"] = dout("o_delta_s", [L, 16, 128, 4, 128])
    O["o_conv_p"] = dout("o_conv_p", [128, L, 12, 3])
    O["o_conv_s"] = dout("o_conv_s", [128, L, 12, 16, 3])
    O["o_ssm_p"] = dout("o_ssm_p", [128, L, 16])
    O["o_ssm_s"] = dout("o_ssm_s", [128, L, 16, 16])
    O["o_pool_p"] = dout("o_pool_p", [128, L, 2, 15])
    O["o_pool_s_old"] = dout("o_pool_s_old", [L, 16, 11, 256])
    O["o_pool_s_new"] = dout("o_pool_s_new", [128, L, 2, 16, 4])
    if dbg:
        O["dbgmix"] = dout("dbgmix", [128, 8, NT])
        O["dbgoo"] = dout("dbgoo", [128, 4, NT])

    with contextlib.ExitStack() as st:
        P = Prog(nc, st)
        sb = lambda n, s, d=F32: st.enter_context(nc.sbuf_tensor(n, list(s), d))
        PS = [st.enter_context(nc.psum_tensor("ps%d" % i, [128, 512], F32)) for i in range(7)]
        PSB = [Buf() for _ in range(7)]
        PT = st.enter_context(nc.psum_tensor("pt", [128, 1024], BF16))
        PTB = Buf()
        X = sb("X", [128, 8, NT]); XB = [Buf() for _ in range(9)]
        CST = sb("CST", [128, 8, 128]); CSTB = Buf()
        CB16 = sb("CB16", [128, 2, 128], BF16)
        NW = sb("NW", [128, L, 4, 8]); CW = sb("CW", [128, L, 12, 4]); DNP = sb("DNP", [64, L, 2, 4])
        ONW = sb("ONW", [128, L]); PSC = sb("PSC", [128, L, 2]); SSMD = sb("SSMD", [128, L, 2]); GLUB = sb("GLUB", [128, L, 2])
        PARB = Buf()
        EPSC = sb("EPSC", [128, 3])

        dmai = [0]

        def nsem(lo=22, hi=40):
            dmai[0] = (dmai[0] + 1) % (hi - lo)
            return lo + dmai[0]

        def mm(out, lhsT, rhs, rd, wr, start=True, stop=True):
            P.op("pe", lambda h: h.matmul(out, lhsT=lhsT, rhs=rhs, start=start, stop=stop), reads=rd, writes=wr)

        def tr(out, in_, ident, rd, wr):
            P.op("pe", lambda h: h.transpose(out, in_, ident), reads=rd, writes=wr)

        def act(out, in_, func, rd, wr, bias=0.0, scale=1.0, eng="act"):
            P.op("act", lambda h: h.activation(out, in_, func, bias=bias, scale=scale), reads=rd, writes=wr)

        def tt(out, a, b, op, rd, wr, eng="dve"):
            P.op(eng, lambda h: h.tensor_tensor(out, a, b, op), reads=rd, writes=wr)

        def ts(out, a, s1, s2, op0, op1, rd, wr, eng="dve"):
            if op1 is None:
                P.op(eng, lambda h: h.tensor_scalar(out, a, s1, None, op0), reads=rd, writes=wr)
            else:
                P.op(eng, lambda h: h.tensor_scalar(out, a, s1, s2, op0, op1), reads=rd, writes=wr)

        def stt(out, a, s, b, op0, op1, rd, wr, eng="dve"):
            P.op(eng, lambda h: h.scalar_tensor_tensor(out, a, s, b, op0, op1), reads=rd, writes=wr)

        def cp(out, in_, rd, wr, eng="dve"):
            if eng == "act":
                P.op("act", lambda h: h.copy(out, in_), reads=rd, writes=wr)
            else:
                P.op(eng, lambda h: h.tensor_copy(out, in_), reads=rd, writes=wr)

        def rcp(out, in_, rd, wr):
            P.op("dve", lambda h: h.reciprocal(out, in_), reads=rd, writes=wr)

        def mset(ap, v, wr, eng="pool"):
            P.op(eng, lambda h: h.memset(ap, v), writes=wr)

        P.dma("sp", CST[:], I["cst"][:], 20, writes=[CSTB])
        for t_, k_ in ((NW, "nw"), (CW, "cw"), (DNP, "dnp"), (ONW, "onw"), (PSC, "pscale"), (SSMD, "ssmd"), (GLUB, "glub")):
            P.dma("sp", t_[:], I[k_][:], 0, writes=[PARB])
        IDF = CST[:, 0, :]
        ONEF = CST[:, 1, :]
        LTRI = {64: CST[:, 2, :], 4: CST[:, 2, :]}
        MLT = {64: CST[:, 3, :], 4: CST[:, 3, :]}
        MSTR = {64: CST[:, 3, :], 4: CST[:, 3, :]}
        MINT = {64: CST[:, 2, :], 4: CST[:, 2, :]}
        SWP = CST[:, 4, :]
        SGN = CST[:, 5, 0:1]
        IDB = CB16[:, 0, :]
        ONEB = CB16[:, 1, :]
        CB16B = Buf()
        cp(CB16[:, 0, :], CST[:, 0, :], [CSTB], [CB16B])
        cp(CB16[:, 1, :], CST[:, 1, :], [CSTB], [CB16B])
        mset(EPSC[:, 0:1], EPS, [PARB], eng="dve")
        mset(EPSC[:, 1:2], 1.0, [PARB], eng="dve")
        mset(EPSC[:, 2:3], math.log(128.0 ** -0.5), [PARB], eng="dve")
        EPS_AP = EPSC[:, 0:1]
        LNQ_AP = EPSC[:, 2:3]

        TILES = [(i * TW, TW, i) for i in range(8)] + [(NP_, NS, 8)]
        for (c0, n, ti) in TILES:
            P.dma("sp", X[:, :, c0:c0 + n], I["xT"][:, :, c0:c0 + n], 1 + ti, writes=[XB[ti]])

        def rms_rstd(src_sq_fn, n, nk, rs_out, rsB, rd, inv_d, sq, sqB, bank, ring=None):
            if ring:
                for kc in range(nk):
                    act(sq[:, kc % ring, :n], src_sq_fn(kc), AF.Square, rd, [sqB[kc % ring]])
                    mm(PS[bank][:, :n], ONEB, sq[:, kc % ring, :n], [sqB[kc % ring], CB16B], [PSB[bank]], start=(kc == 0), stop=(kc == nk - 1))
            else:
                for kc in range(nk):
                    act(sq[:, kc, :n], src_sq_fn(kc), AF.Square, rd, [sqB])
                for kc in range(nk):
                    mm(PS[bank][:, :n], ONEB, sq[:, kc, :n], [sqB, CB16B], [PSB[bank]], start=(kc == 0), stop=(kc == nk - 1))
            act(rs_out[:, :n], PS[bank][:, :n], AF.Ln, [PSB[bank], PARB], [rsB], bias=EPS_AP, scale=inv_d)
            act(rs_out[:, :n], rs_out[:, :n], AF.Exp, [rsB], [rsB], scale=-0.5)

        for l in range(NL):
            last = (l == NL - 1)
            MARKS.append(("L%d start" % l, P.cnt["pe"]))
            with contextlib.ExitStack() as sa:
              if ("A", l) not in skip:
                  sba = lambda n, s, d=F32: sa.enter_context(nc.sbuf_tensor("%s_%d" % (n, l), list(s), d))
                  WINB = Buf(); WOUTB = Buf()
                  GLUW = sba("GLUW", [128, 2, 256], BF16); PWT = sba("PWT", [128, 2, 128], BF16); SMB = Buf()
                  USSM = sba("USSM", [128, 2, NT], BF16); USB = Buf()
                  H = sba("H", [128, 8, TW], BF16); HB = Buf()
                  SQ = sba("SQ", [128, 8, TW], BF16); SQB = Buf()
                  RS = sba("RS", [128, TW]); RSB = Buf()
                  P.dma("pool", GLUW[:], I["gluw"][l], 10, writes=[SMB])
                  P.dma("pool", PWT[:], I["pw"][l], 10, writes=[SMB])

                  def norm_tile(c0, n, ti, which, out, outB):
                      rms_rstd(lambda kc: X[:, kc, c0:c0 + n], n, 8, RS, RSB, [XB[ti]], 1.0 / 1024, SQ, SQB, 0)
                      for kc in range(8):
                          stt(out[:, kc, :n], X[:, kc, c0:c0 + n], NW[:, l, which, kc:kc + 1], RS[:, :n], ALU.mult, ALU.mult,
                              [XB[ti], RSB, PARB], [outB])

                  s1 = contextlib.ExitStack()
                  WSSM = s1.enter_context(nc.sbuf_tensor("WSSM_%d" % l, [128, 8, 256], BF16)); WSB = Buf()
                  P.dma("pool", WSSM[:], I["w_in"][l, :, :, OFF_SSM:OFF_POOL], 31, writes=[WSB])
                  for (c0, n, ti) in TILES:
                      norm_tile(c0, n, ti, 0, H, HB)
                      for oc in range(2):
                          for kc in range(8):
                              mm(PS[1 + oc][:, :n], WSSM[:, kc, oc * 128:(oc + 1) * 128], H[:, kc, :n],
                                 [WSB, HB], [PSB[1 + oc]], start=(kc == 0), stop=(kc == 7))
                          cp(USSM[:, oc, c0:c0 + n], PS[1 + oc][:, :n], [PSB[1 + oc]], [USB], eng="act")
                  P.barrier()
                  s1.close()

                  MARKS.append(("L%d A2 s5" % l, P.cnt["pe"]))
                  with contextlib.ExitStack() as s5:
                      sb5 = lambda n, s, d=F32: s5.enter_context(nc.sbuf_tensor("%s_%d" % (n, l), list(s), d))
                      YS = sb5("YS", [128, 2, NT], BF16); YSB = Buf()
                      LAMP = sb5("LAMP", [128, 16, 3]); BNAT = sb5("BNAT", [128, 16, 2, 16]); CNAT = sb5("CNAT", [128, 16, 16])
                      S0 = sb5("S0", [128, 16, 16]); LB = Buf()
                      P.dma("sp", LAMP[:], I["lamp"][:, l], 12, writes=[LB])
                      P.dma("sp", BNAT[:], I["bnat"][:, l], 12, writes=[LB])
                      P.dma("sp", CNAT[:], I["cnat"][:, l], 12, writes=[LB])
                      P.dma("sp", S0[:], I["sssm"][:, l], 12, writes=[LB])
                      DT = sb5("DT", [128, 16]); ZR = sb5("ZR", [128, 16]); FR = sb5("FR", [128, 16]); FI = sb5("FI", [128, 16, 2], I32)
                      T1 = sb5("T1", [128, 16]); T2 = sb5("T2", [128, 16]); T3 = sb5("T3", [128, 16])
                      AK = sb5("AK", [128, 16, NKS]); BK = sb5("BK", [128, 16, NKS]); TB = Buf()
                      act(DT[:], LAMP[:, :, 2], AF.Exp, [LB], [TB])
                      tt(ZR[:], LAMP[:, :, 0], DT[:], ALU.mult, [LB, TB], [TB])
                      stt(FR[:], LAMP[:, :, 1], 1.0 / (2 * math.pi), DT[:], ALU.mult, ALU.mult, [LB, TB], [TB])

                      def reduce_turns(dst, src):
                          cp(FI[:, :, 0], src, [TB], [TB])
                          cp(T3[:], FI[:, :, 0], [TB], [TB])
                          tt(dst, src, T3[:], ALU.subtract, [TB], [TB])

                      reduce_turns(FR[:], FR[:])
                      for k in range(NKS):
                          act(T1[:], ZR[:], AF.Exp, [TB], [TB], scale=float(2 ** k))
                          act(T2[:], FR[:], AF.Sin, [TB], [TB], scale=2 * math.pi)
                          tt(BK[:, :, k], T1[:], T2[:], ALU.mult, [TB], [TB])
                          ts(T2[:], FR[:], 0.25, None, ALU.add, None, [TB], [TB])
                          reduce_turns(T2[:], T2[:])
                          act(T2[:], T2[:], AF.Sin, [TB], [TB], scale=2 * math.pi)
                          tt(AK[:, :, k], T1[:], T2[:], ALU.mult, [TB], [TB])
                          if k < NKS - 1:
                              ts(FR[:], FR[:], 2.0, None, ALU.mult, None, [TB], [TB])
                              reduce_turns(FR[:], FR[:])
                      FRE = sb5("FRE", [128, 16]); FIM = sb5("FIM", [128, 16]); DEN = sb5("DEN", [128, 16]); AM1 = sb5("AM1", [128, 16])
                      are, aim = LAMP[:, :, 0], LAMP[:, :, 1]
                      tt(DEN[:], are, are, ALU.mult, [LB], [TB]); tt(T1[:], aim, aim, ALU.mult, [LB], [TB])
                      tt(DEN[:], DEN[:], T1[:], ALU.add, [TB], [TB]); rcp(DEN[:], DEN[:], [TB], [TB])
                      ts(AM1[:], AK[:, :, 0], -1.0, None, ALU.add, None, [TB], [TB])
                      tt(T1[:], AM1[:], are, ALU.mult, [TB, LB], [TB]); tt(T2[:], BK[:, :, 0], aim, ALU.mult, [TB, LB], [TB])
                      tt(T1[:], T1[:], T2[:], ALU.add, [TB], [TB]); tt(FRE[:], T1[:], DEN[:], ALU.mult, [TB], [TB])
                      tt(T1[:], BK[:, :, 0], are, ALU.mult, [TB, LB], [TB]); tt(T2[:], AM1[:], aim, ALU.mult, [TB, LB], [TB])
                      tt(T1[:], T1[:], T2[:], ALU.subtract, [TB], [TB]); tt(FIM[:], T1[:], DEN[:], ALU.mult, [TB], [TB])
                      ts(FIM[:], FIM[:], SGN, None, ALU.mult, None, [TB, CSTB], [TB])
                      BB = sb5("BB", [128, 16, 16]); BB2 = sb5("BB2", [128, 16, 16])
                      tt(BB[:], BNAT[:, :, 0, :], FRE[:, :, None].broadcast_to([128, 16, 16]), ALU.mult, [LB, TB], [TB])
                      tt(BB2[:], BNAT[:, :, 1, :], FIM[:, :, None].broadcast_to([128, 16, 16]), ALU.mult, [LB, TB], [TB])
                      tt(BB[:], BB[:], BB2[:], ALU.add, [TB], [TB])
                      CT = sb5("CT", [128, 16, 16])
                      ts(CT[:], CNAT[:], SGN, -1.0, ALU.mult, ALU.mult, [LB, CSTB], [TB])
                      BPAD = sb5("BPAD", [128, 128]); BPB = Buf()
                      BT = sb5("BT", [128, 8, 128], BF16); BTB = [Buf() for _ in range(8)]
                      CPAD = sb5("CPAD", [128, 8, 128], BF16); CPB = Buf()
                      RT = sb5("RT", [128, 8, NKS, 1, 128], BF16); RTB = [Buf() for _ in range(8)]
                      RFA = [sb5("RFA%d" % i, [128, NKS, 128]) for i in range(2)]; RFBt = [sb5("RFBt%d" % i, [128, NKS, 128]) for i in range(2)]
                      RFB = [Buf(), Buf()]
                      XE = sb5("XE", [128, 8, 1 + NP_ + 80], BF16); XEB = [Buf() for _ in range(8)]
                      YF = sb5("YF", [128, TW]); YFB = Buf()
                      SF = sb5("SF", [128, 17, 16]); SFB = Buf()
                      BS = sb5("BS", [128, 16, NKS]);
                      ts(BS[:], BK[:], SGN, -1.0, ALU.mult, ALU.mult, [TB, CSTB], [TB])
                      mset(BPAD[:], 0.0, [BPB]); mset(CPAD[:], 0.0, [CPB])
                      NE = 1 + NP_
                      for oc in range(2):
                          xesv = [XE[:, gi, NE:NE + 80].rearrange("p (b t) -> p b t", t=5) for gi in range(8)]
                          for gi in range(8):
                              g = oc * 8 + gi
                              cp(BPAD[:, gi * 16:(gi + 1) * 16], BB[:, g, :], [TB], [BPB])
                              tr(PS[2][:, 0:128], BPAD[:], IDF, [BPB, CSTB], [PSB[2]])
                              cp(BT[:, gi, :], PS[2][:, 0:128], [PSB[2]], [BTB[gi]])
                              mset(BPAD[:, gi * 16:(gi + 1) * 16], 0.0, [BPB])
                              cp(CPAD[:, gi, gi * 16:(gi + 1) * 16], CT[:, g, :], [TB], [CPB])
                              e_ = "dve" if (gi % 2 == 0) else "pool"
                              r1, r2, rb = RFA[gi % 2], RFBt[gi % 2], RFB[gi % 2]
                              tt(r1[:], SWP[:, None, :].broadcast_to([128, NKS, 128]), BS[:, g, :, None].broadcast_to([128, NKS, 128]), ALU.mult,
                                 [CSTB, TB], [rb], eng=e_)
                              tt(r2[:], IDF[:, None, :].broadcast_to([128, NKS, 128]), AK[:, g, :, None].broadcast_to([128, NKS, 128]), ALU.mult,
                                 [CSTB, TB], [rb], eng=e_)
                              tt(RT[:, gi, :, 0, :], r1[:], r2[:], ALU.add, [rb], [RTB[gi]], eng=e_)
                              mset(XE[:, gi, 0:1], 0.0, [XEB[gi]])
                              cp(xesv[gi][:, :, 0], S0[:, g, :], [LB], [XEB[gi]])
                          bk = [0]

                          def nbank():
                              bk[0] = (bk[0] + 1) % 6
                              return 1 + bk[0]
                          for (c0, n, ti) in TILES:
                              for gi in range(8):
                                  bnk = nbank()
                                  mm(PS[bnk][:, :n], BT[:, gi, :], USSM[:, oc, c0:c0 + n], [BTB[gi], USB], [PSB[bnk]])
                                  if ti < 8:
                                      cp(XE[:, gi, 1 + c0:1 + c0 + n], PS[bnk][:, :n], [PSB[bnk]], [XEB[gi]], eng="act")
                                  else:
                                      cp(xesv[gi][:, :, 1:5], PS[bnk][:, :n].rearrange("p (b t) -> p b t", t=4), [PSB[bnk]], [XEB[gi]], eng="act")
                          for k in range(NKS):
                              s = 2 ** k
                              hi = NE
                              while hi > s:
                                  lo = max(s, hi - 512)
                                  w = hi - lo
                                  for gi in range(8):
                                      bnk = nbank()
                                      mm(PS[bnk][:, :w], RT[:, gi, k, 0, :], XE[:, gi, lo - s:hi - s], [RTB[gi], XEB[gi]], [PSB[bnk]], start=True, stop=False)
                                      mm(PS[bnk][:, :w], IDB, XE[:, gi, lo:hi], [CB16B, XEB[gi]], [PSB[bnk]], start=False, stop=True)
                                      cp(XE[:, gi, lo:hi], PS[bnk][:, :w], [PSB[bnk]], [XEB[gi]], eng=("act" if gi % 2 == 0 else "dve"))
                                  hi = lo
                              if s < 5:
                                  w = 5 - s
                                  for gi in range(8):
                                      bnk = nbank()
                                      pso = PS[bnk][:, :16 * w].rearrange("p (b t) -> p b t", t=w)
                                      mm(pso, RT[:, gi, k, 0, :], xesv[gi][:, :, 0:w], [RTB[gi], XEB[gi]], [PSB[bnk]])
                                      tt(xesv[gi][:, :, s:5], xesv[gi][:, :, s:5], pso, ALU.add, [XEB[gi], PSB[bnk]], [XEB[gi]])
                          for gi in range(8):
                              g = oc * 8 + gi
                              cp(SF[:, 0, g:g + 1], XE[:, gi, NE - 1:NE], [XEB[gi]], [SFB])
                              cp(SF[:, 1:17, g], xesv[gi][:, :, 4], [XEB[gi]], [SFB])
                          if l == 0: MARKS.append(("L0 s5 y oc%d" % oc, P.cnt["pe"]))
                          for (c0, n, ti) in TILES:
                              for gi in range(8):
                                  if ti < 8:
                                      rhs = XE[:, gi, 1 + c0:1 + c0 + n]
                                      mm(PS[3][:, :n], CPAD[:, gi, :], rhs, [CPB, XEB[gi]], [PSB[3]], start=(gi == 0), stop=(gi == 7))
                                  else:
                                      rhs = XE[:, gi, NE:NE + 80].rearrange("p (b t) -> p b t", t=5)[:, :, 1:5]
                                      mm(PS[3][:, :n].rearrange("p (b t) -> p b t", t=4), CPAD[:, gi, :], rhs, [CPB, XEB[gi]], [PSB[3]],
                                         start=(gi == 0), stop=(gi == 7))
                              stt(YF[:, :n], USSM[:, oc, c0:c0 + n], SSMD[:, l, oc:oc + 1], PS[3][:, :n], ALU.mult, ALU.add, [USB, PARB, PSB[3]], [YFB])
                              tt(RS[:, :n], YF[:, :n], YF[:, :n], ALU.mult, [YFB], [RSB])
                              ts(RS[:, :n], RS[:, :n], 0.044715, 1.0, ALU.mult, ALU.add, [RSB], [RSB])
                              tt(RS[:, :n], RS[:, :n], YF[:, :n], ALU.mult, [RSB, YFB], [RSB])
                              act(RS[:, :n], RS[:, :n], AF.Sigmoid, [RSB], [RSB], scale=2.0 * math.sqrt(2.0 / math.pi))
                              tt(YS[:, oc, c0:c0 + n], RS[:, :n], YF[:, :n], ALU.mult, [RSB, YFB], [YSB])
                      P.dma("sp", O["o_ssm_p"][:, l, :], SF[:, 0, :], 25, reads=[SFB])
                      P.dma("sp", O["o_ssm_s"][:, l, :, :], SF[:, 1:17, :], 25, reads=[SFB])
                      for (c0, n, ti) in TILES:
                          for oc in range(2):
                              for kc in range(2):
                                  mm(PS[3][:, :n], GLUW[:, kc, oc * 128:(oc + 1) * 128], YS[:, kc, c0:c0 + n], [SMB, YSB], [PSB[3]],
                                     start=(kc == 0), stop=(kc == 1))
                              act(RS[:, :n], PS[3][:, :n], AF.Sigmoid, [PSB[3], PARB], [RSB], bias=GLUB[:, l, oc:oc + 1])
                              tt(USSM[:, oc, c0:c0 + n], RS[:, :n], YS[:, oc, c0:c0 + n], ALU.mult, [RSB, YSB], [USB])
                      P.barrier()

                  MARKS.append(("L%d A3" % l, P.cnt["pe"]))
                  with contextlib.ExitStack() as s3:
                      sb3 = lambda n, s, d=F32: s3.enter_context(nc.sbuf_tensor("%s_%d" % (n, l), list(s), d))
                      WIN = sb3("WIN", [128, 8, INW], BF16)
                      WOUT = sb3("WOUT", [128, 8, 1024], BF16)
                      P.dma("pool", WIN[:, 0:4], I["w_in"][l, :, 0:4], 11, writes=[WINB])
                      P.dma("pool", WIN[:, 4:8], I["w_in"][l, :, 4:8], 11, writes=[WINB])
                      P.dma("pool", WOUT[:], I["w_out"][l], 21, writes=[WOUTB])
                      PRE = sb3("PRE", [128, 12, 3 + TW], BF16); PREB = Buf()
                      PRES = PRE
                      QKV = sb3("QKV", [128, 12, TW], BF16); QKVB = Buf()
                      CV = sb3("CV", [128, TW]); CVB = Buf()
                      GT = sb3("GT", [128, 4, TW], BF16); GTB = Buf()
                      OO = sb3("OO", [128, 4, TW], BF16); OOB = Buf()
                      MIX = sb3("MIX", [128, 8, TW], BF16); MIXB = Buf()
                      MO = sb3("MO", [128, 8, TW]); MOB = Buf()
                      PX = sb3("PX", [128, 2, 320]); PXB = Buf()
                      PA = sb3("PA", [128, 320]); PBt = sb3("PBt", [128, 320]); PAB = Buf()
                      CTAIL = sb3("CTAIL", [128, 12, 3]); CTB = Buf()
                      SCS = sb3("SCS", [128, 12, 16, 3]); SCSB = Buf()
                      SST = sb3("SST", [128, 4, 128]); SSB = Buf()
                      SBF = sb3("SBF", [128, 4, 128], BF16)
                      NEGA = sb3("NEGA", [64, 4]); NGB = Buf()
                      ZT = sb3("ZT", [64, 8]); LGt = sb3("LGt", [64, 4]); BET = sb3("BET", [64, 4]); EGt = sb3("EGt", [64, 4]); EKD = sb3("EKD", [64, 4])
                      EGL = [sb3("EGL%d" % i, [128, 4]) for i in range(2)]; EGLB = [Buf(), Buf()]; SCB = Buf()
                      LG = sb3("LG", [64, 4, 64]); LGB = Buf()
                      E2 = sb3("E2", [64, 2, 4, 64]); E2B = Buf()
                      EGB = sb3("EGB", [128, 4, 64]); EGBB = Buf()
                      QD = [sb3("QD%d" % i, [128, 4, 64], BF16) for i in range(2)]; QDB = [Buf(), Buf()]
                      KTK = sb3("KTK", [64, 4, 128], BF16); VTK = sb3("VTK", [64, 4, 128], BF16); TKB = Buf()
                      RHK = sb3("RHK", [64, 4, 128], BF16); VBt = sb3("VBt", [64, 4, 128], BF16); KDt = [sb3("KDt%d" % i, [64, 4, 128], BF16) for i in range(2)]; RKB = Buf(); KDB = [Buf(), Buf()]
                      AF_ = sb3("AF_", [64, 4, 64]); AFB = Buf()
                      PB_ = sb3("PB_", [64, 2, 4, 64], BF16); PBB = Buf()
                      QKT = [sb3("QKT%d" % i, [64, 4, 64], BF16) for i in range(2)]; QKB = [Buf(), Buf()]
                      TTb = sb3("TTb", [64, 4, 64], BF16); TTB = Buf()
                      KC = [sb3("KC%d" % i, [128, 4, 64], BF16) for i in range(2)]; KCB = [Buf(), Buf()]
                      WS = [sb3("WS%d" % i, [64, 4, 128]) for i in range(2)]; WSB = [Buf(), Buf()]
                      UU = sb3("UU", [64, 4, 128], BF16); UUB = Buf()
                      act(NEGA[:], DNP[:, l, 0, :], AF.Exp, [PARB], [NGB])
                      ts(NEGA[:], NEGA[:], -1.0, None, ALU.mult, None, [NGB], [NGB])
                      mset(CTAIL[:], 0.0, [CTB])
                      mset(PX[:, :, 0:15], 0.0, [PXB])
                      mset(SST[:], 0.0, [SSB]); mset(SBF[:], 0.0, [SSB])

                      for (c0, n, ti) in TILES:
                          samp = (ti == 8)
                          C = 4 if samp else 64
                          nch = n // C
                          hist = 3
                          norm_tile(c0, n, ti, 0, H, HB)
                          if not samp:
                              cp(PRE[:, :, 0:3], CTAIL[:], [CTB], [PREB])
                              pre_new = lambda blk: PRE[:, blk, 3:3 + n]
                          else:
                              prs = PRE[:, :, 0:112].rearrange("p k (b t) -> p k b t", t=7)
                              P.dma("sp", SCS[:], I["sconv"][:, l], 14, writes=[SCSB])
                              cp(prs[:, :, :, 0:3], SCS[:], [SCSB], [PREB])
                              pre_new = lambda blk: prs[:, blk, :, 3:7]
                          for blk in range(12):
                              col = (blk // 4) * 512 + (blk % 4) * 128
                              bnk = 1 + (blk % 3)
                              for kc in range(8):
                                  mm(PS[bnk][:, :n], WIN[:, kc, col:col + 128], H[:, kc, :n], [WINB, HB], [PSB[bnk]], start=(kc == 0), stop=(kc == 7))
                              if not samp:
                                  cp(pre_new(blk), PS[bnk][:, :n], [PSB[bnk]], [PREB], eng=("act" if blk % 2 == 0 else "dve"))
                              else:
                                  cp(pre_new(blk), PS[bnk][:, :n].rearrange("p (b t) -> p b t", t=4), [PSB[bnk]], [PREB], eng=("act" if blk % 2 == 0 else "dve"))
                          if not samp:
                              cp(CTAIL[:], PRE[:, :, n:n + 3], [PREB], [CTB])
                              if ti == 7:
                                  P.dma("sp", O["o_conv_p"][:, l], CTAIL[:], 23, reads=[CTB])
                          else:
                              cp(SCS[:], prs[:, :, :, 4:7], [PREB], [SCSB])
                              P.dma("sp", O["o_conv_s"][:, l], SCS[:], 24, reads=[SCSB])
                          for blk in range(12):
                              if not samp:
                                  tap = lambda j: PRE[:, blk, j:j + n]
                                  cvv = CV[:, :n]
                              else:
                                  tap = lambda j: prs[:, blk, :, j:j + 4]
                                  cvv = CV[:, :n].rearrange("p (b t) -> p b t", t=4)
                              ts(cvv, tap(0), CW[:, l, blk, 0:1], None, ALU.mult, None, [PREB, PARB], [CVB])
                              for j in range(1, 4):
                                  stt(cvv, tap(j), CW[:, l, blk, j:j + 1], cvv, ALU.mult, ALU.add, [PREB, PARB, CVB], [CVB])
                              act(QKV[:, blk, :n], CV[:, :n], AF.Silu, [CVB], [QKVB])
                          for hh in range(4):
                              col = OFF_G + hh * 128
                              bnk = 1 + (hh % 2)
                              for kc in range(8):
                                  mm(PS[bnk][:, :n], WIN[:, kc, col:col + 128], H[:, kc, :n], [WINB, HB], [PSB[bnk]], start=(kc == 0), stop=(kc == 7))
                              act(GT[:, hh, :n], PS[bnk][:, :n], AF.Silu, [PSB[bnk]], [GTB])
                          for blk in range(8):
                              bnk = 3 + (blk % 2)
                              act(SQ[:, blk, :n], QKV[:, blk, :n], AF.Square, [QKVB], [SQB])
                              mm(PS[bnk][:, :n], ONEB, SQ[:, blk, :n], [SQB, CB16B], [PSB[bnk]])
                              act(MO[:, blk, :n], PS[bnk][:, :n], AF.Ln, [PSB[bnk], PARB], [MOB], bias=EPS_AP, scale=1.0)
                          for blk in range(8):
                              act(MO[:, blk, :n], MO[:, blk, :n], AF.Exp, [MOB], [MOB], scale=-0.5, bias=(LNQ_AP if blk < 4 else 0.0))
                          for blk in range(8):
                              tt(QKV[:, blk, :n], QKV[:, blk, :n], MO[:, blk, :n], ALU.mult, [QKVB, MOB], [QKVB])
                          if l == 0: MARKS.append(("L0 t%d delta" % ti, P.cnt["pe"]))
                          def prep(ci, sl):
                              a0 = ci * C
                              cols = slice(a0, a0 + C)
                              QDs, QKTs, KDs, KCs, WSs, EGLs = QD[sl], QKT[sl], KDt[sl], KC[sl], WS[sl], EGL[sl]
                              for kc in range(8):
                                  mm(PS[2][:C, 0:8], H[:, kc, cols], WIN[:, kc, OFF_A:OFF_A + 8], [HB, WINB], [PSB[2]], start=(kc == 0), stop=(kc == 7))
                              kt_ps = PT[:C, 0:512].rearrange("p (h d) -> p h d", h=4)
                              vt_ps = PT[:C, 512:1024].rearrange("p (h d) -> p h d", h=4)
                              for hh in range(4):
                                  tr(kt_ps[:, hh, :], QKV[:, 4 + hh, cols], IDB, [QKVB, CB16B], [PTB])
                                  tr(vt_ps[:, hh, :], QKV[:, 8 + hh, cols], IDB, [QKVB, CB16B], [PTB])
                              kq = PS[5][:C, :].rearrange("p (a h c) -> p a h c", a=2, h=4)
                              for hh in range(4):
                                  mm(kq[:, 0, hh, :C], QKV[:, 4 + hh, cols], QKV[:, 4 + hh, cols], [QKVB], [PSB[5]])
                                  mm(kq[:, 1, hh, :C], QKV[:, 4 + hh, cols], QKV[:, hh, cols], [QKVB], [PSB[5]])
                              cp(KTK[:C], kt_ps, [PTB], [TKB])
                              cp(VTK[:C], vt_ps, [PTB], [TKB])
                              yield
                              tt(ZT[:C, 0:4], PS[2][:C, 0:4], DNP[:C, l, 1, :], ALU.add, [PSB[2], PARB], [SCB])
                              act(BET[:C], PS[2][:C, 4:8], AF.Exp, [PSB[2]], [SCB], scale=-1.0)
                              act(BET[:C], BET[:C], AF.Ln, [SCB, PARB], [SCB], bias=EPSC[:C, 1:2])
                              act(BET[:C], BET[:C], AF.Exp, [SCB], [SCB], scale=-1.0)
                              act(ZT[:C, 0:4], ZT[:C, 0:4], AF.Exp, [SCB], [SCB])
                              act(ZT[:C, 0:4], ZT[:C, 0:4], AF.Ln, [SCB, PARB], [SCB], bias=EPSC[:C, 1:2])
                              tt(LGt[:C], ZT[:C, 0:4], NEGA[:C], ALU.mult, [SCB, NGB], [SCB])
                              tt(LG[:C, :, :C], LTRI[C][:C, None, :C].broadcast_to([C, 4, C]), LGt[:C, :, None].broadcast_to([C, 4, C]), ALU.mult,
                                 [CSTB, SCB], [LGB])
                              tt(VBt[:C], VTK[:C], BET[:C, :, None].broadcast_to([C, 4, 128]), ALU.mult, [TKB, SCB], [RKB])
                              tt(RHK[:C], KTK[:C], BET[:C, :, None].broadcast_to([C, 4, 128]), ALU.mult, [TKB, SCB], [RKB])
                              yield
                              d2 = PS[3][:C, :].rearrange("p (a h c) -> p a h c", a=2, h=4)
                              for hh in range(4):
                                  mm(d2[:, 0, hh, :C], LG[:C, hh, :C], MLT[C][:C, :C], [LGB, CSTB], [PSB[3]])
                                  mm(d2[:, 1, hh, :C], MLT[C][:C, :C], LG[:C, hh, :C], [LGB, CSTB], [PSB[3]])
                              mm(PS[2][:C, 8:12], LTRI[C][:C, :C], LGt[:C, :], [CSTB, SCB], [PSB[2]])
                              mm(PS[2][:, 16:20], ONEF[:C, :], LGt[:C, :], [CSTB, SCB], [PSB[2]])
                              eg_ps = PS[4][:, 0:4 * C].rearrange("p (h c) -> p h c", h=4)
                              for hh in range(4):
                                  mm(eg_ps[:, hh, :], ONEF[:C, :], LG[:C, hh, :C], [CSTB, LGB], [PSB[4]])
                              act(E2[:C, :, :, :C], d2[:, :, :, :C], AF.Exp, [PSB[3]], [E2B])
                              act(EGt[:C], PS[2][:C, 8:12], AF.Exp, [PSB[2]], [SCB])
                              act(EGLs[:], PS[2][:, 16:20], AF.Exp, [PSB[2]], [EGLB[sl]])
                              cp(ZT[:C, 4:8], PS[2][:C, 8:12], [PSB[2]], [SCB])
                              tt(EKD[:C], PS[2][:C, 16:20], ZT[:C, 4:8], ALU.subtract, [PSB[2], SCB], [SCB])
                              act(EKD[:C], EKD[:C], AF.Exp, [SCB], [SCB])
                              act(EGB[:, :, :C], eg_ps, AF.Exp, [PSB[4]], [EGBB])
                              yield
                              tt(E2[:C, 0, :, :C], E2[:C, 0, :, :C], MSTR[C][:C, None, :C].broadcast_to([C, 4, C]), ALU.mult, [E2B, CSTB], [E2B])
                              tt(E2[:C, 1, :, :C], E2[:C, 1, :, :C], MINT[C][:C, None, :C].broadcast_to([C, 4, C]), ALU.mult, [E2B, CSTB], [E2B])
                              tt(AF_[:C, :, :C], kq[:, 0, :, :C], E2[:C, 0, :, :C], ALU.mult, [PSB[5], E2B], [AFB])
                              tt(AF_[:C, :, :C], AF_[:C, :, :C], BET[:C, :, None].broadcast_to([C, 4, C]), ALU.mult, [AFB, SCB], [AFB])
                              at_ps = PS[6][:C, 0:4 * C].rearrange("p (h c) -> p h c", h=4)
                              for hh in range(4):
                                  tr(at_ps[:, hh, :], AF_[:C, hh, :C], IDF[:C, :C], [AFB, CSTB], [PSB[6]])
                              tt(QKTs[:C, :, :C], kq[:, 1, :, :C], E2[:C, 1, :, :C], ALU.mult, [PSB[5], E2B], [QKB[sl]])
                              tt(QDs[:, :, :C], QKV[:, 0:4, cols], EGB[:, :, :C], ALU.mult, [QKVB, EGBB], [QDB[sl]])
                              tt(RHK[:C], RHK[:C], EGt[:C, :, None].broadcast_to([C, 4, 128]), ALU.mult, [RKB, SCB], [RKB])
                              tt(KDs[:C], KTK[:C], EKD[:C, :, None].broadcast_to([C, 4, 128]), ALU.mult, [TKB, SCB], [KDB[sl]])
                              cp(PB_[:C, 0, :, :C], AF_[:C, :, :C], [AFB], [PBB], eng="act")
                              cp(PB_[:C, 1, :, :C], at_ps, [PSB[6]], [PBB], eng="act")
                              tt(TTb[:C, :, :C], IDF[:C, None, :C].broadcast_to([C, 4, C]), at_ps, ALU.subtract, [CSTB, PSB[6]], [TTB])
                              yield
                              nlev = 5 if C == 64 else 1
                              for lev in range(nlev):
                                  p2 = PS[3][:C, :].rearrange("p (a h c) -> p a h c", a=2, h=4)
                                  for hh in range(4):
                                      mm(p2[:, 0, hh, :C], PB_[:C, 1, hh, :C], PB_[:C, 0, hh, :C], [PBB], [PSB[3]])
                                      mm(p2[:, 1, hh, :C], PB_[:C, 0, hh, :C], PB_[:C, 1, hh, :C], [PBB], [PSB[3]])
                                  cp(PB_[:C, :, :, :C], p2[:, :, :, :C], [PSB[3]], [PBB], eng="act")
                                  tu = PS[6][:C, 0:4 * C].rearrange("p (h c) -> p h c", h=4)
                                  for hh in range(4):
                                      mm(tu[:, hh, :], PB_[:C, 0, hh, :C], TTb[:C, hh, :C], [PBB, TTB], [PSB[6]])
                                  tt(TTb[:C, :, :C], TTb[:C, :, :C], tu, ALU.add, [TTB, PSB[6]], [TTB])
                                  yield
                              w_ps = PS[3][:C, :].rearrange("p (h d) -> p h d", h=4)
                              kc_ps = PS[4][:, 0:4 * C].rearrange("p (h c) -> p h c", h=4)
                              for hh in range(4):
                                  mm(w_ps[:, hh, :], TTb[:C, hh, :C], VBt[:C, hh, :], [TTB, RKB], [PSB[3]])
                                  mm(kc_ps[:, hh, :], RHK[:C, hh, :], TTb[:C, hh, :C], [TTB, RKB], [PSB[4]])
                              cp(WSs[:C], w_ps, [PSB[3]], [WSB[sl]], eng="act")
                              cp(KCs[:, :, :C], kc_ps, [PSB[4]], [KCB[sl]])

                          def step(ci, sl):
                              a0 = ci * C
                              cols = slice(a0, a0 + C)
                              QDs, QKTs, KDs, KCs, WSs, EGLs = QD[sl], QKT[sl], KDt[sl], KC[sl], WS[sl], EGL[sl]
                              if samp:
                                  P.dma("sp", SST[:], I["sdel"][l, ci], 13, writes=[SSB])
                                  cp(SBF[:], SST[:], [SSB], [SSB])
                              p1 = PS[0][:C, :].rearrange("p (h d) -> p h d", h=4)
                              for hh in range(4):
                                  mm(p1[:, hh, :], KCs[:, hh, :C], SBF[:, hh, :], [KCB[sl], SSB], [PSB[0]])
                              tt(UU[:C], WSs[:C], p1, ALU.subtract, [WSB[sl], PSB[0]], [UUB])
                              yield
                              o_ps = PS[1][:, 0:4 * C].rearrange("p (h c) -> p h c", h=4)
                              for hh in range(4):
                                  mm(o_ps[:, hh, :], SBF[:, hh, :], QDs[:, hh, :C], [SSB, QDB[sl]], [PSB[1]], start=True, stop=False)
                                  mm(o_ps[:, hh, :], UU[:C, hh, :], QKTs[:C, hh, :C], [UUB, QKB[sl]], [PSB[1]], start=False, stop=True)
                              ds = PS[0][:, :].rearrange("p (h d) -> p h d", h=4)
                              for hh in range(4):
                                  mm(ds[:, hh, :], KDs[:C, hh, :], UU[:C, hh, :], [KDB[sl], UUB], [PSB[0]])
                              tt(SST[:], SST[:], EGLs[:, :, None].broadcast_to([128, 4, 128]), ALU.mult, [SSB, EGLB[sl]], [SSB])
                              tt(SST[:], SST[:], ds, ALU.add, [SSB, PSB[0]], [SSB])
                              cp(SBF[:], SST[:], [SSB], [SSB])
                              cp(OO[:, :, cols], o_ps, [PSB[1]], [OOB], eng="act")
                              yield
                              if samp:
                                  P.dma("sp", O["o_delta_s"][l, ci], SST[:], 22, reads=[SSB])
                              elif ti == 7 and ci == nch - 1:
                                  P.dma("sp", O["o_delta_p"][l], SST[:], 22, reads=[SSB])

                          def drive(gens):
                              gens = [g_ for g_ in gens if g_ is not None]
                              while gens:
                                  for g_ in list(gens):
                                      try:
                                          next(g_)
                                      except StopIteration:
                                          gens.remove(g_)
                          drive([prep(0, 0)])
                          for ci in range(nch):
                              nxt = prep(ci + 1, (ci + 1) % 2) if ci + 1 < nch else None
                              drive([nxt, step(ci, ci % 2)])
                          if l == 0: MARKS.append(("L0 t%d postdelta" % ti, P.cnt["pe"]))
                          for hh in range(4):
                              act(SQ[:, hh, :n], OO[:, hh, :n], AF.Square, [OOB], [SQB])
                              mm(PS[2][:, :n], ONEB, SQ[:, hh, :n], [SQB, CB16B], [PSB[2]])
                              act(RS[:, :n], PS[2][:, :n], AF.Ln, [PSB[2], PARB], [RSB], bias=EPS_AP, scale=1.0 / 128)
                              act(RS[:, :n], RS[:, :n], AF.Exp, [RSB], [RSB], scale=-0.5)
                              stt(CV[:, :n], OO[:, hh, :n], ONW[:, l:l + 1], RS[:, :n], ALU.mult, ALU.mult, [OOB, PARB, RSB], [CVB])
                              tt(MIX[:, hh, :n], CV[:, :n], GT[:, hh, :n], ALU.mult, [CVB, GTB], [MIXB])
                          for oc in range(2):
                              cp(MIX[:, 4 + oc, :n], USSM[:, oc, c0:c0 + n], [USB], [MIXB], eng="pool")
                          if samp:
                              pxs = PX[:, :, 0:16 * 19].rearrange("p k (b t) -> p k b t", t=19)
                              for k_ in range(2):
                                  P.dma("sp", pxs[:, k_, :, 0:15], I["spool"][:, l, k_], 15, writes=[PXB])
                          for k2 in range(2):
                              col = OFF_POOL + k2 * 128
                              for kc in range(8):
                                  mm(PS[1][:, :n], WIN[:, kc, col:col + 128], H[:, kc, :n], [WINB, HB], [PSB[1]], start=(kc == 0), stop=(kc == 7))
                              if not samp:
                                  cp(PX[:, k2, 15:15 + n], PS[1][:, :n], [PSB[1]], [PXB], eng="act")
                              else:
                                  cp(pxs[:, k2, :, 15:19], PS[1][:, :n].rearrange("p (b t) -> p b t", t=4), [PSB[1]], [PXB], eng="act")
                          if samp:
                              for k_ in range(2):
                                  P.dma("sp", O["o_pool_s_new"][:, l, k_], pxs[:, k_, :, 15:19], 27, reads=[PXB])
                              P.dma("sp", O["o_pool_s_old"][l], I["spool_nat"][l, :, 4:15, :], 28)
                          elif ti == 7:
                              P.dma("sp", O["o_pool_p"][:, l], PX[:, :, n:n + 15], 26, reads=[PXB])
                          for k2 in range(2):
                              if not samp:
                                  W_ = 15 + n
                                  src = PX[:, k2, 0:W_]; a_ = PA[:, 0:W_]; b_ = PBt[:, 0:W_]
                                  sh = lambda v, s_: (v[:, s_:W_], v[:, 0:W_ - s_])
                                  full = lambda v: v
                              else:
                                  W_ = 19
                                  src = pxs[:, k2]; a_ = PA[:, 0:16 * 19].rearrange("p (b t) -> p b t", t=19); b_ = PBt[:, 0:16 * 19].rearrange("p (b t) -> p b t", t=19)
                                  sh = lambda v, s_: (v[:, :, s_:W_], v[:, :, 0:W_ - s_])
                                  full = lambda v: v
                              cp(a_, src, [PXB], [PAB], eng="pool")
                              hi_, lo_ = sh(a_, 1); xh, xl = sh(src, 1)
                              tt(hi_, xh, xl, ALU.add, [PXB, PAB], [PAB])
                              cur, oth = a_, b_
                              for d in range(1, 4):
                                  need_lo = (2 ** d) < (2 if k2 == 0 else 8)
                                  need_hi = (2 ** d) < (4 if k2 == 0 else 16)
                                  if not (need_lo or need_hi):
                                      break
                                  s_ = 2 ** d
                                  cp(oth, cur, [PAB], [PAB], eng="pool")
                                  oh, ol = sh(oth, s_); ch, cl = sh(cur, s_)
                                  if need_lo:
                                      tt(oh[0:64], ch[0:64], cl[0:64], ALU.add, [PAB], [PAB])
                                  if need_hi:
                                      tt(oh[64:128], ch[64:128], cl[64:128], ALU.add, [PAB], [PAB])
                                  cur, oth = oth, cur
                              if not samp:
                                  wv = cur[:, 15:15 + n]; xv = PX[:, k2, 15:15 + n]; rv = oth[:, 15:15 + n]
                              else:
                                  wv = cur[:, :, 15:19]; xv = pxs[:, k2, :, 15:19]; rv = oth[:, :, 15:19]
                              for half in range(2):
                                  wsz = [2, 4, 8, 16][2 * k2 + half]
                                  pr = slice(64 * half, 64 * half + 64)
                                  stt(rv[pr], wv[pr], 1.0 / wsz, xv[pr], ALU.mult, ALU.subtract, [PAB, PXB], [PAB])
                              if ti == 0:
                                  tt(rv[:, 0:16], wv[:, 0:16], CST[:, 6 + k2, 0:16], ALU.mult, [PAB, CSTB], [PAB])
                                  tt(rv[:, 0:16], rv[:, 0:16], xv[:, 0:16], ALU.subtract, [PAB, PXB], [PAB])
                              if not samp:
                                  cp(SQ[:, k2, :n], rv, [PAB], [SQB])
                              else:
                                  cp(SQ[:, k2, :n].rearrange("p (b t) -> p b t", t=4), rv, [PAB], [SQB])
                              mm(PS[2][:, :n], PWT[:, k2, :], SQ[:, k2, :n], [SMB, SQB], [PSB[2]])
                              ts(MIX[:, 6 + k2, :n], PS[2][:, :n], PSC[:, l, k2:k2 + 1], None, ALU.mult, None, [PSB[2], PARB], [MIXB])
                          if not samp:
                              cp(PX[:, :, 0:15], PX[:, :, n:n + 15], [PXB], [PXB], eng="pool")
                          if dbg and l == 0:
                              P.dma("pool", O["dbgmix"][:, :, c0:c0 + n], MIX[:, :, :n], 29, reads=[MIXB])
                              P.dma("pool", O["dbgoo"][:, :, c0:c0 + n], OO[:, :, :n], 30, reads=[OOB])
                          for oc in range(8):
                              bnk = 1 + (oc % 3)
                              for kc in range(8):
                                  mm(PS[bnk][:, :n], WOUT[:, kc, oc * 128:(oc + 1) * 128], MIX[:, kc, :n], [WOUTB, MIXB], [PSB[bnk]], start=(kc == 0), stop=(kc == 7))
                              cp(MO[:, oc, :n], PS[bnk][:, :n], [PSB[bnk]], [MOB], eng=("act" if oc % 2 == 0 else "dve"))
                          rms_rstd(lambda kc: MO[:, kc, :n], n, 8, RS, RSB, [MOB], 1.0 / 1024, SQ, SQB, 0)
                          for kc in range(8):
                              tt(MO[:, kc, :n], MO[:, kc, :n], RS[:, :n], ALU.mult, [MOB, RSB], [MOB])
                              stt(X[:, kc, c0:c0 + n], MO[:, kc, :n], NW[:, l, 1, kc:kc + 1], X[:, kc, c0:c0 + n], ALU.mult, ALU.add,
                                  [MOB, PARB, XB[ti]], [XB[ti]])
                      P.barrier()
            MARKS.append(("L%d FFN" % l, P.cnt["pe"]))
            with contextlib.ExitStack() as sf:
              if ("F", l) not in skip:
                  sbf_ = lambda n, s, d=F32: sf.enter_context(nc.sbuf_tensor("%s_%d" % (n, l), list(s), d))
                  HW_ = 1088
                  H2 = sbf_("H2", [128, 8, HW_], BF16); H2B = Buf()
                  AV = sbf_("AV", [128, 22, HW_], BF16); AVB = Buf()
                  WG = [sbf_("WG%d" % i, [128, 2, 8, 128], BF16) for i in range(2)]; WGB = [Buf() for _ in range(2)]
                  WDA = sbf_("WDA", [128, 22, 1024], BF16); WDAB = Buf()
                  for a_ in range(0, 22, 6):
                      b_ = min(22, a_ + 6)
                      P.dma("pool", WDA[:, a_:b_, :], I["w_down"][l, :, a_:b_, :], 19, writes=[WDAB])
                  SQ2 = sbf_("SQ2", [128, 2, 512], BF16); SQ2B = [Buf(), Buf()]
                  RS2 = sbf_("RS2", [128, 512]); RS2B = Buf()
                  GS = sbf_("GS", [128, 2, 512], BF16); GSB = [Buf(), Buf()]
                  FO = sbf_("FO", [128, 8, 512], BF16); FOB = Buf()
                  unit = [0]
                  for half in range(2):
                      h0 = half * 1024
                      subt = [(0, 512), (512, 512)] if half == 0 else [(1024, 512), (1536, 512), (2048, 64)]
                      xbl = lambda c0, n: [XB[t_] for t_ in range(min(c0 // TW, 8), min((c0 + n - 1) // TW, 8) + 1)]
                      for (c0, n) in subt:
                          xbs = xbl(c0, n)
                          rms_rstd(lambda kc: X[:, kc, c0:c0 + n], n, 8, RS2, RS2B, xbs, 1.0 / 1024, SQ2, SQ2B, 0, ring=2)
                          for kc in range(8):
                              stt(H2[:, kc, c0 - h0:c0 - h0 + n], X[:, kc, c0:c0 + n], NW[:, l, 2, kc:kc + 1], RS2[:, :n], ALU.mult, ALU.mult,
                                  xbs + [RS2B, PARB], [H2B])
                      for fc in range(22):
                          sl = fc % 2
                          P.dma("pool", WG[sl][:], I["w_gu"][l, fc], 16 + sl, writes=[WGB[sl]])
                          for si, (c0, n) in enumerate(subt):
                              r0 = c0 - h0
                              u_ = unit[0] % 2
                              unit[0] += 1
                              bg, bu = 1 + 2 * u_, 2 + 2 * u_
                              for kc in range(8):
                                  mm(PS[bg][:, :n], WG[sl][:, 0, kc, :], H2[:, kc, r0:r0 + n], [WGB[sl], H2B], [PSB[bg]], start=(kc == 0), stop=(kc == 7))
                              for kc in range(8):
                                  mm(PS[bu][:, :n], WG[sl][:, 1, kc, :], H2[:, kc, r0:r0 + n], [WGB[sl], H2B], [PSB[bu]], start=(kc == 0), stop=(kc == 7))
                              act(GS[:, u_, :n], PS[bg][:, :n], AF.Silu, [PSB[bg]], [GSB[u_]])
                              tt(AV[:, fc, r0:r0 + n], GS[:, u_, :n], PS[bu][:, :n], ALU.mult, [GSB[u_], PSB[bu]], [AVB])
                      for si, (c0, n) in enumerate(subt):
                          r0 = c0 - h0
                          xbs = xbl(c0, n)
                          for ogrp in range(2):
                              for fc in range(22):
                                  for o4 in range(4):
                                      oo_ = (ogrp * 4 + o4) * 128
                                      mm(PS[1 + o4][:, :n], WDA[:, fc, oo_:oo_ + 128], AV[:, fc, r0:r0 + n],
                                         [WDAB, AVB], [PSB[1 + o4]], start=(fc == 0), stop=(fc == 21))
                              for o4 in range(4):
                                  cp(FO[:, ogrp * 4 + o4, :n], PS[1 + o4][:, :n], [PSB[1 + o4]], [FOB], eng="act")
                          rms_rstd(lambda kc: FO[:, kc, :n], n, 8, RS2, RS2B, [FOB], 1.0 / 1024, SQ2, SQ2B, 0, ring=2)
                          for kc in range(8):
                              tt(FO[:, kc, :n], FO[:, kc, :n], RS2[:, :n], ALU.mult, [FOB, RS2B], [FOB])
                              stt(X[:, kc, c0:c0 + n], FO[:, kc, :n], NW[:, l, 3, kc:kc + 1], X[:, kc, c0:c0 + n], ALU.mult, ALU.add,
                                  [FOB, PARB] + xbs, xbs)
                  P.barrier()
        for (c0, n, ti) in TILES:
            P.dma("sp", O["yT"][:, :, c0:c0 + n], X[:, :, c0:c0 + n], 1 + ti, reads=[XB[ti]])
        P.barrier()
        P.emit()
    return nc


def _consts():
    c = np.zeros((128, 8, 128), np.float32)
    idx = np.arange(128)
    c[:, 0, :] = np.eye(128, dtype=np.float32)
    c[:, 1, :] = 1.0
    c[:, 2, :] = (idx[:, None] <= idx[None, :])
    c[:, 3, :] = (idx[None, :] < idx[:, None])
    p = idx % 64
    ch = idx // 64
    c[:, 4, :] = ((p[:, None] == p[None, :]) & (ch[:, None] != ch[None, :]))
    c[:, 5, :] = np.where(ch == 0, -1.0, 1.0)[:, None]
    t = np.arange(16)
    for k2 in range(2):
        w = np.where(idx < 64, [2, 8][k2], [4, 16][k2]).astype(np.float32)
        c[:, 6 + k2, 0:16] = 1.0 / np.minimum(t[None, :] + 1.0, w[:, None])
    return c


def _prep_inputs(inp):
    f = lambda a: np.ascontiguousarray(np.asarray(a, dtype=np.float32))
    g = {k: np.asarray(v) for k, v in inp.items()}
    sh = {}
    sh["w_in"] = f(g["w_in"].reshape(L, 8, 128, INW).transpose(0, 2, 1, 3))
    sh["w_out"] = f(g["w_out"].reshape(L, 8, 128, 1024).transpose(0, 2, 1, 3))
    wg = g["ffn_w_gate"].reshape(L, 8, 128, 22, 128).transpose(0, 3, 2, 1, 4)
    wu = g["ffn_w_up"].reshape(L, 8, 128, 22, 128).transpose(0, 3, 2, 1, 4)
    sh["w_gu"] = f(np.stack([wg, wu], 3))
    sh["w_down"] = f(g["ffn_w_down"].reshape(L, 22, 128, 1024).transpose(0, 2, 1, 3))
    nw = np.stack([g["norm_mix_pre"], g["norm_mix_post"], g["norm_ffn_pre"], g["norm_ffn_post"]], 0)
    sh["nw"] = f(nw.reshape(4, L, 8, 128).transpose(3, 1, 0, 2))
    sh["cw"] = f(g["conv_w"].reshape(L, 4, 12, 128).transpose(3, 0, 2, 1))
    dnp = np.stack([g["dn_a_log"], g["dn_dt_bias"]], 1)
    sh["dnp"] = f(np.broadcast_to(dnp[None], (64, L, 2, 4)))
    sh["onw"] = f(g["dn_out_norm"].T)
    lam = np.stack([g["ssm_a_re"], g["ssm_a_im"], np.broadcast_to(g["ssm_log_dt"][:, :, None], (L, 16, 64))], -1)
    lam = lam.transpose(2, 0, 1, 3)
    sh["lamp"] = f(np.concatenate([lam, lam], 0))
    bre = g["ssm_b_re"].transpose(2, 0, 1, 3)
    bim = g["ssm_b_im"].transpose(2, 0, 1, 3)
    sh["bnat"] = f(np.concatenate([np.stack([bre, bim], 3), np.stack([bim, bre], 3)], 0))
    cre = g["ssm_c_re"].transpose(3, 0, 1, 2)
    cim = g["ssm_c_im"].transpose(3, 0, 1, 2)
    sh["cnat"] = f(np.concatenate([cre, cim], 0))
    sh["ssmd"] = f(g["ssm_d"].reshape(L, 2, 128).transpose(2, 0, 1))
    sh["glub"] = f(g["ssm_glu_b"].reshape(L, 2, 128).transpose(2, 0, 1))
    sh["pscale"] = f(g["pool_scale"].reshape(L, 2, 128).transpose(2, 0, 1))
    sh["gluw"] = f(g["ssm_glu_w"].reshape(L, 2, 128, 256).transpose(0, 2, 1, 3))
    pw = np.zeros((L, 128, 2, 128), np.float32)
    for k in range(2):
        for g2 in range(2):
            pw[:, g2 * 64:(g2 + 1) * 64, k, g2 * 64:(g2 + 1) * 64] = g["pool_w"][:, 2 * k + g2]
    sh["pw"] = pw
    sh["cst"] = _consts()
    maps = []
    for c in range(8):
        b0, b1 = 16 * c, 16 * c + 16
        m = dict(sh)
        xt = np.concatenate([g["x_prompt"][c], g["x_sample"][b0:b1].reshape(NS, 1024)], 0)
        m["xT"] = f(xt.T.reshape(8, 128, NT).transpose(1, 0, 2))
        m["sdel"] = f(g["state_delta"][:, b0:b1].transpose(0, 1, 3, 2, 4))
        m["sconv"] = f(g["state_conv"][:, b0:b1].reshape(L, 16, 3, 12, 128).transpose(4, 0, 3, 1, 2))
        m["spool"] = f(g["state_pool"][:, b0:b1].reshape(L, 16, 15, 2, 128).transpose(4, 0, 3, 1, 2))
        m["spool_nat"] = f(g["state_pool"][:, b0:b1])
        sre = g["state_ssm_re"][:, b0:b1].transpose(3, 0, 2, 1)
        sim = g["state_ssm_im"][:, b0:b1].transpose(3, 0, 2, 1)
        m["sssm"] = f(np.concatenate([sre, sim], 0))
        maps.append(m)
    return maps


def _assemble(res):
    yp = np.zeros((8, NP_, 1024), np.float32); ys = np.zeros((128, 4, 1024), np.float32)
    dp = np.zeros((L, 8, 4, 128, 128), np.float32); ds = np.zeros((L, 128, 4, 128, 128), np.float32)
    cvp = np.zeros((L, 8, 3, 1536), np.float32); cvs = np.zeros((L, 128, 3, 1536), np.float32)
    rp = np.zeros((L, 8, 16, 64), np.float32); ip = np.zeros((L, 8, 16, 64), np.float32)
    rs = np.zeros((L, 128, 16, 64), np.float32); is_ = np.zeros((L, 128, 16, 64), np.float32)
    pp = np.zeros((L, 8, 15, 256), np.float32); ps = np.zeros((L, 128, 15, 256), np.float32)
    for c in range(8):
        r = {k: np.asarray(v) for k, v in res[c].items()}
        b0, b1 = 16 * c, 16 * c + 16
        y = r["yT"].transpose(1, 0, 2).reshape(1024, NT).T
        yp[c] = y[:NP_]
        ys[b0:b1] = y[NP_:].reshape(16, 4, 1024)
        dp[:, c] = r["o_delta_p"].transpose(0, 2, 1, 3)
        ds[:, b0:b1] = r["o_delta_s"].transpose(0, 1, 3, 2, 4)
        cvp[:, c] = r["o_conv_p"].transpose(1, 3, 2, 0).reshape(L, 3, 1536)
        cvs[:, b0:b1] = r["o_conv_s"].transpose(1, 3, 4, 2, 0).reshape(L, 16, 3, 1536)
        sp_ = r["o_ssm_p"]
        rp[:, c] = sp_[:64].transpose(1, 2, 0)
        ip[:, c] = sp_[64:].transpose(1, 2, 0)
        ss = r["o_ssm_s"]
        rs[:, b0:b1] = ss[:64].transpose(1, 2, 3, 0)
        is_[:, b0:b1] = ss[64:].transpose(1, 2, 3, 0)
        pp[:, c] = r["o_pool_p"].transpose(1, 3, 2, 0).reshape(L, 15, 256)
        ps[:, b0:b1, 0:11] = r["o_pool_s_old"]
        ps[:, b0:b1, 11:15] = r["o_pool_s_new"].transpose(1, 3, 4, 2, 0).reshape(L, 16, 4, 256)
    return (yp, ys, dp, cvp, rp, ip, pp, ds, cvs, rs, is_, ps)


_NC_CACHE = {}


def kernel(**inputs):
    if "nc" not in _NC_CACHE:
        _NC_CACHE["nc"] = build_program()
    maps = _prep_inputs(inputs)
    res = run_bass_kernel_spmd(_NC_CACHE["nc"], maps, core_ids=list(range(8)))
    return _assemble(res.results)
```

```python
import contextlib
import math
import numpy as np
import concourse.bass as bass
import concourse.mybir as mybir
from concourse.bass_utils import run_bass_kernel_spmd

F32 = mybir.dt.float32
BF16 = mybir.dt.bfloat16
I32 = mybir.dt.int32
ALU = mybir.AluOpType
AF = mybir.ActivationFunctionType

L = 4
NP_, NS, NT = 2048, 64, 2112
TW = 256
EPS = 1e-6
INW = 2568
OFF_A, OFF_G, OFF_SSM, OFF_POOL = 1536, 1544, 2056, 2312
DFF = 2816
NKS = 11


class Buf:
    __slots__ = ("w", "r")

    def __init__(self):
        self.w = None
        self.r = []


class Prog:
    ENGS = ("pe", "act", "dve", "pool", "sp")

    def __init__(self, nc, stack, n_dma_sems=40):
        self.nc = nc
        self.sem = {e: stack.enter_context(nc.semaphore("s_" + e)) for e in self.ENGS}
        self.cnt = {e: 0 for e in self.ENGS}
        self.seen = {e: {} for e in self.ENGS}
        self.stream = {e: [] for e in self.ENGS}
        self.dsem = [stack.enter_context(nc.semaphore("d%d" % i)) for i in range(n_dma_sems)]
        self.dcnt = [0] * n_dma_sems

    def _need(self, eng, dep):
        if dep is None:
            return
        if dep[0] == "dma":
            key, val, sem = ("d", dep[1]), dep[2], self.dsem[dep[1]]
        else:
            e2, idx = dep
            if e2 == eng and (eng == "pe" or idx <= self.cnt[eng] - 2):
                return
            key, val, sem = ("e", e2), idx, self.sem[e2]
        if self.seen[eng].get(key, 0) >= val:
            return
        self.seen[eng][key] = val
        self.stream[eng].append(("wait", sem, val))

    def _deps(self, eng, reads, writes):
        deps = []
        for b in reads:
            deps.append(b.w)
        for b in writes:
            deps.append(b.w)
            deps.extend(b.r)
        best = {}
        for d in deps:
            if d is None:
                continue
            key = (d[0], d[1]) if d[0] == "dma" else ("e", d[0])
            val = d[2] if d[0] == "dma" else d[1]
            if key not in best or val > best[key][0]:
                best[key] = (val, d)
        for key in best:
            self._need(eng, best[key][1])

    def _mark(self, me, reads, writes):
        for b in reads:
            b.r.append(me)
            if len(b.r) > 64:
                b.r = b.r[-48:]
        for b in writes:
            b.w = me
            b.r = []

    def op(self, eng, fn, reads=(), writes=()):
        self._deps(eng, reads, writes)
        self.cnt[eng] += 1
        self.stream[eng].append(("inst", fn, self.sem[eng], 1))
        self._mark((eng, self.cnt[eng]), reads, writes)

    def dma(self, q, out, in_, semi, reads=(), writes=(), **kw):
        self._deps(q, reads, writes)
        self.dcnt[semi] += 16
        self.stream[q].append(("inst", lambda h: h.dma_start(out=out, in_=in_, **kw), self.dsem[semi], 16))
        self._mark(("dma", semi, self.dcnt[semi]), reads, writes)

    def barrier(self):
        for e in self.ENGS:
            for e2 in self.ENGS:
                if e2 != e and self.cnt[e2]:
                    self._need(e, (e2, self.cnt[e2]))
            for i, c in enumerate(self.dcnt):
                if c:
                    self._need(e, ("dma", i, c))

    def emit(self):
        nc = self.nc
        with nc.Block() as block:
            def run(e):
                def body(h):
                    for it in self.stream[e]:
                        if it[0] == "wait":
                            h.wait_ge(it[1], it[2])
                        else:
                            it[1](h).then_inc(it[2], it[3])
                return body
            block.tensor(run("pe"))
            block.scalar(run("act"))
            block.vector(run("dve"))
            block.gpsimd(run("pool"))
            block.sync(run("sp"))


MARKS = []


def build_program(NL=L, dbg=None, skip=()):
    nc = bass.Bass("TRN2", target_bir_lowering=False)
    din = lambda n, s, d=F32: nc.dram_tensor(n, list(s), d, kind="ExternalInput").ap()
    dout = lambda n, s, d=F32: nc.dram_tensor(n, list(s), d, kind="ExternalOutput").ap()
    I = {}
    I["xT"] = din("xT", [128, 8, NT])
    I["w_in"] = din("w_in", [L, 128, 8, INW])
    I["w_out"] = din("w_out", [L, 128, 8, 1024])
    I["w_gu"] = din("w_gu", [L, 22, 128, 2, 8, 128])
    I["w_down"] = din("w_down", [L, 128, 22, 1024])
    I["nw"] = din("nw", [128, L, 4, 8])
    I["cw"] = din("cw", [128, L, 12, 4])
    I["dnp"] = din("dnp", [64, L, 2, 4])
    I["onw"] = din("onw", [128, L])
    I["sdel"] = din("sdel", [L, 16, 128, 4, 128])
    I["sconv"] = din("sconv", [128, L, 12, 16, 3])
    I["spool"] = din("spool", [128, L, 2, 16, 15])
    I["spool_nat"] = din("spool_nat", [L, 16, 15, 256])
    I["lamp"] = din("lamp", [128, L, 16, 3])
    I["bnat"] = din("bnat", [128, L, 16, 2, 16])
    I["cnat"] = din("cnat", [128, L, 16, 16])
    I["sssm"] = din("sssm", [128, L, 16, 16])
    I["ssmd"] = din("ssmd", [128, L, 2])
    I["glub"] = din("glub", [128, L, 2])
    I["gluw"] = din("gluw", [L, 128, 2, 256])
    I["pw"] = din("pw", [L, 128, 2, 128])
    I["pscale"] = din("pscale", [128, L, 2])
    I["cst"] = din("cst", [128, 8, 128])
    O = {}
    O["yT"] = dout("yT", [128, 8, NT])
    O["o_delta_p"] = dout("o_delta_p", [L, 128, 4, 128])
    O["o_delta_s"] = dout("o_delta_s", [L, 16, 128, 4, 128])
    O["o_conv_p"] = dout("o_conv_p", [128, L, 12, 3])
    O["o_conv_s"] = dout("o_conv_s", [128, L, 12, 16, 3])
    O["o_ssm_p"] = dout("o_ssm_p", [128, L, 16])
    O["o_ssm_s"] = dout("o_ssm_s", [128, L, 16, 16])
    O["o_pool_p"] = dout("o_pool_p", [128, L, 2, 15])
    O["o_pool_s_old"] = dout("o_pool_s_old", [L, 16, 11, 256])
    O["o_pool_s_new"] = dout("o_pool_s_new", [128, L, 2, 16, 4])
    if dbg:
        O["dbgmix"] = dout("dbgmix", [128, 8, NT])
        O["dbgoo"] = dout("dbgoo", [128, 4, NT])

    with contextlib.ExitStack() as st:
        P = Prog(nc, st)
        sb = lambda n, s, d=F32: st.enter_context(nc.sbuf_tensor(n, list(s), d))
        PS = [st.enter_context(nc.psum_tensor("ps%d" % i, [128, 512], F32)) for i in range(7)]
        PSB = [Buf() for _ in range(7)]
        PT = st.enter_context(nc.psum_tensor("pt", [128, 1024], BF16))
        PTB = Buf()
        X = sb("X", [128, 8, NT]); XB = [Buf() for _ in range(9)]
        CST = sb("CST", [128, 8, 128]); CSTB = Buf()
        CB16 = sb("CB16", [128, 2, 128], BF16)
        NW = sb("NW", [128, L, 4, 8]); CW = sb("CW", [128, L, 12, 4]); DNP = sb("DNP", [64, L, 2, 4])
        ONW = sb("ONW", [128, L]); PSC = sb("PSC", [128, L, 2]); SSMD = sb("SSMD", [128, L, 2]); GLUB = sb("GLUB", [128, L, 2])
        PARB = Buf()
        EPSC = sb("EPSC", [128, 3])

        dmai = [0]

        def nsem(lo=22, hi=40):
            dmai[0] = (dmai[0] + 1) % (hi - lo)
            return lo + dmai[0]

        def mm(out, lhsT, rhs, rd, wr, start=True, stop=True):
            P.op("pe", lambda h: h.matmul(out, lhsT=lhsT, rhs=rhs, start=start, stop=stop), reads=rd, writes=wr)

        def tr(out, in_, ident, rd, wr):
            P.op("pe", lambda h: h.transpose(out, in_, ident), reads=rd, writes=wr)

        def act(out, in_, func, rd, wr, bias=0.0, scale=1.0, eng="act"):
            P.op("act", lambda h: h.activation(out, in_, func, bias=bias, scale=scale), reads=rd, writes=wr)

        def tt(out, a, b, op, rd, wr, eng="dve"):
            P.op(eng, lambda h: h.tensor_tensor(out, a, b, op), reads=rd, writes=wr)

        def ts(out, a, s1, s2, op0, op1, rd, wr, eng="dve"):
            if op1 is None:
                P.op(eng, lambda h: h.tensor_scalar(out, a, s1, None, op0), reads=rd, writes=wr)
            else:
                P.op(eng, lambda h: h.tensor_scalar(out, a, s1, s2, op0, op1), reads=rd, writes=wr)

        def stt(out, a, s, b, op0, op1, rd, wr, eng="dve"):
            P.op(eng, lambda h: h.scalar_tensor_tensor(out, a, s, b, op0, op1), reads=rd, writes=wr)

        def cp(out, in_, rd, wr, eng="dve"):
            if eng == "act":
                P.op("act", lambda h: h.copy(out, in_), reads=rd, writes=wr)
            else:
                P.op(eng, lambda h: h.tensor_copy(out, in_), reads=rd, writes=wr)

        def rcp(out, in_, rd, wr):
            P.op("dve", lambda h: h.reciprocal(out, in_), reads=rd, writes=wr)

        def mset(ap, v, wr, eng="pool"):
            P.op(eng, lambda h: h.memset(ap, v), writes=wr)

        P.dma("sp", CST[:], I["cst"][:], 20, writes=[CSTB])
        for t_, k_ in ((NW, "nw"), (CW, "cw"), (DNP, "dnp"), (ONW, "onw"), (PSC, "pscale"), (SSMD, "ssmd"), (GLUB, "glub")):
            P.dma("sp", t_[:], I[k_][:], 0, writes=[PARB])
        IDF = CST[:, 0, :]
        ONEF = CST[:, 1, :]
        LTRI = {64: CST[:, 2, :], 4: CST[:, 2, :]}
        MLT = {64: CST[:, 3, :], 4: CST[:, 3, :]}
        MSTR = {64: CST[:, 3, :], 4: CST[:, 3, :]}
        MINT = {64: CST[:, 2, :], 4: CST[:, 2, :]}
        SWP = CST[:, 4, :]
        SGN = CST[:, 5, 0:1]
        IDB = CB16[:, 0, :]
        ONEB = CB16[:, 1, :]
        CB16B = Buf()
        cp(CB16[:, 0, :], CST[:, 0, :], [CSTB], [CB16B])
        cp(CB16[:, 1, :], CST[:, 1, :], [CSTB], [CB16B])
        mset(EPSC[:, 0:1], EPS, [PARB], eng="dve")
        mset(EPSC[:, 1:2], 1.0, [PARB], eng="dve")
        mset(EPSC[:, 2:3], math.log(128.0 ** -0.5), [PARB], eng="dve")
        EPS_AP = EPSC[:, 0:1]
        LNQ_AP = EPSC[:, 2:3]

        TILES = [(i * TW, TW, i) for i in range(8)] + [(NP_, NS, 8)]
        for (c0, n, ti) in TILES:
            P.dma("sp", X[:, :, c0:c0 + n], I["xT"][:, :, c0:c0 + n], 1 + ti, writes=[XB[ti]])

        def rms_rstd(src_sq_fn, n, nk, rs_out, rsB, rd, inv_d, sq, sqB, bank, ring=None):
            if ring:
                for kc in range(nk):
                    act(sq[:, kc % ring, :n], src_sq_fn(kc), AF.Square, rd, [sqB[kc % ring]])
                    mm(PS[bank][:, :n], ONEB, sq[:, kc % ring, :n], [sqB[kc % ring], CB16B], [PSB[bank]], start=(kc == 0), stop=(kc == nk - 1))
            else:
                for kc in range(nk):
                    act(sq[:, kc, :n], src_sq_fn(kc), AF.Square, rd, [sqB])
                for kc in range(nk):
                    mm(PS[bank][:, :n], ONEB, sq[:, kc, :n], [sqB, CB16B], [PSB[bank]], start=(kc == 0), stop=(kc == nk - 1))
            act(rs_out[:, :n], PS[bank][:, :n], AF.Ln, [PSB[bank], PARB], [rsB], bias=EPS_AP, scale=inv_d)
            act(rs_out[:, :n], rs_out[:, :n], AF.Exp, [rsB], [rsB], scale=-0.5)

        for l in range(NL):
            last = (l == NL - 1)
            MARKS.append(("L%d start" % l, P.cnt["pe"]))
            with contextlib.ExitStack() as sa:
              if ("A", l) not in skip:
                  sba = lambda n, s, d=F32: sa.enter_context(nc.sbuf_tensor("%s_%d" % (n, l), list(s), d))
                  WINB = Buf(); WOUTB = Buf()
                  GLUW = sba("GLUW", [128, 2, 256], BF16); PWT = sba("PWT", [128, 2, 128], BF16); SMB = Buf()
                  USSM = sba("USSM", [128, 2, NT], BF16); USB = Buf()
                  H = sba("H", [128, 8, TW], BF16); HB = Buf()
                  SQ = sba("SQ", [128, 8, TW], BF16); SQB = Buf()
                  RS = sba("RS", [128, TW]); RSB = Buf()
                  P.dma("pool", GLUW[:], I["gluw"][l], 10, writes=[SMB])
                  P.dma("pool", PWT[:], I["pw"][l], 10, writes=[SMB])

                  def norm_tile(c0, n, ti, which, out, outB, rs=None, rsB=None, sq=None, sqB=None, bank=0):
                      rs = RS if rs is None else rs
                      rsB = RSB if rsB is None else rsB
                      sq = SQ if sq is None else sq
                      sqB = SQB if sqB is None else sqB
                      rms_rstd(lambda kc: X[:, kc, c0:c0 + n], n, 8, rs, rsB, [XB[ti]], 1.0 / 1024, sq, sqB, bank)
                      for kc in range(8):
                          stt(out[:, kc, :n], X[:, kc, c0:c0 + n], NW[:, l, which, kc:kc + 1], rs[:, :n], ALU.mult, ALU.mult,
                              [XB[ti], rsB, PARB], [outB])

                  s1 = contextlib.ExitStack()
                  WSSM = s1.enter_context(nc.sbuf_tensor("WSSM_%d" % l, [128, 8, 256], BF16)); WSB = Buf()
                  P.dma("pool", WSSM[:], I["w_in"][l, :, :, OFF_SSM:OFF_POOL], 31, writes=[WSB])
                  HA = s1.enter_context(nc.sbuf_tensor("HA_%d" % l, [128, 8, TW], BF16)); HAB = Buf()
                  SQA = s1.enter_context(nc.sbuf_tensor("SQA_%d" % l, [128, 8, TW], BF16)); SQAB = Buf()
                  RSA = s1.enter_context(nc.sbuf_tensor("RSA_%d" % l, [128, TW], F32)); RSAB = Buf()
                  for (c0, n, ti) in TILES:
                      if ti % 2 == 0:
                          h_, hb_ = H, HB
                          norm_tile(c0, n, ti, 0, H, HB)
                      else:
                          h_, hb_ = HA, HAB
                          norm_tile(c0, n, ti, 0, HA, HAB, rs=RSA, rsB=RSAB, sq=SQA, sqB=SQAB, bank=3)
                      pb = 1 if ti % 2 == 0 else 4
                      for oc in range(2):
                          for kc in range(8):
                              mm(PS[pb + oc][:, :n], WSSM[:, kc, oc * 128:(oc + 1) * 128], h_[:, kc, :n],
                                 [WSB, hb_], [PSB[pb + oc]], start=(kc == 0), stop=(kc == 7))
                          cp(USSM[:, oc, c0:c0 + n], PS[pb + oc][:, :n], [PSB[pb + oc]], [USB], eng=("act" if oc == 0 else "dve"))
                  P.barrier()
                  s1.close()

                  MARKS.append(("L%d A2 s5" % l, P.cnt["pe"]))
                  with contextlib.ExitStack() as s5:
                      sb5 = lambda n, s, d=F32: s5.enter_context(nc.sbuf_tensor("%s_%d" % (n, l), list(s), d))
                      YS = sb5("YS", [128, 2, NT], BF16); YSB = Buf()
                      LAMP = sb5("LAMP", [128, 16, 3]); BNAT = sb5("BNAT", [128, 16, 2, 16]); CNAT = sb5("CNAT", [128, 16, 16])
                      S0 = sb5("S0", [128, 16, 16]); LB = Buf()
                      P.dma("sp", LAMP[:], I["lamp"][:, l], 12, writes=[LB])
                      P.dma("sp", BNAT[:], I["bnat"][:, l], 12, writes=[LB])
                      P.dma("sp", CNAT[:], I["cnat"][:, l], 12, writes=[LB])
                      P.dma("sp", S0[:], I["sssm"][:, l], 12, writes=[LB])
                      DT = sb5("DT", [128, 16]); ZR = sb5("ZR", [128, 16]); FR = sb5("FR", [128, 16]); FI = sb5("FI", [128, 16, 2], I32)
                      T1 = sb5("T1", [128, 16]); T2 = sb5("T2", [128, 16]); T3 = sb5("T3", [128, 16])
                      AK = sb5("AK", [128, 16, NKS]); BK = sb5("BK", [128, 16, NKS]); TB = Buf()
                      act(DT[:], LAMP[:, :, 2], AF.Exp, [LB], [TB])
                      tt(ZR[:], LAMP[:, :, 0], DT[:], ALU.mult, [LB, TB], [TB])
                      stt(FR[:], LAMP[:, :, 1], 1.0 / (2 * math.pi), DT[:], ALU.mult, ALU.mult, [LB, TB], [TB])

                      def reduce_turns(dst, src):
                          cp(FI[:, :, 0], src, [TB], [TB])
                          cp(T3[:], FI[:, :, 0], [TB], [TB])
                          tt(dst, src, T3[:], ALU.subtract, [TB], [TB])

                      reduce_turns(FR[:], FR[:])
                      for k in range(NKS):
                          act(T1[:], ZR[:], AF.Exp, [TB], [TB], scale=float(2 ** k))
                          act(T2[:], FR[:], AF.Sin, [TB], [TB], scale=2 * math.pi)
                          tt(BK[:, :, k], T1[:], T2[:], ALU.mult, [TB], [TB])
                          ts(T2[:], FR[:], 0.25, None, ALU.add, None, [TB], [TB])
                          reduce_turns(T2[:], T2[:])
                          act(T2[:], T2[:], AF.Sin, [TB], [TB], scale=2 * math.pi)
                          tt(AK[:, :, k], T1[:], T2[:], ALU.mult, [TB], [TB])
                          if k < NKS - 1:
                              ts(FR[:], FR[:], 2.0, None, ALU.mult, None, [TB], [TB])
                              reduce_turns(FR[:], FR[:])
                      FRE = sb5("FRE", [128, 16]); FIM = sb5("FIM", [128, 16]); DEN = sb5("DEN", [128, 16]); AM1 = sb5("AM1", [128, 16])
                      are, aim = LAMP[:, :, 0], LAMP[:, :, 1]
                      tt(DEN[:], are, are, ALU.mult, [LB], [TB]); tt(T1[:], aim, aim, ALU.mult, [LB], [TB])
                      tt(DEN[:], DEN[:], T1[:], ALU.add, [TB], [TB]); rcp(DEN[:], DEN[:], [TB], [TB])
                      ts(AM1[:], AK[:, :, 0], -1.0, None, ALU.add, None, [TB], [TB])
                      tt(T1[:], AM1[:], are, ALU.mult, [TB, LB], [TB]); tt(T2[:], BK[:, :, 0], aim, ALU.mult, [TB, LB], [TB])
                      tt(T1[:], T1[:], T2[:], ALU.add, [TB], [TB]); tt(FRE[:], T1[:], DEN[:], ALU.mult, [TB], [TB])
                      tt(T1[:], BK[:, :, 0], are, ALU.mult, [TB, LB], [TB]); tt(T2[:], AM1[:], aim, ALU.mult, [TB, LB], [TB])
                      tt(T1[:], T1[:], T2[:], ALU.subtract, [TB], [TB]); tt(FIM[:], T1[:], DEN[:], ALU.mult, [TB], [TB])
                      ts(FIM[:], FIM[:], SGN, None, ALU.mult, None, [TB, CSTB], [TB])
                      BB = sb5("BB", [128, 16, 16]); BB2 = sb5("BB2", [128, 16, 16])
                      tt(BB[:], BNAT[:, :, 0, :], FRE[:, :, None].broadcast_to([128, 16, 16]), ALU.mult, [LB, TB], [TB])
                      tt(BB2[:], BNAT[:, :, 1, :], FIM[:, :, None].broadcast_to([128, 16, 16]), ALU.mult, [LB, TB], [TB])
                      tt(BB[:], BB[:], BB2[:], ALU.add, [TB], [TB])
                      CT = sb5("CT", [128, 16, 16])
                      ts(CT[:], CNAT[:], SGN, -1.0, ALU.mult, ALU.mult, [LB, CSTB], [TB])
                      BPAD = sb5("BPAD", [128, 128]); BPB = Buf()
                      BT = sb5("BT", [128, 8, 128], BF16); BTB = [Buf() for _ in range(8)]
                      CPAD = sb5("CPAD", [128, 8, 128], BF16); CPB = Buf()
                      RT = sb5("RT", [128, 8, NKS, 1, 128], BF16); RTB = [Buf() for _ in range(8)]
                      RFA = [sb5("RFA%d" % i, [128, NKS, 128]) for i in range(2)]; RFBt = [sb5("RFBt%d" % i, [128, NKS, 128]) for i in range(2)]
                      RFB = [Buf(), Buf()]
                      XE = sb5("XE", [128, 8, 1 + NP_ + 80], BF16); XEB = [Buf() for _ in range(8)]
                      YF = sb5("YF", [128, TW]); YFB = Buf()
                      SF = sb5("SF", [128, 17, 16]); SFB = Buf()
                      BS = sb5("BS", [128, 16, NKS]);
                      ts(BS[:], BK[:], SGN, -1.0, ALU.mult, ALU.mult, [TB, CSTB], [TB])
                      mset(BPAD[:], 0.0, [BPB]); mset(CPAD[:], 0.0, [CPB])
                      NE = 1 + NP_
                      for oc in range(2):
                          xesv = [XE[:, gi, NE:NE + 80].rearrange("p (b t) -> p b t", t=5) for gi in range(8)]
                          for gi in range(8):
                              g = oc * 8 + gi
                              cp(BPAD[:, gi * 16:(gi + 1) * 16], BB[:, g, :], [TB], [BPB])
                              tr(PS[2][:, 0:128], BPAD[:], IDF, [BPB, CSTB], [PSB[2]])
                              cp(BT[:, gi, :], PS[2][:, 0:128], [PSB[2]], [BTB[gi]])
                              mset(BPAD[:, gi * 16:(gi + 1) * 16], 0.0, [BPB])
                              cp(CPAD[:, gi, gi * 16:(gi + 1) * 16], CT[:, g, :], [TB], [CPB])
                              e_ = "dve" if (gi % 2 == 0) else "pool"
                              r1, r2, rb = RFA[gi % 2], RFBt[gi % 2], RFB[gi % 2]
                              tt(r1[:], SWP[:, None, :].broadcast_to([128, NKS, 128]), BS[:, g, :, None].broadcast_to([128, NKS, 128]), ALU.mult,
                                 [CSTB, TB], [rb], eng=e_)
                              tt(r2[:], IDF[:, None, :].broadcast_to([128, NKS, 128]), AK[:, g, :, None].broadcast_to([128, NKS, 128]), ALU.mult,
                                 [CSTB, TB], [rb], eng=e_)
                              tt(RT[:, gi, :, 0, :], r1[:], r2[:], ALU.add, [rb], [RTB[gi]], eng=e_)
                              mset(XE[:, gi, 0:1], 0.0, [XEB[gi]])
                              cp(xesv[gi][:, :, 0], S0[:, g, :], [LB], [XEB[gi]])
                          bk = [0]

                          def nbank():
                              bk[0] = (bk[0] + 1) % 6
                              return 1 + bk[0]
                          for (c0, n, ti) in TILES:
                              for gi in range(8):
                                  bnk = nbank()
                                  mm(PS[bnk][:, :n], BT[:, gi, :], USSM[:, oc, c0:c0 + n], [BTB[gi], USB], [PSB[bnk]])
                                  if ti < 8:
                                      cp(XE[:, gi, 1 + c0:1 + c0 + n], PS[bnk][:, :n], [PSB[bnk]], [XEB[gi]], eng="act")
                                  else:
                                      cp(xesv[gi][:, :, 1:5], PS[bnk][:, :n].rearrange("p (b t) -> p b t", t=4), [PSB[bnk]], [XEB[gi]], eng="act")
                          for k in range(NKS):
                              s = 2 ** k
                              hi = NE
                              while hi > s:
                                  lo = max(s, hi - 512)
                                  w = hi - lo
                                  for gi in range(8):
                                      bnk = nbank()
                                      mm(PS[bnk][:, :w], RT[:, gi, k, 0, :], XE[:, gi, lo - s:hi - s], [RTB[gi], XEB[gi]], [PSB[bnk]], start=True, stop=False)
                                      mm(PS[bnk][:, :w], IDB, XE[:, gi, lo:hi], [CB16B, XEB[gi]], [PSB[bnk]], start=False, stop=True)
                                      cp(XE[:, gi, lo:hi], PS[bnk][:, :w], [PSB[bnk]], [XEB[gi]], eng=("act" if gi % 2 == 0 else "dve"))
                                  hi = lo
                              if s < 5:
                                  w = 5 - s
                                  for gi in range(8):
                                      bnk = nbank()
                                      pso = PS[bnk][:, :16 * w].rearrange("p (b t) -> p b t", t=w)
                                      mm(pso, RT[:, gi, k, 0, :], xesv[gi][:, :, 0:w], [RTB[gi], XEB[gi]], [PSB[bnk]])
                                      tt(xesv[gi][:, :, s:5], xesv[gi][:, :, s:5], pso, ALU.add, [XEB[gi], PSB[bnk]], [XEB[gi]])
                          for gi in range(8):
                              g = oc * 8 + gi
                              cp(SF[:, 0, g:g + 1], XE[:, gi, NE - 1:NE], [XEB[gi]], [SFB])
                              cp(SF[:, 1:17, g], xesv[gi][:, :, 4], [XEB[gi]], [SFB])
                          if l == 0: MARKS.append(("L0 s5 y oc%d" % oc, P.cnt["pe"]))
                          for (c0, n, ti) in TILES:
                              for gi in range(8):
                                  if ti < 8:
                                      rhs = XE[:, gi, 1 + c0:1 + c0 + n]
                                      mm(PS[3][:, :n], CPAD[:, gi, :], rhs, [CPB, XEB[gi]], [PSB[3]], start=(gi == 0), stop=(gi == 7))
                                  else:
                                      rhs = XE[:, gi, NE:NE + 80].rearrange("p (b t) -> p b t", t=5)[:, :, 1:5]
                                      mm(PS[3][:, :n].rearrange("p (b t) -> p b t", t=4), CPAD[:, gi, :], rhs, [CPB, XEB[gi]], [PSB[3]],
                                         start=(gi == 0), stop=(gi == 7))
                              stt(YF[:, :n], USSM[:, oc, c0:c0 + n], SSMD[:, l, oc:oc + 1], PS[3][:, :n], ALU.mult, ALU.add, [USB, PARB, PSB[3]], [YFB])
                              tt(RS[:, :n], YF[:, :n], YF[:, :n], ALU.mult, [YFB], [RSB])
                              ts(RS[:, :n], RS[:, :n], 0.044715, 1.0, ALU.mult, ALU.add, [RSB], [RSB])
                              tt(RS[:, :n], RS[:, :n], YF[:, :n], ALU.mult, [RSB, YFB], [RSB])
                              act(RS[:, :n], RS[:, :n], AF.Sigmoid, [RSB], [RSB], scale=2.0 * math.sqrt(2.0 / math.pi))
                              tt(YS[:, oc, c0:c0 + n], RS[:, :n], YF[:, :n], ALU.mult, [RSB, YFB], [YSB])
                      P.dma("sp", O["o_ssm_p"][:, l, :], SF[:, 0, :], 25, reads=[SFB])
                      P.dma("sp", O["o_ssm_s"][:, l, :, :], SF[:, 1:17, :], 25, reads=[SFB])
                      for (c0, n, ti) in TILES:
                          for oc in range(2):
                              for kc in range(2):
                                  mm(PS[3][:, :n], GLUW[:, kc, oc * 128:(oc + 1) * 128], YS[:, kc, c0:c0 + n], [SMB, YSB], [PSB[3]],
                                     start=(kc == 0), stop=(kc == 1))
                              act(RS[:, :n], PS[3][:, :n], AF.Sigmoid, [PSB[3], PARB], [RSB], bias=GLUB[:, l, oc:oc + 1])
                              tt(USSM[:, oc, c0:c0 + n], RS[:, :n], YS[:, oc, c0:c0 + n], ALU.mult, [RSB, YSB], [USB])
                      P.barrier()

                  MARKS.append(("L%d A3" % l, P.cnt["pe"]))
                  with contextlib.ExitStack() as s3:
                      sb3 = lambda n, s, d=F32: s3.enter_context(nc.sbuf_tensor("%s_%d" % (n, l), list(s), d))
                      WIN = sb3("WIN", [128, 8, INW], BF16)
                      WOUT = sb3("WOUT", [128, 8, 1024], BF16)
                      P.dma("pool", WIN[:, 0:4], I["w_in"][l, :, 0:4], 11, writes=[WINB])
                      P.dma("pool", WIN[:, 4:8], I["w_in"][l, :, 4:8], 11, writes=[WINB])
                      P.dma("pool", WOUT[:], I["w_out"][l], 21, writes=[WOUTB])
                      PRE = sb3("PRE", [128, 12, 3 + TW], BF16); PREBs = [Buf() for _ in range(12)]
                      PRES = PRE
                      QKV = sb3("QKV", [128, 12, TW], BF16); QKVB = Buf()
                      CV = sb3("CV", [128, TW]); CVB = Buf()
                      GT = sb3("GT", [128, 4, TW], BF16); GTB = Buf()
                      OO = sb3("OO", [128, 4, TW], BF16); OOB = Buf()
                      MIX = sb3("MIX", [128, 8, TW], BF16); MIXB = Buf()
                      MO = sb3("MO", [128, 8, TW]); MOB = Buf()
                      PX = sb3("PX", [128, 2, 320]); PXB = Buf()
                      PA = sb3("PA", [128, 320]); PBt = sb3("PBt", [128, 320]); PAB = Buf()
                      PQ = sb3("PQ", [128, 2, TW], BF16); PQB = Buf()
                      CTAIL = sb3("CTAIL", [128, 12, 3]); CTB = Buf()
                      SCS = sb3("SCS", [128, 12, 16, 3]); SCSB = Buf()
                      SST = sb3("SST", [128, 4, 128]); SSB = Buf()
                      SBF = sb3("SBF", [128, 4, 128], BF16)
                      NEGA = sb3("NEGA", [64, 4]); NGB = Buf()
                      ZT = sb3("ZT", [64, 8]); LGt = sb3("LGt", [64, 4]); BET = sb3("BET", [64, 4]); EGt = sb3("EGt", [64, 4]); EKD = sb3("EKD", [64, 4])
                      EGL = [sb3("EGL%d" % i, [128, 4]) for i in range(2)]; EGLB = [Buf(), Buf()]; SCB = Buf()
                      LG = sb3("LG", [64, 4, 64]); LGB = Buf()
                      E2 = sb3("E2", [64, 2, 4, 64]); E2B = Buf()
                      EGB = sb3("EGB", [128, 4, 64], BF16); EGBB = Buf()
                      QD = [sb3("QD%d" % i, [128, 4, 64], BF16) for i in range(2)]; QDB = [Buf(), Buf()]
                      KTK = sb3("KTK", [64, 4, 128], BF16); VTK = sb3("VTK", [64, 4, 128], BF16); TKB = Buf()
                      RHK = sb3("RHK", [64, 4, 128], BF16); VBt = sb3("VBt", [64, 4, 128], BF16); KDt = [sb3("KDt%d" % i, [64, 4, 128], BF16) for i in range(2)]; RKB = Buf(); KDB = [Buf(), Buf()]
                      AF_ = sb3("AF_", [64, 4, 64]); AFB = Buf()
                      PB_ = sb3("PB_", [64, 2, 4, 64], BF16); PBB = Buf()
                      QKT = [sb3("QKT%d" % i, [64, 4, 64], BF16) for i in range(2)]; QKB = [Buf(), Buf()]
                      TTb = sb3("TTb", [64, 4, 64], BF16); TTB = Buf()
                      KC = [sb3("KC%d" % i, [128, 4, 64], BF16) for i in range(2)]; KCB = [Buf(), Buf()]
                      WS = [sb3("WS%d" % i, [64, 4, 128]) for i in range(2)]; WSB = [Buf(), Buf()]
                      UU = sb3("UU", [64, 4, 128], BF16); UUB = Buf()
                      act(NEGA[:], DNP[:, l, 0, :], AF.Exp, [PARB], [NGB])
                      ts(NEGA[:], NEGA[:], -1.0, None, ALU.mult, None, [NGB], [NGB])
                      mset(CTAIL[:], 0.0, [CTB])
                      mset(PX[:, :, 0:15], 0.0, [PXB])
                      mset(SST[:], 0.0, [SSB]); mset(SBF[:], 0.0, [SSB])

                      for (c0, n, ti) in TILES:
                          samp = (ti == 8)
                          C = 4 if samp else 64
                          nch = n // C
                          hist = 3
                          norm_tile(c0, n, ti, 0, H, HB)
                          if not samp:
                              cp(PRE[:, :, 0:3], CTAIL[:], [CTB], PREBs)
                              pre_new = lambda blk: PRE[:, blk, 3:3 + n]
                          else:
                              prs = PRE[:, :, 0:112].rearrange("p k (b t) -> p k b t", t=7)
                              P.dma("sp", SCS[:], I["sconv"][:, l], 14, writes=[SCSB])
                              cp(prs[:, :, :, 0:3], SCS[:], [SCSB], PREBs)
                              pre_new = lambda blk: prs[:, blk, :, 3:7]

                          def conv_blk(blk):
                              if not samp:
                                  tap = lambda j: PRE[:, blk, j:j + n]
                                  cvv = CV[:, :n]
                              else:
                                  tap = lambda j: prs[:, blk, :, j:j + 4]
                                  cvv = CV[:, :n].rearrange("p (b t) -> p b t", t=4)
                              ts(cvv, tap(0), CW[:, l, blk, 0:1], None, ALU.mult, None, [PREBs[blk], PARB], [CVB])
                              for j in range(1, 4):
                                  stt(cvv, tap(j), CW[:, l, blk, j:j + 1], cvv, ALU.mult, ALU.add, [PREBs[blk], PARB, CVB], [CVB])
                              act(QKV[:, blk, :n], CV[:, :n], AF.Silu, [CVB], [QKVB])
                          for blk in range(12):
                              col = (blk // 4) * 512 + (blk % 4) * 128
                              bnk = 1 + (blk % 3)
                              for kc in range(8):
                                  mm(PS[bnk][:, :n], WIN[:, kc, col:col + 128], H[:, kc, :n], [WINB, HB], [PSB[bnk]], start=(kc == 0), stop=(kc == 7))
                              if not samp:
                                  cp(pre_new(blk), PS[bnk][:, :n], [PSB[bnk]], [PREBs[blk]], eng="act")
                              else:
                                  cp(pre_new(blk), PS[bnk][:, :n].rearrange("p (b t) -> p b t", t=4), [PSB[bnk]], [PREBs[blk]], eng="act")
                              if blk >= 1:
                                  conv_blk(blk - 1)
                          conv_blk(11)
                          if not samp:
                              cp(CTAIL[:], PRE[:, :, n:n + 3], PREBs, [CTB])
                              if ti == 7:
                                  P.dma("sp", O["o_conv_p"][:, l], CTAIL[:], 23, reads=[CTB])
                          else:
                              cp(SCS[:], prs[:, :, :, 4:7], PREBs, [SCSB])
                              P.dma("sp", O["o_conv_s"][:, l], SCS[:], 24, reads=[SCSB])
                          for hh in range(4):
                              col = OFF_G + hh * 128
                              bnk = 1 + (hh % 2)
                              for kc in range(8):
                                  mm(PS[bnk][:, :n], WIN[:, kc, col:col + 128], H[:, kc, :n], [WINB, HB], [PSB[bnk]], start=(kc == 0), stop=(kc == 7))
                              act(GT[:, hh, :n], PS[bnk][:, :n], AF.Silu, [PSB[bnk]], [GTB])
                          for blk in range(8):
                              bnk = 3 + (blk % 2)
                              act(SQ[:, blk, :n], QKV[:, blk, :n], AF.Square, [QKVB], [SQB])
                              mm(PS[bnk][:, :n], ONEB, SQ[:, blk, :n], [SQB, CB16B], [PSB[bnk]])
                              act(MO[:, blk, :n], PS[bnk][:, :n], AF.Ln, [PSB[bnk], PARB], [MOB], bias=EPS_AP, scale=1.0)
                          for blk in range(8):
                              act(MO[:, blk, :n], MO[:, blk, :n], AF.Exp, [MOB], [MOB], scale=-0.5, bias=(LNQ_AP if blk < 4 else 0.0))
                          for blk in range(8):
                              tt(QKV[:, blk, :n], QKV[:, blk, :n], MO[:, blk, :n], ALU.mult, [QKVB, MOB], [QKVB])
                          if l == 0: MARKS.append(("L0 t%d delta" % ti, P.cnt["pe"]))
                          def poolgen():
                              for oc in range(2):
                                  cp(MIX[:, 4 + oc, :n], USSM[:, oc, c0:c0 + n], [USB], [MIXB], eng="pool")
                              if samp:
                                  pxs = PX[:, :, 0:16 * 19].rearrange("p k (b t) -> p k b t", t=19)
                                  for k_ in range(2):
                                      P.dma("sp", pxs[:, k_, :, 0:15], I["spool"][:, l, k_], 15, writes=[PXB])
                              for k2 in range(2):
                                  col = OFF_POOL + k2 * 128
                                  for kc in range(8):
                                      mm(PS[1][:, :n], WIN[:, kc, col:col + 128], H[:, kc, :n], [WINB, HB], [PSB[1]], start=(kc == 0), stop=(kc == 7))
                                  if not samp:
                                      cp(PX[:, k2, 15:15 + n], PS[1][:, :n], [PSB[1]], [PXB], eng="act")
                                  else:
                                      cp(pxs[:, k2, :, 15:19], PS[1][:, :n].rearrange("p (b t) -> p b t", t=4), [PSB[1]], [PXB], eng="act")
                              yield
                              if samp:
                                  for k_ in range(2):
                                      P.dma("sp", O["o_pool_s_new"][:, l, k_], pxs[:, k_, :, 15:19], 27, reads=[PXB])
                                  P.dma("sp", O["o_pool_s_old"][l], I["spool_nat"][l, :, 4:15, :], 28)
                              elif ti == 7:
                                  P.dma("sp", O["o_pool_p"][:, l], PX[:, :, n:n + 15], 26, reads=[PXB])
                              for k2 in range(2):
                                  if not samp:
                                      W_ = 15 + n
                                      src = PX[:, k2, 0:W_]; a_ = PA[:, 0:W_]; b_ = PBt[:, 0:W_]
                                      sh = lambda v, s_: (v[:, s_:W_], v[:, 0:W_ - s_])
                                      full = lambda v: v
                                  else:
                                      W_ = 19
                                      src = pxs[:, k2]; a_ = PA[:, 0:16 * 19].rearrange("p (b t) -> p b t", t=19); b_ = PBt[:, 0:16 * 19].rearrange("p (b t) -> p b t", t=19)
                                      sh = lambda v, s_: (v[:, :, s_:W_], v[:, :, 0:W_ - s_])
                                      full = lambda v: v
                                  cp(a_, src, [PXB], [PAB], eng="pool")
                                  hi_, lo_ = sh(a_, 1); xh, xl = sh(src, 1)
                                  tt(hi_, xh, xl, ALU.add, [PXB, PAB], [PAB])
                                  cur, oth = a_, b_
                                  for d in range(1, 4):
                                      need_lo = (2 ** d) < (2 if k2 == 0 else 8)
                                      need_hi = (2 ** d) < (4 if k2 == 0 else 16)
                                      if not (need_lo or need_hi):
                                          break
                                      s_ = 2 ** d
                                      cp(oth, cur, [PAB], [PAB], eng="pool")
                                      oh, ol = sh(oth, s_); ch, cl = sh(cur, s_)
                                      if need_lo:
                                          tt(oh[0:64], ch[0:64], cl[0:64], ALU.add, [PAB], [PAB])
                                      if need_hi:
                                          tt(oh[64:128], ch[64:128], cl[64:128], ALU.add, [PAB], [PAB])
                                      cur, oth = oth, cur
                                  yield
                                  if not samp:
                                      wv = cur[:, 15:15 + n]; xv = PX[:, k2, 15:15 + n]; rv = oth[:, 15:15 + n]
                                  else:
                                      wv = cur[:, :, 15:19]; xv = pxs[:, k2, :, 15:19]; rv = oth[:, :, 15:19]
                                  for half in range(2):
                                      wsz = [2, 4, 8, 16][2 * k2 + half]
                                      pr = slice(64 * half, 64 * half + 64)
                                      stt(rv[pr], wv[pr], 1.0 / wsz, xv[pr], ALU.mult, ALU.subtract, [PAB, PXB], [PAB])
                                  if ti == 0:
                                      tt(rv[:, 0:16], wv[:, 0:16], CST[:, 6 + k2, 0:16], ALU.mult, [PAB, CSTB], [PAB])
                                      tt(rv[:, 0:16], rv[:, 0:16], xv[:, 0:16], ALU.subtract, [PAB, PXB], [PAB])
                                  if not samp:
                                      cp(PQ[:, k2, :n], rv, [PAB], [PQB])
                                  else:
                                      cp(PQ[:, k2, :n].rearrange("p (b t) -> p b t", t=4), rv, [PAB], [PQB])
                                  mm(PS[1][:, :n], PWT[:, k2, :], PQ[:, k2, :n], [SMB, PQB], [PSB[1]])
                                  ts(MIX[:, 6 + k2, :n], PS[1][:, :n], PSC[:, l, k2:k2 + 1], None, ALU.mult, None, [PSB[1], PARB], [MIXB])
                                  yield
                              if not samp:
                                  cp(PX[:, :, 0:15], PX[:, :, n:n + 15], [PXB], [PXB], eng="pool")

                          def prep(ci, sl):
                              a0 = ci * C
                              cols = slice(a0, a0 + C)
                              QDs, QKTs, KDs, KCs, WSs, EGLs = QD[sl], QKT[sl], KDt[sl], KC[sl], WS[sl], EGL[sl]
                              for kc in range(8):
                                  mm(PS[2][:C, 0:8], H[:, kc, cols], WIN[:, kc, OFF_A:OFF_A + 8], [HB, WINB], [PSB[2]], start=(kc == 0), stop=(kc == 7))
                              kt_ps = PT[:C, 0:512].rearrange("p (h d) -> p h d", h=4)
                              vt_ps = PT[:C, 512:1024].rearrange("p (h d) -> p h d", h=4)
                              for hh in range(4):
                                  tr(kt_ps[:, hh, :], QKV[:, 4 + hh, cols], IDB, [QKVB, CB16B], [PTB])
                                  tr(vt_ps[:, hh, :], QKV[:, 8 + hh, cols], IDB, [QKVB, CB16B], [PTB])
                              kq = PS[5][:C, :].rearrange("p (a h c) -> p a h c", a=2, h=4)
                              for hh in range(4):
                                  mm(kq[:, 0, hh, :C], QKV[:, 4 + hh, cols], QKV[:, 4 + hh, cols], [QKVB], [PSB[5]])
                                  mm(kq[:, 1, hh, :C], QKV[:, 4 + hh, cols], QKV[:, hh, cols], [QKVB], [PSB[5]])
                              cp(KTK[:C], kt_ps, [PTB], [TKB])
                              cp(VTK[:C], vt_ps, [PTB], [TKB])
                              yield
                              tt(ZT[:C, 0:4], PS[2][:C, 0:4], DNP[:C, l, 1, :], ALU.add, [PSB[2], PARB], [SCB])
                              act(BET[:C], PS[2][:C, 4:8], AF.Exp, [PSB[2]], [SCB], scale=-1.0)
                              act(BET[:C], BET[:C], AF.Ln, [SCB, PARB], [SCB], bias=EPSC[:C, 1:2])
                              act(BET[:C], BET[:C], AF.Exp, [SCB], [SCB], scale=-1.0)
                              act(ZT[:C, 0:4], ZT[:C, 0:4], AF.Exp, [SCB], [SCB])
                              act(ZT[:C, 0:4], ZT[:C, 0:4], AF.Ln, [SCB, PARB], [SCB], bias=EPSC[:C, 1:2])
                              tt(LGt[:C], ZT[:C, 0:4], NEGA[:C], ALU.mult, [SCB, NGB], [SCB])
                              tt(LG[:C, :, :C], LTRI[C][:C, None, :C].broadcast_to([C, 4, C]), LGt[:C, :, None].broadcast_to([C, 4, C]), ALU.mult,
                                 [CSTB, SCB], [LGB])
                              tt(VBt[:C], VTK[:C], BET[:C, :, None].broadcast_to([C, 4, 128]), ALU.mult, [TKB, SCB], [RKB])
                              tt(RHK[:C], KTK[:C], BET[:C, :, None].broadcast_to([C, 4, 128]), ALU.mult, [TKB, SCB], [RKB])
                              yield
                              d2 = PS[3][:C, :].rearrange("p (a h c) -> p a h c", a=2, h=4)
                              for hh in range(4):
                                  mm(d2[:, 0, hh, :C], LG[:C, hh, :C], MLT[C][:C, :C], [LGB, CSTB], [PSB[3]])
                              if C == 64:
                                  mm(d2[:, 1, :, :], MLT[C][:C, :C], LG[:C, :, :], [LGB, CSTB], [PSB[3]])
                              else:
                                  for hh in range(4):
                                      mm(d2[:, 1, hh, :C], MLT[C][:C, :C], LG[:C, hh, :C], [LGB, CSTB], [PSB[3]])
                              mm(PS[2][:C, 8:12], LTRI[C][:C, :C], LGt[:C, :], [CSTB, SCB], [PSB[2]])
                              mm(PS[2][:, 16:20], ONEF[:C, :], LGt[:C, :], [CSTB, SCB], [PSB[2]])
                              eg_ps = PS[4][:, 0:4 * C].rearrange("p (h c) -> p h c", h=4)
                              if C == 64:
                                  mm(eg_ps, ONEF[:C, :], LG[:C, :, :], [CSTB, LGB], [PSB[4]])
                              else:
                                  for hh in range(4):
                                      mm(eg_ps[:, hh, :], ONEF[:C, :], LG[:C, hh, :C], [CSTB, LGB], [PSB[4]])
                              act(E2[:C, :, :, :C], d2[:, :, :, :C], AF.Exp, [PSB[3]], [E2B])
                              act(EGt[:C], PS[2][:C, 8:12], AF.Exp, [PSB[2]], [SCB])
                              act(EGLs[:], PS[2][:, 16:20], AF.Exp, [PSB[2]], [EGLB[sl]])
                              cp(ZT[:C, 4:8], PS[2][:C, 8:12], [PSB[2]], [SCB])
                              tt(EKD[:C], PS[2][:C, 16:20], ZT[:C, 4:8], ALU.subtract, [PSB[2], SCB], [SCB])
                              act(EKD[:C], EKD[:C], AF.Exp, [SCB], [SCB])
                              act(EGB[:, :, :C], eg_ps, AF.Exp, [PSB[4]], [EGBB])
                              yield
                              tt(E2[:C, 0, :, :C], E2[:C, 0, :, :C], MSTR[C][:C, None, :C].broadcast_to([C, 4, C]), ALU.mult, [E2B, CSTB], [E2B])
                              tt(E2[:C, 1, :, :C], E2[:C, 1, :, :C], MINT[C][:C, None, :C].broadcast_to([C, 4, C]), ALU.mult, [E2B, CSTB], [E2B])
                              tt(AF_[:C, :, :C], kq[:, 0, :, :C], E2[:C, 0, :, :C], ALU.mult, [PSB[5], E2B], [AFB])
                              tt(AF_[:C, :, :C], AF_[:C, :, :C], BET[:C, :, None].broadcast_to([C, 4, C]), ALU.mult, [AFB, SCB], [AFB])
                              at_ps = PS[6][:C, 0:4 * C].rearrange("p (h c) -> p h c", h=4)
                              for hh in range(4):
                                  tr(at_ps[:, hh, :], AF_[:C, hh, :C], IDF[:C, :C], [AFB, CSTB], [PSB[6]])
                              tt(QKTs[:C, :, :C], kq[:, 1, :, :C], E2[:C, 1, :, :C], ALU.mult, [PSB[5], E2B], [QKB[sl]])
                              tt(QDs[:, :, :C], QKV[:, 0:4, cols], EGB[:, :, :C], ALU.mult, [QKVB, EGBB], [QDB[sl]])
                              tt(RHK[:C], RHK[:C], EGt[:C, :, None].broadcast_to([C, 4, 128]), ALU.mult, [RKB, SCB], [RKB])
                              tt(KDs[:C], KTK[:C], EKD[:C, :, None].broadcast_to([C, 4, 128]), ALU.mult, [TKB, SCB], [KDB[sl]])
                              cp(PB_[:C, 0, :, :C], AF_[:C, :, :C], [AFB], [PBB], eng="act")
                              cp(PB_[:C, 1, :, :C], at_ps, [PSB[6]], [PBB], eng="act")
                              tt(TTb[:C, :, :C], IDF[:C, None, :C].broadcast_to([C, 4, C]), at_ps, ALU.subtract, [CSTB, PSB[6]], [TTB])
                              yield
                              nlev = 5 if C == 64 else 1
                              for lev in range(nlev):
                                  p2 = PS[3][:C, :].rearrange("p (a h c) -> p a h c", a=2, h=4)
                                  for hh in range(4):
                                      mm(p2[:, 0, hh, :C], PB_[:C, 1, hh, :C], PB_[:C, 0, hh, :C], [PBB], [PSB[3]])
                                      mm(p2[:, 1, hh, :C], PB_[:C, 0, hh, :C], PB_[:C, 1, hh, :C], [PBB], [PSB[3]])
                                  cp(PB_[:C, :, :, :C], p2[:, :, :, :C], [PSB[3]], [PBB], eng="act")
                                  tu = PS[6][:C, 0:4 * C].rearrange("p (h c) -> p h c", h=4)
                                  for hh in range(4):
                                      mm(tu[:, hh, :], PB_[:C, 0, hh, :C], TTb[:C, hh, :C], [PBB, TTB], [PSB[6]])
                                  tt(TTb[:C, :, :C], TTb[:C, :, :C], tu, ALU.add, [TTB, PSB[6]], [TTB])
                                  yield
                              w_ps = PS[3][:C, :].rearrange("p (h d) -> p h d", h=4)
                              kc_ps = PS[4][:, 0:4 * C].rearrange("p (h c) -> p h c", h=4)
                              for hh in range(4):
                                  mm(w_ps[:, hh, :], TTb[:C, hh, :C], VBt[:C, hh, :], [TTB, RKB], [PSB[3]])
                                  mm(kc_ps[:, hh, :], RHK[:C, hh, :], TTb[:C, hh, :C], [TTB, RKB], [PSB[4]])
                              cp(WSs[:C], w_ps, [PSB[3]], [WSB[sl]], eng="act")
                              cp(KCs[:, :, :C], kc_ps, [PSB[4]], [KCB[sl]])

                          def step(ci, sl):
                              a0 = ci * C
                              cols = slice(a0, a0 + C)
                              QDs, QKTs, KDs, KCs, WSs, EGLs = QD[sl], QKT[sl], KDt[sl], KC[sl], WS[sl], EGL[sl]
                              if samp:
                                  P.dma("sp", SST[:], I["sdel"][l, ci], 13, writes=[SSB])
                                  cp(SBF[:], SST[:], [SSB], [SSB])
                              p1 = PS[0][:C, :].rearrange("p (h d) -> p h d", h=4)
                              for hh in range(4):
                                  mm(p1[:, hh, :], KCs[:, hh, :C], SBF[:, hh, :], [KCB[sl], SSB], [PSB[0]])
                              tt(UU[:C], WSs[:C], p1, ALU.subtract, [WSB[sl], PSB[0]], [UUB])
                              yield
                              o_ps = PS[1][:, 0:4 * C].rearrange("p (h c) -> p h c", h=4)
                              for hh in range(4):
                                  mm(o_ps[:, hh, :], SBF[:, hh, :], QDs[:, hh, :C], [SSB, QDB[sl]], [PSB[1]], start=True, stop=False)
                                  mm(o_ps[:, hh, :], UU[:C, hh, :], QKTs[:C, hh, :C], [UUB, QKB[sl]], [PSB[1]], start=False, stop=True)
                              ds = PS[0][:, :].rearrange("p (h d) -> p h d", h=4)
                              for hh in range(4):
                                  mm(ds[:, hh, :], KDs[:C, hh, :], UU[:C, hh, :], [KDB[sl], UUB], [PSB[0]])
                              tt(SST[:], SST[:], EGLs[:, :, None].broadcast_to([128, 4, 128]), ALU.mult, [SSB, EGLB[sl]], [SSB])
                              tt(SST[:], SST[:], ds, ALU.add, [SSB, PSB[0]], [SSB])
                              cp(SBF[:], SST[:], [SSB], [SSB])
                              cp(OO[:, :, cols], o_ps, [PSB[1]], [OOB], eng="act")
                              yield
                              if samp:
                                  P.dma("sp", O["o_delta_s"][l, ci], SST[:], 22, reads=[SSB])
                              elif ti == 7 and ci == nch - 1:
                                  P.dma("sp", O["o_delta_p"][l], SST[:], 22, reads=[SSB])

                          bg = [poolgen()]

                          def drive(gens):
                              gens = [g_ for g_ in gens if g_ is not None]
                              while gens:
                                  for g_ in list(gens):
                                      try:
                                          next(g_)
                                      except StopIteration:
                                          gens.remove(g_)
                                  for g_ in list(bg):
                                      try:
                                          next(g_)
                                      except StopIteration:
                                          bg.remove(g_)
                          drive([prep(0, 0)])
                          for ci in range(nch):
                              nxt = prep(ci + 1, (ci + 1) % 2) if ci + 1 < nch else None
                              drive([nxt, step(ci, ci % 2)])
                          while bg:
                              for g_ in list(bg):
                                  try:
                                      next(g_)
                                  except StopIteration:
                                      bg.remove(g_)
                          if l == 0: MARKS.append(("L0 t%d postdelta" % ti, P.cnt["pe"]))
                          for hh in range(4):
                              act(SQ[:, hh, :n], OO[:, hh, :n], AF.Square, [OOB], [SQB])
                              mm(PS[2][:, :n], ONEB, SQ[:, hh, :n], [SQB, CB16B], [PSB[2]])
                              act(RS[:, :n], PS[2][:, :n], AF.Ln, [PSB[2], PARB], [RSB], bias=EPS_AP, scale=1.0 / 128)
                              act(RS[:, :n], RS[:, :n], AF.Exp, [RSB], [RSB], scale=-0.5)
                              stt(CV[:, :n], OO[:, hh, :n], ONW[:, l:l + 1], RS[:, :n], ALU.mult, ALU.mult, [OOB, PARB, RSB], [CVB])
                              tt(MIX[:, hh, :n], CV[:, :n], GT[:, hh, :n], ALU.mult, [CVB, GTB], [MIXB])
                          if dbg and l == 0:
                              P.dma("pool", O["dbgmix"][:, :, c0:c0 + n], MIX[:, :, :n], 29, reads=[MIXB])
                              P.dma("pool", O["dbgoo"][:, :, c0:c0 + n], OO[:, :, :n], 30, reads=[OOB])
                          for oc in range(8):
                              bnk = 1 + (oc % 3)
                              for kc in range(8):
                                  mm(PS[bnk][:, :n], WOUT[:, kc, oc * 128:(oc + 1) * 128], MIX[:, kc, :n], [WOUTB, MIXB], [PSB[bnk]], start=(kc == 0), stop=(kc == 7))
                              cp(MO[:, oc, :n], PS[bnk][:, :n], [PSB[bnk]], [MOB], eng=("act" if oc % 2 == 0 else "dve"))
                          rms_rstd(lambda kc: MO[:, kc, :n], n, 8, RS, RSB, [MOB], 1.0 / 1024, SQ, SQB, 0)
                          for kc in range(8):
                              tt(MO[:, kc, :n], MO[:, kc, :n], RS[:, :n], ALU.mult, [MOB, RSB], [MOB])
                              stt(X[:, kc, c0:c0 + n], MO[:, kc, :n], NW[:, l, 1, kc:kc + 1], X[:, kc, c0:c0 + n], ALU.mult, ALU.add,
                                  [MOB, PARB, XB[ti]], [XB[ti]])
                      P.barrier()
            MARKS.append(("L%d FFN" % l, P.cnt["pe"]))
            with contextlib.ExitStack() as sf:
              if ("F", l) not in skip:
                  sbf_ = lambda n, s, d=F32: sf.enter_context(nc.sbuf_tensor("%s_%d" % (n, l), list(s), d))
                  HW_ = 1088
                  H2 = sbf_("H2", [128, 8, HW_], BF16); H2B = Buf()
                  AV = sbf_("AV", [128, 22, HW_], BF16); AVB = Buf()
                  WG = [sbf_("WG%d" % i, [128, 2, 8, 128], BF16) for i in range(2)]; WGB = [Buf() for _ in range(2)]
                  WDA = sbf_("WDA", [128, 22, 1024], BF16); WDAB = Buf()
                  for a_ in range(0, 22, 6):
                      b_ = min(22, a_ + 6)
                      P.dma("pool", WDA[:, a_:b_, :], I["w_down"][l, :, a_:b_, :], 19, writes=[WDAB])
                  SQ2 = sbf_("SQ2", [128, 2, 512], BF16); SQ2B = [Buf(), Buf()]
                  RS2 = sbf_("RS2", [128, 512]); RS2B = Buf()
                  GS = sbf_("GS", [128, 2, 512], BF16); GSB = [Buf(), Buf()]
                  FO = sbf_("FO", [128, 8, 512], BF16); FOB = Buf()
                  unit = [0]
                  for half in range(2):
                      h0 = half * 1024
                      subt = [(0, 512), (512, 512)] if half == 0 else [(1024, 512), (1536, 512), (2048, 64)]
                      xbl = lambda c0, n: [XB[t_] for t_ in range(min(c0 // TW, 8), min((c0 + n - 1) // TW, 8) + 1)]
                      for (c0, n) in subt:
                          xbs = xbl(c0, n)
                          rms_rstd(lambda kc: X[:, kc, c0:c0 + n], n, 8, RS2, RS2B, xbs, 1.0 / 1024, SQ2, SQ2B, 0, ring=2)
                          for kc in range(8):
                              stt(H2[:, kc, c0 - h0:c0 - h0 + n], X[:, kc, c0:c0 + n], NW[:, l, 2, kc:kc + 1], RS2[:, :n], ALU.mult, ALU.mult,
                                  xbs + [RS2B, PARB], [H2B])
                      for fc in range(22):
                          sl = fc % 2
                          P.dma("pool", WG[sl][:], I["w_gu"][l, fc], 16 + sl, writes=[WGB[sl]])
                          for si, (c0, n) in enumerate(subt):
                              r0 = c0 - h0
                              u_ = unit[0] % 2
                              unit[0] += 1
                              bg, bu = 1 + 2 * u_, 2 + 2 * u_
                              for kc in range(8):
                                  mm(PS[bg][:, :n], WG[sl][:, 0, kc, :], H2[:, kc, r0:r0 + n], [WGB[sl], H2B], [PSB[bg]], start=(kc == 0), stop=(kc == 7))
                              for kc in range(8):
                                  mm(PS[bu][:, :n], WG[sl][:, 1, kc, :], H2[:, kc, r0:r0 + n], [WGB[sl], H2B], [PSB[bu]], start=(kc == 0), stop=(kc == 7))
                              act(GS[:, u_, :n], PS[bg][:, :n], AF.Silu, [PSB[bg]], [GSB[u_]])
                              tt(AV[:, fc, r0:r0 + n], GS[:, u_, :n], PS[bu][:, :n], ALU.mult, [GSB[u_], PSB[bu]], [AVB])
                      for si, (c0, n) in enumerate(subt):
                          r0 = c0 - h0
                          xbs = xbl(c0, n)
                          for ogrp in range(2):
                              for fc in range(22):
                                  for o4 in range(4):
                                      oo_ = (ogrp * 4 + o4) * 128
                                      mm(PS[1 + o4][:, :n], WDA[:, fc, oo_:oo_ + 128], AV[:, fc, r0:r0 + n],
                                         [WDAB, AVB], [PSB[1 + o4]], start=(fc == 0), stop=(fc == 21))
                              for o4 in range(4):
                                  cp(FO[:, ogrp * 4 + o4, :n], PS[1 + o4][:, :n], [PSB[1 + o4]], [FOB], eng="act")
                          rms_rstd(lambda kc: FO[:, kc, :n], n, 8, RS2, RS2B, [FOB], 1.0 / 1024, SQ2, SQ2B, 0, ring=2)
                          for kc in range(8):
                              tt(FO[:, kc, :n], FO[:, kc, :n], RS2[:, :n], ALU.mult, [FOB, RS2B], [FOB])
                              stt(X[:, kc, c0:c0 + n], FO[:, kc, :n], NW[:, l, 3, kc:kc + 1], X[:, kc, c0:c0 + n], ALU.mult, ALU.add,
                                  [FOB, PARB] + xbs, xbs)
                  P.barrier()
        for (c0, n, ti) in TILES:
            P.dma("sp", O["yT"][:, :, c0:c0 + n], X[:, :, c0:c0 + n], 1 + ti, reads=[XB[ti]])
        P.barrier()
        P.emit()
    return nc


def _consts():
    c = np.zeros((128, 8, 128), np.float32)
    idx = np.arange(128)
    c[:, 0, :] = np.eye(128, dtype=np.float32)
    c[:, 1, :] = 1.0
    c[:, 2, :] = (idx[:, None] <= idx[None, :])
    c[:, 3, :] = (idx[None, :] < idx[:, None])
    p = idx % 64
    ch = idx // 64
    c[:, 4, :] = ((p[:, None] == p[None, :]) & (ch[:, None] != ch[None, :]))
    c[:, 5, :] = np.where(ch == 0, -1.0, 1.0)[:, None]
    t = np.arange(16)
    for k2 in range(2):
        w = np.where(idx < 64, [2, 8][k2], [4, 16][k2]).astype(np.float32)
        c[:, 6 + k2, 0:16] = 1.0 / np.minimum(t[None, :] + 1.0, w[:, None])
    return c


def _prep_inputs(inp):
    f = lambda a: np.ascontiguousarray(np.asarray(a, dtype=np.float32))
    g = {k: np.asarray(v) for k, v in inp.items()}
    sh = {}
    sh["w_in"] = f(g["w_in"].reshape(L, 8, 128, INW).transpose(0, 2, 1, 3))
    sh["w_out"] = f(g["w_out"].reshape(L, 8, 128, 1024).transpose(0, 2, 1, 3))
    wg = g["ffn_w_gate"].reshape(L, 8, 128, 22, 128).transpose(0, 3, 2, 1, 4)
    wu = g["ffn_w_up"].reshape(L, 8, 128, 22, 128).transpose(0, 3, 2, 1, 4)
    sh["w_gu"] = f(np.stack([wg, wu], 3))
    sh["w_down"] = f(g["ffn_w_down"].reshape(L, 22, 128, 1024).transpose(0, 2, 1, 3))
    nw = np.stack([g["norm_mix_pre"], g["norm_mix_post"], g["norm_ffn_pre"], g["norm_ffn_post"]], 0)
    sh["nw"] = f(nw.reshape(4, L, 8, 128).transpose(3, 1, 0, 2))
    sh["cw"] = f(g["conv_w"].reshape(L, 4, 12, 128).transpose(3, 0, 2, 1))
    dnp = np.stack([g["dn_a_log"], g["dn_dt_bias"]], 1)
    sh["dnp"] = f(np.broadcast_to(dnp[None], (64, L, 2, 4)))
    sh["onw"] = f(g["dn_out_norm"].T)
    lam = np.stack([g["ssm_a_re"], g["ssm_a_im"], np.broadcast_to(g["ssm_log_dt"][:, :, None], (L, 16, 64))], -1)
    lam = lam.transpose(2, 0, 1, 3)
    sh["lamp"] = f(np.concatenate([lam, lam], 0))
    bre = g["ssm_b_re"].transpose(2, 0, 1, 3)
    bim = g["ssm_b_im"].transpose(2, 0, 1, 3)
    sh["bnat"] = f(np.concatenate([np.stack([bre, bim], 3), np.stack([bim, bre], 3)], 0))
    cre = g["ssm_c_re"].transpose(3, 0, 1, 2)
    cim = g["ssm_c_im"].transpose(3, 0, 1, 2)
    sh["cnat"] = f(np.concatenate([cre, cim], 0))
    sh["ssmd"] = f(g["ssm_d"].reshape(L, 2, 128).transpose(2, 0, 1))
    sh["glub"] = f(g["ssm_glu_b"].reshape(L, 2, 128).transpose(2, 0, 1))
    sh["pscale"] = f(g["pool_scale"].reshape(L, 2, 128).transpose(2, 0, 1))
    sh["gluw"] = f(g["ssm_glu_w"].reshape(L, 2, 128, 256).transpose(0, 2, 1, 3))
    pw = np.zeros((L, 128, 2, 128), np.float32)
    for k in range(2):
        for g2 in range(2):
            pw[:, g2 * 64:(g2 + 1) * 64, k, g2 * 64:(g2 + 1) * 64] = g["pool_w"][:, 2 * k + g2]
    sh["pw"] = pw
    sh["cst"] = _consts()
    maps = []
    for c in range(8):
        b0, b1 = 16 * c, 16 * c + 16
        m = dict(sh)
        xt = np.concatenate([g["x_prompt"][c], g["x_sample"][b0:b1].reshape(NS, 1024)], 0)
        m["xT"] = f(xt.T.reshape(8, 128, NT).transpose(1, 0, 2))
        m["sdel"] = f(g["state_delta"][:, b0:b1].transpose(0, 1, 3, 2, 4))
        m["sconv"] = f(g["state_conv"][:, b0:b1].reshape(L, 16, 3, 12, 128).transpose(4, 0, 3, 1, 2))
        m["spool"] = f(g["state_pool"][:, b0:b1].reshape(L, 16, 15, 2, 128).transpose(4, 0, 3, 1, 2))
        m["spool_nat"] = f(g["state_pool"][:, b0:b1])
        sre = g["state_ssm_re"][:, b0:b1].transpose(3, 0, 2, 1)
        sim = g["state_ssm_im"][:, b0:b1].transpose(3, 0, 2, 1)
        m["sssm"] = f(np.concatenate([sre, sim], 0))
        maps.append(m)
    return maps


def _assemble(res):
    yp = np.zeros((8, NP_, 1024), np.float32); ys = np.zeros((128, 4, 1024), np.float32)
    dp = np.zeros((L, 8, 4, 128, 128), np.float32); ds = np.zeros((L, 128, 4, 128, 128), np.float32)
    cvp = np.zeros((L, 8, 3, 1536), np.float32); cvs = np.zeros((L, 128, 3, 1536), np.float32)
    rp = np.zeros((L, 8, 16, 64), np.float32); ip = np.zeros((L, 8, 16, 64), np.float32)
    rs = np.zeros((L, 128, 16, 64), np.float32); is_ = np.zeros((L, 128, 16, 64), np.float32)
    pp = np.zeros((L, 8, 15, 256), np.float32); ps = np.zeros((L, 128, 15, 256), np.float32)
    for c in range(8):
        r = {k: np.asarray(v) for k, v in res[c].items()}
        b0, b1 = 16 * c, 16 * c + 16
        y = r["yT"].transpose(1, 0, 2).reshape(1024, NT).T
        yp[c] = y[:NP_]
        ys[b0:b1] = y[NP_:].reshape(16, 4, 1024)
        dp[:, c] = r["o_delta_p"].transpose(0, 2, 1, 3)
        ds[:, b0:b1] = r["o_delta_s"].transpose(0, 1, 3, 2, 4)
        cvp[:, c] = r["o_conv_p"].transpose(1, 3, 2, 0).reshape(L, 3, 1536)
        cvs[:, b0:b1] = r["o_conv_s"].transpose(1, 3, 4, 2, 0).reshape(L, 16, 3, 1536)
        sp_ = r["o_ssm_p"]
        rp[:, c] = sp_[:64].transpose(1, 2, 0)
        ip[:, c] = sp_[64:].transpose(1, 2, 0)
        ss = r["o_ssm_s"]
        rs[:, b0:b1] = ss[:64].transpose(1, 2, 3, 0)
        is_[:, b0:b1] = ss[64:].transpose(1, 2, 3, 0)
        pp[:, c] = r["o_pool_p"].transpose(1, 3, 2, 0).reshape(L, 15, 256)
        ps[:, b0:b1, 0:11] = r["o_pool_s_old"]
        ps[:, b0:b1, 11:15] = r["o_pool_s_new"].transpose(1, 3, 4, 2, 0).reshape(L, 16, 4, 256)
    return (yp, ys, dp, cvp, rp, ip, pp, ds, cvs, rs, is_, ps)


_NC_CACHE = {}


def kernel(**inputs):
    if "nc" not in _NC_CACHE:
        _NC_CACHE["nc"] = build_program()
    maps = _prep_inputs(inputs)
    res = run_bass_kernel_spmd(_NC_CACHE["nc"], maps, core_ids=list(range(8)))
    return _assemble(res.results)
```

```python
import contextlib
import math
import numpy as np
import concourse.bass as bass
import concourse.mybir as mybir
from concourse.bass_utils import run_bass_kernel_spmd

F32 = mybir.dt.float32
BF16 = mybir.dt.bfloat16
I32 = mybir.dt.int32
ALU = mybir.AluOpType
AF = mybir.ActivationFunctionType

L = 4
NP_, NS, NT = 2048, 64, 2112
TW = 256
EPS = 1e-6
INW = 2568
OFF_A, OFF_G, OFF_SSM, OFF_POOL = 1536, 1544, 2056, 2312
DFF = 2816
NKS = 11


class Buf:
    __slots__ = ("w", "r")

    def __init__(self):
        self.w = None
        self.r = []


class Prog:
    ENGS = ("pe", "act", "dve", "pool", "sp")

    def __init__(self, nc, stack, n_dma_sems=40):
        self.nc = nc
        self.sem = {e: stack.enter_context(nc.semaphore("s_" + e)) for e in self.ENGS}
        self.cnt = {e: 0 for e in self.ENGS}
        self.seen = {e: {} for e in self.ENGS}
        self.stream = {e: [] for e in self.ENGS}
        self.dsem = [stack.enter_context(nc.semaphore("d%d" % i)) for i in range(n_dma_sems)]
        self.dcnt = [0] * n_dma_sems

    def _need(self, eng, dep):
        if dep is None:
            return
        if dep[0] == "dma":
            key, val, sem = ("d", dep[1]), dep[2], self.dsem[dep[1]]
        else:
            e2, idx = dep
            if e2 == eng and (eng == "pe" or idx <= self.cnt[eng] - 2):
                return
            key, val, sem = ("e", e2), idx, self.sem[e2]
        if self.seen[eng].get(key, 0) >= val:
            return
        self.seen[eng][key] = val
        self.stream[eng].append(("wait", sem, val))

    def _deps(self, eng, reads, writes):
        deps = []
        for b in reads:
            deps.append(b.w)
        for b in writes:
            deps.append(b.w)
            deps.extend(b.r)
        best = {}
        for d in deps:
            if d is None:
                continue
            key = (d[0], d[1]) if d[0] == "dma" else ("e", d[0])
            val = d[2] if d[0] == "dma" else d[1]
            if key not in best or val > best[key][0]:
                best[key] = (val, d)
        for key in best:
            self._need(eng, best[key][1])

    def _mark(self, me, reads, writes):
        for b in reads:
            b.r.append(me)
            if len(b.r) > 64:
                b.r = b.r[-48:]
        for b in writes:
            b.w = me
            b.r = []

    def op(self, eng, fn, reads=(), writes=()):
        self._deps(eng, reads, writes)
        self.cnt[eng] += 1
        self.stream[eng].append(("inst", fn, self.sem[eng], 1))
        self._mark((eng, self.cnt[eng]), reads, writes)

    def dma(self, q, out, in_, semi, reads=(), writes=(), **kw):
        self._deps(q, reads, writes)
        self.dcnt[semi] += 16
        self.stream[q].append(("inst", lambda h: h.dma_start(out=out, in_=in_, **kw), self.dsem[semi], 16))
        self._mark(("dma", semi, self.dcnt[semi]), reads, writes)

    def barrier(self):
        for e in self.ENGS:
            for e2 in self.ENGS:
                if e2 != e and self.cnt[e2]:
                    self._need(e, (e2, self.cnt[e2]))
            for i, c in enumerate(self.dcnt):
                if c:
                    self._need(e, ("dma", i, c))

    def emit(self):
        nc = self.nc
        with nc.Block() as block:
            def run(e):
                def body(h):
                    for it in self.stream[e]:
                        if it[0] == "wait":
                            h.wait_ge(it[1], it[2])
                        else:
                            it[1](h).then_inc(it[2], it[3])
                return body
            block.tensor(run("pe"))
            block.scalar(run("act"))
            block.vector(run("dve"))
            block.gpsimd(run("pool"))
            block.sync(run("sp"))


MARKS = []


def build_program(NL=L, dbg=None, skip=()):
    nc = bass.Bass("TRN2", target_bir_lowering=False)
    din = lambda n, s, d=F32: nc.dram_tensor(n, list(s), d, kind="ExternalInput").ap()
    dout = lambda n, s, d=F32: nc.dram_tensor(n, list(s), d, kind="ExternalOutput").ap()
    I = {}
    I["xT"] = din("xT", [128, 8, NT])
    I["w_in"] = din("w_in", [L, 128, 8, INW])
    I["w_out"] = din("w_out", [L, 128, 8, 1024])
    I["w_gu"] = din("w_gu", [L, 22, 128, 2, 8, 128])
    I["w_down"] = din("w_down", [L, 128, 22, 1024])
    I["nw"] = din("nw", [128, L, 4, 8])
    I["cw"] = din("cw", [128, L, 12, 4])
    I["dnp"] = din("dnp", [64, L, 2, 4])
    I["onw"] = din("onw", [128, L])
    I["sdel"] = din("sdel", [L, 16, 128, 4, 128])
    I["sconv"] = din("sconv", [128, L, 12, 16, 3])
    I["spool"] = din("spool", [128, L, 2, 16, 15])
    I["spool_nat"] = din("spool_nat", [L, 16, 15, 256])
    I["lamp"] = din("lamp", [128, L, 16, 3])
    I["bnat"] = din("bnat", [128, L, 16, 2, 16])
    I["cnat"] = din("cnat", [128, L, 16, 16])
    I["sssm"] = din("sssm", [128, L, 16, 16])
    I["ssmd"] = din("ssmd", [128, L, 2])
    I["glub"] = din("glub", [128, L, 2])
    I["gluw"] = din("gluw", [L, 128, 2, 256])
    I["pw"] = din("pw", [L, 128, 2, 128])
    I["pscale"] = din("pscale", [128, L, 2])
    I["cst"] = din("cst", [128, 8, 128])
    O = {}
    O["yT"] = dout("yT", [128, 8, NT])
    O["o_delta_p"] = dout("o_delta_p", [L, 128, 4, 128])
    O["o_delta_s"] = dout("o_delta_s", [L, 16, 128, 4, 128])
    O["o_conv_p"] = dout("o_conv_p", [128, L, 12, 3])
    O["o_conv_s"] = dout("o_conv_s", [128, L, 12, 16, 3])
    O["o_ssm_p"] = dout("o_ssm_p", [128, L, 16])
    O["o_ssm_s"] = dout("o_ssm_s", [128, L, 16, 16])
    O["o_pool_p"] = dout("o_pool_p", [128, L, 2, 15])
    O["o_pool_s_old"] = dout("o_pool_s_old", [L, 16, 11, 256])
    O["o_pool_s_new"] = dout("o_pool_s_new", [128, L, 2, 16, 4])
    if dbg:
        O["dbgmix"] = dout("dbgmix", [128, 8, NT])
        O["dbgoo"] = dout("dbgoo", [128, 4, NT])

    with contextlib.ExitStack() as st:
        P = Prog(nc, st)
        sb = lambda n, s, d=F32: st.enter_context(nc.sbuf_tensor(n, list(s), d))
        PS = [st.enter_context(nc.psum_tensor("ps%d" % i, [128, 512], F32)) for i in range(7)]
        PSB = [Buf() for _ in range(7)]
        PT = st.enter_context(nc.psum_tensor("pt", [128, 1024], BF16))
        PTB = Buf()
        X = sb("X", [128, 8, NT]); XB = [Buf() for _ in range(9)]
        CST = sb("CST", [128, 8, 128]); CSTB = Buf()
        CB16 = sb("CB16", [128, 2, 128], BF16)
        NW = sb("NW", [128, L, 4, 8]); CW = sb("CW", [128, L, 12, 4]); DNP = sb("DNP", [64, L, 2, 4])
        ONW = sb("ONW", [128, L]); PSC = sb("PSC", [128, L, 2]); SSMD = sb("SSMD", [128, L, 2]); GLUB = sb("GLUB", [128, L, 2])
        PARB = Buf()
        EPSC = sb("EPSC", [128, 3])

        dmai = [0]

        def nsem(lo=22, hi=40):
            dmai[0] = (dmai[0] + 1) % (hi - lo)
            return lo + dmai[0]

        def mm(out, lhsT, rhs, rd, wr, start=True, stop=True):
            P.op("pe", lambda h: h.matmul(out, lhsT=lhsT, rhs=rhs, start=start, stop=stop), reads=rd, writes=wr)

        def tr(out, in_, ident, rd, wr):
            P.op("pe", lambda h: h.transpose(out, in_, ident), reads=rd, writes=wr)

        def act(out, in_, func, rd, wr, bias=0.0, scale=1.0, eng="act"):
            P.op("act", lambda h: h.activation(out, in_, func, bias=bias, scale=scale), reads=rd, writes=wr)

        def tt(out, a, b, op, rd, wr, eng="dve"):
            P.op(eng, lambda h: h.tensor_tensor(out, a, b, op), reads=rd, writes=wr)

        def ts(out, a, s1, s2, op0, op1, rd, wr, eng="dve"):
            if op1 is None:
                P.op(eng, lambda h: h.tensor_scalar(out, a, s1, None, op0), reads=rd, writes=wr)
            else:
                P.op(eng, lambda h: h.tensor_scalar(out, a, s1, s2, op0, op1), reads=rd, writes=wr)

        def stt(out, a, s, b, op0, op1, rd, wr, eng="dve"):
            P.op(eng, lambda h: h.scalar_tensor_tensor(out, a, s, b, op0, op1), reads=rd, writes=wr)

        def cp(out, in_, rd, wr, eng="dve"):
            if eng == "act":
                P.op("act", lambda h: h.copy(out, in_), reads=rd, writes=wr)
            else:
                P.op(eng, lambda h: h.tensor_copy(out, in_), reads=rd, writes=wr)

        def rcp(out, in_, rd, wr):
            P.op("dve", lambda h: h.reciprocal(out, in_), reads=rd, writes=wr)

        def mset(ap, v, wr, eng="pool"):
            P.op(eng, lambda h: h.memset(ap, v), writes=wr)

        P.dma("sp", CST[:], I["cst"][:], 20, writes=[CSTB])
        for t_, k_ in ((NW, "nw"), (CW, "cw"), (DNP, "dnp"), (ONW, "onw"), (PSC, "pscale"), (SSMD, "ssmd"), (GLUB, "glub")):
            P.dma("sp", t_[:], I[k_][:], 0, writes=[PARB])
        IDF = CST[:, 0, :]
        ONEF = CST[:, 1, :]
        LTRI = {64: CST[:, 2, :], 4: CST[:, 2, :]}
        MLT = {64: CST[:, 3, :], 4: CST[:, 3, :]}
        MSTR = {64: CST[:, 3, :], 4: CST[:, 3, :]}
        MINT = {64: CST[:, 2, :], 4: CST[:, 2, :]}
        SWP = CST[:, 4, :]
        SGN = CST[:, 5, 0:1]
        IDB = CB16[:, 0, :]
        ONEB = CB16[:, 1, :]
        CB16B = Buf()
        cp(CB16[:, 0, :], CST[:, 0, :], [CSTB], [CB16B])
        cp(CB16[:, 1, :], CST[:, 1, :], [CSTB], [CB16B])
        mset(EPSC[:, 0:1], EPS, [PARB], eng="dve")
        mset(EPSC[:, 1:2], 1.0, [PARB], eng="dve")
        mset(EPSC[:, 2:3], math.log(128.0 ** -0.5), [PARB], eng="dve")
        EPS_AP = EPSC[:, 0:1]
        LNQ_AP = EPSC[:, 2:3]

        TILES = [(i * TW, TW, i) for i in range(8)] + [(NP_, NS, 8)]
        for (c0, n, ti) in TILES:
            P.dma("sp", X[:, :, c0:c0 + n], I["xT"][:, :, c0:c0 + n], 1 + ti, writes=[XB[ti]])

        def rms_rstd(src_sq_fn, n, nk, rs_out, rsB, rd, inv_d, sq, sqB, bank, ring=None):
            if ring:
                for kc in range(nk):
                    act(sq[:, kc % ring, :n], src_sq_fn(kc), AF.Square, rd, [sqB[kc % ring]])
                    mm(PS[bank][:, :n], ONEB, sq[:, kc % ring, :n], [sqB[kc % ring], CB16B], [PSB[bank]], start=(kc == 0), stop=(kc == nk - 1))
            else:
                for kc in range(nk):
                    act(sq[:, kc, :n], src_sq_fn(kc), AF.Square, rd, [sqB])
                for kc in range(nk):
                    mm(PS[bank][:, :n], ONEB, sq[:, kc, :n], [sqB, CB16B], [PSB[bank]], start=(kc == 0), stop=(kc == nk - 1))
            act(rs_out[:, :n], PS[bank][:, :n], AF.Ln, [PSB[bank], PARB], [rsB], bias=EPS_AP, scale=inv_d)
            act(rs_out[:, :n], rs_out[:, :n], AF.Exp, [rsB], [rsB], scale=-0.5)

        for l in range(NL):
            last = (l == NL - 1)
            MARKS.append(("L%d start" % l, P.cnt["pe"]))
            with contextlib.ExitStack() as sa:
              if ("A", l) not in skip:
                  sba = lambda n, s, d=F32: sa.enter_context(nc.sbuf_tensor("%s_%d" % (n, l), list(s), d))
                  WINB = Buf(); WOUTB = Buf()
                  GLUW = sba("GLUW", [128, 2, 256], BF16); PWT = sba("PWT", [128, 2, 128], BF16); SMB = Buf()
                  USSM = sba("USSM", [128, 2, NT], BF16); USB = Buf()
                  H = sba("H", [128, 8, TW], BF16); HB = Buf()
                  SQ = sba("SQ", [128, 8, TW], BF16); SQB = Buf()
                  RS = sba("RS", [128, TW]); RSB = Buf()
                  P.dma("pool", GLUW[:], I["gluw"][l], 10, writes=[SMB])
                  P.dma("pool", PWT[:], I["pw"][l], 10, writes=[SMB])

                  def norm_tile(c0, n, ti, which, out, outB, rs=None, rsB=None, sq=None, sqB=None, bank=0):
                      rs = RS if rs is None else rs
                      rsB = RSB if rsB is None else rsB
                      sq = SQ if sq is None else sq
                      sqB = SQB if sqB is None else sqB
                      rms_rstd(lambda kc: X[:, kc, c0:c0 + n], n, 8, rs, rsB, [XB[ti]], 1.0 / 1024, sq, sqB, bank)
                      for kc in range(8):
                          stt(out[:, kc, :n], X[:, kc, c0:c0 + n], NW[:, l, which, kc:kc + 1], rs[:, :n], ALU.mult, ALU.mult,
                              [XB[ti], rsB, PARB], [outB])

                  s1 = contextlib.ExitStack()
                  WSSM = s1.enter_context(nc.sbuf_tensor("WSSM_%d" % l, [128, 8, 256], BF16)); WSB = Buf()
                  P.dma("pool", WSSM[:], I["w_in"][l, :, :, OFF_SSM:OFF_POOL], 31, writes=[WSB])
                  HA = s1.enter_context(nc.sbuf_tensor("HA_%d" % l, [128, 8, TW], BF16)); HAB = Buf()
                  SQA = s1.enter_context(nc.sbuf_tensor("SQA_%d" % l, [128, 8, TW], BF16)); SQAB = Buf()
                  RSA = s1.enter_context(nc.sbuf_tensor("RSA_%d" % l, [128, TW], F32)); RSAB = Buf()
                  for (c0, n, ti) in TILES:
                      if ti % 2 == 0:
                          h_, hb_ = H, HB
                          norm_tile(c0, n, ti, 0, H, HB)
                      else:
                          h_, hb_ = HA, HAB
                          norm_tile(c0, n, ti, 0, HA, HAB, rs=RSA, rsB=RSAB, sq=SQA, sqB=SQAB, bank=3)
                      pb = 1 if ti % 2 == 0 else 4
                      for oc in range(2):
                          for kc in range(8):
                              mm(PS[pb + oc][:, :n], WSSM[:, kc, oc * 128:(oc + 1) * 128], h_[:, kc, :n],
                                 [WSB, hb_], [PSB[pb + oc]], start=(kc == 0), stop=(kc == 7))
                          cp(USSM[:, oc, c0:c0 + n], PS[pb + oc][:, :n], [PSB[pb + oc]], [USB], eng=("act" if oc == 0 else "dve"))
                  P.barrier()
                  s1.close()

                  MARKS.append(("L%d A2 s5" % l, P.cnt["pe"]))
                  with contextlib.ExitStack() as s5:
                      sb5 = lambda n, s, d=F32: s5.enter_context(nc.sbuf_tensor("%s_%d" % (n, l), list(s), d))
                      YS = sb5("YS", [128, 2, NT], BF16); YSB = Buf()
                      LAMP = sb5("LAMP", [128, 16, 3]); BNAT = sb5("BNAT", [128, 16, 2, 16]); CNAT = sb5("CNAT", [128, 16, 16])
                      S0 = sb5("S0", [128, 16, 16]); LB = Buf()
                      P.dma("sp", LAMP[:], I["lamp"][:, l], 12, writes=[LB])
                      P.dma("sp", BNAT[:], I["bnat"][:, l], 12, writes=[LB])
                      P.dma("sp", CNAT[:], I["cnat"][:, l], 12, writes=[LB])
                      P.dma("sp", S0[:], I["sssm"][:, l], 12, writes=[LB])
                      DT = sb5("DT", [128, 16]); ZR = sb5("ZR", [128, 16]); FR = sb5("FR", [128, 16]); FI = sb5("FI", [128, 16, 2], I32)
                      T1 = sb5("T1", [128, 16]); T2 = sb5("T2", [128, 16]); T3 = sb5("T3", [128, 16])
                      AK = sb5("AK", [128, 16, NKS]); BK = sb5("BK", [128, 16, NKS]); TB = Buf()
                      act(DT[:], LAMP[:, :, 2], AF.Exp, [LB], [TB])
                      tt(ZR[:], LAMP[:, :, 0], DT[:], ALU.mult, [LB, TB], [TB])
                      stt(FR[:], LAMP[:, :, 1], 1.0 / (2 * math.pi), DT[:], ALU.mult, ALU.mult, [LB, TB], [TB])

                      def reduce_turns(dst, src):
                          cp(FI[:, :, 0], src, [TB], [TB])
                          cp(T3[:], FI[:, :, 0], [TB], [TB])
                          tt(dst, src, T3[:], ALU.subtract, [TB], [TB])

                      reduce_turns(FR[:], FR[:])
                      for k in range(NKS):
                          act(T1[:], ZR[:], AF.Exp, [TB], [TB], scale=float(2 ** k))
                          act(T2[:], FR[:], AF.Sin, [TB], [TB], scale=2 * math.pi)
                          tt(BK[:, :, k], T1[:], T2[:], ALU.mult, [TB], [TB])
                          ts(T2[:], FR[:], 0.25, None, ALU.add, None, [TB], [TB])
                          reduce_turns(T2[:], T2[:])
                          act(T2[:], T2[:], AF.Sin, [TB], [TB], scale=2 * math.pi)
                          tt(AK[:, :, k], T1[:], T2[:], ALU.mult, [TB], [TB])
                          if k < NKS - 1:
                              ts(FR[:], FR[:], 2.0, None, ALU.mult, None, [TB], [TB])
                              reduce_turns(FR[:], FR[:])
                      FRE = sb5("FRE", [128, 16]); FIM = sb5("FIM", [128, 16]); DEN = sb5("DEN", [128, 16]); AM1 = sb5("AM1", [128, 16])
                      are, aim = LAMP[:, :, 0], LAMP[:, :, 1]
                      tt(DEN[:], are, are, ALU.mult, [LB], [TB]); tt(T1[:], aim, aim, ALU.mult, [LB], [TB])
                      tt(DEN[:], DEN[:], T1[:], ALU.add, [TB], [TB]); rcp(DEN[:], DEN[:], [TB], [TB])
                      ts(AM1[:], AK[:, :, 0], -1.0, None, ALU.add, None, [TB], [TB])
                      tt(T1[:], AM1[:], are, ALU.mult, [TB, LB], [TB]); tt(T2[:], BK[:, :, 0], aim, ALU.mult, [TB, LB], [TB])
                      tt(T1[:], T1[:], T2[:], ALU.add, [TB], [TB]); tt(FRE[:], T1[:], DEN[:], ALU.mult, [TB], [TB])
                      tt(T1[:], BK[:, :, 0], are, ALU.mult, [TB, LB], [TB]); tt(T2[:], AM1[:], aim, ALU.mult, [TB, LB], [TB])
                      tt(T1[:], T1[:], T2[:], ALU.subtract, [TB], [TB]); tt(FIM[:], T1[:], DEN[:], ALU.mult, [TB], [TB])
                      ts(FIM[:], FIM[:], SGN, None, ALU.mult, None, [TB, CSTB], [TB])
                      BB = sb5("BB", [128, 16, 16]); BB2 = sb5("BB2", [128, 16, 16])
                      tt(BB[:], BNAT[:, :, 0, :], FRE[:, :, None].broadcast_to([128, 16, 16]), ALU.mult, [LB, TB], [TB])
                      tt(BB2[:], BNAT[:, :, 1, :], FIM[:, :, None].broadcast_to([128, 16, 16]), ALU.mult, [LB, TB], [TB])
                      tt(BB[:], BB[:], BB2[:], ALU.add, [TB], [TB])
                      CT = sb5("CT", [128, 16, 16])
                      ts(CT[:], CNAT[:], SGN, -1.0, ALU.mult, ALU.mult, [LB, CSTB], [TB])
                      BPAD = sb5("BPAD", [128, 128]); BPB = Buf()
                      BT = sb5("BT", [128, 8, 128], BF16); BTB = [Buf() for _ in range(8)]
                      CPAD = sb5("CPAD", [128, 8, 128], BF16); CPB = Buf()
                      RT = sb5("RT", [128, 8, NKS, 1, 128], BF16); RTB = [Buf() for _ in range(8)]
                      RFA = [sb5("RFA%d" % i, [128, NKS, 128]) for i in range(2)]; RFBt = [sb5("RFBt%d" % i, [128, NKS, 128]) for i in range(2)]
                      RFB = [Buf(), Buf()]
                      XE = sb5("XE", [128, 8, 1 + NP_ + 80], BF16); XEB = [Buf() for _ in range(8)]
                      YF = sb5("YF", [128, TW]); YFB = Buf()
                      SF = sb5("SF", [128, 17, 16]); SFB = Buf()
                      BS = sb5("BS", [128, 16, NKS]);
                      ts(BS[:], BK[:], SGN, -1.0, ALU.mult, ALU.mult, [TB, CSTB], [TB])
                      mset(BPAD[:], 0.0, [BPB]); mset(CPAD[:], 0.0, [CPB])
                      NE = 1 + NP_
                      for oc in range(2):
                          xesv = [XE[:, gi, NE:NE + 80].rearrange("p (b t) -> p b t", t=5) for gi in range(8)]
                          for gi in range(8):
                              g = oc * 8 + gi
                              cp(BPAD[:, gi * 16:(gi + 1) * 16], BB[:, g, :], [TB], [BPB])
                              tr(PS[2][:, 0:128], BPAD[:], IDF, [BPB, CSTB], [PSB[2]])
                              cp(BT[:, gi, :], PS[2][:, 0:128], [PSB[2]], [BTB[gi]])
                              mset(BPAD[:, gi * 16:(gi + 1) * 16], 0.0, [BPB])
                              cp(CPAD[:, gi, gi * 16:(gi + 1) * 16], CT[:, g, :], [TB], [CPB])
                              e_ = "dve" if (gi % 2 == 0) else "pool"
                              r1, r2, rb = RFA[gi % 2], RFBt[gi % 2], RFB[gi % 2]
                              tt(r1[:], SWP[:, None, :].broadcast_to([128, NKS, 128]), BS[:, g, :, None].broadcast_to([128, NKS, 128]), ALU.mult,
                                 [CSTB, TB], [rb], eng=e_)
                              tt(r2[:], IDF[:, None, :].broadcast_to([128, NKS, 128]), AK[:, g, :, None].broadcast_to([128, NKS, 128]), ALU.mult,
                                 [CSTB, TB], [rb], eng=e_)
                              tt(RT[:, gi, :, 0, :], r1[:], r2[:], ALU.add, [rb], [RTB[gi]], eng=e_)
                              mset(XE[:, gi, 0:1], 0.0, [XEB[gi]])
                              cp(xesv[gi][:, :, 0], S0[:, g, :], [LB], [XEB[gi]])
                          bk = [0]

                          def nbank():
                              bk[0] = (bk[0] + 1) % 6
                              return 1 + bk[0]
                          for (c0, n, ti) in TILES:
                              for gi in range(8):
                                  bnk = nbank()
                                  mm(PS[bnk][:, :n], BT[:, gi, :], USSM[:, oc, c0:c0 + n], [BTB[gi], USB], [PSB[bnk]])
                                  if ti < 8:
                                      cp(XE[:, gi, 1 + c0:1 + c0 + n], PS[bnk][:, :n], [PSB[bnk]], [XEB[gi]], eng="act")
                                  else:
                                      cp(xesv[gi][:, :, 1:5], PS[bnk][:, :n].rearrange("p (b t) -> p b t", t=4), [PSB[bnk]], [XEB[gi]], eng="act")
                          for k in range(NKS):
                              s = 2 ** k
                              hi = NE
                              while hi > s:
                                  lo = max(s, hi - 512)
                                  w = hi - lo
                                  for gi in range(8):
                                      bnk = nbank()
                                      if gi < 5:
                                          mm(PS[bnk][:, :w], RT[:, gi, k, 0, :], XE[:, gi, lo - s:hi - s], [RTB[gi], XEB[gi]], [PSB[bnk]], start=True, stop=False)
                                          mm(PS[bnk][:, :w], IDB, XE[:, gi, lo:hi], [CB16B, XEB[gi]], [PSB[bnk]], start=False, stop=True)
                                          cp(XE[:, gi, lo:hi], PS[bnk][:, :w], [PSB[bnk]], [XEB[gi]], eng=("act" if gi != 3 else "dve"))
                                      else:
                                          mm(PS[bnk][:, :w], RT[:, gi, k, 0, :], XE[:, gi, lo - s:hi - s], [RTB[gi], XEB[gi]], [PSB[bnk]])
                                          tt(XE[:, gi, lo:hi], XE[:, gi, lo:hi], PS[bnk][:, :w], ALU.add, [XEB[gi], PSB[bnk]], [XEB[gi]])
                                  hi = lo
                              if s < 5:
                                  w = 5 - s
                                  for gi in range(8):
                                      bnk = nbank()
                                      pso = PS[bnk][:, :16 * w].rearrange("p (b t) -> p b t", t=w)
                                      mm(pso, RT[:, gi, k, 0, :], xesv[gi][:, :, 0:w], [RTB[gi], XEB[gi]], [PSB[bnk]])
                                      tt(xesv[gi][:, :, s:5], xesv[gi][:, :, s:5], pso, ALU.add, [XEB[gi], PSB[bnk]], [XEB[gi]])
                          for gi in range(8):
                              g = oc * 8 + gi
                              cp(SF[:, 0, g:g + 1], XE[:, gi, NE - 1:NE], [XEB[gi]], [SFB])
                              cp(SF[:, 1:17, g], xesv[gi][:, :, 4], [XEB[gi]], [SFB])
                          if l == 0: MARKS.append(("L0 s5 y oc%d" % oc, P.cnt["pe"]))
                          for (c0, n, ti) in TILES:
                              for gi in range(8):
                                  if ti < 8:
                                      rhs = XE[:, gi, 1 + c0:1 + c0 + n]
                                      mm(PS[3][:, :n], CPAD[:, gi, :], rhs, [CPB, XEB[gi]], [PSB[3]], start=(gi == 0), stop=(gi == 7))
                                  else:
                                      rhs = XE[:, gi, NE:NE + 80].rearrange("p (b t) -> p b t", t=5)[:, :, 1:5]
                                      mm(PS[3][:, :n].rearrange("p (b t) -> p b t", t=4), CPAD[:, gi, :], rhs, [CPB, XEB[gi]], [PSB[3]],
                                         start=(gi == 0), stop=(gi == 7))
                              stt(YF[:, :n], USSM[:, oc, c0:c0 + n], SSMD[:, l, oc:oc + 1], PS[3][:, :n], ALU.mult, ALU.add, [USB, PARB, PSB[3]], [YFB])
                              tt(RS[:, :n], YF[:, :n], YF[:, :n], ALU.mult, [YFB], [RSB])
                              ts(RS[:, :n], RS[:, :n], 0.044715, 1.0, ALU.mult, ALU.add, [RSB], [RSB])
                              tt(RS[:, :n], RS[:, :n], YF[:, :n], ALU.mult, [RSB, YFB], [RSB])
                              act(RS[:, :n], RS[:, :n], AF.Sigmoid, [RSB], [RSB], scale=2.0 * math.sqrt(2.0 / math.pi))
                              tt(YS[:, oc, c0:c0 + n], RS[:, :n], YF[:, :n], ALU.mult, [RSB, YFB], [YSB])
                      P.dma("sp", O["o_ssm_p"][:, l, :], SF[:, 0, :], 25, reads=[SFB])
                      P.dma("sp", O["o_ssm_s"][:, l, :, :], SF[:, 1:17, :], 25, reads=[SFB])
                      for (c0, n, ti) in TILES:
                          for oc in range(2):
                              for kc in range(2):
                                  mm(PS[3][:, :n], GLUW[:, kc, oc * 128:(oc + 1) * 128], YS[:, kc, c0:c0 + n], [SMB, YSB], [PSB[3]],
                                     start=(kc == 0), stop=(kc == 1))
                              act(RS[:, :n], PS[3][:, :n], AF.Sigmoid, [PSB[3], PARB], [RSB], bias=GLUB[:, l, oc:oc + 1])
                              tt(USSM[:, oc, c0:c0 + n], RS[:, :n], YS[:, oc, c0:c0 + n], ALU.mult, [RSB, YSB], [USB])
                      P.barrier()

                  MARKS.append(("L%d A3" % l, P.cnt["pe"]))
                  with contextlib.ExitStack() as s3:
                      sb3 = lambda n, s, d=F32: s3.enter_context(nc.sbuf_tensor("%s_%d" % (n, l), list(s), d))
                      WIN = sb3("WIN", [128, 8, INW], BF16)
                      WOUT = sb3("WOUT", [128, 8, 1024], BF16)
                      P.dma("pool", WIN[:, 0:4], I["w_in"][l, :, 0:4], 11, writes=[WINB])
                      P.dma("pool", WIN[:, 4:8], I["w_in"][l, :, 4:8], 11, writes=[WINB])
                      P.dma("pool", WOUT[:], I["w_out"][l], 21, writes=[WOUTB])
                      PRE = sb3("PRE", [128, 12, 3 + TW], BF16); PREBs = [Buf() for _ in range(12)]
                      PRES = PRE
                      QKV = sb3("QKV", [128, 12, TW], BF16); QKVB = Buf()
                      CV = sb3("CV", [128, TW]); CVB = Buf()
                      GT = sb3("GT", [128, 4, TW], BF16); GTB = Buf()
                      OO = sb3("OO", [128, 4, TW], BF16); OOB = Buf()
                      MIX = sb3("MIX", [128, 8, TW], BF16); MIXB = Buf()
                      MO = sb3("MO", [128, 8, TW]); MOB = Buf()
                      PX = sb3("PX", [128, 2, 320]); PXB = Buf()
                      PA = sb3("PA", [128, 320]); PBt = sb3("PBt", [128, 320]); PAB = Buf()
                      PQ = sb3("PQ", [128, 2, TW], BF16); PQB = Buf()
                      CTAIL = sb3("CTAIL", [128, 12, 3]); CTB = Buf()
                      SCS = sb3("SCS", [128, 12, 16, 3]); SCSB = Buf()
                      SST = sb3("SST", [128, 4, 128]); SSB = Buf()
                      SBF = sb3("SBF", [128, 4, 128], BF16)
                      NEGA = sb3("NEGA", [64, 4]); NGB = Buf()
                      ZT = sb3("ZT", [64, 8]); LGt = sb3("LGt", [64, 4]); BET = sb3("BET", [64, 4]); EGt = sb3("EGt", [64, 4]); EKD = sb3("EKD", [64, 4])
                      EGL = [sb3("EGL%d" % i, [128, 4]) for i in range(2)]; EGLB = [Buf(), Buf()]; SCB = Buf()
                      LG = sb3("LG", [64, 4, 64]); LGB = Buf()
                      E2 = sb3("E2", [64, 2, 4, 64]); E2B = Buf()
                      EGB = sb3("EGB", [128, 4, 64], BF16); EGBB = Buf()
                      QD = [sb3("QD%d" % i, [128, 4, 64], BF16) for i in range(2)]; QDB = [Buf(), Buf()]
                      KTK = sb3("KTK", [64, 4, 128], BF16); VTK = sb3("VTK", [64, 4, 128], BF16); TKB = Buf()
                      RHK = sb3("RHK", [64, 4, 128], BF16); VBt = sb3("VBt", [64, 4, 128], BF16); KDt = [sb3("KDt%d" % i, [64, 4, 128], BF16) for i in range(2)]; RKB = Buf(); KDB = [Buf(), Buf()]
                      AF_ = sb3("AF_", [64, 4, 64]); AFB = Buf()
                      PB_ = sb3("PB_", [64, 2, 4, 64], BF16); PBB = Buf()
                      QKT = [sb3("QKT%d" % i, [64, 4, 64], BF16) for i in range(2)]; QKB = [Buf(), Buf()]
                      TTb = sb3("TTb", [64, 4, 64], BF16); TTB = Buf()
                      KC = [sb3("KC%d" % i, [128, 4, 64], BF16) for i in range(2)]; KCB = [Buf(), Buf()]
                      WS = [sb3("WS%d" % i, [64, 4, 128]) for i in range(2)]; WSB = [Buf(), Buf()]
                      UU = sb3("UU", [64, 4, 128], BF16); UUB = Buf()
                      act(NEGA[:], DNP[:, l, 0, :], AF.Exp, [PARB], [NGB])
                      ts(NEGA[:], NEGA[:], -1.0, None, ALU.mult, None, [NGB], [NGB])
                      mset(CTAIL[:], 0.0, [CTB])
                      mset(PX[:, :, 0:15], 0.0, [PXB])
                      mset(SST[:], 0.0, [SSB]); mset(SBF[:], 0.0, [SSB])

                      for (c0, n, ti) in TILES:
                          samp = (ti == 8)
                          C = 4 if samp else 64
                          nch = n // C
                          hist = 3
                          norm_tile(c0, n, ti, 0, H, HB)
                          if not samp:
                              cp(PRE[:, :, 0:3], CTAIL[:], [CTB], PREBs)
                              pre_new = lambda blk: PRE[:, blk, 3:3 + n]
                          else:
                              prs = PRE[:, :, 0:112].rearrange("p k (b t) -> p k b t", t=7)
                              P.dma("sp", SCS[:], I["sconv"][:, l], 14, writes=[SCSB])
                              cp(prs[:, :, :, 0:3], SCS[:], [SCSB], PREBs)
                              pre_new = lambda blk: prs[:, blk, :, 3:7]

                          def conv_blk(blk):
                              if not samp:
                                  tap = lambda j: PRE[:, blk, j:j + n]
                                  cvv = CV[:, :n]
                              else:
                                  tap = lambda j: prs[:, blk, :, j:j + 4]
                                  cvv = CV[:, :n].rearrange("p (b t) -> p b t", t=4)
                              ts(cvv, tap(0), CW[:, l, blk, 0:1], None, ALU.mult, None, [PREBs[blk], PARB], [CVB])
                              for j in range(1, 4):
                                  stt(cvv, tap(j), CW[:, l, blk, j:j + 1], cvv, ALU.mult, ALU.add, [PREBs[blk], PARB, CVB], [CVB])
                              act(QKV[:, blk, :n], CV[:, :n], AF.Silu, [CVB], [QKVB])
                          for blk in range(12):
                              col = (blk // 4) * 512 + (blk % 4) * 128
                              bnk = 1 + (blk % 3)
                              for kc in range(8):
                                  mm(PS[bnk][:, :n], WIN[:, kc, col:col + 128], H[:, kc, :n], [WINB, HB], [PSB[bnk]], start=(kc == 0), stop=(kc == 7))
                              if not samp:
                                  cp(pre_new(blk), PS[bnk][:, :n], [PSB[bnk]], [PREBs[blk]], eng="act")
                              else:
                                  cp(pre_new(blk), PS[bnk][:, :n].rearrange("p (b t) -> p b t", t=4), [PSB[bnk]], [PREBs[blk]], eng="act")
                              if blk >= 1:
                                  conv_blk(blk - 1)
                          conv_blk(11)
                          if not samp:
                              cp(CTAIL[:], PRE[:, :, n:n + 3], PREBs, [CTB])
                              if ti == 7:
                                  P.dma("sp", O["o_conv_p"][:, l], CTAIL[:], 23, reads=[CTB])
                          else:
                              cp(SCS[:], prs[:, :, :, 4:7], PREBs, [SCSB])
                              P.dma("sp", O["o_conv_s"][:, l], SCS[:], 24, reads=[SCSB])
                          for hh in range(4):
                              col = OFF_G + hh * 128
                              bnk = 1 + (hh % 2)
                              for kc in range(8):
                                  mm(PS[bnk][:, :n], WIN[:, kc, col:col + 128], H[:, kc, :n], [WINB, HB], [PSB[bnk]], start=(kc == 0), stop=(kc == 7))
                              act(GT[:, hh, :n], PS[bnk][:, :n], AF.Silu, [PSB[bnk]], [GTB])
                          for blk in range(8):
                              bnk = 3 + (blk % 2)
                              act(SQ[:, blk, :n], QKV[:, blk, :n], AF.Square, [QKVB], [SQB])
                              mm(PS[bnk][:, :n], ONEB, SQ[:, blk, :n], [SQB, CB16B], [PSB[bnk]])
                              act(MO[:, blk, :n], PS[bnk][:, :n], AF.Ln, [PSB[bnk], PARB], [MOB], bias=EPS_AP, scale=1.0)
                          for blk in range(8):
                              act(MO[:, blk, :n], MO[:, blk, :n], AF.Exp, [MOB], [MOB], scale=-0.5, bias=(LNQ_AP if blk < 4 else 0.0))
                          for blk in range(8):
                              tt(QKV[:, blk, :n], QKV[:, blk, :n], MO[:, blk, :n], ALU.mult, [QKVB, MOB], [QKVB])
                          if l == 0: MARKS.append(("L0 t%d delta" % ti, P.cnt["pe"]))
                          def poolgen():
                              for oc in range(2):
                                  cp(MIX[:, 4 + oc, :n], USSM[:, oc, c0:c0 + n], [USB], [MIXB], eng="pool")
                              if samp:
                                  pxs = PX[:, :, 0:16 * 19].rearrange("p k (b t) -> p k b t", t=19)
                                  for k_ in range(2):
                                      P.dma("sp", pxs[:, k_, :, 0:15], I["spool"][:, l, k_], 15, writes=[PXB])
                              for k2 in range(2):
                                  col = OFF_POOL + k2 * 128
                                  for kc in range(8):
                                      mm(PS[1][:, :n], WIN[:, kc, col:col + 128], H[:, kc, :n], [WINB, HB], [PSB[1]], start=(kc == 0), stop=(kc == 7))
                                  if not samp:
                                      cp(PX[:, k2, 15:15 + n], PS[1][:, :n], [PSB[1]], [PXB], eng="act")
                                  else:
                                      cp(pxs[:, k2, :, 15:19], PS[1][:, :n].rearrange("p (b t) -> p b t", t=4), [PSB[1]], [PXB], eng="act")
                              yield
                              if samp:
                                  for k_ in range(2):
                                      P.dma("sp", O["o_pool_s_new"][:, l, k_], pxs[:, k_, :, 15:19], 27, reads=[PXB])
                                  P.dma("sp", O["o_pool_s_old"][l], I["spool_nat"][l, :, 4:15, :], 28)
                              elif ti == 7:
                                  P.dma("sp", O["o_pool_p"][:, l], PX[:, :, n:n + 15], 26, reads=[PXB])
                              for k2 in range(2):
                                  if not samp:
                                      W_ = 15 + n
                                      src = PX[:, k2, 0:W_]; a_ = PA[:, 0:W_]; b_ = PBt[:, 0:W_]
                                      sh = lambda v, s_: (v[:, s_:W_], v[:, 0:W_ - s_])
                                      full = lambda v: v
                                  else:
                                      W_ = 19
                                      src = pxs[:, k2]; a_ = PA[:, 0:16 * 19].rearrange("p (b t) -> p b t", t=19); b_ = PBt[:, 0:16 * 19].rearrange("p (b t) -> p b t", t=19)
                                      sh = lambda v, s_: (v[:, :, s_:W_], v[:, :, 0:W_ - s_])
                                      full = lambda v: v
                                  cp(a_, src, [PXB], [PAB], eng="pool")
                                  hi_, lo_ = sh(a_, 1); xh, xl = sh(src, 1)
                                  tt(hi_, xh, xl, ALU.add, [PXB, PAB], [PAB])
                                  cur, oth = a_, b_
                                  for d in range(1, 4):
                                      need_lo = (2 ** d) < (2 if k2 == 0 else 8)
                                      need_hi = (2 ** d) < (4 if k2 == 0 else 16)
                                      if not (need_lo or need_hi):
                                          break
                                      s_ = 2 ** d
                                      cp(oth, cur, [PAB], [PAB], eng="pool")
                                      oh, ol = sh(oth, s_); ch, cl = sh(cur, s_)
                                      if need_lo:
                                          tt(oh[0:64], ch[0:64], cl[0:64], ALU.add, [PAB], [PAB])
                                      if need_hi:
                                          tt(oh[64:128], ch[64:128], cl[64:128], ALU.add, [PAB], [PAB])
                                      cur, oth = oth, cur
                                  yield
                                  if not samp:
                                      wv = cur[:, 15:15 + n]; xv = PX[:, k2, 15:15 + n]; rv = oth[:, 15:15 + n]
                                  else:
                                      wv = cur[:, :, 15:19]; xv = pxs[:, k2, :, 15:19]; rv = oth[:, :, 15:19]
                                  for half in range(2):
                                      wsz = [2, 4, 8, 16][2 * k2 + half]
                                      pr = slice(64 * half, 64 * half + 64)
                                      stt(rv[pr], wv[pr], 1.0 / wsz, xv[pr], ALU.mult, ALU.subtract, [PAB, PXB], [PAB])
                                  if ti == 0:
                                      tt(rv[:, 0:16], wv[:, 0:16], CST[:, 6 + k2, 0:16], ALU.mult, [PAB, CSTB], [PAB])
                                      tt(rv[:, 0:16], rv[:, 0:16], xv[:, 0:16], ALU.subtract, [PAB, PXB], [PAB])
                                  if not samp:
                                      cp(PQ[:, k2, :n], rv, [PAB], [PQB])
                                  else:
                                      cp(PQ[:, k2, :n].rearrange("p (b t) -> p b t", t=4), rv, [PAB], [PQB])
                                  mm(PS[1][:, :n], PWT[:, k2, :], PQ[:, k2, :n], [SMB, PQB], [PSB[1]])
                                  ts(MIX[:, 6 + k2, :n], PS[1][:, :n], PSC[:, l, k2:k2 + 1], None, ALU.mult, None, [PSB[1], PARB], [MIXB])
                                  yield
                              if not samp:
                                  cp(PX[:, :, 0:15], PX[:, :, n:n + 15], [PXB], [PXB], eng="pool")

                          def prep(ci, sl):
                              a0 = ci * C
                              cols = slice(a0, a0 + C)
                              QDs, QKTs, KDs, KCs, WSs, EGLs = QD[sl], QKT[sl], KDt[sl], KC[sl], WS[sl], EGL[sl]
                              for kc in range(8):
                                  mm(PS[2][:C, 0:8], H[:, kc, cols], WIN[:, kc, OFF_A:OFF_A + 8], [HB, WINB], [PSB[2]], start=(kc == 0), stop=(kc == 7))
                              kt_ps = PT[:C, 0:512].rearrange("p (h d) -> p h d", h=4)
                              vt_ps = PT[:C, 512:1024].rearrange("p (h d) -> p h d", h=4)
                              for hh in range(4):
                                  tr(kt_ps[:, hh, :], QKV[:, 4 + hh, cols], IDB, [QKVB, CB16B], [PTB])
                                  tr(vt_ps[:, hh, :], QKV[:, 8 + hh, cols], IDB, [QKVB, CB16B], [PTB])
                              kq = PS[5][:C, :].rearrange("p (a h c) -> p a h c", a=2, h=4)
                              for hh in range(4):
                                  mm(kq[:, 0, hh, :C], QKV[:, 4 + hh, cols], QKV[:, 4 + hh, cols], [QKVB], [PSB[5]])
                                  mm(kq[:, 1, hh, :C], QKV[:, 4 + hh, cols], QKV[:, hh, cols], [QKVB], [PSB[5]])
                              cp(KTK[:C], kt_ps, [PTB], [TKB])
                              cp(VTK[:C], vt_ps, [PTB], [TKB])
                              yield
                              tt(ZT[:C, 0:4], PS[2][:C, 0:4], DNP[:C, l, 1, :], ALU.add, [PSB[2], PARB], [SCB])
                              act(BET[:C], PS[2][:C, 4:8], AF.Exp, [PSB[2]], [SCB], scale=-1.0)
                              act(BET[:C], BET[:C], AF.Ln, [SCB, PARB], [SCB], bias=EPSC[:C, 1:2])
                              act(BET[:C], BET[:C], AF.Exp, [SCB], [SCB], scale=-1.0)
                              act(ZT[:C, 0:4], ZT[:C, 0:4], AF.Exp, [SCB], [SCB])
                              act(ZT[:C, 0:4], ZT[:C, 0:4], AF.Ln, [SCB, PARB], [SCB], bias=EPSC[:C, 1:2])
                              tt(LGt[:C], ZT[:C, 0:4], NEGA[:C], ALU.mult, [SCB, NGB], [SCB])
                              tt(LG[:C, :, :C], LTRI[C][:C, None, :C].broadcast_to([C, 4, C]), LGt[:C, :, None].broadcast_to([C, 4, C]), ALU.mult,
                                 [CSTB, SCB], [LGB])
                              tt(VBt[:C], VTK[:C], BET[:C, :, None].broadcast_to([C, 4, 128]), ALU.mult, [TKB, SCB], [RKB])
                              tt(RHK[:C], KTK[:C], BET[:C, :, None].broadcast_to([C, 4, 128]), ALU.mult, [TKB, SCB], [RKB])
                              yield
                              d2 = PS[3][:C, :].rearrange("p (a h c) -> p a h c", a=2, h=4)
                              for hh in range(4):
                                  mm(d2[:, 0, hh, :C], LG[:C, hh, :C], MLT[C][:C, :C], [LGB, CSTB], [PSB[3]])
                              if C == 64:
                                  mm(d2[:, 1, :, :], MLT[C][:C, :C], LG[:C, :, :], [LGB, CSTB], [PSB[3]])
                              else:
                                  for hh in range(4):
                                      mm(d2[:, 1, hh, :C], MLT[C][:C, :C], LG[:C, hh, :C], [LGB, CSTB], [PSB[3]])
                              mm(PS[2][:C, 8:12], LTRI[C][:C, :C], LGt[:C, :], [CSTB, SCB], [PSB[2]])
                              mm(PS[2][:, 16:20], ONEF[:C, :], LGt[:C, :], [CSTB, SCB], [PSB[2]])
                              eg_ps = PS[4][:, 0:4 * C].rearrange("p (h c) -> p h c", h=4)
                              if C == 64:
                                  mm(eg_ps, ONEF[:C, :], LG[:C, :, :], [CSTB, LGB], [PSB[4]])
                              else:
                                  for hh in range(4):
                                      mm(eg_ps[:, hh, :], ONEF[:C, :], LG[:C, hh, :C], [CSTB, LGB], [PSB[4]])
                              act(E2[:C, :, :, :C], d2[:, :, :, :C], AF.Exp, [PSB[3]], [E2B])
                              act(EGt[:C], PS[2][:C, 8:12], AF.Exp, [PSB[2]], [SCB])
                              act(EGLs[:], PS[2][:, 16:20], AF.Exp, [PSB[2]], [EGLB[sl]])
                              cp(ZT[:C, 4:8], PS[2][:C, 8:12], [PSB[2]], [SCB])
                              tt(EKD[:C], PS[2][:C, 16:20], ZT[:C, 4:8], ALU.subtract, [PSB[2], SCB], [SCB])
                              act(EKD[:C], EKD[:C], AF.Exp, [SCB], [SCB])
                              act(EGB[:, :, :C], eg_ps, AF.Exp, [PSB[4]], [EGBB])
                              yield
                              tt(E2[:C, 0, :, :C], E2[:C, 0, :, :C], MSTR[C][:C, None, :C].broadcast_to([C, 4, C]), ALU.mult, [E2B, CSTB], [E2B])
                              tt(E2[:C, 1, :, :C], E2[:C, 1, :, :C], MINT[C][:C, None, :C].broadcast_to([C, 4, C]), ALU.mult, [E2B, CSTB], [E2B])
                              tt(AF_[:C, :, :C], kq[:, 0, :, :C], E2[:C, 0, :, :C], ALU.mult, [PSB[5], E2B], [AFB])
                              tt(AF_[:C, :, :C], AF_[:C, :, :C], BET[:C, :, None].broadcast_to([C, 4, C]), ALU.mult, [AFB, SCB], [AFB])
                              at_ps = PS[6][:C, 0:4 * C].rearrange("p (h c) -> p h c", h=4)
                              for hh in range(4):
                                  tr(at_ps[:, hh, :], AF_[:C, hh, :C], IDF[:C, :C], [AFB, CSTB], [PSB[6]])
                              tt(QKTs[:C, :, :C], kq[:, 1, :, :C], E2[:C, 1, :, :C], ALU.mult, [PSB[5], E2B], [QKB[sl]])
                              tt(QDs[:, :, :C], QKV[:, 0:4, cols], EGB[:, :, :C], ALU.mult, [QKVB, EGBB], [QDB[sl]])
                              tt(RHK[:C], RHK[:C], EGt[:C, :, None].broadcast_to([C, 4, 128]), ALU.mult, [RKB, SCB], [RKB])
                              tt(KDs[:C], KTK[:C], EKD[:C, :, None].broadcast_to([C, 4, 128]), ALU.mult, [TKB, SCB], [KDB[sl]])
                              cp(PB_[:C, 0, :, :C], AF_[:C, :, :C], [AFB], [PBB], eng="act")
                              cp(PB_[:C, 1, :, :C], at_ps, [PSB[6]], [PBB], eng="act")
                              tt(TTb[:C, :, :C], IDF[:C, None, :C].broadcast_to([C, 4, C]), at_ps, ALU.subtract, [CSTB, PSB[6]], [TTB])
                              yield
                              nlev = 5 if C == 64 else 1
                              for lev in range(nlev):
                                  p2 = PS[3][:C, :].rearrange("p (a h c) -> p a h c", a=2, h=4)
                                  for hh in range(4):
                                      mm(p2[:, 0, hh, :C], PB_[:C, 1, hh, :C], PB_[:C, 0, hh, :C], [PBB], [PSB[3]])
                                      mm(p2[:, 1, hh, :C], PB_[:C, 0, hh, :C], PB_[:C, 1, hh, :C], [PBB], [PSB[3]])
                                  cp(PB_[:C, :, :, :C], p2[:, :, :, :C], [PSB[3]], [PBB], eng="act")
                                  tu = PS[6][:C, 0:4 * C].rearrange("p (h c) -> p h c", h=4)
                                  for hh in range(4):
                                      mm(tu[:, hh, :], PB_[:C, 0, hh, :C], TTb[:C, hh, :C], [PBB, TTB], [PSB[6]])
                                  tt(TTb[:C, :, :C], TTb[:C, :, :C], tu, ALU.add, [TTB, PSB[6]], [TTB])
                                  yield
                              w_ps = PS[3][:C, :].rearrange("p (h d) -> p h d", h=4)
                              kc_ps = PS[4][:, 0:4 * C].rearrange("p (h c) -> p h c", h=4)
                              for hh in range(4):
                                  mm(w_ps[:, hh, :], TTb[:C, hh, :C], VBt[:C, hh, :], [TTB, RKB], [PSB[3]])
                                  mm(kc_ps[:, hh, :], RHK[:C, hh, :], TTb[:C, hh, :C], [TTB, RKB], [PSB[4]])
                              cp(WSs[:C], w_ps, [PSB[3]], [WSB[sl]], eng="act")
                              cp(KCs[:, :, :C], kc_ps, [PSB[4]], [KCB[sl]])

                          def step(ci, sl):
                              a0 = ci * C
                              cols = slice(a0, a0 + C)
                              QDs, QKTs, KDs, KCs, WSs, EGLs = QD[sl], QKT[sl], KDt[sl], KC[sl], WS[sl], EGL[sl]
                              if samp:
                                  P.dma("sp", SST[:], I["sdel"][l, ci], 13, writes=[SSB])
                                  cp(SBF[:], SST[:], [SSB], [SSB])
                              p1 = PS[0][:C, :].rearrange("p (h d) -> p h d", h=4)
                              for hh in range(4):
                                  mm(p1[:, hh, :], KCs[:, hh, :C], SBF[:, hh, :], [KCB[sl], SSB], [PSB[0]])
                              tt(UU[:C], WSs[:C], p1, ALU.subtract, [WSB[sl], PSB[0]], [UUB])
                              yield
                              o_ps = PS[1][:, 0:4 * C].rearrange("p (h c) -> p h c", h=4)
                              for hh in range(4):
                                  mm(o_ps[:, hh, :], SBF[:, hh, :], QDs[:, hh, :C], [SSB, QDB[sl]], [PSB[1]], start=True, stop=False)
                                  mm(o_ps[:, hh, :], UU[:C, hh, :], QKTs[:C, hh, :C], [UUB, QKB[sl]], [PSB[1]], start=False, stop=True)
                              ds = PS[0][:, :].rearrange("p (h d) -> p h d", h=4)
                              for hh in range(4):
                                  mm(ds[:, hh, :], KDs[:C, hh, :], UU[:C, hh, :], [KDB[sl], UUB], [PSB[0]])
                              tt(SST[:], SST[:], EGLs[:, :, None].broadcast_to([128, 4, 128]), ALU.mult, [SSB, EGLB[sl]], [SSB])
                              tt(SST[:], SST[:], ds, ALU.add, [SSB, PSB[0]], [SSB])
                              cp(SBF[:], SST[:], [SSB], [SSB])
                              cp(OO[:, :, cols], o_ps, [PSB[1]], [OOB], eng="act")
                              yield
                              if samp:
                                  P.dma("sp", O["o_delta_s"][l, ci], SST[:], 22, reads=[SSB])
                              elif ti == 7 and ci == nch - 1:
                                  P.dma("sp", O["o_delta_p"][l], SST[:], 22, reads=[SSB])

                          bg = [poolgen()]

                          def drive(gens):
                              gens = [g_ for g_ in gens if g_ is not None]
                              while gens:
                                  for g_ in list(gens):
                                      try:
                                          next(g_)
                                      except StopIteration:
                                          gens.remove(g_)
                                  for g_ in list(bg):
                                      try:
                                          next(g_)
                                      except StopIteration:
                                          bg.remove(g_)
                          drive([prep(0, 0)])
                          for ci in range(nch):
                              nxt = prep(ci + 1, (ci + 1) % 2) if ci + 1 < nch else None
                              drive([nxt, step(ci, ci % 2)])
                          while bg:
                              for g_ in list(bg):
                                  try:
                                      next(g_)
                                  except StopIteration:
                                      bg.remove(g_)
                          if l == 0: MARKS.append(("L0 t%d postdelta" % ti, P.cnt["pe"]))
                          for hh in range(4):
                              act(SQ[:, hh, :n], OO[:, hh, :n], AF.Square, [OOB], [SQB])
                          for hh in range(4):
                              mm(PS[2 + hh][:, :n], ONEB, SQ[:, hh, :n], [SQB, CB16B], [PSB[2 + hh]])
                          for hh in range(4):
                              act(MO[:, hh, :n], PS[2 + hh][:, :n], AF.Ln, [PSB[2 + hh], PARB], [MOB], bias=EPS_AP, scale=1.0 / 128)
                          for hh in range(4):
                              act(MO[:, hh, :n], MO[:, hh, :n], AF.Exp, [MOB], [MOB], scale=-0.5)
                          for hh in range(4):
                              stt(MO[:, 4 + hh, :n], OO[:, hh, :n], ONW[:, l:l + 1], MO[:, hh, :n], ALU.mult, ALU.mult, [OOB, PARB, MOB], [MOB])
                          for hh in range(4):
                              tt(MIX[:, hh, :n], MO[:, 4 + hh, :n], GT[:, hh, :n], ALU.mult, [MOB, GTB], [MIXB])
                          if dbg and l == 0:
                              P.dma("pool", O["dbgmix"][:, :, c0:c0 + n], MIX[:, :, :n], 29, reads=[MIXB])
                              P.dma("pool", O["dbgoo"][:, :, c0:c0 + n], OO[:, :, :n], 30, reads=[OOB])
                          for oc in range(8):
                              bnk = 1 + (oc % 3)
                              for kc in range(8):
                                  mm(PS[bnk][:, :n], WOUT[:, kc, oc * 128:(oc + 1) * 128], MIX[:, kc, :n], [WOUTB, MIXB], [PSB[bnk]], start=(kc == 0), stop=(kc == 7))
                              cp(MO[:, oc, :n], PS[bnk][:, :n], [PSB[bnk]], [MOB], eng=("act" if oc % 2 == 0 else "dve"))
                          rms_rstd(lambda kc: MO[:, kc, :n], n, 8, RS, RSB, [MOB], 1.0 / 1024, SQ, SQB, 0)
                          for kc in range(8):
                              tt(MO[:, kc, :n], MO[:, kc, :n], RS[:, :n], ALU.mult, [MOB, RSB], [MOB])
                              stt(X[:, kc, c0:c0 + n], MO[:, kc, :n], NW[:, l, 1, kc:kc + 1], X[:, kc, c0:c0 + n], ALU.mult, ALU.add,
                                  [MOB, PARB, XB[ti]], [XB[ti]])
                      P.barrier()
            MARKS.append(("L%d FFN" % l, P.cnt["pe"]))
            with contextlib.ExitStack() as sf:
              if ("F", l) not in skip:
                  sbf_ = lambda n, s, d=F32: sf.enter_context(nc.sbuf_tensor("%s_%d" % (n, l), list(s), d))
                  HW_ = 1088
                  H2 = sbf_("H2", [128, 8, HW_], BF16); H2B = Buf()
                  AV = sbf_("AV", [128, 22, HW_], BF16); AVB = Buf()
                  WG = [sbf_("WG%d" % i, [128, 2, 8, 128], BF16) for i in range(2)]; WGB = [Buf() for _ in range(2)]
                  WDA = sbf_("WDA", [128, 22, 1024], BF16); WDAB = Buf()
                  for a_ in range(0, 22, 6):
                      b_ = min(22, a_ + 6)
                      P.dma("pool", WDA[:, a_:b_, :], I["w_down"][l, :, a_:b_, :], 19, writes=[WDAB])
                  SQ2 = sbf_("SQ2", [128, 2, 512], BF16); SQ2B = [Buf(), Buf()]
                  RS2 = sbf_("RS2", [128, 512]); RS2B = Buf()
                  GS = sbf_("GS", [128, 2, 512], BF16); GSB = [Buf(), Buf()]
                  FO = sbf_("FO", [128, 8, 512], BF16); FOB = Buf()
                  unit = [0]
                  for half in range(2):
                      h0 = half * 1024
                      subt = [(0, 512), (512, 512)] if half == 0 else [(1024, 512), (1536, 512), (2048, 64)]
                      xbl = lambda c0, n: [XB[t_] for t_ in range(min(c0 // TW, 8), min((c0 + n - 1) // TW, 8) + 1)]
                      for (c0, n) in subt:
                          xbs = xbl(c0, n)
                          rms_rstd(lambda kc: X[:, kc, c0:c0 + n], n, 8, RS2, RS2B, xbs, 1.0 / 1024, SQ2, SQ2B, 0, ring=2)
                          for kc in range(8):
                              stt(H2[:, kc, c0 - h0:c0 - h0 + n], X[:, kc, c0:c0 + n], NW[:, l, 2, kc:kc + 1], RS2[:, :n], ALU.mult, ALU.mult,
                                  xbs + [RS2B, PARB], [H2B])
                      for fc in range(22):
                          sl = fc % 2
                          P.dma("pool", WG[sl][:], I["w_gu"][l, fc], 16 + sl, writes=[WGB[sl]])
                          for si, (c0, n) in enumerate(subt):
                              r0 = c0 - h0
                              u_ = unit[0] % 2
                              unit[0] += 1
                              bg, bu = 1 + 2 * u_, 2 + 2 * u_
                              for kc in range(8):
                                  mm(PS[bg][:, :n], WG[sl][:, 0, kc, :], H2[:, kc, r0:r0 + n], [WGB[sl], H2B], [PSB[bg]], start=(kc == 0), stop=(kc == 7))
                              for kc in range(8):
                                  mm(PS[bu][:, :n], WG[sl][:, 1, kc, :], H2[:, kc, r0:r0 + n], [WGB[sl], H2B], [PSB[bu]], start=(kc == 0), stop=(kc == 7))
                              act(GS[:, u_, :n], PS[bg][:, :n], AF.Silu, [PSB[bg]], [GSB[u_]])
                              tt(AV[:, fc, r0:r0 + n], GS[:, u_, :n], PS[bu][:, :n], ALU.mult, [GSB[u_], PSB[bu]], [AVB])
                      for si, (c0, n) in enumerate(subt):
                          r0 = c0 - h0
                          xbs = xbl(c0, n)
                          for ogrp in range(2):
                              for fc in range(22):
                                  for o4 in range(4):
                                      oo_ = (ogrp * 4 + o4) * 128
                                      mm(PS[1 + o4][:, :n], WDA[:, fc, oo_:oo_ + 128], AV[:, fc, r0:r0 + n],
                                         [WDAB, AVB], [PSB[1 + o4]], start=(fc == 0), stop=(fc == 21))
                              for o4 in range(4):
                                  cp(FO[:, ogrp * 4 + o4, :n], PS[1 + o4][:, :n], [PSB[1 + o4]], [FOB], eng="act")
                          rms_rstd(lambda kc: FO[:, kc, :n], n, 8, RS2, RS2B, [FOB], 1.0 / 1024, SQ2, SQ2B, 0, ring=2)
                          for kc in range(8):
                              tt(FO[:, kc, :n], FO[:, kc, :n], RS2[:, :n], ALU.mult, [FOB, RS2B], [FOB])
                              stt(X[:, kc, c0:c0 + n], FO[:, kc, :n], NW[:, l, 3, kc:kc + 1], X[:, kc, c0:c0 + n], ALU.mult, ALU.add,
                                  [FOB, PARB] + xbs, xbs)
                  P.barrier()
        for (c0, n, ti) in TILES:
            P.dma("sp", O["yT"][:, :, c0:c0 + n], X[:, :, c0:c0 + n], 1 + ti, reads=[XB[ti]])
        P.barrier()
        P.emit()
    return nc


def _consts():
    c = np.zeros((128, 8, 128), np.float32)
    idx = np.arange(128)
    c[:, 0, :] = np.eye(128, dtype=np.float32)
    c[:, 1, :] = 1.0
    c[:, 2, :] = (idx[:, None] <= idx[None, :])
    c[:, 3, :] = (idx[None, :] < idx[:, None])
    p = idx % 64
    ch = idx // 64
    c[:, 4, :] = ((p[:, None] == p[None, :]) & (ch[:, None] != ch[None, :]))
    c[:, 5, :] = np.where(ch == 0, -1.0, 1.0)[:, None]
    t = np.arange(16)
    for k2 in range(2):
        w = np.where(idx < 64, [2, 8][k2], [4, 16][k2]).astype(np.float32)
        c[:, 6 + k2, 0:16] = 1.0 / np.minimum(t[None, :] + 1.0, w[:, None])
    return c


def _prep_inputs(inp):
    f = lambda a: np.ascontiguousarray(np.asarray(a, dtype=np.float32))
    g = {k: np.asarray(v) for k, v in inp.items()}
    sh = {}
    sh["w_in"] = f(g["w_in"].reshape(L, 8, 128, INW).transpose(0, 2, 1, 3))
    sh["w_out"] = f(g["w_out"].reshape(L, 8, 128, 1024).transpose(0, 2, 1, 3))
    wg = g["ffn_w_gate"].reshape(L, 8, 128, 22, 128).transpose(0, 3, 2, 1, 4)
    wu = g["ffn_w_up"].reshape(L, 8, 128, 22, 128).transpose(0, 3, 2, 1, 4)
    sh["w_gu"] = f(np.stack([wg, wu], 3))
    sh["w_down"] = f(g["ffn_w_down"].reshape(L, 22, 128, 1024).transpose(0, 2, 1, 3))
    nw = np.stack([g["norm_mix_pre"], g["norm_mix_post"], g["norm_ffn_pre"], g["norm_ffn_post"]], 0)
    sh["nw"] = f(nw.reshape(4, L, 8, 128).transpose(3, 1, 0, 2))
    sh["cw"] = f(g["conv_w"].reshape(L, 4, 12, 128).transpose(3, 0, 2, 1))
    dnp = np.stack([g["dn_a_log"], g["dn_dt_bias"]], 1)
    sh["dnp"] = f(np.broadcast_to(dnp[None], (64, L, 2, 4)))
    sh["onw"] = f(g["dn_out_norm"].T)
    lam = np.stack([g["ssm_a_re"], g["ssm_a_im"], np.broadcast_to(g["ssm_log_dt"][:, :, None], (L, 16, 64))], -1)
    lam = lam.transpose(2, 0, 1, 3)
    sh["lamp"] = f(np.concatenate([lam, lam], 0))
    bre = g["ssm_b_re"].transpose(2, 0, 1, 3)
    bim = g["ssm_b_im"].transpose(2, 0, 1, 3)
    sh["bnat"] = f(np.concatenate([np.stack([bre, bim], 3), np.stack([bim, bre], 3)], 0))
    cre = g["ssm_c_re"].transpose(3, 0, 1, 2)
    cim = g["ssm_c_im"].transpose(3, 0, 1, 2)
    sh["cnat"] = f(np.concatenate([cre, cim], 0))
    sh["ssmd"] = f(g["ssm_d"].reshape(L, 2, 128).transpose(2, 0, 1))
    sh["glub"] = f(g["ssm_glu_b"].reshape(L, 2, 128).transpose(2, 0, 1))
    sh["pscale"] = f(g["pool_scale"].reshape(L, 2, 128).transpose(2, 0, 1))
    sh["gluw"] = f(g["ssm_glu_w"].reshape(L, 2, 128, 256).transpose(0, 2, 1, 3))
    pw = np.zeros((L, 128, 2, 128), np.float32)
    for k in range(2):
        for g2 in range(2):
            pw[:, g2 * 64:(g2 + 1) * 64, k, g2 * 64:(g2 + 1) * 64] = g["pool_w"][:, 2 * k + g2]
    sh["pw"] = pw
    sh["cst"] = _consts()
    maps = []
    for c in range(8):
        b0, b1 = 16 * c, 16 * c + 16
        m = dict(sh)
        xt = np.concatenate([g["x_prompt"][c], g["x_sample"][b0:b1].reshape(NS, 1024)], 0)
        m["xT"] = f(xt.T.reshape(8, 128, NT).transpose(1, 0, 2))
        m["sdel"] = f(g["state_delta"][:, b0:b1].transpose(0, 1, 3, 2, 4))
        m["sconv"] = f(g["state_conv"][:, b0:b1].reshape(L, 16, 3, 12, 128).transpose(4, 0, 3, 1, 2))
        m["spool"] = f(g["state_pool"][:, b0:b1].reshape(L, 16, 15, 2, 128).transpose(4, 0, 3, 1, 2))
        m["spool_nat"] = f(g["state_pool"][:, b0:b1])
        sre = g["state_ssm_re"][:, b0:b1].transpose(3, 0, 2, 1)
        sim = g["state_ssm_im"][:, b0:b1].transpose(3, 0, 2, 1)
        m["sssm"] = f(np.concatenate([sre, sim], 0))
        maps.append(m)
    return maps


def _assemble(res):
    yp = np.zeros((8, NP_, 1024), np.float32); ys = np.zeros((128, 4, 1024), np.float32)
    dp = np.zeros((L, 8, 4, 128, 128), np.float32); ds = np.zeros((L, 128, 4, 128, 128), np.float32)
    cvp = np.zeros((L, 8, 3, 1536), np.float32); cvs = np.zeros((L, 128, 3, 1536), np.float32)
    rp = np.zeros((L, 8, 16, 64), np.float32); ip = np.zeros((L, 8, 16, 64), np.float32)
    rs = np.zeros((L, 128, 16, 64), np.float32); is_ = np.zeros((L, 128, 16, 64), np.float32)
    pp = np.zeros((L, 8, 15, 256), np.float32); ps = np.zeros((L, 128, 15, 256), np.float32)
    for c in range(8):
        r = {k: np.asarray(v) for k, v in res[c].items()}
        b0, b1 = 16 * c, 16 * c + 16
        y = r["yT"].transpose(1, 0, 2).reshape(1024, NT).T
        yp[c] = y[:NP_]
        ys[b0:b1] = y[NP_:].reshape(16, 4, 1024)
        dp[:, c] = r["o_delta_p"].transpose(0, 2, 1, 3)
        ds[:, b0:b1] = r["o_delta_s"].transpose(0, 1, 3, 2, 4)
        cvp[:, c] = r["o_conv_p"].transpose(1, 3, 2, 0).reshape(L, 3, 1536)
        cvs[:, b0:b1] = r["o_conv_s"].transpose(1, 3, 4, 2, 0).reshape(L, 16, 3, 1536)
        sp_ = r["o_ssm_p"]
        rp[:, c] = sp_[:64].transpose(1, 2, 0)
        ip[:, c] = sp_[64:].transpose(1, 2, 0)
        ss = r["o_ssm_s"]
        rs[:, b0:b1] = ss[:64].transpose(1, 2, 3, 0)
        is_[:, b0:b1] = ss[64:].transpose(1, 2, 3, 0)
        pp[:, c] = r["o_pool_p"].transpose(1, 3, 2, 0).reshape(L, 15, 256)
        ps[:, b0:b1, 0:11] = r["o_pool_s_old"]
        ps[:, b0:b1, 11:15] = r["o_pool_s_new"].transpose(1, 3, 4, 2, 0).reshape(L, 16, 4, 256)
    return (yp, ys, dp, cvp, rp, ip, pp, ds, cvs, rs, is_, ps)


_NC_CACHE = {}


def kernel(**inputs):
    if "nc" not in _NC_CACHE:
        _NC_CACHE["nc"] = build_program()
    maps = _prep_inputs(inputs)
    res = run_bass_kernel_spmd(_NC_CACHE["nc"], maps, core_ids=list(range(8)))
    return _assemble(res.results)
```

```python
import contextlib
import math
import numpy as np
import concourse.bass as bass
import concourse.mybir as mybir
from concourse.bass_utils import run_bass_kernel_spmd

F32 = mybir.dt.float32
BF16 = mybir.dt.bfloat16
I32 = mybir.dt.int32
ALU = mybir.AluOpType
AF = mybir.ActivationFunctionType

L = 4
NP_, NS, NT = 2048, 64, 2112
TW = 256
EPS = 1e-6
INW = 2568
OFF_A, OFF_G, OFF_SSM, OFF_POOL = 1536, 1544, 2056, 2312
DFF = 2816
NKS = 11


class Buf:
    __slots__ = ("w", "r")

    def __init__(self):
        self.w = None
        self.r = []


class Prog:
    ENGS = ("pe", "act", "dve", "pool", "sp")

    def __init__(self, nc, stack, n_dma_sems=40):
        self.nc = nc
        self.sem = {e: stack.enter_context(nc.semaphore("s_" + e)) for e in self.ENGS}
        self.cnt = {e: 0 for e in self.ENGS}
        self.seen = {e: {} for e in self.ENGS}
        self.stream = {e: [] for e in self.ENGS}
        self.dsem = [stack.enter_context(nc.semaphore("d%d" % i)) for i in range(n_dma_sems)]
        self.dcnt = [0] * n_dma_sems

    def _need(self, eng, dep):
        if dep is None:
            return
        if dep[0] == "dma":
            key, val, sem = ("d", dep[1]), dep[2], self.dsem[dep[1]]
        else:
            e2, idx = dep
            if e2 == eng and (eng == "pe" or idx <= self.cnt[eng] - 2):
                return
            key, val, sem = ("e", e2), idx, self.sem[e2]
        if self.seen[eng].get(key, 0) >= val:
            return
        self.seen[eng][key] = val
        self.stream[eng].append(("wait", sem, val))

    def _deps(self, eng, reads, writes):
        deps = []
        for b in reads:
            deps.append(b.w)
        for b in writes:
            deps.append(b.w)
            deps.extend(b.r)
        best = {}
        for d in deps:
            if d is None:
                continue
            key = (d[0], d[1]) if d[0] == "dma" else ("e", d[0])
            val = d[2] if d[0] == "dma" else d[1]
            if key not in best or val > best[key][0]:
                best[key] = (val, d)
        for key in best:
            self._need(eng, best[key][1])

    def _mark(self, me, reads, writes):
        for b in reads:
            b.r.append(me)
            if len(b.r) > 64:
                b.r = b.r[-48:]
        for b in writes:
            b.w = me
            b.r = []

    def op(self, eng, fn, reads=(), writes=()):
        self._deps(eng, reads, writes)
        self.cnt[eng] += 1
        self.stream[eng].append(("inst", fn, self.sem[eng], 1))
        self._mark((eng, self.cnt[eng]), reads, writes)

    def dma(self, q, out, in_, semi, reads=(), writes=(), **kw):
        self._deps(q, reads, writes)
        self.dcnt[semi] += 16
        self.stream[q].append(("inst", lambda h: h.dma_start(out=out, in_=in_, **kw), self.dsem[semi], 16))
        self._mark(("dma", semi, self.dcnt[semi]), reads, writes)

    def barrier(self):
        for e in self.ENGS:
            for e2 in self.ENGS:
                if e2 != e and self.cnt[e2]:
                    self._need(e, (e2, self.cnt[e2]))
            for i, c in enumerate(self.dcnt):
                if c:
                    self._need(e, ("dma", i, c))

    def emit(self):
        nc = self.nc
        with nc.Block() as block:
            def run(e):
                def body(h):
                    for it in self.stream[e]:
                        if it[0] == "wait":
                            h.wait_ge(it[1], it[2])
                        else:
                            it[1](h).then_inc(it[2], it[3])
                return body
            block.tensor(run("pe"))
            block.scalar(run("act"))
            block.vector(run("dve"))
            block.gpsimd(run("pool"))
            block.sync(run("sp"))


MARKS = []


def build_program(NL=L, dbg=None, skip=()):
    nc = bass.Bass("TRN2", target_bir_lowering=False)
    din = lambda n, s, d=F32: nc.dram_tensor(n, list(s), d, kind="ExternalInput").ap()
    dout = lambda n, s, d=F32: nc.dram_tensor(n, list(s), d, kind="ExternalOutput").ap()
    I = {}
    I["xT"] = din("xT", [128, 8, NT])
    I["w_in"] = din("w_in", [L, 128, 8, INW])
    I["w_out"] = din("w_out", [L, 128, 8, 1024])
    I["w_gu"] = din("w_gu", [L, 22, 128, 2, 8, 128])
    I["w_down"] = din("w_down", [L, 128, 22, 1024])
    I["nw"] = din("nw", [128, L, 4, 8])
    I["cw"] = din("cw", [128, L, 12, 4])
    I["dnp"] = din("dnp", [64, L, 2, 4])
    I["onw"] = din("onw", [128, L])
    I["sdel"] = din("sdel", [L, 16, 128, 4, 128])
    I["sconv"] = din("sconv", [128, L, 12, 16, 3])
    I["spool"] = din("spool", [128, L, 2, 16, 15])
    I["spool_nat"] = din("spool_nat", [L, 16, 15, 256])
    I["lamp"] = din("lamp", [128, L, 16, 3])
    I["bnat"] = din("bnat", [128, L, 16, 2, 16])
    I["cnat"] = din("cnat", [128, L, 16, 16])
    I["sssm"] = din("sssm", [128, L, 16, 16])
    I["ssmd"] = din("ssmd", [128, L, 2])
    I["glub"] = din("glub", [128, L, 2])
    I["gluw"] = din("gluw", [L, 128, 2, 256])
    I["pw"] = din("pw", [L, 128, 2, 128])
    I["pscale"] = din("pscale", [128, L, 2])
    I["cst"] = din("cst", [128, 8, 128])
    O = {}
    O["yT"] = dout("yT", [128, 8, NT])
    O["o_delta_p"] = dout("o_delta_p", [L, 128, 4, 128])
    O["o_delta_s"] = dout("o_delta_s", [L, 16, 128, 4, 128])
    O["o_conv_p"] = dout("o_conv_p", [128, L, 12, 3])
    O["o_conv_s"] = dout("o_conv_s", [128, L, 12, 16, 3])
    O["o_ssm_p"] = dout("o_ssm_p", [128, L, 16])
    O["o_ssm_s"] = dout("o_ssm_s", [128, L, 16, 16])
    O["o_pool_p"] = dout("o_pool_p", [128, L, 2, 15])
    O["o_pool_s_old"] = dout("o_pool_s_old", [L, 16, 11, 256])
    O["o_pool_s_new"] = dout("o_pool_s_new", [128, L, 2, 16, 4])
    if dbg:
        O["dbgmix"] = dout("dbgmix", [128, 8, NT])
        O["dbgoo"] = dout("dbgoo", [128, 4, NT])

    with contextlib.ExitStack() as st:
        P = Prog(nc, st)
        sb = lambda n, s, d=F32: st.enter_context(nc.sbuf_tensor(n, list(s), d))
        PS = [st.enter_context(nc.psum_tensor("ps%d" % i, [128, 512], F32)) for i in range(7)]
        PSB = [Buf() for _ in range(7)]
        PT = st.enter_context(nc.psum_tensor("pt", [128, 1024], BF16))
        PTB = Buf()
        X = sb("X", [128, 8, NT]); XB = [Buf() for _ in range(9)]
        CST = sb("CST", [128, 8, 128]); CSTB = Buf()
        CB16 = sb("CB16", [128, 2, 128], BF16)
        NW = sb("NW", [128, L, 4, 8]); CW = sb("CW", [128, L, 12, 4]); DNP = sb("DNP", [64, L, 2, 4])
        ONW = sb("ONW", [128, L]); PSC = sb("PSC", [128, L, 2]); SSMD = sb("SSMD", [128, L, 2]); GLUB = sb("GLUB", [128, L, 2])
        PARB = Buf()
        EPSC = sb("EPSC", [128, 3])

        dmai = [0]

        def nsem(lo=22, hi=40):
            dmai[0] = (dmai[0] + 1) % (hi - lo)
            return lo + dmai[0]

        def mm(out, lhsT, rhs, rd, wr, start=True, stop=True):
            P.op("pe", lambda h: h.matmul(out, lhsT=lhsT, rhs=rhs, start=start, stop=stop), reads=rd, writes=wr)

        def tr(out, in_, ident, rd, wr):
            P.op("pe", lambda h: h.transpose(out, in_, ident), reads=rd, writes=wr)

        def act(out, in_, func, rd, wr, bias=0.0, scale=1.0, eng="act"):
            P.op("act", lambda h: h.activation(out, in_, func, bias=bias, scale=scale), reads=rd, writes=wr)

        def tt(out, a, b, op, rd, wr, eng="dve"):
            P.op(eng, lambda h: h.tensor_tensor(out, a, b, op), reads=rd, writes=wr)

        def ts(out, a, s1, s2, op0, op1, rd, wr, eng="dve"):
            if op1 is None:
                P.op(eng, lambda h: h.tensor_scalar(out, a, s1, None, op0), reads=rd, writes=wr)
            else:
                P.op(eng, lambda h: h.tensor_scalar(out, a, s1, s2, op0, op1), reads=rd, writes=wr)

        def stt(out, a, s, b, op0, op1, rd, wr, eng="dve"):
            P.op(eng, lambda h: h.scalar_tensor_tensor(out, a, s, b, op0, op1), reads=rd, writes=wr)

        def cp(out, in_, rd, wr, eng="dve"):
            if eng == "act":
                P.op("act", lambda h: h.copy(out, in_), reads=rd, writes=wr)
            else:
                P.op(eng, lambda h: h.tensor_copy(out, in_), reads=rd, writes=wr)

        def rcp(out, in_, rd, wr):
            P.op("dve", lambda h: h.reciprocal(out, in_), reads=rd, writes=wr)

        def mset(ap, v, wr, eng="pool"):
            P.op(eng, lambda h: h.memset(ap, v), writes=wr)

        P.dma("sp", CST[:], I["cst"][:], 20, writes=[CSTB])
        for t_, k_ in ((NW, "nw"), (CW, "cw"), (DNP, "dnp"), (ONW, "onw"), (PSC, "pscale"), (SSMD, "ssmd"), (GLUB, "glub")):
            P.dma("sp", t_[:], I[k_][:], 0, writes=[PARB])
        IDF = CST[:, 0, :]
        ONEF = CST[:, 1, :]
        LTRI = {64: CST[:, 2, :], 4: CST[:, 2, :]}
        MLT = {64: CST[:, 3, :], 4: CST[:, 3, :]}
        MSTR = {64: CST[:, 3, :], 4: CST[:, 3, :]}
        MINT = {64: CST[:, 2, :], 4: CST[:, 2, :]}
        SWP = CST[:, 4, :]
        SGN = CST[:, 5, 0:1]
        IDB = CB16[:, 0, :]
        ONEB = CB16[:, 1, :]
        CB16B = Buf()
        cp(CB16[:, 0, :], CST[:, 0, :], [CSTB], [CB16B])
        cp(CB16[:, 1, :], CST[:, 1, :], [CSTB], [CB16B])
        mset(EPSC[:, 0:1], EPS, [PARB], eng="dve")
        mset(EPSC[:, 1:2], 1.0, [PARB], eng="dve")
        mset(EPSC[:, 2:3], math.log(128.0 ** -0.5), [PARB], eng="dve")
        EPS_AP = EPSC[:, 0:1]
        LNQ_AP = EPSC[:, 2:3]

        TILES = [(i * TW, TW, i) for i in range(8)] + [(NP_, NS, 8)]
        for (c0, n, ti) in TILES:
            P.dma("sp", X[:, :, c0:c0 + n], I["xT"][:, :, c0:c0 + n], 1 + ti, writes=[XB[ti]])

        def rms_rstd(src_sq_fn, n, nk, rs_out, rsB, rd, inv_d, sq, sqB, bank, ring=None):
            if ring:
                for kc in range(nk):
                    act(sq[:, kc % ring, :n], src_sq_fn(kc), AF.Square, rd, [sqB[kc % ring]])
                    mm(PS[bank][:, :n], ONEB, sq[:, kc % ring, :n], [sqB[kc % ring], CB16B], [PSB[bank]], start=(kc == 0), stop=(kc == nk - 1))
            else:
                for kc in range(nk):
                    act(sq[:, kc, :n], src_sq_fn(kc), AF.Square, rd, [sqB])
                for kc in range(nk):
                    mm(PS[bank][:, :n], ONEB, sq[:, kc, :n], [sqB, CB16B], [PSB[bank]], start=(kc == 0), stop=(kc == nk - 1))
            act(rs_out[:, :n], PS[bank][:, :n], AF.Ln, [PSB[bank], PARB], [rsB], bias=EPS_AP, scale=inv_d)
            act(rs_out[:, :n], rs_out[:, :n], AF.Exp, [rsB], [rsB], scale=-0.5)

        for l in range(NL):
            last = (l == NL - 1)
            MARKS.append(("L%d start" % l, P.cnt["pe"]))
            with contextlib.ExitStack() as sa:
              if ("A", l) not in skip:
                  sba = lambda n, s, d=F32: sa.enter_context(nc.sbuf_tensor("%s_%d" % (n, l), list(s), d))
                  WINB = Buf(); WOUTB = Buf()
                  GLUW = sba("GLUW", [128, 2, 256], BF16); PWT = sba("PWT", [128, 2, 128], BF16); SMB = Buf()
                  USSM = sba("USSM", [128, 2, NT], BF16); USB = Buf()
                  H = sba("H", [128, 8, TW], BF16); HB = Buf()
                  SQ = sba("SQ", [128, 8, TW], BF16); SQB = Buf()
                  RS = sba("RS", [128, TW]); RSB = Buf()
                  P.dma("pool", GLUW[:], I["gluw"][l], 10, writes=[SMB])
                  P.dma("pool", PWT[:], I["pw"][l], 10, writes=[SMB])

                  def norm_tile(c0, n, ti, which, out, outB, rs=None, rsB=None, sq=None, sqB=None, bank=0):
                      rs = RS if rs is None else rs
                      rsB = RSB if rsB is None else rsB
                      sq = SQ if sq is None else sq
                      sqB = SQB if sqB is None else sqB
                      rms_rstd(lambda kc: X[:, kc, c0:c0 + n], n, 8, rs, rsB, [XB[ti]], 1.0 / 1024, sq, sqB, bank)
                      for kc in range(8):
                          stt(out[:, kc, :n], X[:, kc, c0:c0 + n], NW[:, l, which, kc:kc + 1], rs[:, :n], ALU.mult, ALU.mult,
                              [XB[ti], rsB, PARB], [outB])

                  s1 = contextlib.ExitStack()
                  WSSM = s1.enter_context(nc.sbuf_tensor("WSSM_%d" % l, [128, 8, 256], BF16)); WSB = Buf()
                  P.dma("pool", WSSM[:], I["w_in"][l, :, :, OFF_SSM:OFF_POOL], 31, writes=[WSB])
                  HA = s1.enter_context(nc.sbuf_tensor("HA_%d" % l, [128, 8, TW], BF16)); HAB = Buf()
                  SQA = s1.enter_context(nc.sbuf_tensor("SQA_%d" % l, [128, 8, TW], BF16)); SQAB = Buf()
                  RSA = s1.enter_context(nc.sbuf_tensor("RSA_%d" % l, [128, TW], F32)); RSAB = Buf()
                  for (c0, n, ti) in TILES:
                      if ti % 2 == 0:
                          h_, hb_ = H, HB
                          norm_tile(c0, n, ti, 0, H, HB)
                      else:
                          h_, hb_ = HA, HAB
                          norm_tile(c0, n, ti, 0, HA, HAB, rs=RSA, rsB=RSAB, sq=SQA, sqB=SQAB, bank=3)
                      pb = 1 if ti % 2 == 0 else 4
                      for oc in range(2):
                          for kc in range(8):
                              mm(PS[pb + oc][:, :n], WSSM[:, kc, oc * 128:(oc + 1) * 128], h_[:, kc, :n],
                                 [WSB, hb_], [PSB[pb + oc]], start=(kc == 0), stop=(kc == 7))
                          cp(USSM[:, oc, c0:c0 + n], PS[pb + oc][:, :n], [PSB[pb + oc]], [USB], eng=("act" if oc == 0 else "dve"))
                  P.barrier()
                  s1.close()

                  MARKS.append(("L%d A2 s5" % l, P.cnt["pe"]))
                  with contextlib.ExitStack() as s5:
                      sb5 = lambda n, s, d=F32: s5.enter_context(nc.sbuf_tensor("%s_%d" % (n, l), list(s), d))
                      YS = sb5("YS", [128, 2, NT], BF16); YSB = Buf()
                      LAMP = sb5("LAMP", [128, 16, 3]); BNAT = sb5("BNAT", [128, 16, 2, 16]); CNAT = sb5("CNAT", [128, 16, 16])
                      S0 = sb5("S0", [128, 16, 16]); LB = Buf()
                      P.dma("sp", LAMP[:], I["lamp"][:, l], 12, writes=[LB])
                      P.dma("sp", BNAT[:], I["bnat"][:, l], 12, writes=[LB])
                      P.dma("sp", CNAT[:], I["cnat"][:, l], 12, writes=[LB])
                      P.dma("sp", S0[:], I["sssm"][:, l], 12, writes=[LB])
                      DT = sb5("DT", [128, 16]); ZR = sb5("ZR", [128, 16]); FR = sb5("FR", [128, 16]); FI = sb5("FI", [128, 16, 2], I32)
                      T1 = sb5("T1", [128, 16]); T2 = sb5("T2", [128, 16]); T3 = sb5("T3", [128, 16])
                      AK = sb5("AK", [128, 16, NKS]); BK = sb5("BK", [128, 16, NKS]); TB = Buf()
                      act(DT[:], LAMP[:, :, 2], AF.Exp, [LB], [TB])
                      tt(ZR[:], LAMP[:, :, 0], DT[:], ALU.mult, [LB, TB], [TB])
                      stt(FR[:], LAMP[:, :, 1], 1.0 / (2 * math.pi), DT[:], ALU.mult, ALU.mult, [LB, TB], [TB])

                      def reduce_turns(dst, src):
                          cp(FI[:, :, 0], src, [TB], [TB])
                          cp(T3[:], FI[:, :, 0], [TB], [TB])
                          tt(dst, src, T3[:], ALU.subtract, [TB], [TB])

                      reduce_turns(FR[:], FR[:])
                      for k in range(NKS):
                          act(T1[:], ZR[:], AF.Exp, [TB], [TB], scale=float(2 ** k))
                          act(T2[:], FR[:], AF.Sin, [TB], [TB], scale=2 * math.pi)
                          tt(BK[:, :, k], T1[:], T2[:], ALU.mult, [TB], [TB])
                          ts(T2[:], FR[:], 0.25, None, ALU.add, None, [TB], [TB])
                          reduce_turns(T2[:], T2[:])
                          act(T2[:], T2[:], AF.Sin, [TB], [TB], scale=2 * math.pi)
                          tt(AK[:, :, k], T1[:], T2[:], ALU.mult, [TB], [TB])
                          if k < NKS - 1:
                              ts(FR[:], FR[:], 2.0, None, ALU.mult, None, [TB], [TB])
                              reduce_turns(FR[:], FR[:])
                      FRE = sb5("FRE", [128, 16]); FIM = sb5("FIM", [128, 16]); DEN = sb5("DEN", [128, 16]); AM1 = sb5("AM1", [128, 16])
                      are, aim = LAMP[:, :, 0], LAMP[:, :, 1]
                      tt(DEN[:], are, are, ALU.mult, [LB], [TB]); tt(T1[:], aim, aim, ALU.mult, [LB], [TB])
                      tt(DEN[:], DEN[:], T1[:], ALU.add, [TB], [TB]); rcp(DEN[:], DEN[:], [TB], [TB])
                      ts(AM1[:], AK[:, :, 0], -1.0, None, ALU.add, None, [TB], [TB])
                      tt(T1[:], AM1[:], are, ALU.mult, [TB, LB], [TB]); tt(T2[:], BK[:, :, 0], aim, ALU.mult, [TB, LB], [TB])
                      tt(T1[:], T1[:], T2[:], ALU.add, [TB], [TB]); tt(FRE[:], T1[:], DEN[:], ALU.mult, [TB], [TB])
                      tt(T1[:], BK[:, :, 0], are, ALU.mult, [TB, LB], [TB]); tt(T2[:], AM1[:], aim, ALU.mult, [TB, LB], [TB])
                      tt(T1[:], T1[:], T2[:], ALU.subtract, [TB], [TB]); tt(FIM[:], T1[:], DEN[:], ALU.mult, [TB], [TB])
                      ts(FIM[:], FIM[:], SGN, None, ALU.mult, None, [TB, CSTB], [TB])
                      BB = sb5("BB", [128, 16, 16]); BB2 = sb5("BB2", [128, 16, 16])
                      tt(BB[:], BNAT[:, :, 0, :], FRE[:, :, None].broadcast_to([128, 16, 16]), ALU.mult, [LB, TB], [TB])
                      tt(BB2[:], BNAT[:, :, 1, :], FIM[:, :, None].broadcast_to([128, 16, 16]), ALU.mult, [LB, TB], [TB])
                      tt(BB[:], BB[:], BB2[:], ALU.add, [TB], [TB])
                      CT = sb5("CT", [128, 16, 16])
                      ts(CT[:], CNAT[:], SGN, -1.0, ALU.mult, ALU.mult, [LB, CSTB], [TB])
                      BPAD = sb5("BPAD", [128, 128]); BPB = Buf()
                      BT = sb5("BT", [128, 8, 128], BF16); BTB = [Buf() for _ in range(8)]
                      CPAD = sb5("CPAD", [128, 8, 128], BF16); CPB = Buf()
                      RT = sb5("RT", [128, 8, NKS, 1, 128], BF16); RTB = [Buf() for _ in range(8)]
                      RFA = [sb5("RFA%d" % i, [128, NKS, 128]) for i in range(2)]; RFBt = [sb5("RFBt%d" % i, [128, NKS, 128]) for i in range(2)]
                      RFB = [Buf(), Buf()]
                      XE = sb5("XE", [128, 8, 1 + NP_ + 80], BF16); XEB = [Buf() for _ in range(8)]
                      YF2 = sb5("YF2", [128, 2, TW]); YFB2 = [Buf(), Buf()]
                      GE2 = sb5("GE2", [128, 2, TW]); GEB2 = [Buf(), Buf()]
                      SF = sb5("SF", [128, 17, 16]); SFB = Buf()
                      BS = sb5("BS", [128, 16, NKS]);
                      ts(BS[:], BK[:], SGN, -1.0, ALU.mult, ALU.mult, [TB, CSTB], [TB])
                      mset(BPAD[:], 0.0, [BPB]); mset(CPAD[:], 0.0, [CPB])
                      NE = 1 + NP_
                      for oc in range(2):
                          xesv = [XE[:, gi, NE:NE + 80].rearrange("p (b t) -> p b t", t=5) for gi in range(8)]
                          for gi in range(8):
                              g = oc * 8 + gi
                              cp(BPAD[:, gi * 16:(gi + 1) * 16], BB[:, g, :], [TB], [BPB])
                              tr(PS[2][:, 0:128], BPAD[:], IDF, [BPB, CSTB], [PSB[2]])
                              cp(BT[:, gi, :], PS[2][:, 0:128], [PSB[2]], [BTB[gi]])
                              mset(BPAD[:, gi * 16:(gi + 1) * 16], 0.0, [BPB])
                              cp(CPAD[:, gi, gi * 16:(gi + 1) * 16], CT[:, g, :], [TB], [CPB])
                              e_ = "dve" if (gi % 2 == 0) else "pool"
                              r1, r2, rb = RFA[gi % 2], RFBt[gi % 2], RFB[gi % 2]
                              tt(r1[:], SWP[:, None, :].broadcast_to([128, NKS, 128]), BS[:, g, :, None].broadcast_to([128, NKS, 128]), ALU.mult,
                                 [CSTB, TB], [rb], eng=e_)
                              tt(r2[:], IDF[:, None, :].broadcast_to([128, NKS, 128]), AK[:, g, :, None].broadcast_to([128, NKS, 128]), ALU.mult,
                                 [CSTB, TB], [rb], eng=e_)
                              tt(RT[:, gi, :, 0, :], r1[:], r2[:], ALU.add, [rb], [RTB[gi]], eng=e_)
                              mset(XE[:, gi, 0:1], 0.0, [XEB[gi]])
                              cp(xesv[gi][:, :, 0], S0[:, g, :], [LB], [XEB[gi]])
                          bk = [0]

                          def nbank():
                              bk[0] = (bk[0] + 1) % 6
                              return 1 + bk[0]
                          for (c0, n, ti) in TILES:
                              for gi in range(8):
                                  bnk = nbank()
                                  mm(PS[bnk][:, :n], BT[:, gi, :], USSM[:, oc, c0:c0 + n], [BTB[gi], USB], [PSB[bnk]])
                                  if ti < 8:
                                      cp(XE[:, gi, 1 + c0:1 + c0 + n], PS[bnk][:, :n], [PSB[bnk]], [XEB[gi]], eng="act")
                                  else:
                                      cp(xesv[gi][:, :, 1:5], PS[bnk][:, :n].rearrange("p (b t) -> p b t", t=4), [PSB[bnk]], [XEB[gi]], eng="act")
                          for k in range(NKS):
                              s = 2 ** k
                              hi = NE
                              while hi > s:
                                  lo = max(s, hi - 512)
                                  w = hi - lo
                                  for gi in range(8):
                                      bnk = nbank()
                                      if gi < 5:
                                          mm(PS[bnk][:, :w], RT[:, gi, k, 0, :], XE[:, gi, lo - s:hi - s], [RTB[gi], XEB[gi]], [PSB[bnk]], start=True, stop=False)
                                          mm(PS[bnk][:, :w], IDB, XE[:, gi, lo:hi], [CB16B, XEB[gi]], [PSB[bnk]], start=False, stop=True)
                                          cp(XE[:, gi, lo:hi], PS[bnk][:, :w], [PSB[bnk]], [XEB[gi]], eng=("act" if gi != 3 else "dve"))
                                      else:
                                          mm(PS[bnk][:, :w], RT[:, gi, k, 0, :], XE[:, gi, lo - s:hi - s], [RTB[gi], XEB[gi]], [PSB[bnk]])
                                          tt(XE[:, gi, lo:hi], XE[:, gi, lo:hi], PS[bnk][:, :w], ALU.add, [XEB[gi], PSB[bnk]], [XEB[gi]])
                                  hi = lo
                              if s < 5:
                                  w = 5 - s
                                  for gi in range(8):
                                      bnk = nbank()
                                      pso = PS[bnk][:, :16 * w].rearrange("p (b t) -> p b t", t=w)
                                      mm(pso, RT[:, gi, k, 0, :], xesv[gi][:, :, 0:w], [RTB[gi], XEB[gi]], [PSB[bnk]])
                                      tt(xesv[gi][:, :, s:5], xesv[gi][:, :, s:5], pso, ALU.add, [XEB[gi], PSB[bnk]], [XEB[gi]])
                          for gi in range(8):
                              g = oc * 8 + gi
                              cp(SF[:, 0, g:g + 1], XE[:, gi, NE - 1:NE], [XEB[gi]], [SFB])
                              cp(SF[:, 1:17, g], xesv[gi][:, :, 4], [XEB[gi]], [SFB])
                          if l == 0: MARKS.append(("L0 s5 y oc%d" % oc, P.cnt["pe"]))
                          for (c0, n, ti) in TILES:
                              pz = ti % 2
                              yb = 1 + pz
                              YF = YF2[:, pz, :]; YFB = YFB2[pz]
                              GE = GE2[:, pz, :]; GEB = GEB2[pz]
                              for gi in range(8):
                                  if ti < 8:
                                      rhs = XE[:, gi, 1 + c0:1 + c0 + n]
                                      mm(PS[yb][:, :n], CPAD[:, gi, :], rhs, [CPB, XEB[gi]], [PSB[yb]], start=(gi == 0), stop=(gi == 7))
                                  else:
                                      rhs = XE[:, gi, NE:NE + 80].rearrange("p (b t) -> p b t", t=5)[:, :, 1:5]
                                      mm(PS[yb][:, :n].rearrange("p (b t) -> p b t", t=4), CPAD[:, gi, :], rhs, [CPB, XEB[gi]], [PSB[yb]],
                                         start=(gi == 0), stop=(gi == 7))
                              stt(YF[:, :n], USSM[:, oc, c0:c0 + n], SSMD[:, l, oc:oc + 1], PS[yb][:, :n], ALU.mult, ALU.add, [USB, PARB, PSB[yb]], [YFB])
                              tt(GE[:, :n], YF[:, :n], YF[:, :n], ALU.mult, [YFB], [GEB])
                              ts(GE[:, :n], GE[:, :n], 0.044715, 1.0, ALU.mult, ALU.add, [GEB], [GEB])
                              tt(GE[:, :n], GE[:, :n], YF[:, :n], ALU.mult, [GEB, YFB], [GEB])
                              act(GE[:, :n], GE[:, :n], AF.Sigmoid, [GEB], [GEB], scale=2.0 * math.sqrt(2.0 / math.pi))
                              tt(YS[:, oc, c0:c0 + n], GE[:, :n], YF[:, :n], ALU.mult, [GEB, YFB], [YSB])
                      P.dma("sp", O["o_ssm_p"][:, l, :], SF[:, 0, :], 25, reads=[SFB])
                      P.dma("sp", O["o_ssm_s"][:, l, :, :], SF[:, 1:17, :], 25, reads=[SFB])
                      for (c0, n, ti) in TILES:
                          for oc in range(2):
                              gb = 3 + oc
                              for kc in range(2):
                                  mm(PS[gb][:, :n], GLUW[:, kc, oc * 128:(oc + 1) * 128], YS[:, kc, c0:c0 + n], [SMB, YSB], [PSB[gb]],
                                     start=(kc == 0), stop=(kc == 1))
                              act(GE2[:, oc, :n], PS[gb][:, :n], AF.Sigmoid, [PSB[gb], PARB], [GEB2[oc]], bias=GLUB[:, l, oc:oc + 1])
                              tt(USSM[:, oc, c0:c0 + n], GE2[:, oc, :n], YS[:, oc, c0:c0 + n], ALU.mult, [GEB2[oc], YSB], [USB])
                      P.barrier()

                  MARKS.append(("L%d A3" % l, P.cnt["pe"]))
                  with contextlib.ExitStack() as s3:
                      sb3 = lambda n, s, d=F32: s3.enter_context(nc.sbuf_tensor("%s_%d" % (n, l), list(s), d))
                      WIN = sb3("WIN", [128, 8, INW], BF16)
                      WOUT = sb3("WOUT", [128, 8, 1024], BF16)
                      P.dma("pool", WIN[:, 0:4], I["w_in"][l, :, 0:4], 11, writes=[WINB])
                      P.dma("pool", WIN[:, 4:8], I["w_in"][l, :, 4:8], 11, writes=[WINB])
                      P.dma("pool", WOUT[:], I["w_out"][l], 21, writes=[WOUTB])
                      PRE = sb3("PRE", [128, 12, 3 + TW], BF16); PREBs = [Buf() for _ in range(12)]
                      PRES = PRE
                      QKV = sb3("QKV", [128, 12, TW], BF16); QKVB = Buf()
                      CV = sb3("CV", [128, TW]); CVB = Buf()
                      GT = sb3("GT", [128, 4, TW], BF16); GTB = Buf()
                      OO = sb3("OO", [128, 4, TW], BF16); OOB = Buf()
                      MIX = sb3("MIX", [128, 8, TW], BF16); MIXB = Buf()
                      MO = sb3("MO", [128, 8, TW]); MOB = Buf()
                      PX = sb3("PX", [128, 2, 320]); PXB = Buf()
                      PA = sb3("PA", [128, 320]); PBt = sb3("PBt", [128, 320]); PAB = Buf()
                      PQ = sb3("PQ", [128, 2, TW], BF16); PQB = Buf()
                      CTAIL = sb3("CTAIL", [128, 12, 3]); CTB = Buf()
                      SCS = sb3("SCS", [128, 12, 16, 3]); SCSB = Buf()
                      SST = sb3("SST", [128, 4, 128]); SSB = Buf()
                      SBF = sb3("SBF", [128, 4, 128], BF16)
                      NEGA = sb3("NEGA", [64, 4]); NGB = Buf()
                      ZT = sb3("ZT", [64, 8]); LGt = sb3("LGt", [64, 4]); BET = sb3("BET", [64, 4]); EGt = sb3("EGt", [64, 4]); EKD = sb3("EKD", [64, 4])
                      EGL = [sb3("EGL%d" % i, [128, 4]) for i in range(2)]; EGLB = [Buf(), Buf()]; SCB = Buf()
                      LG = sb3("LG", [64, 4, 64]); LGB = Buf()
                      E2 = sb3("E2", [64, 2, 4, 64]); E2B = Buf()
                      EGB = sb3("EGB", [128, 4, 64], BF16); EGBB = Buf()
                      QD = [sb3("QD%d" % i, [128, 4, 64], BF16) for i in range(2)]; QDB = [Buf(), Buf()]
                      KTK = sb3("KTK", [64, 4, 128], BF16); VTK = sb3("VTK", [64, 4, 128], BF16); TKB = Buf()
                      RHK = sb3("RHK", [64, 4, 128], BF16); VBt = sb3("VBt", [64, 4, 128], BF16); KDt = [sb3("KDt%d" % i, [64, 4, 128], BF16) for i in range(2)]; RKB = Buf(); KDB = [Buf(), Buf()]
                      AF_ = sb3("AF_", [64, 4, 64]); AFB = Buf()
                      PB_ = sb3("PB_", [64, 2, 4, 64], BF16); PBB = Buf()
                      QKT = [sb3("QKT%d" % i, [64, 4, 64], BF16) for i in range(2)]; QKB = [Buf(), Buf()]
                      TTb = sb3("TTb", [64, 4, 64], BF16); TTB = Buf()
                      KC = [sb3("KC%d" % i, [128, 4, 64], BF16) for i in range(2)]; KCB = [Buf(), Buf()]
                      WS = [sb3("WS%d" % i, [64, 4, 128]) for i in range(2)]; WSB = [Buf(), Buf()]
                      UU = sb3("UU", [64, 4, 128], BF16); UUB = Buf()
                      act(NEGA[:], DNP[:, l, 0, :], AF.Exp, [PARB], [NGB])
                      ts(NEGA[:], NEGA[:], -1.0, None, ALU.mult, None, [NGB], [NGB])
                      mset(CTAIL[:], 0.0, [CTB])
                      mset(PX[:, :, 0:15], 0.0, [PXB])
                      mset(SST[:], 0.0, [SSB]); mset(SBF[:], 0.0, [SSB])

                      for (c0, n, ti) in TILES:
                          samp = (ti == 8)
                          C = 4 if samp else 64
                          nch = n // C
                          hist = 3
                          norm_tile(c0, n, ti, 0, H, HB)
                          if not samp:
                              cp(PRE[:, :, 0:3], CTAIL[:], [CTB], PREBs)
                              pre_new = lambda blk: PRE[:, blk, 3:3 + n]
                          else:
                              prs = PRE[:, :, 0:112].rearrange("p k (b t) -> p k b t", t=7)
                              P.dma("sp", SCS[:], I["sconv"][:, l], 14, writes=[SCSB])
                              cp(prs[:, :, :, 0:3], SCS[:], [SCSB], PREBs)
                              pre_new = lambda blk: prs[:, blk, :, 3:7]

                          def conv_blk(blk):
                              if not samp:
                                  tap = lambda j: PRE[:, blk, j:j + n]
                                  cvv = CV[:, :n]
                              else:
                                  tap = lambda j: prs[:, blk, :, j:j + 4]
                                  cvv = CV[:, :n].rearrange("p (b t) -> p b t", t=4)
                              ts(cvv, tap(0), CW[:, l, blk, 0:1], None, ALU.mult, None, [PREBs[blk], PARB], [CVB])
                              for j in range(1, 4):
                                  stt(cvv, tap(j), CW[:, l, blk, j:j + 1], cvv, ALU.mult, ALU.add, [PREBs[blk], PARB, CVB], [CVB])
                              act(QKV[:, blk, :n], CV[:, :n], AF.Silu, [CVB], [QKVB])
                          for blk in range(12):
                              col = (blk // 4) * 512 + (blk % 4) * 128
                              bnk = 1 + (blk % 3)
                              for kc in range(8):
                                  mm(PS[bnk][:, :n], WIN[:, kc, col:col + 128], H[:, kc, :n], [WINB, HB], [PSB[bnk]], start=(kc == 0), stop=(kc == 7))
                              if not samp:
                                  cp(pre_new(blk), PS[bnk][:, :n], [PSB[bnk]], [PREBs[blk]], eng="act")
                              else:
                                  cp(pre_new(blk), PS[bnk][:, :n].rearrange("p (b t) -> p b t", t=4), [PSB[bnk]], [PREBs[blk]], eng="act")
                              if blk >= 1:
                                  conv_blk(blk - 1)
                          conv_blk(11)
                          if not samp:
                              cp(CTAIL[:], PRE[:, :, n:n + 3], PREBs, [CTB])
                              if ti == 7:
                                  P.dma("sp", O["o_conv_p"][:, l], CTAIL[:], 23, reads=[CTB])
                          else:
                              cp(SCS[:], prs[:, :, :, 4:7], PREBs, [SCSB])
                              P.dma("sp", O["o_conv_s"][:, l], SCS[:], 24, reads=[SCSB])
                          for hh in range(4):
                              col = OFF_G + hh * 128
                              bnk = 1 + (hh % 2)
                              for kc in range(8):
                                  mm(PS[bnk][:, :n], WIN[:, kc, col:col + 128], H[:, kc, :n], [WINB, HB], [PSB[bnk]], start=(kc == 0), stop=(kc == 7))
                              act(GT[:, hh, :n], PS[bnk][:, :n], AF.Silu, [PSB[bnk]], [GTB])
                          for blk in range(8):
                              bnk = 3 + (blk % 2)
                              act(SQ[:, blk, :n], QKV[:, blk, :n], AF.Square, [QKVB], [SQB])
                              mm(PS[bnk][:, :n], ONEB, SQ[:, blk, :n], [SQB, CB16B], [PSB[bnk]])
                              act(MO[:, blk, :n], PS[bnk][:, :n], AF.Ln, [PSB[bnk], PARB], [MOB], bias=EPS_AP, scale=1.0)
                          for blk in range(8):
                              act(MO[:, blk, :n], MO[:, blk, :n], AF.Exp, [MOB], [MOB], scale=-0.5, bias=(LNQ_AP if blk < 4 else 0.0))
                          for blk in range(8):
                              tt(QKV[:, blk, :n], QKV[:, blk, :n], MO[:, blk, :n], ALU.mult, [QKVB, MOB], [QKVB])
                          if l == 0: MARKS.append(("L0 t%d delta" % ti, P.cnt["pe"]))
                          def poolgen():
                              for oc in range(2):
                                  cp(MIX[:, 4 + oc, :n], USSM[:, oc, c0:c0 + n], [USB], [MIXB], eng="pool")
                              if samp:
                                  pxs = PX[:, :, 0:16 * 19].rearrange("p k (b t) -> p k b t", t=19)
                                  for k_ in range(2):
                                      P.dma("sp", pxs[:, k_, :, 0:15], I["spool"][:, l, k_], 15, writes=[PXB])
                              for k2 in range(2):
                                  col = OFF_POOL + k2 * 128
                                  for kc in range(8):
                                      mm(PS[1][:, :n], WIN[:, kc, col:col + 128], H[:, kc, :n], [WINB, HB], [PSB[1]], start=(kc == 0), stop=(kc == 7))
                                  if not samp:
                                      cp(PX[:, k2, 15:15 + n], PS[1][:, :n], [PSB[1]], [PXB], eng="act")
                                  else:
                                      cp(pxs[:, k2, :, 15:19], PS[1][:, :n].rearrange("p (b t) -> p b t", t=4), [PSB[1]], [PXB], eng="act")
                              yield
                              if samp:
                                  for k_ in range(2):
                                      P.dma("sp", O["o_pool_s_new"][:, l, k_], pxs[:, k_, :, 15:19], 27, reads=[PXB])
                                  P.dma("sp", O["o_pool_s_old"][l], I["spool_nat"][l, :, 4:15, :], 28)
                              elif ti == 7:
                                  P.dma("sp", O["o_pool_p"][:, l], PX[:, :, n:n + 15], 26, reads=[PXB])
                              for k2 in range(2):
                                  if not samp:
                                      W_ = 15 + n
                                      src = PX[:, k2, 0:W_]; a_ = PA[:, 0:W_]; b_ = PBt[:, 0:W_]
                                      sh = lambda v, s_: (v[:, s_:W_], v[:, 0:W_ - s_])
                                      full = lambda v: v
                                  else:
                                      W_ = 19
                                      src = pxs[:, k2]; a_ = PA[:, 0:16 * 19].rearrange("p (b t) -> p b t", t=19); b_ = PBt[:, 0:16 * 19].rearrange("p (b t) -> p b t", t=19)
                                      sh = lambda v, s_: (v[:, :, s_:W_], v[:, :, 0:W_ - s_])
                                      full = lambda v: v
                                  cp(a_, src, [PXB], [PAB], eng="pool")
                                  hi_, lo_ = sh(a_, 1); xh, xl = sh(src, 1)
                                  tt(hi_, xh, xl, ALU.add, [PXB, PAB], [PAB])
                                  cur, oth = a_, b_
                                  for d in range(1, 4):
                                      need_lo = (2 ** d) < (2 if k2 == 0 else 8)
                                      need_hi = (2 ** d) < (4 if k2 == 0 else 16)
                                      if not (need_lo or need_hi):
                                          break
                                      s_ = 2 ** d
                                      cp(oth, cur, [PAB], [PAB], eng="pool")
                                      oh, ol = sh(oth, s_); ch, cl = sh(cur, s_)
                                      if need_lo:
                                          tt(oh[0:64], ch[0:64], cl[0:64], ALU.add, [PAB], [PAB])
                                      if need_hi:
                                          tt(oh[64:128], ch[64:128], cl[64:128], ALU.add, [PAB], [PAB])
                                      cur, oth = oth, cur
                                  yield
                                  if not samp:
                                      wv = cur[:, 15:15 + n]; xv = PX[:, k2, 15:15 + n]; rv = oth[:, 15:15 + n]
                                  else:
                                      wv = cur[:, :, 15:19]; xv = pxs[:, k2, :, 15:19]; rv = oth[:, :, 15:19]
                                  for half in range(2):
                                      wsz = [2, 4, 8, 16][2 * k2 + half]
                                      pr = slice(64 * half, 64 * half + 64)
                                      stt(rv[pr], wv[pr], 1.0 / wsz, xv[pr], ALU.mult, ALU.subtract, [PAB, PXB], [PAB])
                                  if ti == 0:
                                      tt(rv[:, 0:16], wv[:, 0:16], CST[:, 6 + k2, 0:16], ALU.mult, [PAB, CSTB], [PAB])
                                      tt(rv[:, 0:16], rv[:, 0:16], xv[:, 0:16], ALU.subtract, [PAB, PXB], [PAB])
                                  if not samp:
                                      cp(PQ[:, k2, :n], rv, [PAB], [PQB])
                                  else:
                                      cp(PQ[:, k2, :n].rearrange("p (b t) -> p b t", t=4), rv, [PAB], [PQB])
                                  mm(PS[1][:, :n], PWT[:, k2, :], PQ[:, k2, :n], [SMB, PQB], [PSB[1]])
                                  ts(MIX[:, 6 + k2, :n], PS[1][:, :n], PSC[:, l, k2:k2 + 1], None, ALU.mult, None, [PSB[1], PARB], [MIXB])
                                  yield
                              if not samp:
                                  cp(PX[:, :, 0:15], PX[:, :, n:n + 15], [PXB], [PXB], eng="pool")

                          def prep(ci, sl):
                              a0 = ci * C
                              cols = slice(a0, a0 + C)
                              QDs, QKTs, KDs, KCs, WSs, EGLs = QD[sl], QKT[sl], KDt[sl], KC[sl], WS[sl], EGL[sl]
                              for kc in range(8):
                                  mm(PS[2][:C, 0:8], H[:, kc, cols], WIN[:, kc, OFF_A:OFF_A + 8], [HB, WINB], [PSB[2]], start=(kc == 0), stop=(kc == 7))
                              kt_ps = PT[:C, 0:512].rearrange("p (h d) -> p h d", h=4)
                              vt_ps = PT[:C, 512:1024].rearrange("p (h d) -> p h d", h=4)
                              for hh in range(4):
                                  tr(kt_ps[:, hh, :], QKV[:, 4 + hh, cols], IDB, [QKVB, CB16B], [PTB])
                                  tr(vt_ps[:, hh, :], QKV[:, 8 + hh, cols], IDB, [QKVB, CB16B], [PTB])
                              kq = PS[5][:C, :].rearrange("p (a h c) -> p a h c", a=2, h=4)
                              for hh in range(4):
                                  mm(kq[:, 0, hh, :C], QKV[:, 4 + hh, cols], QKV[:, 4 + hh, cols], [QKVB], [PSB[5]])
                                  mm(kq[:, 1, hh, :C], QKV[:, 4 + hh, cols], QKV[:, hh, cols], [QKVB], [PSB[5]])
                              cp(KTK[:C], kt_ps, [PTB], [TKB])
                              cp(VTK[:C], vt_ps, [PTB], [TKB])
                              yield
                              tt(ZT[:C, 0:4], PS[2][:C, 0:4], DNP[:C, l, 1, :], ALU.add, [PSB[2], PARB], [SCB])
                              act(BET[:C], PS[2][:C, 4:8], AF.Exp, [PSB[2]], [SCB], scale=-1.0)
                              act(BET[:C], BET[:C], AF.Ln, [SCB, PARB], [SCB], bias=EPSC[:C, 1:2])
                              act(BET[:C], BET[:C], AF.Exp, [SCB], [SCB], scale=-1.0)
                              act(ZT[:C, 0:4], ZT[:C, 0:4], AF.Exp, [SCB], [SCB])
                              act(ZT[:C, 0:4], ZT[:C, 0:4], AF.Ln, [SCB, PARB], [SCB], bias=EPSC[:C, 1:2])
                              tt(LGt[:C], ZT[:C, 0:4], NEGA[:C], ALU.mult, [SCB, NGB], [SCB])
                              tt(LG[:C, :, :C], LTRI[C][:C, None, :C].broadcast_to([C, 4, C]), LGt[:C, :, None].broadcast_to([C, 4, C]), ALU.mult,
                                 [CSTB, SCB], [LGB])
                              tt(VBt[:C], VTK[:C], BET[:C, :, None].broadcast_to([C, 4, 128]), ALU.mult, [TKB, SCB], [RKB])
                              tt(RHK[:C], KTK[:C], BET[:C, :, None].broadcast_to([C, 4, 128]), ALU.mult, [TKB, SCB], [RKB])
                              yield
                              d2 = PS[3][:C, :].rearrange("p (a h c) -> p a h c", a=2, h=4)
                              for hh in range(4):
                                  mm(d2[:, 0, hh, :C], LG[:C, hh, :C], MLT[C][:C, :C], [LGB, CSTB], [PSB[3]])
                              if C == 64:
                                  mm(d2[:, 1, :, :], MLT[C][:C, :C], LG[:C, :, :], [LGB, CSTB], [PSB[3]])
                              else:
                                  for hh in range(4):
                                      mm(d2[:, 1, hh, :C], MLT[C][:C, :C], LG[:C, hh, :C], [LGB, CSTB], [PSB[3]])
                              mm(PS[2][:C, 8:12], LTRI[C][:C, :C], LGt[:C, :], [CSTB, SCB], [PSB[2]])
                              mm(PS[2][:, 16:20], ONEF[:C, :], LGt[:C, :], [CSTB, SCB], [PSB[2]])
                              eg_ps = PS[4][:, 0:4 * C].rearrange("p (h c) -> p h c", h=4)
                              if C == 64:
                                  mm(eg_ps, ONEF[:C, :], LG[:C, :, :], [CSTB, LGB], [PSB[4]])
                              else:
                                  for hh in range(4):
                                      mm(eg_ps[:, hh, :], ONEF[:C, :], LG[:C, hh, :C], [CSTB, LGB], [PSB[4]])
                              act(E2[:C, :, :, :C], d2[:, :, :, :C], AF.Exp, [PSB[3]], [E2B])
                              act(EGt[:C], PS[2][:C, 8:12], AF.Exp, [PSB[2]], [SCB])
                              act(EGLs[:], PS[2][:, 16:20], AF.Exp, [PSB[2]], [EGLB[sl]])
                              cp(ZT[:C, 4:8], PS[2][:C, 8:12], [PSB[2]], [SCB])
                              tt(EKD[:C], PS[2][:C, 16:20], ZT[:C, 4:8], ALU.subtract, [PSB[2], SCB], [SCB])
                              act(EKD[:C], EKD[:C], AF.Exp, [SCB], [SCB])
                              act(EGB[:, :, :C], eg_ps, AF.Exp, [PSB[4]], [EGBB])
                              yield
                              tt(E2[:C, 0, :, :C], E2[:C, 0, :, :C], MSTR[C][:C, None, :C].broadcast_to([C, 4, C]), ALU.mult, [E2B, CSTB], [E2B])
                              tt(E2[:C, 1, :, :C], E2[:C, 1, :, :C], MINT[C][:C, None, :C].broadcast_to([C, 4, C]), ALU.mult, [E2B, CSTB], [E2B])
                              tt(AF_[:C, :, :C], kq[:, 0, :, :C], E2[:C, 0, :, :C], ALU.mult, [PSB[5], E2B], [AFB])
                              tt(AF_[:C, :, :C], AF_[:C, :, :C], BET[:C, :, None].broadcast_to([C, 4, C]), ALU.mult, [AFB, SCB], [AFB])
                              at_ps = PS[6][:C, 0:4 * C].rearrange("p (h c) -> p h c", h=4)
                              for hh in range(4):
                                  tr(at_ps[:, hh, :], AF_[:C, hh, :C], IDF[:C, :C], [AFB, CSTB], [PSB[6]])
                              tt(QKTs[:C, :, :C], kq[:, 1, :, :C], E2[:C, 1, :, :C], ALU.mult, [PSB[5], E2B], [QKB[sl]])
                              tt(QDs[:, :, :C], QKV[:, 0:4, cols], EGB[:, :, :C], ALU.mult, [QKVB, EGBB], [QDB[sl]])
                              tt(RHK[:C], RHK[:C], EGt[:C, :, None].broadcast_to([C, 4, 128]), ALU.mult, [RKB, SCB], [RKB])
                              tt(KDs[:C], KTK[:C], EKD[:C, :, None].broadcast_to([C, 4, 128]), ALU.mult, [TKB, SCB], [KDB[sl]])
                              cp(PB_[:C, 0, :, :C], AF_[:C, :, :C], [AFB], [PBB], eng="act")
                              cp(PB_[:C, 1, :, :C], at_ps, [PSB[6]], [PBB], eng="act")
                              tt(TTb[:C, :, :C], IDF[:C, None, :C].broadcast_to([C, 4, C]), at_ps, ALU.subtract, [CSTB, PSB[6]], [TTB])
                              yield
                              nlev = 5 if C == 64 else 1
                              for lev in range(nlev):
                                  p2 = PS[3][:C, :].rearrange("p (a h c) -> p a h c", a=2, h=4)
                                  lastlev = (lev == nlev - 1)
                                  for hh in range(4):
                                      mm(p2[:, 0, hh, :C], PB_[:C, 1, hh, :C], PB_[:C, 0, hh, :C], [PBB], [PSB[3]])
                                      if not lastlev:
                                          mm(p2[:, 1, hh, :C], PB_[:C, 0, hh, :C], PB_[:C, 1, hh, :C], [PBB], [PSB[3]])
                                  if lastlev:
                                      cp(PB_[:C, 0, :, :C], p2[:, 0, :, :C], [PSB[3]], [PBB], eng="act")
                                  else:
                                      cp(PB_[:C, :, :, :C], p2[:, :, :, :C], [PSB[3]], [PBB], eng="act")
                                  tu = PS[6][:C, 0:4 * C].rearrange("p (h c) -> p h c", h=4)
                                  for hh in range(4):
                                      mm(tu[:, hh, :], PB_[:C, 0, hh, :C], TTb[:C, hh, :C], [PBB, TTB], [PSB[6]])
                                  tt(TTb[:C, :, :C], TTb[:C, :, :C], tu, ALU.add, [TTB, PSB[6]], [TTB])
                                  yield
                              w_ps = PS[3][:C, :].rearrange("p (h d) -> p h d", h=4)
                              kc_ps = PS[4][:, 0:4 * C].rearrange("p (h c) -> p h c", h=4)
                              for hh in range(4):
                                  mm(w_ps[:, hh, :], TTb[:C, hh, :C], VBt[:C, hh, :], [TTB, RKB], [PSB[3]])
                                  mm(kc_ps[:, hh, :], RHK[:C, hh, :], TTb[:C, hh, :C], [TTB, RKB], [PSB[4]])
                              cp(WSs[:C], w_ps, [PSB[3]], [WSB[sl]], eng="act")
                              cp(KCs[:, :, :C], kc_ps, [PSB[4]], [KCB[sl]])

                          def step(ci, sl):
                              a0 = ci * C
                              cols = slice(a0, a0 + C)
                              QDs, QKTs, KDs, KCs, WSs, EGLs = QD[sl], QKT[sl], KDt[sl], KC[sl], WS[sl], EGL[sl]
                              if samp:
                                  P.dma("sp", SST[:], I["sdel"][l, ci], 13, writes=[SSB])
                                  cp(SBF[:], SST[:], [SSB], [SSB])
                              p1 = PS[0][:C, :].rearrange("p (h d) -> p h d", h=4)
                              for hh in range(4):
                                  mm(p1[:, hh, :], KCs[:, hh, :C], SBF[:, hh, :], [KCB[sl], SSB], [PSB[0]])
                              tt(UU[:C], WSs[:C], p1, ALU.subtract, [WSB[sl], PSB[0]], [UUB])
                              yield
                              o_ps = PS[1][:, 0:4 * C].rearrange("p (h c) -> p h c", h=4)
                              for hh in range(4):
                                  mm(o_ps[:, hh, :], SBF[:, hh, :], QDs[:, hh, :C], [SSB, QDB[sl]], [PSB[1]], start=True, stop=False)
                                  mm(o_ps[:, hh, :], UU[:C, hh, :], QKTs[:C, hh, :C], [UUB, QKB[sl]], [PSB[1]], start=False, stop=True)
                              ds = PS[0][:, :].rearrange("p (h d) -> p h d", h=4)
                              for hh in range(4):
                                  mm(ds[:, hh, :], KDs[:C, hh, :], UU[:C, hh, :], [KDB[sl], UUB], [PSB[0]])
                              tt(SST[:], SST[:], EGLs[:, :, None].broadcast_to([128, 4, 128]), ALU.mult, [SSB, EGLB[sl]], [SSB])
                              tt(SST[:], SST[:], ds, ALU.add, [SSB, PSB[0]], [SSB])
                              cp(SBF[:], SST[:], [SSB], [SSB])
                              cp(OO[:, :, cols], o_ps, [PSB[1]], [OOB], eng="act")
                              yield
                              if samp:
                                  P.dma("sp", O["o_delta_s"][l, ci], SST[:], 22, reads=[SSB])
                              elif ti == 7 and ci == nch - 1:
                                  P.dma("sp", O["o_delta_p"][l], SST[:], 22, reads=[SSB])

                          bg = [poolgen()]

                          def drive(gens):
                              gens = [g_ for g_ in gens if g_ is not None]
                              while gens:
                                  for g_ in list(gens):
                                      try:
                                          next(g_)
                                      except StopIteration:
                                          gens.remove(g_)
                                  for g_ in list(bg):
                                      try:
                                          next(g_)
                                      except StopIteration:
                                          bg.remove(g_)
                          drive([prep(0, 0)])
                          for ci in range(nch):
                              nxt = prep(ci + 1, (ci + 1) % 2) if ci + 1 < nch else None
                              drive([nxt, step(ci, ci % 2)])
                          while bg:
                              for g_ in list(bg):
                                  try:
                                      next(g_)
                                  except StopIteration:
                                      bg.remove(g_)
                          if l == 0: MARKS.append(("L0 t%d postdelta" % ti, P.cnt["pe"]))
                          for hh in range(4):
                              act(SQ[:, hh, :n], OO[:, hh, :n], AF.Square, [OOB], [SQB])
                          for hh in range(4):
                              mm(PS[2 + hh][:, :n], ONEB, SQ[:, hh, :n], [SQB, CB16B], [PSB[2 + hh]])
                          for hh in range(4):
                              act(MO[:, hh, :n], PS[2 + hh][:, :n], AF.Ln, [PSB[2 + hh], PARB], [MOB], bias=EPS_AP, scale=1.0 / 128)
                          for hh in range(4):
                              act(MO[:, hh, :n], MO[:, hh, :n], AF.Exp, [MOB], [MOB], scale=-0.5)
                          for hh in range(4):
                              stt(MO[:, 4 + hh, :n], OO[:, hh, :n], ONW[:, l:l + 1], MO[:, hh, :n], ALU.mult, ALU.mult, [OOB, PARB, MOB], [MOB])
                          for hh in range(4):
                              tt(MIX[:, hh, :n], MO[:, 4 + hh, :n], GT[:, hh, :n], ALU.mult, [MOB, GTB], [MIXB])
                          if dbg and l == 0:
                              P.dma("pool", O["dbgmix"][:, :, c0:c0 + n], MIX[:, :, :n], 29, reads=[MIXB])
                              P.dma("pool", O["dbgoo"][:, :, c0:c0 + n], OO[:, :, :n], 30, reads=[OOB])
                          for oc in range(8):
                              bnk = 1 + (oc % 3)
                              for kc in range(8):
                                  mm(PS[bnk][:, :n], WOUT[:, kc, oc * 128:(oc + 1) * 128], MIX[:, kc, :n], [WOUTB, MIXB], [PSB[bnk]], start=(kc == 0), stop=(kc == 7))
                              cp(MO[:, oc, :n], PS[bnk][:, :n], [PSB[bnk]], [MOB], eng=("act" if oc % 2 == 0 else "dve"))
                          rms_rstd(lambda kc: MO[:, kc, :n], n, 8, RS, RSB, [MOB], 1.0 / 1024, SQ, SQB, 0)
                          for kc in range(8):
                              tt(MO[:, kc, :n], MO[:, kc, :n], RS[:, :n], ALU.mult, [MOB, RSB], [MOB])
                              stt(X[:, kc, c0:c0 + n], MO[:, kc, :n], NW[:, l, 1, kc:kc + 1], X[:, kc, c0:c0 + n], ALU.mult, ALU.add,
                                  [MOB, PARB, XB[ti]], [XB[ti]])
                      P.barrier()
            MARKS.append(("L%d FFN" % l, P.cnt["pe"]))
            with contextlib.ExitStack() as sf:
              if ("F", l) not in skip:
                  sbf_ = lambda n, s, d=F32: sf.enter_context(nc.sbuf_tensor("%s_%d" % (n, l), list(s), d))
                  HW_ = 1088
                  H2 = sbf_("H2", [128, 8, HW_], BF16); H2B = Buf()
                  AV = sbf_("AV", [128, 22, HW_], BF16); AVB = Buf()
                  WG = [sbf_("WG%d" % i, [128, 2, 8, 128], BF16) for i in range(3)]; WGB = [Buf() for _ in range(3)]
                  WDA = sbf_("WDA", [128, 22, 1024], BF16); WDAB = Buf()
                  for a_ in range(0, 22, 6):
                      b_ = min(22, a_ + 6)
                      P.dma("pool", WDA[:, a_:b_, :], I["w_down"][l, :, a_:b_, :], 19, writes=[WDAB])
                  SQ2 = sbf_("SQ2", [128, 2, 512], BF16); SQ2B = [Buf(), Buf()]
                  RS2 = sbf_("RS2", [128, 512]); RS2B = Buf()
                  GS = sbf_("GS", [128, 2, 512], BF16); GSB = [Buf(), Buf()]
                  FO = sbf_("FO", [128, 8, 512], BF16); FOB = Buf()
                  unit = [0]
                  for half in range(2):
                      h0 = half * 1024
                      subt = [(0, 512), (512, 512)] if half == 0 else [(1024, 512), (1536, 512), (2048, 64)]
                      xbl = lambda c0, n: [XB[t_] for t_ in range(min(c0 // TW, 8), min((c0 + n - 1) // TW, 8) + 1)]
                      for (c0, n) in subt:
                          xbs = xbl(c0, n)
                          rms_rstd(lambda kc: X[:, kc, c0:c0 + n], n, 8, RS2, RS2B, xbs, 1.0 / 1024, SQ2, SQ2B, 0, ring=2)
                          for kc in range(8):
                              stt(H2[:, kc, c0 - h0:c0 - h0 + n], X[:, kc, c0:c0 + n], NW[:, l, 2, kc:kc + 1], RS2[:, :n], ALU.mult, ALU.mult,
                                  xbs + [RS2B, PARB], [H2B])
                      for fc in range(22):
                          sl = fc % 3
                          P.dma("pool", WG[sl][:], I["w_gu"][l, fc], 16 + sl, writes=[WGB[sl]])
                          for si, (c0, n) in enumerate(subt):
                              r0 = c0 - h0
                              u_ = unit[0] % 2
                              unit[0] += 1
                              bg, bu = 1 + 2 * u_, 2 + 2 * u_
                              for kc in range(8):
                                  mm(PS[bg][:, :n], WG[sl][:, 0, kc, :], H2[:, kc, r0:r0 + n], [WGB[sl], H2B], [PSB[bg]], start=(kc == 0), stop=(kc == 7))
                              for kc in range(8):
                                  mm(PS[bu][:, :n], WG[sl][:, 1, kc, :], H2[:, kc, r0:r0 + n], [WGB[sl], H2B], [PSB[bu]], start=(kc == 0), stop=(kc == 7))
                              act(GS[:, u_, :n], PS[bg][:, :n], AF.Silu, [PSB[bg]], [GSB[u_]])
                              tt(AV[:, fc, r0:r0 + n], GS[:, u_, :n], PS[bu][:, :n], ALU.mult, [GSB[u_], PSB[bu]], [AVB])
                      for si, (c0, n) in enumerate(subt):
                          r0 = c0 - h0
                          xbs = xbl(c0, n)
                          for ogrp in range(2):
                              for fc in range(22):
                                  for o4 in range(4):
                                      oo_ = (ogrp * 4 + o4) * 128
                                      mm(PS[1 + o4][:, :n], WDA[:, fc, oo_:oo_ + 128], AV[:, fc, r0:r0 + n],
                                         [WDAB, AVB], [PSB[1 + o4]], start=(fc == 0), stop=(fc == 21))
                              for o4 in range(4):
                                  cp(FO[:, ogrp * 4 + o4, :n], PS[1 + o4][:, :n], [PSB[1 + o4]], [FOB], eng="act")
                          rms_rstd(lambda kc: FO[:, kc, :n], n, 8, RS2, RS2B, [FOB], 1.0 / 1024, SQ2, SQ2B, 0, ring=2)
                          for kc in range(8):
                              tt(FO[:, kc, :n], FO[:, kc, :n], RS2[:, :n], ALU.mult, [FOB, RS2B], [FOB])
                              stt(X[:, kc, c0:c0 + n], FO[:, kc, :n], NW[:, l, 3, kc:kc + 1], X[:, kc, c0:c0 + n], ALU.mult, ALU.add,
                                  [FOB, PARB] + xbs, xbs)
                  P.barrier()
        for (c0, n, ti) in TILES:
            P.dma("sp", O["yT"][:, :, c0:c0 + n], X[:, :, c0:c0 + n], 1 + ti, reads=[XB[ti]])
        P.barrier()
        P.emit()
    return nc


def _consts():
    c = np.zeros((128, 8, 128), np.float32)
    idx = np.arange(128)
    c[:, 0, :] = np.eye(128, dtype=np.float32)
    c[:, 1, :] = 1.0
    c[:, 2, :] = (idx[:, None] <= idx[None, :])
    c[:, 3, :] = (idx[None, :] < idx[:, None])
    p = idx % 64
    ch = idx // 64
    c[:, 4, :] = ((p[:, None] == p[None, :]) & (ch[:, None] != ch[None, :]))
    c[:, 5, :] = np.where(ch == 0, -1.0, 1.0)[:, None]
    t = np.arange(16)
    for k2 in range(2):
        w = np.where(idx < 64, [2, 8][k2], [4, 16][k2]).astype(np.float32)
        c[:, 6 + k2, 0:16] = 1.0 / np.minimum(t[None, :] + 1.0, w[:, None])
    return c


def _prep_inputs(inp):
    f = lambda a: np.ascontiguousarray(np.asarray(a, dtype=np.float32))
    g = {k: np.asarray(v) for k, v in inp.items()}
    sh = {}
    sh["w_in"] = f(g["w_in"].reshape(L, 8, 128, INW).transpose(0, 2, 1, 3))
    sh["w_out"] = f(g["w_out"].reshape(L, 8, 128, 1024).transpose(0, 2, 1, 3))
    wg = g["ffn_w_gate"].reshape(L, 8, 128, 22, 128).transpose(0, 3, 2, 1, 4)
    wu = g["ffn_w_up"].reshape(L, 8, 128, 22, 128).transpose(0, 3, 2, 1, 4)
    sh["w_gu"] = f(np.stack([wg, wu], 3))
    sh["w_down"] = f(g["ffn_w_down"].reshape(L, 22, 128, 1024).transpose(0, 2, 1, 3))
    nw = np.stack([g["norm_mix_pre"], g["norm_mix_post"], g["norm_ffn_pre"], g["norm_ffn_post"]], 0)
    sh["nw"] = f(nw.reshape(4, L, 8, 128).transpose(3, 1, 0, 2))
    sh["cw"] = f(g["conv_w"].reshape(L, 4, 12, 128).transpose(3, 0, 2, 1))
    dnp = np.stack([g["dn_a_log"], g["dn_dt_bias"]], 1)
    sh["dnp"] = f(np.broadcast_to(dnp[None], (64, L, 2, 4)))
    sh["onw"] = f(g["dn_out_norm"].T)
    lam = np.stack([g["ssm_a_re"], g["ssm_a_im"], np.broadcast_to(g["ssm_log_dt"][:, :, None], (L, 16, 64))], -1)
    lam = lam.transpose(2, 0, 1, 3)
    sh["lamp"] = f(np.concatenate([lam, lam], 0))
    bre = g["ssm_b_re"].transpose(2, 0, 1, 3)
    bim = g["ssm_b_im"].transpose(2, 0, 1, 3)
    sh["bnat"] = f(np.concatenate([np.stack([bre, bim], 3), np.stack([bim, bre], 3)], 0))
    cre = g["ssm_c_re"].transpose(3, 0, 1, 2)
    cim = g["ssm_c_im"].transpose(3, 0, 1, 2)
    sh["cnat"] = f(np.concatenate([cre, cim], 0))
    sh["ssmd"] = f(g["ssm_d"].reshape(L, 2, 128).transpose(2, 0, 1))
    sh["glub"] = f(g["ssm_glu_b"].reshape(L, 2, 128).transpose(2, 0, 1))
    sh["pscale"] = f(g["pool_scale"].reshape(L, 2, 128).transpose(2, 0, 1))
    sh["gluw"] = f(g["ssm_glu_w"].reshape(L, 2, 128, 256).transpose(0, 2, 1, 3))
    pw = np.zeros((L, 128, 2, 128), np.float32)
    for k in range(2):
        for g2 in range(2):
            pw[:, g2 * 64:(g2 + 1) * 64, k, g2 * 64:(g2 + 1) * 64] = g["pool_w"][:, 2 * k + g2]
    sh["pw"] = pw
    sh["cst"] = _consts()
    maps = []
    for c in range(8):
        b0, b1 = 16 * c, 16 * c + 16
        m = dict(sh)
        xt = np.concatenate([g["x_prompt"][c], g["x_sample"][b0:b1].reshape(NS, 1024)], 0)
        m["xT"] = f(xt.T.reshape(8, 128, NT).transpose(1, 0, 2))
        m["sdel"] = f(g["state_delta"][:, b0:b1].transpose(0, 1, 3, 2, 4))
        m["sconv"] = f(g["state_conv"][:, b0:b1].reshape(L, 16, 3, 12, 128).transpose(4, 0, 3, 1, 2))
        m["spool"] = f(g["state_pool"][:, b0:b1].reshape(L, 16, 15, 2, 128).transpose(4, 0, 3, 1, 2))
        m["spool_nat"] = f(g["state_pool"][:, b0:b1])
        sre = g["state_ssm_re"][:, b0:b1].transpose(3, 0, 2, 1)
        sim = g["state_ssm_im"][:, b0:b1].transpose(3, 0, 2, 1)
        m["sssm"] = f(np.concatenate([sre, sim], 0))
        maps.append(m)
    return maps


def _assemble(res):
    yp = np.zeros((8, NP_, 1024), np.float32); ys = np.zeros((128, 4, 1024), np.float32)
    dp = np.zeros((L, 8, 4, 128, 128), np.float32); ds = np.zeros((L, 128, 4, 128, 128), np.float32)
    cvp = np.zeros((L, 8, 3, 1536), np.float32); cvs = np.zeros((L, 128, 3, 1536), np.float32)
    rp = np.zeros((L, 8, 16, 64), np.float32); ip = np.zeros((L, 8, 16, 64), np.float32)
    rs = np.zeros((L, 128, 16, 64), np.float32); is_ = np.zeros((L, 128, 16, 64), np.float32)
    pp = np.zeros((L, 8, 15, 256), np.float32); ps = np.zeros((L, 128, 15, 256), np.float32)
    for c in range(8):
        r = {k: np.asarray(v) for k, v in res[c].items()}
        b0, b1 = 16 * c, 16 * c + 16
        y = r["yT"].transpose(1, 0, 2).reshape(1024, NT).T
        yp[c] = y[:NP_]
        ys[b0:b1] = y[NP_:].reshape(16, 4, 1024)
        dp[:, c] = r["o_delta_p"].transpose(0, 2, 1, 3)
        ds[:, b0:b1] = r["o_delta_s"].transpose(0, 1, 3, 2, 4)
        cvp[:, c] = r["o_conv_p"].transpose(1, 3, 2, 0).reshape(L, 3, 1536)
        cvs[:, b0:b1] = r["o_conv_s"].transpose(1, 3, 4, 2, 0).reshape(L, 16, 3, 1536)
        sp_ = r["o_ssm_p"]
        rp[:, c] = sp_[:64].transpose(1, 2, 0)
        ip[:, c] = sp_[64:].transpose(1, 2, 0)
        ss = r["o_ssm_s"]
        rs[:, b0:b1] = ss[:64].transpose(1, 2, 3, 0)
        is_[:, b0:b1] = ss[64:].transpose(1, 2, 3, 0)
        pp[:, c] = r["o_pool_p"].transpose(1, 3, 2, 0).reshape(L, 15, 256)
        ps[:, b0:b1, 0:11] = r["o_pool_s_old"]
        ps[:, b0:b1, 11:15] = r["o_pool_s_new"].transpose(1, 3, 4, 2, 0).reshape(L, 16, 4, 256)
    return (yp, ys, dp, cvp, rp, ip, pp, ds, cvs, rs, is_, ps)


_NC_CACHE = {}


def kernel(**inputs):
    if "nc" not in _NC_CACHE:
        _NC_CACHE["nc"] = build_program()
    maps = _prep_inputs(inputs)
    res = run_bass_kernel_spmd(_NC_CACHE["nc"], maps, core_ids=list(range(8)))
    return _assemble(res.results)
```

```python
import contextlib
import math
import numpy as np
import concourse.bass as bass
import concourse.mybir as mybir
from concourse.bass_utils import run_bass_kernel_spmd

F32 = mybir.dt.float32
BF16 = mybir.dt.bfloat16
I32 = mybir.dt.int32
ALU = mybir.AluOpType
AF = mybir.ActivationFunctionType

L = 4
NP_, NS, NT = 2048, 64, 2112
TW = 256
EPS = 1e-6
INW = 2568
OFF_A, OFF_G, OFF_SSM, OFF_POOL = 1536, 1544, 2056, 2312
DFF = 2816
NKS = 11


class Buf:
    __slots__ = ("w", "r")

    def __init__(self):
        self.w = None
        self.r = []


class Prog:
    ENGS = ("pe", "act", "dve", "pool", "sp")

    def __init__(self, nc, stack, n_dma_sems=40):
        self.nc = nc
        self.sem = {e: stack.enter_context(nc.semaphore("s_" + e)) for e in self.ENGS}
        self.cnt = {e: 0 for e in self.ENGS}
        self.seen = {e: {} for e in self.ENGS}
        self.stream = {e: [] for e in self.ENGS}
        self.dsem = [stack.enter_context(nc.semaphore("d%d" % i)) for i in range(n_dma_sems)]
        self.dcnt = [0] * n_dma_sems

    def _need(self, eng, dep):
        if dep is None:
            return
        if dep[0] == "dma":
            key, val, sem = ("d", dep[1]), dep[2], self.dsem[dep[1]]
        else:
            e2, idx = dep
            if e2 == eng and (eng == "pe" or idx <= self.cnt[eng] - 2):
                return
            key, val, sem = ("e", e2), idx, self.sem[e2]
        if self.seen[eng].get(key, 0) >= val:
            return
        self.seen[eng][key] = val
        self.stream[eng].append(("wait", sem, val))

    def _deps(self, eng, reads, writes):
        deps = []
        for b in reads:
            deps.append(b.w)
        for b in writes:
            deps.append(b.w)
            deps.extend(b.r)
        best = {}
        for d in deps:
            if d is None:
                continue
            key = (d[0], d[1]) if d[0] == "dma" else ("e", d[0])
            val = d[2] if d[0] == "dma" else d[1]
            if key not in best or val > best[key][0]:
                best[key] = (val, d)
        for key in best:
            self._need(eng, best[key][1])

    def _mark(self, me, reads, writes):
        for b in reads:
            b.r.append(me)
            if len(b.r) > 64:
                b.r = b.r[-48:]
        for b in writes:
            b.w = me
            b.r = []

    def op(self, eng, fn, reads=(), writes=()):
        self._deps(eng, reads, writes)
        self.cnt[eng] += 1
        self.stream[eng].append(("inst", fn, self.sem[eng], 1))
        self._mark((eng, self.cnt[eng]), reads, writes)

    def dma(self, q, out, in_, semi, reads=(), writes=(), **kw):
        self._deps(q, reads, writes)
        self.dcnt[semi] += 16
        self.stream[q].append(("inst", lambda h: h.dma_start(out=out, in_=in_, **kw), self.dsem[semi], 16))
        self._mark(("dma", semi, self.dcnt[semi]), reads, writes)

    def barrier(self):
        for e in self.ENGS:
            for e2 in self.ENGS:
                if e2 != e and self.cnt[e2]:
                    self._need(e, (e2, self.cnt[e2]))
            for i, c in enumerate(self.dcnt):
                if c:
                    self._need(e, ("dma", i, c))

    def emit(self):
        nc = self.nc
        with nc.Block() as block:
            def run(e):
                def body(h):
                    for it in self.stream[e]:
                        if it[0] == "wait":
                            h.wait_ge(it[1], it[2])
                        else:
                            it[1](h).then_inc(it[2], it[3])
                return body
            block.tensor(run("pe"))
            block.scalar(run("act"))
            block.vector(run("dve"))
            block.gpsimd(run("pool"))
            block.sync(run("sp"))


MARKS = []


def build_program(NL=L, dbg=None, skip=()):
    nc = bass.Bass("TRN2", target_bir_lowering=False)
    din = lambda n, s, d=F32: nc.dram_tensor(n, list(s), d, kind="ExternalInput").ap()
    dout = lambda n, s, d=F32: nc.dram_tensor(n, list(s), d, kind="ExternalOutput").ap()
    I = {}
    I["xT"] = din("xT", [128, 8, NT])
    I["w_in"] = din("w_in", [L, 128, 8, INW])
    I["w_out"] = din("w_out", [L, 128, 8, 1024])
    I["w_gu"] = din("w_gu", [L, 22, 128, 2, 8, 128])
    I["w_down"] = din("w_down", [L, 128, 22, 1024])
    I["nw"] = din("nw", [128, L, 4, 8])
    I["cw"] = din("cw", [128, L, 12, 4])
    I["dnp"] = din("dnp", [64, L, 2, 4])
    I["onw"] = din("onw", [128, L])
    I["sdel"] = din("sdel", [L, 16, 128, 4, 128])
    I["sconv"] = din("sconv", [128, L, 12, 16, 3])
    I["spool"] = din("spool", [128, L, 2, 16, 15])
    I["spool_nat"] = din("spool_nat", [L, 16, 15, 256])
    I["lamp"] = din("lamp", [128, L, 16, 3])
    I["bnat"] = din("bnat", [128, L, 16, 2, 16])
    I["cnat"] = din("cnat", [128, L, 16, 16])
    I["sssm"] = din("sssm", [128, L, 16, 16])
    I["ssmd"] = din("ssmd", [128, L, 2])
    I["glub"] = din("glub", [128, L, 2])
    I["gluw"] = din("gluw", [L, 128, 2, 256])
    I["pw"] = din("pw", [L, 128, 2, 128])
    I["pscale"] = din("pscale", [128, L, 2])
    I["cst"] = din("cst", [128, 8, 128])
    O = {}
    O["yT"] = dout("yT", [128, 8, NT])
    O["o_delta_p"] = dout("o_delta_p", [L, 128, 4, 128])
    O["o_delta_s"] = dout("o_delta_s", [L, 16, 128, 4, 128])
    O["o_conv_p"] = dout("o_conv_p", [128, L, 12, 3])
    O["o_conv_s"] = dout("o_conv_s", [128, L, 12, 16, 3])
    O["o_ssm_p"] = dout("o_ssm_p", [128, L, 16])
    O["o_ssm_s"] = dout("o_ssm_s", [128, L, 16, 16])
    O["o_pool_p"] = dout("o_pool_p", [128, L, 2, 15])
    O["o_pool_s_old"] = dout("o_pool_s_old", [L, 16, 11, 256])
    O["o_pool_s_new"] = dout("o_pool_s_new", [128, L, 2, 16, 4])
    if dbg:
        O["dbgmix"] = dout("dbgmix", [128, 8, NT])
        O["dbgoo"] = dout("dbgoo", [128, 4, NT])

    with contextlib.ExitStack() as st:
        P = Prog(nc, st)
        sb = lambda n, s, d=F32: st.enter_context(nc.sbuf_tensor(n, list(s), d))
        PS = [st.enter_context(nc.psum_tensor("ps%d" % i, [128, 512], F32)) for i in range(7)]
        PSB = [Buf() for _ in range(7)]
        PT = st.enter_context(nc.psum_tensor("pt", [128, 1024], BF16))
        PTB = Buf()
        X = sb("X", [128, 8, NT]); XB = [Buf() for _ in range(9)]
        CST = sb("CST", [128, 8, 128]); CSTB = Buf()
        CB16 = sb("CB16", [128, 2, 128], BF16)
        NW = sb("NW", [128, L, 4, 8]); CW = sb("CW", [128, L, 12, 4]); DNP = sb("DNP", [64, L, 2, 4])
        ONW = sb("ONW", [128, L]); PSC = sb("PSC", [128, L, 2]); SSMD = sb("SSMD", [128, L, 2]); GLUB = sb("GLUB", [128, L, 2])
        PARB = Buf()
        EPSC = sb("EPSC", [128, 3])

        dmai = [0]

        def nsem(lo=22, hi=40):
            dmai[0] = (dmai[0] + 1) % (hi - lo)
            return lo + dmai[0]

        def mm(out, lhsT, rhs, rd, wr, start=True, stop=True):
            P.op("pe", lambda h: h.matmul(out, lhsT=lhsT, rhs=rhs, start=start, stop=stop), reads=rd, writes=wr)

        def tr(out, in_, ident, rd, wr):
            P.op("pe", lambda h: h.transpose(out, in_, ident), reads=rd, writes=wr)

        def act(out, in_, func, rd, wr, bias=0.0, scale=1.0, eng="act"):
            P.op("act", lambda h: h.activation(out, in_, func, bias=bias, scale=scale), reads=rd, writes=wr)

        def tt(out, a, b, op, rd, wr, eng="dve"):
            P.op(eng, lambda h: h.tensor_tensor(out, a, b, op), reads=rd, writes=wr)

        def ts(out, a, s1, s2, op0, op1, rd, wr, eng="dve"):
            if op1 is None:
                P.op(eng, lambda h: h.tensor_scalar(out, a, s1, None, op0), reads=rd, writes=wr)
            else:
                P.op(eng, lambda h: h.tensor_scalar(out, a, s1, s2, op0, op1), reads=rd, writes=wr)

        def stt(out, a, s, b, op0, op1, rd, wr, eng="dve"):
            P.op(eng, lambda h: h.scalar_tensor_tensor(out, a, s, b, op0, op1), reads=rd, writes=wr)

        def cp(out, in_, rd, wr, eng="dve"):
            if eng == "act":
                P.op("act", lambda h: h.copy(out, in_), reads=rd, writes=wr)
            else:
                P.op(eng, lambda h: h.tensor_copy(out, in_), reads=rd, writes=wr)

        def rcp(out, in_, rd, wr):
            P.op("dve", lambda h: h.reciprocal(out, in_), reads=rd, writes=wr)

        def mset(ap, v, wr, eng="pool"):
            P.op(eng, lambda h: h.memset(ap, v), writes=wr)

        P.dma("sp", CST[:], I["cst"][:], 20, writes=[CSTB])
        for t_, k_ in ((NW, "nw"), (CW, "cw"), (DNP, "dnp"), (ONW, "onw"), (PSC, "pscale"), (SSMD, "ssmd"), (GLUB, "glub")):
            P.dma("sp", t_[:], I[k_][:], 0, writes=[PARB])
        IDF = CST[:, 0, :]
        ONEF = CST[:, 1, :]
        LTRI = {64: CST[:, 2, :], 4: CST[:, 2, :]}
        MLT = {64: CST[:, 3, :], 4: CST[:, 3, :]}
        MSTR = {64: CST[:, 3, :], 4: CST[:, 3, :]}
        MINT = {64: CST[:, 2, :], 4: CST[:, 2, :]}
        SWP = CST[:, 4, :]
        SGN = CST[:, 5, 0:1]
        IDB = CB16[:, 0, :]
        ONEB = CB16[:, 1, :]
        CB16B = Buf()
        cp(CB16[:, 0, :], CST[:, 0, :], [CSTB], [CB16B])
        cp(CB16[:, 1, :], CST[:, 1, :], [CSTB], [CB16B])
        mset(EPSC[:, 0:1], EPS, [PARB], eng="dve")
        mset(EPSC[:, 1:2], 1.0, [PARB], eng="dve")
        mset(EPSC[:, 2:3], math.log(128.0 ** -0.5), [PARB], eng="dve")
        EPS_AP = EPSC[:, 0:1]
        LNQ_AP = EPSC[:, 2:3]

        TILES = [(i * TW, TW, i) for i in range(8)] + [(NP_, NS, 8)]
        for (c0, n, ti) in TILES:
            P.dma("sp", X[:, :, c0:c0 + n], I["xT"][:, :, c0:c0 + n], 1 + ti, writes=[XB[ti]])

        def rms_rstd(src_sq_fn, n, nk, rs_out, rsB, rd, inv_d, sq, sqB, bank, ring=None):
            if ring:
                for kc in range(nk):
                    act(sq[:, kc % ring, :n], src_sq_fn(kc), AF.Square, rd, [sqB[kc % ring]])
                    mm(PS[bank][:, :n], ONEB, sq[:, kc % ring, :n], [sqB[kc % ring], CB16B], [PSB[bank]], start=(kc == 0), stop=(kc == nk - 1))
            else:
                for kc in range(nk):
                    act(sq[:, kc, :n], src_sq_fn(kc), AF.Square, rd, [sqB])
                for kc in range(nk):
                    mm(PS[bank][:, :n], ONEB, sq[:, kc, :n], [sqB, CB16B], [PSB[bank]], start=(kc == 0), stop=(kc == nk - 1))
            act(rs_out[:, :n], PS[bank][:, :n], AF.Ln, [PSB[bank], PARB], [rsB], bias=EPS_AP, scale=inv_d)
            act(rs_out[:, :n], rs_out[:, :n], AF.Exp, [rsB], [rsB], scale=-0.5)

        for l in range(NL):
            last = (l == NL - 1)
            MARKS.append(("L%d start" % l, P.cnt["pe"]))
            with contextlib.ExitStack() as sa:
              if ("A", l) not in skip:
                  sba = lambda n, s, d=F32: sa.enter_context(nc.sbuf_tensor("%s_%d" % (n, l), list(s), d))
                  WINB = Buf(); WOUTB = Buf()
                  GLUW = sba("GLUW", [128, 2, 256], BF16); PWT = sba("PWT", [128, 2, 128], BF16); SMB = Buf()
                  USSM = sba("USSM", [128, 2, NT], BF16); USB = Buf()
                  H = sba("H", [128, 8, TW], BF16); HB = Buf()
                  SQ = sba("SQ", [128, 8, TW], BF16); SQB = Buf()
                  RS = sba("RS", [128, TW]); RSB = Buf()
                  P.dma("pool", GLUW[:], I["gluw"][l], 10, writes=[SMB])
                  P.dma("pool", PWT[:], I["pw"][l], 10, writes=[SMB])

                  def norm_tile(c0, n, ti, which, out, outB, rs=None, rsB=None, sq=None, sqB=None, bank=0):
                      rs = RS if rs is None else rs
                      rsB = RSB if rsB is None else rsB
                      sq = SQ if sq is None else sq
                      sqB = SQB if sqB is None else sqB
                      rms_rstd(lambda kc: X[:, kc, c0:c0 + n], n, 8, rs, rsB, [XB[ti]], 1.0 / 1024, sq, sqB, bank)
                      for kc in range(8):
                          stt(out[:, kc, :n], X[:, kc, c0:c0 + n], NW[:, l, which, kc:kc + 1], rs[:, :n], ALU.mult, ALU.mult,
                              [XB[ti], rsB, PARB], [outB])

                  s1 = contextlib.ExitStack()
                  WSSM = s1.enter_context(nc.sbuf_tensor("WSSM_%d" % l, [128, 8, 256], BF16)); WSB = Buf()
                  P.dma("pool", WSSM[:], I["w_in"][l, :, :, OFF_SSM:OFF_POOL], 31, writes=[WSB])
                  HA = s1.enter_context(nc.sbuf_tensor("HA_%d" % l, [128, 8, TW], BF16)); HAB = Buf()
                  SQA = s1.enter_context(nc.sbuf_tensor("SQA_%d" % l, [128, 8, TW], BF16)); SQAB = Buf()
                  RSA = s1.enter_context(nc.sbuf_tensor("RSA_%d" % l, [128, TW], F32)); RSAB = Buf()
                  for (c0, n, ti) in TILES:
                      if ti % 2 == 0:
                          h_, hb_ = H, HB
                          norm_tile(c0, n, ti, 0, H, HB)
                      else:
                          h_, hb_ = HA, HAB
                          norm_tile(c0, n, ti, 0, HA, HAB, rs=RSA, rsB=RSAB, sq=SQA, sqB=SQAB, bank=3)
                      pb = 1 if ti % 2 == 0 else 4
                      for oc in range(2):
                          for kc in range(8):
                              mm(PS[pb + oc][:, :n], WSSM[:, kc, oc * 128:(oc + 1) * 128], h_[:, kc, :n],
                                 [WSB, hb_], [PSB[pb + oc]], start=(kc == 0), stop=(kc == 7))
                          cp(USSM[:, oc, c0:c0 + n], PS[pb + oc][:, :n], [PSB[pb + oc]], [USB], eng=("act" if oc == 0 else "dve"))
                  P.barrier()
                  s1.close()

                  MARKS.append(("L%d A2 s5" % l, P.cnt["pe"]))
                  with contextlib.ExitStack() as s5:
                      sb5 = lambda n, s, d=F32: s5.enter_context(nc.sbuf_tensor("%s_%d" % (n, l), list(s), d))
                      YS = sb5("YS", [128, 2, NT], BF16); YSB = Buf()
                      LAMP = sb5("LAMP", [128, 16, 3]); BNAT = sb5("BNAT", [128, 16, 2, 16]); CNAT = sb5("CNAT", [128, 16, 16])
                      S0 = sb5("S0", [128, 16, 16]); LB = Buf()
                      P.dma("sp", LAMP[:], I["lamp"][:, l], 12, writes=[LB])
                      P.dma("sp", BNAT[:], I["bnat"][:, l], 12, writes=[LB])
                      P.dma("sp", CNAT[:], I["cnat"][:, l], 12, writes=[LB])
                      P.dma("sp", S0[:], I["sssm"][:, l], 12, writes=[LB])
                      DT = sb5("DT", [128, 16]); ZR = sb5("ZR", [128, 16]); FR = sb5("FR", [128, 16]); FI = sb5("FI", [128, 16, 2], I32)
                      T1 = sb5("T1", [128, 16]); T2 = sb5("T2", [128, 16]); T3 = sb5("T3", [128, 16])
                      AK = sb5("AK", [128, 16, NKS]); BK = sb5("BK", [128, 16, NKS]); TB = Buf()
                      act(DT[:], LAMP[:, :, 2], AF.Exp, [LB], [TB])
                      tt(ZR[:], LAMP[:, :, 0], DT[:], ALU.mult, [LB, TB], [TB])
                      stt(FR[:], LAMP[:, :, 1], 1.0 / (2 * math.pi), DT[:], ALU.mult, ALU.mult, [LB, TB], [TB])

                      def reduce_turns(dst, src):
                          cp(FI[:, :, 0], src, [TB], [TB])
                          cp(T3[:], FI[:, :, 0], [TB], [TB])
                          tt(dst, src, T3[:], ALU.subtract, [TB], [TB])

                      reduce_turns(FR[:], FR[:])
                      for k in range(NKS):
                          act(T1[:], ZR[:], AF.Exp, [TB], [TB], scale=float(2 ** k))
                          act(T2[:], FR[:], AF.Sin, [TB], [TB], scale=2 * math.pi)
                          tt(BK[:, :, k], T1[:], T2[:], ALU.mult, [TB], [TB])
                          ts(T2[:], FR[:], 0.25, None, ALU.add, None, [TB], [TB])
                          reduce_turns(T2[:], T2[:])
                          act(T2[:], T2[:], AF.Sin, [TB], [TB], scale=2 * math.pi)
                          tt(AK[:, :, k], T1[:], T2[:], ALU.mult, [TB], [TB])
                          if k < NKS - 1:
                              ts(FR[:], FR[:], 2.0, None, ALU.mult, None, [TB], [TB])
                              reduce_turns(FR[:], FR[:])
                      FRE = sb5("FRE", [128, 16]); FIM = sb5("FIM", [128, 16]); DEN = sb5("DEN", [128, 16]); AM1 = sb5("AM1", [128, 16])
                      are, aim = LAMP[:, :, 0], LAMP[:, :, 1]
                      tt(DEN[:], are, are, ALU.mult, [LB], [TB]); tt(T1[:], aim, aim, ALU.mult, [LB], [TB])
                      tt(DEN[:], DEN[:], T1[:], ALU.add, [TB], [TB]); rcp(DEN[:], DEN[:], [TB], [TB])
                      ts(AM1[:], AK[:, :, 0], -1.0, None, ALU.add, None, [TB], [TB])
                      tt(T1[:], AM1[:], are, ALU.mult, [TB, LB], [TB]); tt(T2[:], BK[:, :, 0], aim, ALU.mult, [TB, LB], [TB])
                      tt(T1[:], T1[:], T2[:], ALU.add, [TB], [TB]); tt(FRE[:], T1[:], DEN[:], ALU.mult, [TB], [TB])
                      tt(T1[:], BK[:, :, 0], are, ALU.mult, [TB, LB], [TB]); tt(T2[:], AM1[:], aim, ALU.mult, [TB, LB], [TB])
                      tt(T1[:], T1[:], T2[:], ALU.subtract, [TB], [TB]); tt(FIM[:], T1[:], DEN[:], ALU.mult, [TB], [TB])
                      ts(FIM[:], FIM[:], SGN, None, ALU.mult, None, [TB, CSTB], [TB])
                      BB = sb5("BB", [128, 16, 16]); BB2 = sb5("BB2", [128, 16, 16])
                      tt(BB[:], BNAT[:, :, 0, :], FRE[:, :, None].broadcast_to([128, 16, 16]), ALU.mult, [LB, TB], [TB])
                      tt(BB2[:], BNAT[:, :, 1, :], FIM[:, :, None].broadcast_to([128, 16, 16]), ALU.mult, [LB, TB], [TB])
                      tt(BB[:], BB[:], BB2[:], ALU.add, [TB], [TB])
                      CT = sb5("CT", [128, 16, 16])
                      ts(CT[:], CNAT[:], SGN, -1.0, ALU.mult, ALU.mult, [LB, CSTB], [TB])
                      BPAD = sb5("BPAD", [128, 128]); BPB = Buf()
                      BT = sb5("BT", [128, 8, 128], BF16); BTB = [Buf() for _ in range(8)]
                      CPAD = sb5("CPAD", [128, 8, 128], BF16); CPB = Buf()
                      RT = sb5("RT", [128, 8, NKS, 1, 128], BF16); RTB = [Buf() for _ in range(8)]
                      RFA = [sb5("RFA%d" % i, [128, NKS, 128]) for i in range(2)]; RFBt = [sb5("RFBt%d" % i, [128, NKS, 128]) for i in range(2)]
                      RFB = [Buf(), Buf()]
                      XE = sb5("XE", [128, 8, 1 + NP_ + 80], BF16); XEB = [Buf() for _ in range(8)]
                      YF2 = sb5("YF2", [128, 2, TW]); YFB2 = [Buf(), Buf()]
                      GE2 = sb5("GE2", [128, 2, TW]); GEB2 = [Buf(), Buf()]
                      SF = sb5("SF", [128, 17, 16]); SFB = Buf()
                      BS = sb5("BS", [128, 16, NKS]);
                      ts(BS[:], BK[:], SGN, -1.0, ALU.mult, ALU.mult, [TB, CSTB], [TB])
                      mset(BPAD[:], 0.0, [BPB]); mset(CPAD[:], 0.0, [CPB])
                      NE = 1 + NP_
                      for oc in range(2):
                          xesv = [XE[:, gi, NE:NE + 80].rearrange("p (b t) -> p b t", t=5) for gi in range(8)]
                          for gi in range(8):
                              g = oc * 8 + gi
                              cp(BPAD[:, gi * 16:(gi + 1) * 16], BB[:, g, :], [TB], [BPB])
                              tr(PS[2][:, 0:128], BPAD[:], IDF, [BPB, CSTB], [PSB[2]])
                              cp(BT[:, gi, :], PS[2][:, 0:128], [PSB[2]], [BTB[gi]])
                              mset(BPAD[:, gi * 16:(gi + 1) * 16], 0.0, [BPB])
                              cp(CPAD[:, gi, gi * 16:(gi + 1) * 16], CT[:, g, :], [TB], [CPB])
                              e_ = "dve" if (gi % 2 == 0) else "pool"
                              r1, r2, rb = RFA[gi % 2], RFBt[gi % 2], RFB[gi % 2]
                              tt(r1[:], SWP[:, None, :].broadcast_to([128, NKS, 128]), BS[:, g, :, None].broadcast_to([128, NKS, 128]), ALU.mult,
                                 [CSTB, TB], [rb], eng=e_)
                              tt(r2[:], IDF[:, None, :].broadcast_to([128, NKS, 128]), AK[:, g, :, None].broadcast_to([128, NKS, 128]), ALU.mult,
                                 [CSTB, TB], [rb], eng=e_)
                              tt(RT[:, gi, :, 0, :], r1[:], r2[:], ALU.add, [rb], [RTB[gi]], eng=e_)
                              mset(XE[:, gi, 0:1], 0.0, [XEB[gi]])
                              cp(xesv[gi][:, :, 0], S0[:, g, :], [LB], [XEB[gi]])
                          bk = [0]

                          def nbank():
                              bk[0] = (bk[0] + 1) % 6
                              return 1 + bk[0]
                          for (c0, n, ti) in TILES:
                              for gi in range(8):
                                  bnk = nbank()
                                  mm(PS[bnk][:, :n], BT[:, gi, :], USSM[:, oc, c0:c0 + n], [BTB[gi], USB], [PSB[bnk]])
                                  if ti < 8:
                                      cp(XE[:, gi, 1 + c0:1 + c0 + n], PS[bnk][:, :n], [PSB[bnk]], [XEB[gi]], eng="act")
                                  else:
                                      cp(xesv[gi][:, :, 1:5], PS[bnk][:, :n].rearrange("p (b t) -> p b t", t=4), [PSB[bnk]], [XEB[gi]], eng="act")
                          for k in range(NKS):
                              s = 2 ** k
                              hi = NE
                              while hi > s:
                                  lo = max(s, hi - 512)
                                  w = hi - lo
                                  for gi in range(8):
                                      bnk = nbank()
                                      if gi < 5:
                                          mm(PS[bnk][:, :w], RT[:, gi, k, 0, :], XE[:, gi, lo - s:hi - s], [RTB[gi], XEB[gi]], [PSB[bnk]], start=True, stop=False)
                                          mm(PS[bnk][:, :w], IDB, XE[:, gi, lo:hi], [CB16B, XEB[gi]], [PSB[bnk]], start=False, stop=True)
                                          cp(XE[:, gi, lo:hi], PS[bnk][:, :w], [PSB[bnk]], [XEB[gi]], eng=("act" if gi != 3 else "dve"))
                                      else:
                                          mm(PS[bnk][:, :w], RT[:, gi, k, 0, :], XE[:, gi, lo - s:hi - s], [RTB[gi], XEB[gi]], [PSB[bnk]])
                                          tt(XE[:, gi, lo:hi], XE[:, gi, lo:hi], PS[bnk][:, :w], ALU.add, [XEB[gi], PSB[bnk]], [XEB[gi]])
                                  hi = lo
                              if s < 5:
                                  w = 5 - s
                                  for gi in range(8):
                                      bnk = nbank()
                                      pso = PS[bnk][:, :16 * w].rearrange("p (b t) -> p b t", t=w)
                                      mm(pso, RT[:, gi, k, 0, :], xesv[gi][:, :, 0:w], [RTB[gi], XEB[gi]], [PSB[bnk]])
                                      tt(xesv[gi][:, :, s:5], xesv[gi][:, :, s:5], pso, ALU.add, [XEB[gi], PSB[bnk]], [XEB[gi]])
                          for gi in range(8):
                              g = oc * 8 + gi
                              cp(SF[:, 0, g:g + 1], XE[:, gi, NE - 1:NE], [XEB[gi]], [SFB])
                              cp(SF[:, 1:17, g], xesv[gi][:, :, 4], [XEB[gi]], [SFB])
                          if l == 0: MARKS.append(("L0 s5 y oc%d" % oc, P.cnt["pe"]))
                          for (c0, n, ti) in TILES:
                              pz = ti % 2
                              yb = 1 + pz
                              YF = YF2[:, pz, :]; YFB = YFB2[pz]
                              GE = GE2[:, pz, :]; GEB = GEB2[pz]
                              for gi in range(8):
                                  if ti < 8:
                                      rhs = XE[:, gi, 1 + c0:1 + c0 + n]
                                      mm(PS[yb][:, :n], CPAD[:, gi, :], rhs, [CPB, XEB[gi]], [PSB[yb]], start=(gi == 0), stop=(gi == 7))
                                  else:
                                      rhs = XE[:, gi, NE:NE + 80].rearrange("p (b t) -> p b t", t=5)[:, :, 1:5]
                                      mm(PS[yb][:, :n].rearrange("p (b t) -> p b t", t=4), CPAD[:, gi, :], rhs, [CPB, XEB[gi]], [PSB[yb]],
                                         start=(gi == 0), stop=(gi == 7))
                              stt(YF[:, :n], USSM[:, oc, c0:c0 + n], SSMD[:, l, oc:oc + 1], PS[yb][:, :n], ALU.mult, ALU.add, [USB, PARB, PSB[yb]], [YFB])
                              tt(GE[:, :n], YF[:, :n], YF[:, :n], ALU.mult, [YFB], [GEB])
                              ts(GE[:, :n], GE[:, :n], 0.044715, 1.0, ALU.mult, ALU.add, [GEB], [GEB])
                              tt(GE[:, :n], GE[:, :n], YF[:, :n], ALU.mult, [GEB, YFB], [GEB])
                              act(GE[:, :n], GE[:, :n], AF.Sigmoid, [GEB], [GEB], scale=2.0 * math.sqrt(2.0 / math.pi))
                              tt(YS[:, oc, c0:c0 + n], GE[:, :n], YF[:, :n], ALU.mult, [GEB, YFB], [YSB])
                      P.dma("sp", O["o_ssm_p"][:, l, :], SF[:, 0, :], 25, reads=[SFB])
                      P.dma("sp", O["o_ssm_s"][:, l, :, :], SF[:, 1:17, :], 25, reads=[SFB])
                      for (c0, n, ti) in TILES:
                          for oc in range(2):
                              gb = 3 + oc
                              for kc in range(2):
                                  mm(PS[gb][:, :n], GLUW[:, kc, oc * 128:(oc + 1) * 128], YS[:, kc, c0:c0 + n], [SMB, YSB], [PSB[gb]],
                                     start=(kc == 0), stop=(kc == 1))
                              act(GE2[:, oc, :n], PS[gb][:, :n], AF.Sigmoid, [PSB[gb], PARB], [GEB2[oc]], bias=GLUB[:, l, oc:oc + 1])
                              tt(USSM[:, oc, c0:c0 + n], GE2[:, oc, :n], YS[:, oc, c0:c0 + n], ALU.mult, [GEB2[oc], YSB], [USB])
                      P.barrier()

                  MARKS.append(("L%d A3" % l, P.cnt["pe"]))
                  with contextlib.ExitStack() as s3:
                      sb3 = lambda n, s, d=F32: s3.enter_context(nc.sbuf_tensor("%s_%d" % (n, l), list(s), d))
                      WIN = sb3("WIN", [128, 8, INW], BF16)
                      WOUT = sb3("WOUT", [128, 8, 1024], BF16)
                      P.dma("pool", WIN[:, 0:4], I["w_in"][l, :, 0:4], 11, writes=[WINB])
                      P.dma("pool", WIN[:, 4:8], I["w_in"][l, :, 4:8], 11, writes=[WINB])
                      P.dma("pool", WOUT[:], I["w_out"][l], 21, writes=[WOUTB])
                      PRE = sb3("PRE", [128, 12, 3 + TW], BF16); PREBs = [Buf() for _ in range(12)]
                      PRES = PRE
                      QKV = sb3("QKV", [128, 12, TW], BF16); QKVB = Buf()
                      CV = sb3("CV", [128, TW]); CVB = Buf()
                      GT = sb3("GT", [128, 4, TW], BF16); GTB = Buf()
                      OO = sb3("OO", [128, 4, TW], BF16); OOB = Buf()
                      MIX = sb3("MIX", [128, 8, TW], BF16); MIXB = Buf()
                      MO = sb3("MO", [128, 8, TW]); MOB = Buf()
                      PX = sb3("PX", [128, 2, 320]); PXB = Buf()
                      PA = sb3("PA", [128, 320]); PBt = sb3("PBt", [128, 320]); PAB = Buf()
                      PQ = sb3("PQ", [128, 2, TW], BF16); PQB = Buf()
                      CTAIL = sb3("CTAIL", [128, 12, 3]); CTB = Buf()
                      SCS = sb3("SCS", [128, 12, 16, 3]); SCSB = Buf()
                      SST = sb3("SST", [128, 4, 128]); SSB = Buf()
                      SBF = sb3("SBF", [128, 4, 128], BF16)
                      NEGA = sb3("NEGA", [64, 4]); NGB = Buf()
                      ZT = sb3("ZT", [64, 8]); LGt = sb3("LGt", [64, 4]); BET = sb3("BET", [64, 4]); EGt = sb3("EGt", [64, 4]); EKD = sb3("EKD", [64, 4])
                      EGL = [sb3("EGL%d" % i, [128, 4]) for i in range(2)]; EGLB = [Buf(), Buf()]; SCB = Buf()
                      LG = sb3("LG", [64, 4, 64]); LGB = Buf()
                      E2 = sb3("E2", [64, 2, 4, 64]); E2B = Buf()
                      EGB = sb3("EGB", [128, 4, 64], BF16); EGBB = Buf()
                      QD = [sb3("QD%d" % i, [128, 4, 64], BF16) for i in range(2)]; QDB = [Buf(), Buf()]
                      KTK = sb3("KTK", [64, 4, 128], BF16); VTK = sb3("VTK", [64, 4, 128], BF16); TKB = Buf()
                      RHK = sb3("RHK", [64, 4, 128], BF16); VBt = sb3("VBt", [64, 4, 128], BF16); KDt = [sb3("KDt%d" % i, [64, 4, 128], BF16) for i in range(2)]; RKB = Buf(); KDB = [Buf(), Buf()]
                      AF_ = sb3("AF_", [64, 4, 64]); AFB = Buf()
                      PB_ = sb3("PB_", [64, 2, 4, 64], BF16); PBB = Buf()
                      QKT = [sb3("QKT%d" % i, [64, 4, 64], BF16) for i in range(2)]; QKB = [Buf(), Buf()]
                      TTb = sb3("TTb", [64, 4, 64], BF16); TTB = Buf()
                      KC = [sb3("KC%d" % i, [128, 4, 64], BF16) for i in range(2)]; KCB = [Buf(), Buf()]
                      WS = [sb3("WS%d" % i, [64, 4, 128]) for i in range(2)]; WSB = [Buf(), Buf()]
                      UU = sb3("UU", [64, 4, 128], BF16); UUB = Buf()
                      act(NEGA[:], DNP[:, l, 0, :], AF.Exp, [PARB], [NGB])
                      ts(NEGA[:], NEGA[:], -1.0, None, ALU.mult, None, [NGB], [NGB])
                      mset(CTAIL[:], 0.0, [CTB])
                      mset(PX[:, :, 0:15], 0.0, [PXB])
                      mset(SST[:], 0.0, [SSB]); mset(SBF[:], 0.0, [SSB])

                      for (c0, n, ti) in TILES:
                          samp = (ti == 8)
                          C = 4 if samp else 64
                          nch = n // C
                          hist = 3
                          norm_tile(c0, n, ti, 0, H, HB)
                          if not samp:
                              cp(PRE[:, :, 0:3], CTAIL[:], [CTB], PREBs)
                              pre_new = lambda blk: PRE[:, blk, 3:3 + n]
                          else:
                              prs = PRE[:, :, 0:112].rearrange("p k (b t) -> p k b t", t=7)
                              P.dma("sp", SCS[:], I["sconv"][:, l], 14, writes=[SCSB])
                              cp(prs[:, :, :, 0:3], SCS[:], [SCSB], PREBs)
                              pre_new = lambda blk: prs[:, blk, :, 3:7]

                          def conv_blk(blk):
                              if not samp:
                                  tap = lambda j: PRE[:, blk, j:j + n]
                                  cvv = CV[:, :n]
                              else:
                                  tap = lambda j: prs[:, blk, :, j:j + 4]
                                  cvv = CV[:, :n].rearrange("p (b t) -> p b t", t=4)
                              ts(cvv, tap(0), CW[:, l, blk, 0:1], None, ALU.mult, None, [PREBs[blk], PARB], [CVB])
                              for j in range(1, 4):
                                  stt(cvv, tap(j), CW[:, l, blk, j:j + 1], cvv, ALU.mult, ALU.add, [PREBs[blk], PARB, CVB], [CVB])
                              act(QKV[:, blk, :n], CV[:, :n], AF.Silu, [CVB], [QKVB])
                          for blk in range(12):
                              col = (blk // 4) * 512 + (blk % 4) * 128
                              bnk = 1 + (blk % 3)
                              for kc in range(8):
                                  mm(PS[bnk][:, :n], WIN[:, kc, col:col + 128], H[:, kc, :n], [WINB, HB], [PSB[bnk]], start=(kc == 0), stop=(kc == 7))
                              if not samp:
                                  cp(pre_new(blk), PS[bnk][:, :n], [PSB[bnk]], [PREBs[blk]], eng="act")
                              else:
                                  cp(pre_new(blk), PS[bnk][:, :n].rearrange("p (b t) -> p b t", t=4), [PSB[bnk]], [PREBs[blk]], eng="act")
                              if blk >= 1:
                                  conv_blk(blk - 1)
                          conv_blk(11)
                          if not samp:
                              cp(CTAIL[:], PRE[:, :, n:n + 3], PREBs, [CTB])
                              if ti == 7:
                                  P.dma("sp", O["o_conv_p"][:, l], CTAIL[:], 23, reads=[CTB])
                          else:
                              cp(SCS[:], prs[:, :, :, 4:7], PREBs, [SCSB])
                              P.dma("sp", O["o_conv_s"][:, l], SCS[:], 24, reads=[SCSB])
                          for hh in range(4):
                              col = OFF_G + hh * 128
                              bnk = 1 + (hh % 2)
                              for kc in range(8):
                                  mm(PS[bnk][:, :n], WIN[:, kc, col:col + 128], H[:, kc, :n], [WINB, HB], [PSB[bnk]], start=(kc == 0), stop=(kc == 7))
                              act(GT[:, hh, :n], PS[bnk][:, :n], AF.Silu, [PSB[bnk]], [GTB])
                          for blk in range(8):
                              bnk = 3 + (blk % 2)
                              act(SQ[:, blk, :n], QKV[:, blk, :n], AF.Square, [QKVB], [SQB])
                              mm(PS[bnk][:, :n], ONEB, SQ[:, blk, :n], [SQB, CB16B], [PSB[bnk]])
                              act(MO[:, blk, :n], PS[bnk][:, :n], AF.Ln, [PSB[bnk], PARB], [MOB], bias=EPS_AP, scale=1.0)
                          for blk in range(8):
                              act(MO[:, blk, :n], MO[:, blk, :n], AF.Exp, [MOB], [MOB], scale=-0.5, bias=(LNQ_AP if blk < 4 else 0.0))
                          for blk in range(8):
                              tt(QKV[:, blk, :n], QKV[:, blk, :n], MO[:, blk, :n], ALU.mult, [QKVB, MOB], [QKVB])
                          if l == 0: MARKS.append(("L0 t%d delta" % ti, P.cnt["pe"]))
                          def poolgen():
                              for oc in range(2):
                                  cp(MIX[:, 4 + oc, :n], USSM[:, oc, c0:c0 + n], [USB], [MIXB], eng="pool")
                              if samp:
                                  pxs = PX[:, :, 0:16 * 19].rearrange("p k (b t) -> p k b t", t=19)
                                  for k_ in range(2):
                                      P.dma("sp", pxs[:, k_, :, 0:15], I["spool"][:, l, k_], 15, writes=[PXB])
                              for k2 in range(2):
                                  col = OFF_POOL + k2 * 128
                                  for kc in range(8):
                                      mm(PS[1][:, :n], WIN[:, kc, col:col + 128], H[:, kc, :n], [WINB, HB], [PSB[1]], start=(kc == 0), stop=(kc == 7))
                                  if not samp:
                                      cp(PX[:, k2, 15:15 + n], PS[1][:, :n], [PSB[1]], [PXB], eng="act")
                                  else:
                                      cp(pxs[:, k2, :, 15:19], PS[1][:, :n].rearrange("p (b t) -> p b t", t=4), [PSB[1]], [PXB], eng="act")
                              yield
                              if samp:
                                  for k_ in range(2):
                                      P.dma("sp", O["o_pool_s_new"][:, l, k_], pxs[:, k_, :, 15:19], 27, reads=[PXB])
                                  P.dma("sp", O["o_pool_s_old"][l], I["spool_nat"][l, :, 4:15, :], 28)
                              elif ti == 7:
                                  P.dma("sp", O["o_pool_p"][:, l], PX[:, :, n:n + 15], 26, reads=[PXB])
                              for k2 in range(2):
                                  if not samp:
                                      W_ = 15 + n
                                      src = PX[:, k2, 0:W_]; a_ = PA[:, 0:W_]; b_ = PBt[:, 0:W_]
                                      sh = lambda v, s_: (v[:, s_:W_], v[:, 0:W_ - s_])
                                      full = lambda v: v
                                  else:
                                      W_ = 19
                                      src = pxs[:, k2]; a_ = PA[:, 0:16 * 19].rearrange("p (b t) -> p b t", t=19); b_ = PBt[:, 0:16 * 19].rearrange("p (b t) -> p b t", t=19)
                                      sh = lambda v, s_: (v[:, :, s_:W_], v[:, :, 0:W_ - s_])
                                      full = lambda v: v
                                  cp(a_, src, [PXB], [PAB], eng="pool")
                                  hi_, lo_ = sh(a_, 1); xh, xl = sh(src, 1)
                                  tt(hi_, xh, xl, ALU.add, [PXB, PAB], [PAB])
                                  cur, oth = a_, b_
                                  for d in range(1, 4):
                                      need_lo = (2 ** d) < (2 if k2 == 0 else 8)
                                      need_hi = (2 ** d) < (4 if k2 == 0 else 16)
                                      if not (need_lo or need_hi):
                                          break
                                      s_ = 2 ** d
                                      cp(oth, cur, [PAB], [PAB], eng="pool")
                                      oh, ol = sh(oth, s_); ch, cl = sh(cur, s_)
                                      if need_lo:
                                          tt(oh[0:64], ch[0:64], cl[0:64], ALU.add, [PAB], [PAB])
                                      if need_hi:
                                          tt(oh[64:128], ch[64:128], cl[64:128], ALU.add, [PAB], [PAB])
                                      cur, oth = oth, cur
                                  yield
                                  if not samp:
                                      wv = cur[:, 15:15 + n]; xv = PX[:, k2, 15:15 + n]; rv = oth[:, 15:15 + n]
                                  else:
                                      wv = cur[:, :, 15:19]; xv = pxs[:, k2, :, 15:19]; rv = oth[:, :, 15:19]
                                  for half in range(2):
                                      wsz = [2, 4, 8, 16][2 * k2 + half]
                                      pr = slice(64 * half, 64 * half + 64)
                                      stt(rv[pr], wv[pr], 1.0 / wsz, xv[pr], ALU.mult, ALU.subtract, [PAB, PXB], [PAB])
                                  if ti == 0:
                                      tt(rv[:, 0:16], wv[:, 0:16], CST[:, 6 + k2, 0:16], ALU.mult, [PAB, CSTB], [PAB])
                                      tt(rv[:, 0:16], rv[:, 0:16], xv[:, 0:16], ALU.subtract, [PAB, PXB], [PAB])
                                  if not samp:
                                      cp(PQ[:, k2, :n], rv, [PAB], [PQB])
                                  else:
                                      cp(PQ[:, k2, :n].rearrange("p (b t) -> p b t", t=4), rv, [PAB], [PQB])
                                  mm(PS[1][:, :n], PWT[:, k2, :], PQ[:, k2, :n], [SMB, PQB], [PSB[1]])
                                  ts(MIX[:, 6 + k2, :n], PS[1][:, :n], PSC[:, l, k2:k2 + 1], None, ALU.mult, None, [PSB[1], PARB], [MIXB])
                                  yield
                              if not samp:
                                  cp(PX[:, :, 0:15], PX[:, :, n:n + 15], [PXB], [PXB], eng="pool")

                          def prep(ci, sl):
                              a0 = ci * C
                              cols = slice(a0, a0 + C)
                              QDs, QKTs, KDs, KCs, WSs, EGLs = QD[sl], QKT[sl], KDt[sl], KC[sl], WS[sl], EGL[sl]
                              tr(PS[2][:C, 0:8], CV[0:8, cols], IDF[0:8, 0:8], [CVB, CSTB], [PSB[2]])
                              kt_ps = PT[:C, 0:512].rearrange("p (h d) -> p h d", h=4)
                              vt_ps = PT[:C, 512:1024].rearrange("p (h d) -> p h d", h=4)
                              for hh in range(4):
                                  tr(kt_ps[:, hh, :], QKV[:, 4 + hh, cols], IDB, [QKVB, CB16B], [PTB])
                                  tr(vt_ps[:, hh, :], QKV[:, 8 + hh, cols], IDB, [QKVB, CB16B], [PTB])
                              kq = PS[5][:C, :].rearrange("p (a h c) -> p a h c", a=2, h=4)
                              for hh in range(4):
                                  mm(kq[:, 0, hh, :C], QKV[:, 4 + hh, cols], QKV[:, 4 + hh, cols], [QKVB], [PSB[5]])
                                  mm(kq[:, 1, hh, :C], QKV[:, 4 + hh, cols], QKV[:, hh, cols], [QKVB], [PSB[5]])
                              cp(KTK[:C], kt_ps, [PTB], [TKB])
                              cp(VTK[:C], vt_ps, [PTB], [TKB])
                              yield
                              tt(ZT[:C, 0:4], PS[2][:C, 0:4], DNP[:C, l, 1, :], ALU.add, [PSB[2], PARB], [SCB])
                              act(BET[:C], PS[2][:C, 4:8], AF.Exp, [PSB[2]], [SCB], scale=-1.0)
                              act(BET[:C], BET[:C], AF.Ln, [SCB, PARB], [SCB], bias=EPSC[:C, 1:2])
                              act(BET[:C], BET[:C], AF.Exp, [SCB], [SCB], scale=-1.0)
                              act(ZT[:C, 0:4], ZT[:C, 0:4], AF.Exp, [SCB], [SCB])
                              act(ZT[:C, 0:4], ZT[:C, 0:4], AF.Ln, [SCB, PARB], [SCB], bias=EPSC[:C, 1:2])
                              tt(LGt[:C], ZT[:C, 0:4], NEGA[:C], ALU.mult, [SCB, NGB], [SCB])
                              tt(LG[:C, :, :C], LTRI[C][:C, None, :C].broadcast_to([C, 4, C]), LGt[:C, :, None].broadcast_to([C, 4, C]), ALU.mult,
                                 [CSTB, SCB], [LGB])
                              tt(VBt[:C], VTK[:C], BET[:C, :, None].broadcast_to([C, 4, 128]), ALU.mult, [TKB, SCB], [RKB])
                              tt(RHK[:C], KTK[:C], BET[:C, :, None].broadcast_to([C, 4, 128]), ALU.mult, [TKB, SCB], [RKB])
                              yield
                              d2 = PS[3][:C, :].rearrange("p (a h c) -> p a h c", a=2, h=4)
                              for hh in range(4):
                                  mm(d2[:, 0, hh, :C], LG[:C, hh, :C], MLT[C][:C, :C], [LGB, CSTB], [PSB[3]])
                              if C == 64:
                                  mm(d2[:, 1, :, :], MLT[C][:C, :C], LG[:C, :, :], [LGB, CSTB], [PSB[3]])
                              else:
                                  for hh in range(4):
                                      mm(d2[:, 1, hh, :C], MLT[C][:C, :C], LG[:C, hh, :C], [LGB, CSTB], [PSB[3]])
                              mm(PS[2][:C, 8:12], LTRI[C][:C, :C], LGt[:C, :], [CSTB, SCB], [PSB[2]])
                              mm(PS[2][:, 16:20], ONEF[:C, :], LGt[:C, :], [CSTB, SCB], [PSB[2]])
                              eg_ps = PS[4][:, 0:4 * C].rearrange("p (h c) -> p h c", h=4)
                              if C == 64:
                                  mm(eg_ps, ONEF[:C, :], LG[:C, :, :], [CSTB, LGB], [PSB[4]])
                              else:
                                  for hh in range(4):
                                      mm(eg_ps[:, hh, :], ONEF[:C, :], LG[:C, hh, :C], [CSTB, LGB], [PSB[4]])
                              act(E2[:C, :, :, :C], d2[:, :, :, :C], AF.Exp, [PSB[3]], [E2B])
                              act(EGt[:C], PS[2][:C, 8:12], AF.Exp, [PSB[2]], [SCB])
                              act(EGLs[:], PS[2][:, 16:20], AF.Exp, [PSB[2]], [EGLB[sl]])
                              cp(ZT[:C, 4:8], PS[2][:C, 8:12], [PSB[2]], [SCB])
                              tt(EKD[:C], PS[2][:C, 16:20], ZT[:C, 4:8], ALU.subtract, [PSB[2], SCB], [SCB])
                              act(EKD[:C], EKD[:C], AF.Exp, [SCB], [SCB])
                              act(EGB[:, :, :C], eg_ps, AF.Exp, [PSB[4]], [EGBB])
                              yield
                              tt(E2[:C, 0, :, :C], E2[:C, 0, :, :C], MSTR[C][:C, None, :C].broadcast_to([C, 4, C]), ALU.mult, [E2B, CSTB], [E2B])
                              tt(E2[:C, 1, :, :C], E2[:C, 1, :, :C], MINT[C][:C, None, :C].broadcast_to([C, 4, C]), ALU.mult, [E2B, CSTB], [E2B])
                              tt(AF_[:C, :, :C], kq[:, 0, :, :C], E2[:C, 0, :, :C], ALU.mult, [PSB[5], E2B], [AFB])
                              tt(AF_[:C, :, :C], AF_[:C, :, :C], BET[:C, :, None].broadcast_to([C, 4, C]), ALU.mult, [AFB, SCB], [AFB])
                              at_ps = PS[6][:C, 0:4 * C].rearrange("p (h c) -> p h c", h=4)
                              for hh in range(4):
                                  tr(at_ps[:, hh, :], AF_[:C, hh, :C], IDF[:C, :C], [AFB, CSTB], [PSB[6]])
                              tt(QKTs[:C, :, :C], kq[:, 1, :, :C], E2[:C, 1, :, :C], ALU.mult, [PSB[5], E2B], [QKB[sl]])
                              tt(QDs[:, :, :C], QKV[:, 0:4, cols], EGB[:, :, :C], ALU.mult, [QKVB, EGBB], [QDB[sl]])
                              tt(RHK[:C], RHK[:C], EGt[:C, :, None].broadcast_to([C, 4, 128]), ALU.mult, [RKB, SCB], [RKB])
                              tt(KDs[:C], KTK[:C], EKD[:C, :, None].broadcast_to([C, 4, 128]), ALU.mult, [TKB, SCB], [KDB[sl]])
                              cp(PB_[:C, 0, :, :C], AF_[:C, :, :C], [AFB], [PBB], eng="act")
                              cp(PB_[:C, 1, :, :C], at_ps, [PSB[6]], [PBB], eng="act")
                              tt(TTb[:C, :, :C], IDF[:C, None, :C].broadcast_to([C, 4, C]), at_ps, ALU.subtract, [CSTB, PSB[6]], [TTB])
                              yield
                              nlev = 5 if C == 64 else 1
                              for lev in range(nlev):
                                  p2 = PS[3][:C, :].rearrange("p (a h c) -> p a h c", a=2, h=4)
                                  lastlev = (lev == nlev - 1)
                                  for hh in range(4):
                                      mm(p2[:, 0, hh, :C], PB_[:C, 1, hh, :C], PB_[:C, 0, hh, :C], [PBB], [PSB[3]])
                                      if not lastlev:
                                          mm(p2[:, 1, hh, :C], PB_[:C, 0, hh, :C], PB_[:C, 1, hh, :C], [PBB], [PSB[3]])
                                  if lastlev:
                                      cp(PB_[:C, 0, :, :C], p2[:, 0, :, :C], [PSB[3]], [PBB], eng="act")
                                  else:
                                      cp(PB_[:C, :, :, :C], p2[:, :, :, :C], [PSB[3]], [PBB], eng="act")
                                  tu = PS[6][:C, 0:4 * C].rearrange("p (h c) -> p h c", h=4)
                                  for hh in range(4):
                                      mm(tu[:, hh, :], PB_[:C, 0, hh, :C], TTb[:C, hh, :C], [PBB, TTB], [PSB[6]])
                                  tt(TTb[:C, :, :C], TTb[:C, :, :C], tu, ALU.add, [TTB, PSB[6]], [TTB])
                                  yield
                              w_ps = PS[3][:C, :].rearrange("p (h d) -> p h d", h=4)
                              kc_ps = PS[4][:, 0:4 * C].rearrange("p (h c) -> p h c", h=4)
                              for hh in range(4):
                                  mm(w_ps[:, hh, :], TTb[:C, hh, :C], VBt[:C, hh, :], [TTB, RKB], [PSB[3]])
                                  mm(kc_ps[:, hh, :], RHK[:C, hh, :], TTb[:C, hh, :C], [TTB, RKB], [PSB[4]])
                              cp(WSs[:C], w_ps, [PSB[3]], [WSB[sl]], eng="act")
                              cp(KCs[:, :, :C], kc_ps, [PSB[4]], [KCB[sl]])

                          def step(ci, sl):
                              a0 = ci * C
                              cols = slice(a0, a0 + C)
                              QDs, QKTs, KDs, KCs, WSs, EGLs = QD[sl], QKT[sl], KDt[sl], KC[sl], WS[sl], EGL[sl]
                              if samp:
                                  P.dma("sp", SST[:], I["sdel"][l, ci], 13, writes=[SSB])
                                  cp(SBF[:], SST[:], [SSB], [SSB])
                              p1 = PS[0][:C, :].rearrange("p (h d) -> p h d", h=4)
                              for hh in range(4):
                                  mm(p1[:, hh, :], KCs[:, hh, :C], SBF[:, hh, :], [KCB[sl], SSB], [PSB[0]])
                              tt(UU[:C], WSs[:C], p1, ALU.subtract, [WSB[sl], PSB[0]], [UUB])
                              yield
                              o_ps = PS[1][:, 0:4 * C].rearrange("p (h c) -> p h c", h=4)
                              for hh in range(4):
                                  mm(o_ps[:, hh, :], SBF[:, hh, :], QDs[:, hh, :C], [SSB, QDB[sl]], [PSB[1]], start=True, stop=False)
                                  mm(o_ps[:, hh, :], UU[:C, hh, :], QKTs[:C, hh, :C], [UUB, QKB[sl]], [PSB[1]], start=False, stop=True)
                              ds = PS[0][:, :].rearrange("p (h d) -> p h d", h=4)
                              for hh in range(4):
                                  mm(ds[:, hh, :], KDs[:C, hh, :], UU[:C, hh, :], [KDB[sl], UUB], [PSB[0]])
                              tt(SST[:], SST[:], EGLs[:, :, None].broadcast_to([128, 4, 128]), ALU.mult, [SSB, EGLB[sl]], [SSB])
                              tt(SST[:], SST[:], ds, ALU.add, [SSB, PSB[0]], [SSB])
                              cp(SBF[:], SST[:], [SSB], [SSB])
                              cp(OO[:, :, cols], o_ps, [PSB[1]], [OOB], eng="act")
                              yield
                              if samp:
                                  P.dma("sp", O["o_delta_s"][l, ci], SST[:], 22, reads=[SSB])
                              elif ti == 7 and ci == nch - 1:
                                  P.dma("sp", O["o_delta_p"][l], SST[:], 22, reads=[SSB])

                          for kc in range(8):
                              mm(PS[2][0:8, :n], WIN[:, kc, OFF_A:OFF_A + 8], H[:, kc, :n], [WINB, HB], [PSB[2]], start=(kc == 0), stop=(kc == 7))
                          cp(CV[0:8, :n], PS[2][0:8, :n], [PSB[2]], [CVB], eng="act")
                          bg = [poolgen()]

                          def drive(gens):
                              gens = [g_ for g_ in gens if g_ is not None]
                              while gens:
                                  for g_ in list(gens):
                                      try:
                                          next(g_)
                                      except StopIteration:
                                          gens.remove(g_)
                                  for g_ in list(bg):
                                      try:
                                          next(g_)
                                      except StopIteration:
                                          bg.remove(g_)
                          drive([prep(0, 0)])
                          for ci in range(nch):
                              nxt = prep(ci + 1, (ci + 1) % 2) if ci + 1 < nch else None
                              drive([nxt, step(ci, ci % 2)])
                          while bg:
                              for g_ in list(bg):
                                  try:
                                      next(g_)
                                  except StopIteration:
                                      bg.remove(g_)
                          if l == 0: MARKS.append(("L0 t%d postdelta" % ti, P.cnt["pe"]))
                          for hh in range(4):
                              act(SQ[:, hh, :n], OO[:, hh, :n], AF.Square, [OOB], [SQB])
                          for hh in range(4):
                              mm(PS[2 + hh][:, :n], ONEB, SQ[:, hh, :n], [SQB, CB16B], [PSB[2 + hh]])
                          for hh in range(4):
                              act(MO[:, hh, :n], PS[2 + hh][:, :n], AF.Ln, [PSB[2 + hh], PARB], [MOB], bias=EPS_AP, scale=1.0 / 128)
                          for hh in range(4):
                              act(MO[:, hh, :n], MO[:, hh, :n], AF.Exp, [MOB], [MOB], scale=-0.5)
                          for hh in range(4):
                              stt(MO[:, 4 + hh, :n], OO[:, hh, :n], ONW[:, l:l + 1], MO[:, hh, :n], ALU.mult, ALU.mult, [OOB, PARB, MOB], [MOB])
                          for hh in range(4):
                              tt(MIX[:, hh, :n], MO[:, 4 + hh, :n], GT[:, hh, :n], ALU.mult, [MOB, GTB], [MIXB])
                          if dbg and l == 0:
                              P.dma("pool", O["dbgmix"][:, :, c0:c0 + n], MIX[:, :, :n], 29, reads=[MIXB])
                              P.dma("pool", O["dbgoo"][:, :, c0:c0 + n], OO[:, :, :n], 30, reads=[OOB])
                          for oc in range(8):
                              bnk = 1 + (oc % 3)
                              for kc in range(8):
                                  mm(PS[bnk][:, :n], WOUT[:, kc, oc * 128:(oc + 1) * 128], MIX[:, kc, :n], [WOUTB, MIXB], [PSB[bnk]], start=(kc == 0), stop=(kc == 7))
                              cp(MO[:, oc, :n], PS[bnk][:, :n], [PSB[bnk]], [MOB], eng=("act" if oc % 2 == 0 else "dve"))
                          rms_rstd(lambda kc: MO[:, kc, :n], n, 8, RS, RSB, [MOB], 1.0 / 1024, SQ, SQB, 0)
                          for kc in range(8):
                              tt(MO[:, kc, :n], MO[:, kc, :n], RS[:, :n], ALU.mult, [MOB, RSB], [MOB])
                              stt(X[:, kc, c0:c0 + n], MO[:, kc, :n], NW[:, l, 1, kc:kc + 1], X[:, kc, c0:c0 + n], ALU.mult, ALU.add,
                                  [MOB, PARB, XB[ti]], [XB[ti]])
                      P.barrier()
            MARKS.append(("L%d FFN" % l, P.cnt["pe"]))
            with contextlib.ExitStack() as sf:
              if ("F", l) not in skip:
                  sbf_ = lambda n, s, d=F32: sf.enter_context(nc.sbuf_tensor("%s_%d" % (n, l), list(s), d))
                  HW_ = 1088
                  H2 = sbf_("H2", [128, 8, HW_], BF16); H2B = Buf()
                  AV = sbf_("AV", [128, 22, HW_], BF16); AVB = Buf()
                  WG = [sbf_("WG%d" % i, [128, 2, 8, 128], BF16) for i in range(3)]; WGB = [Buf() for _ in range(3)]
                  WDA = sbf_("WDA", [128, 22, 1024], BF16); WDAB = Buf()
                  for a_ in range(0, 22, 6):
                      b_ = min(22, a_ + 6)
                      P.dma("pool", WDA[:, a_:b_, :], I["w_down"][l, :, a_:b_, :], 19, writes=[WDAB])
                  SQ2 = sbf_("SQ2", [128, 2, 512], BF16); SQ2B = [Buf(), Buf()]
                  RS2 = sbf_("RS2", [128, 512]); RS2B = Buf()
                  GS = sbf_("GS", [128, 2, 512], BF16); GSB = [Buf(), Buf()]
                  FO = sbf_("FO", [128, 8, 512], BF16); FOB = Buf()
                  unit = [0]
                  for half in range(2):
                      h0 = half * 1024
                      subt = [(0, 512), (512, 512)] if half == 0 else [(1024, 512), (1536, 512), (2048, 64)]
                      xbl = lambda c0, n: [XB[t_] for t_ in range(min(c0 // TW, 8), min((c0 + n - 1) // TW, 8) + 1)]
                      for (c0, n) in subt:
                          xbs = xbl(c0, n)
                          rms_rstd(lambda kc: X[:, kc, c0:c0 + n], n, 8, RS2, RS2B, xbs, 1.0 / 1024, SQ2, SQ2B, 0, ring=2)
                          for kc in range(8):
                              stt(H2[:, kc, c0 - h0:c0 - h0 + n], X[:, kc, c0:c0 + n], NW[:, l, 2, kc:kc + 1], RS2[:, :n], ALU.mult, ALU.mult,
                                  xbs + [RS2B, PARB], [H2B])
                      for fc in range(22):
                          sl = fc % 3
                          P.dma("pool", WG[sl][:], I["w_gu"][l, fc], 16 + sl, writes=[WGB[sl]])
                          for si, (c0, n) in enumerate(subt):
                              r0 = c0 - h0
                              u_ = unit[0] % 2
                              unit[0] += 1
                              bg, bu = 1 + 2 * u_, 2 + 2 * u_
                              for kc in range(8):
                                  mm(PS[bg][:, :n], WG[sl][:, 0, kc, :], H2[:, kc, r0:r0 + n], [WGB[sl], H2B], [PSB[bg]], start=(kc == 0), stop=(kc == 7))
                              for kc in range(8):
                                  mm(PS[bu][:, :n], WG[sl][:, 1, kc, :], H2[:, kc, r0:r0 + n], [WGB[sl], H2B], [PSB[bu]], start=(kc == 0), stop=(kc == 7))
                              act(GS[:, u_, :n], PS[bg][:, :n], AF.Silu, [PSB[bg]], [GSB[u_]])
                              tt(AV[:, fc, r0:r0 + n], GS[:, u_, :n], PS[bu][:, :n], ALU.mult, [GSB[u_], PSB[bu]], [AVB])
                      for si, (c0, n) in enumerate(subt):
                          r0 = c0 - h0
                          xbs = xbl(c0, n)
                          for ogrp in range(2):
                              for fc in range(22):
                                  for o4 in range(4):
                                      oo_ = (ogrp * 4 + o4) * 128
                                      mm(PS[1 + o4][:, :n], WDA[:, fc, oo_:oo_ + 128], AV[:, fc, r0:r0 + n],
                                         [WDAB, AVB], [PSB[1 + o4]], start=(fc == 0), stop=(fc == 21))
                              for o4 in range(4):
                                  cp(FO[:, ogrp * 4 + o4, :n], PS[1 + o4][:, :n], [PSB[1 + o4]], [FOB], eng="act")
                          rms_rstd(lambda kc: FO[:, kc, :n], n, 8, RS2, RS2B, [FOB], 1.0 / 1024, SQ2, SQ2B, 0, ring=2)
                          for kc in range(8):
                              tt(FO[:, kc, :n], FO[:, kc, :n], RS2[:, :n], ALU.mult, [FOB, RS2B], [FOB])
                              stt(X[:, kc, c0:c0 + n], FO[:, kc, :n], NW[:, l, 3, kc:kc + 1], X[:, kc, c0:c0 + n], ALU.mult, ALU.add,
                                  [FOB, PARB] + xbs, xbs)
                  P.barrier()
        for (c0, n, ti) in TILES:
            P.dma("sp", O["yT"][:, :, c0:c0 + n], X[:, :, c0:c0 + n], 1 + ti, reads=[XB[ti]])
        P.barrier()
        P.emit()
    return nc


def _consts():
    c = np.zeros((128, 8, 128), np.float32)
    idx = np.arange(128)
    c[:, 0, :] = np.eye(128, dtype=np.float32)
    c[:, 1, :] = 1.0
    c[:, 2, :] = (idx[:, None] <= idx[None, :])
    c[:, 3, :] = (idx[None, :] < idx[:, None])
    p = idx % 64
    ch = idx // 64
    c[:, 4, :] = ((p[:, None] == p[None, :]) & (ch[:, None] != ch[None, :]))
    c[:, 5, :] = np.where(ch == 0, -1.0, 1.0)[:, None]
    t = np.arange(16)
    for k2 in range(2):
        w = np.where(idx < 64, [2, 8][k2], [4, 16][k2]).astype(np.float32)
        c[:, 6 + k2, 0:16] = 1.0 / np.minimum(t[None, :] + 1.0, w[:, None])
    return c


def _prep_inputs(inp):
    f = lambda a: np.ascontiguousarray(np.asarray(a, dtype=np.float32))
    g = {k: np.asarray(v) for k, v in inp.items()}
    sh = {}
    sh["w_in"] = f(g["w_in"].reshape(L, 8, 128, INW).transpose(0, 2, 1, 3))
    sh["w_out"] = f(g["w_out"].reshape(L, 8, 128, 1024).transpose(0, 2, 1, 3))
    wg = g["ffn_w_gate"].reshape(L, 8, 128, 22, 128).transpose(0, 3, 2, 1, 4)
    wu = g["ffn_w_up"].reshape(L, 8, 128, 22, 128).transpose(0, 3, 2, 1, 4)
    sh["w_gu"] = f(np.stack([wg, wu], 3))
    sh["w_down"] = f(g["ffn_w_down"].reshape(L, 22, 128, 1024).transpose(0, 2, 1, 3))
    nw = np.stack([g["norm_mix_pre"], g["norm_mix_post"], g["norm_ffn_pre"], g["norm_ffn_post"]], 0)
    sh["nw"] = f(nw.reshape(4, L, 8, 128).transpose(3, 1, 0, 2))
    sh["cw"] = f(g["conv_w"].reshape(L, 4, 12, 128).transpose(3, 0, 2, 1))
    dnp = np.stack([g["dn_a_log"], g["dn_dt_bias"]], 1)
    sh["dnp"] = f(np.broadcast_to(dnp[None], (64, L, 2, 4)))
    sh["onw"] = f(g["dn_out_norm"].T)
    lam = np.stack([g["ssm_a_re"], g["ssm_a_im"], np.broadcast_to(g["ssm_log_dt"][:, :, None], (L, 16, 64))], -1)
    lam = lam.transpose(2, 0, 1, 3)
    sh["lamp"] = f(np.concatenate([lam, lam], 0))
    bre = g["ssm_b_re"].transpose(2, 0, 1, 3)
    bim = g["ssm_b_im"].transpose(2, 0, 1, 3)
    sh["bnat"] = f(np.concatenate([np.stack([bre, bim], 3), np.stack([bim, bre], 3)], 0))
    cre = g["ssm_c_re"].transpose(3, 0, 1, 2)
    cim = g["ssm_c_im"].transpose(3, 0, 1, 2)
    sh["cnat"] = f(np.concatenate([cre, cim], 0))
    sh["ssmd"] = f(g["ssm_d"].reshape(L, 2, 128).transpose(2, 0, 1))
    sh["glub"] = f(g["ssm_glu_b"].reshape(L, 2, 128).transpose(2, 0, 1))
    sh["pscale"] = f(g["pool_scale"].reshape(L, 2, 128).transpose(2, 0, 1))
    sh["gluw"] = f(g["ssm_glu_w"].reshape(L, 2, 128, 256).transpose(0, 2, 1, 3))
    pw = np.zeros((L, 128, 2, 128), np.float32)
    for k in range(2):
        for g2 in range(2):
            pw[:, g2 * 64:(g2 + 1) * 64, k, g2 * 64:(g2 + 1) * 64] = g["pool_w"][:, 2 * k + g2]
    sh["pw"] = pw
    sh["cst"] = _consts()
    maps = []
    for c in range(8):
        b0, b1 = 16 * c, 16 * c + 16
        m = dict(sh)
        xt = np.concatenate([g["x_prompt"][c], g["x_sample"][b0:b1].reshape(NS, 1024)], 0)
        m["xT"] = f(xt.T.reshape(8, 128, NT).transpose(1, 0, 2))
        m["sdel"] = f(g["state_delta"][:, b0:b1].transpose(0, 1, 3, 2, 4))
        m["sconv"] = f(g["state_conv"][:, b0:b1].reshape(L, 16, 3, 12, 128).transpose(4, 0, 3, 1, 2))
        m["spool"] = f(g["state_pool"][:, b0:b1].reshape(L, 16, 15, 2, 128).transpose(4, 0, 3, 1, 2))
        m["spool_nat"] = f(g["state_pool"][:, b0:b1])
        sre = g["state_ssm_re"][:, b0:b1].transpose(3, 0, 2, 1)
        sim = g["state_ssm_im"][:, b0:b1].transpose(3, 0, 2, 1)
        m["sssm"] = f(np.concatenate([sre, sim], 0))
        maps.append(m)
    return maps


def _assemble(res):
    yp = np.zeros((8, NP_, 1024), np.float32); ys = np.zeros((128, 4, 1024), np.float32)
    dp = np.zeros((L, 8, 4, 128, 128), np.float32); ds = np.zeros((L, 128, 4, 128, 128), np.float32)
    cvp = np.zeros((L, 8, 3, 1536), np.float32); cvs = np.zeros((L, 128, 3, 1536), np.float32)
    rp = np.zeros((L, 8, 16, 64), np.float32); ip = np.zeros((L, 8, 16, 64), np.float32)
    rs = np.zeros((L, 128, 16, 64), np.float32); is_ = np.zeros((L, 128, 16, 64), np.float32)
    pp = np.zeros((L, 8, 15, 256), np.float32); ps = np.zeros((L, 128, 15, 256), np.float32)
    for c in range(8):
        r = {k: np.asarray(v) for k, v in res[c].items()}
        b0, b1 = 16 * c, 16 * c + 16
        y = r["yT"].transpose(1, 0, 2).reshape(1024, NT).T
        yp[c] = y[:NP_]
        ys[b0:b1] = y[NP_:].reshape(16, 4, 1024)
        dp[:, c] = r["o_delta_p"].transpose(0, 2, 1, 3)
        ds[:, b0:b1] = r["o_delta_s"].transpose(0, 1, 3, 2, 4)
        cvp[:, c] = r["o_conv_p"].transpose(1, 3, 2, 0).reshape(L, 3, 1536)
        cvs[:, b0:b1] = r["o_conv_s"].transpose(1, 3, 4, 2, 0).reshape(L, 16, 3, 1536)
        sp_ = r["o_ssm_p"]
        rp[:, c] = sp_[:64].transpose(1, 2, 0)
        ip[:, c] = sp_[64:].transpose(1, 2, 0)
        ss = r["o_ssm_s"]
        rs[:, b0:b1] = ss[:64].transpose(1, 2, 3, 0)
        is_[:, b0:b1] = ss[64:].transpose(1, 2, 3, 0)
        pp[:, c] = r["o_pool_p"].transpose(1, 3, 2, 0).reshape(L, 15, 256)
        ps[:, b0:b1, 0:11] = r["o_pool_s_old"]
        ps[:, b0:b1, 11:15] = r["o_pool_s_new"].transpose(1, 3, 4, 2, 0).reshape(L, 16, 4, 256)
    return (yp, ys, dp, cvp, rp, ip, pp, ds, cvs, rs, is_, ps)


_NC_CACHE = {}


def kernel(**inputs):
    if "nc" not in _NC_CACHE:
        _NC_CACHE["nc"] = build_program()
    maps = _prep_inputs(inputs)
    res = run_bass_kernel_spmd(_NC_CACHE["nc"], maps, core_ids=list(range(8)))
    return _assemble(res.results)
```
